# Optimizing a Trainium2 kernel written in Bass

```python
import math
import jax, jax.numpy as jnp
from jax import lax
import numpy as np

D_MODEL = 1024
BATCH = 8
SEQ = 2048
DEPTH = 4

CTX_LEN = 256
GRID_W = 64
D_SSM = 512
SSM_CH = 16
SSM_GROUPS = D_SSM // SSM_CH
SSM_STATE = 64
D_SGU = D_MODEL - D_SSM
SGU_HEADS = 4
SGU_HEAD_DIM = D_SGU // SGU_HEADS
CHUNK = 128
D_FF = 2816
CONV_K = 3
EPS = 1e-6
D_IN = D_SSM + 2 * D_SGU

kernel_name = "hybrid_s5_gmlp_convffn_prefix_dit"


def rmsnorm(x, g):
    xf = x.astype(jnp.float32)
    y = xf * lax.rsqrt(jnp.mean(jnp.square(xf), axis=-1, keepdims=True) + EPS)
    return (y * g.astype(jnp.float32)).astype(x.dtype)


def modulate(h, shift, scale):
    return h * (1.0 + scale) + shift


def _combine(left, right):
    a_l, b_l = left
    a_r, b_r = right
    return a_r * a_l, a_r * b_l + b_r


def diag_scan(a_bar, bu, reverse):
    a = jnp.broadcast_to(a_bar, bu.shape)
    _, h = lax.associative_scan(_combine, (a, bu), reverse=reverse, axis=1)
    return h


def s5_bidirectional(u_ctx, u_lat, a_re, a_im, b_re, b_im, c_re, c_im, log_dt, d_skip, need_ctx):
    f32 = jnp.float32
    n_b, l_ctx, _ = u_ctx.shape
    l_lat = u_lat.shape[1]
    uc = u_ctx.astype(f32)
    ul = u_lat.astype(f32)
    uc_g = uc.reshape(n_b, l_ctx, SSM_GROUPS, SSM_CH).astype(jnp.complex64)
    ul_g = ul.reshape(n_b, l_lat, SSM_GROUPS, SSM_CH).astype(jnp.complex64)
    d = d_skip.astype(f32)
    y_lat = ul * d
    y_ctx = uc * d if need_ctx else None
    for direction in range(2):
        reverse = direction == 1
        a = lax.complex(a_re[direction].astype(f32), a_im[direction].astype(f32))
        a_dt = a * jnp.exp(log_dt[direction].astype(f32))[:, None]
        a_bar = jnp.exp(a_dt)
        b_mat = lax.complex(b_re[direction].astype(f32), b_im[direction].astype(f32))
        b_bar = ((a_bar - 1.0) / a)[:, :, None] * b_mat
        c_mat = lax.complex(c_re[direction].astype(f32), c_im[direction].astype(f32))
        h_ctx = diag_scan(a_bar, jnp.einsum("blgc,gpc->blgp", uc_g, b_bar), reverse)
        h0 = h_ctx[:, 0] if reverse else h_ctx[:, -1]
        steps = (jnp.arange(l_lat, 0, -1) if reverse else jnp.arange(1, l_lat + 1)).astype(f32)
        carry = jnp.exp(steps[:, None, None] * a_dt)
        h_lat = diag_scan(a_bar, jnp.einsum("blgc,gpc->blgp", ul_g, b_bar), reverse) + carry[None] * h0[:, None]
        y_lat = y_lat + jnp.real(jnp.einsum("blgp,gcp->blgc", h_lat, c_mat)).reshape(n_b, l_lat, D_SSM)
        if need_ctx:
            y_ctx = y_ctx + jnp.real(jnp.einsum("blgp,gcp->blgc", h_ctx, c_mat)).reshape(n_b, l_ctx, D_SSM)
    return y_ctx, y_lat


def s5_glu(y, w_glu, b_glu, dtype):
    y = jax.nn.gelu(y)
    return (y * jax.nn.sigmoid(y @ w_glu.astype(jnp.float32) + b_glu.astype(jnp.float32))).astype(dtype)


def spatial_gating(u, v, g_sgu, w_spatial, b_spatial):
    n_b, length, _ = u.shape
    v = rmsnorm(v, g_sgu).reshape(n_b, length // CHUNK, CHUNK, SGU_HEADS, SGU_HEAD_DIM)
    mixed = jnp.einsum("hpq,bnqhd->bnphd", w_spatial, v) + b_spatial.T[None, None, :, :, None]
    return u * mixed.reshape(n_b, length, D_SGU)


def conv_ffn(h, w_up, w_conv, w_down, rows):
    n_b, length, _ = h.shape
    up = (h @ w_up).reshape(n_b, rows, length // rows, 2 * D_FF)
    up = lax.conv_general_dilated(up, w_conv[:, :, None, :], (1, 1), "SAME",
                                  dimension_numbers=("NHWC", "HWIO", "NHWC"),
                                  feature_group_count=2 * D_FF)
    gate, val = jnp.split(up.reshape(n_b, length, 2 * D_FF), 2, axis=-1)
    return (jax.nn.silu(gate) * val) @ w_down


def mixer_inputs(h, w_in):
    p = h @ w_in
    u_ssm = p[..., :D_SSM]
    u_sgu = jax.nn.gelu(p[..., D_SSM:D_SSM + D_SGU])
    v_sgu = jax.nn.gelu(p[..., D_SSM + D_SGU:])
    return u_ssm, u_sgu, v_sgu


def setup_inputs(seed: int = 0) -> dict:
    key = jax.random.key(seed)
    ks = jax.random.split(key, 27)
    f32 = jnp.float32

    def nrm(k, shape, scale):
        return scale * jax.random.normal(k, shape, f32)

    G, P, CH = SSM_GROUPS, SSM_STATE, SSM_CH
    a_im = math.pi * jnp.broadcast_to(jnp.arange(P, dtype=f32), (DEPTH, 2, G, P)) + nrm(ks[9], (DEPTH, 2, G, P), 0.01)
    return {
        "x": nrm(ks[0], (BATCH, SEQ, D_MODEL), 1.0),
        "c": nrm(ks[1], (BATCH, D_MODEL), 1.0),
        "ctx": nrm(ks[2], (BATCH, CTX_LEN, D_MODEL), 1.0),
        "c_ctx": nrm(ks[3], (D_MODEL,), 1.0),
        "w_ada": nrm(ks[4], (DEPTH, D_MODEL, 6 * D_MODEL), 0.5 * D_MODEL ** -0.5),
        "b_ada": nrm(ks[5], (DEPTH, 6 * D_MODEL), 0.01),
        "g_mix": 1.0 + nrm(ks[6], (DEPTH, D_MODEL), 0.01),
        "w_in": nrm(ks[7], (DEPTH, D_MODEL, D_IN), D_MODEL ** -0.5),
        "ssm_a_re": -0.5 + nrm(ks[8], (DEPTH, 2, G, P), 0.01),
        "ssm_a_im": a_im,
        "ssm_b_re": nrm(ks[10], (DEPTH, 2, G, P, CH), (2 * CH) ** -0.5),
        "ssm_b_im": nrm(ks[11], (DEPTH, 2, G, P, CH), (2 * CH) ** -0.5),
        "ssm_c_re": nrm(ks[12], (DEPTH, 2, G, CH, P), P ** -0.5),
        "ssm_c_im": nrm(ks[13], (DEPTH, 2, G, CH, P), P ** -0.5),
        "ssm_log_dt": jax.random.uniform(ks[14], (DEPTH, 2, G), f32, math.log(1e-3), math.log(1e-1)),
        "ssm_d": nrm(ks[15], (DEPTH, D_SSM), 1.0),
        "w_glu": nrm(ks[16], (DEPTH, D_SSM, D_SSM), D_SSM ** -0.5),
        "b_glu": nrm(ks[17], (DEPTH, D_SSM), 0.01),
        "g_sgu": 1.0 + nrm(ks[18], (DEPTH, D_SGU), 0.01),
        "w_spatial": nrm(ks[19], (DEPTH, SGU_HEADS, CHUNK, CHUNK), 0.5 * CHUNK ** -0.5),
        "b_spatial": 1.0 + nrm(ks[20], (DEPTH, SGU_HEADS, CHUNK), 0.01),
        "w_out": nrm(ks[21], (DEPTH, D_MODEL, D_MODEL), D_MODEL ** -0.5),
        "g_ffn": 1.0 + nrm(ks[22], (DEPTH, D_MODEL), 0.01),
        "w_up": nrm(ks[23], (DEPTH, D_MODEL, 2 * D_FF), D_MODEL ** -0.5),
        "w_conv": nrm(ks[24], (DEPTH, CONV_K, CONV_K, 2 * D_FF), 1.0 / CONV_K),
        "w_down": nrm(ks[25], (DEPTH, D_FF, D_MODEL), D_FF ** -0.5),
        "g_final": 1.0 + nrm(ks[26], (D_MODEL,), 0.01),
    }


def reference(x, c, ctx, c_ctx, w_ada, b_ada, g_mix, w_in, ssm_a_re, ssm_a_im, ssm_b_re, ssm_b_im,
              ssm_c_re, ssm_c_im, ssm_log_dt, ssm_d, w_glu, b_glu, g_sgu, w_spatial, b_spatial,
              w_out, g_ffn, w_up, w_conv, w_down, g_final):
    rows = x.shape[1] // GRID_W
    cond_lat = jax.nn.silu(c)[:, None, :]
    cond_ctx = jax.nn.silu(c_ctx)[None, None, :]
    xc = ctx
    for i in range(DEPTH):
        need_ctx = i < DEPTH - 1
        mod_l = jnp.split(cond_lat @ w_ada[i] + b_ada[i], 6, axis=-1)
        mod_c = jnp.split(cond_ctx @ w_ada[i] + b_ada[i], 6, axis=-1)
        h_l = modulate(rmsnorm(x, g_mix[i]), mod_l[0], mod_l[1])
        h_c = modulate(rmsnorm(xc, g_mix[i]), mod_c[0], mod_c[1])
        us_l, ug_l, vg_l = mixer_inputs(h_l, w_in[i])
        if need_ctx:
            us_c, ug_c, vg_c = mixer_inputs(h_c, w_in[i])
        else:
            us_c = h_c @ w_in[i][:, :D_SSM]
        y_ssm_c, y_ssm_l = s5_bidirectional(us_c, us_l, ssm_a_re[i], ssm_a_im[i], ssm_b_re[i], ssm_b_im[i],
                                            ssm_c_re[i], ssm_c_im[i], ssm_log_dt[i], ssm_d[i], need_ctx)
        mix_l = jnp.concatenate([s5_glu(y_ssm_l, w_glu[i], b_glu[i], x.dtype),
                                 spatial_gating(ug_l, vg_l, g_sgu[i], w_spatial[i], b_spatial[i])], axis=-1)
        x = x + mod_l[2] * (mix_l @ w_out[i])
        hf_l = modulate(rmsnorm(x, g_ffn[i]), mod_l[3], mod_l[4])
        x = x + mod_l[5] * conv_ffn(hf_l, w_up[i], w_conv[i], w_down[i], rows)
        if need_ctx:
            mix_c = jnp.concatenate([s5_glu(y_ssm_c, w_glu[i], b_glu[i], xc.dtype),
                                     spatial_gating(ug_c, vg_c, g_sgu[i], w_spatial[i], b_spatial[i])], axis=-1)
            xc = xc + mod_c[2] * (mix_c @ w_out[i])
            hf_c = modulate(rmsnorm(xc, g_ffn[i]), mod_c[3], mod_c[4])
            xc = xc + mod_c[5] * conv_ffn(hf_c, w_up[i], w_conv[i], w_down[i], 1)
    return rmsnorm(x, g_final)
```

```python
import numpy as np
from contextlib import ExitStack
import concourse.bass as bass
import concourse.mybir as mybir
from concourse.bass_utils import run_bass_kernel_spmd

F32, BF16, I32 = mybir.dt.float32, mybir.dt.bfloat16, mybir.dt.int32
AF = mybir.ActivationFunctionType
ALU = mybir.AluOpType

D = 1024
NT = 2304
LC = 256
LL = 2048
DFF = 2816
EPS = 1e-6
NJ = 288
TILES = [(0, 256)] + [(256 + 512 * i, 512) for i in range(4)]
TWO_PI = 6.283185307179586


class Sched:
    def __init__(self, nc):
        self.nc = nc
        self.stages = [[]]

    def op(self, eng, fn, dma=False):
        self.stages[-1].append((eng, dma, fn))

    def dma(self, eng, out, in_, slow=False):
        if slow:
            self.op(eng, lambda e, o=out, i=in_: e.dma_start(out=o, in_=i, allow_slow_non_contiguous=True), dma=True)
        else:
            self.op(eng, lambda e, o=out, i=in_: e.dma_start(out=o, in_=i), dma=True)

    def bar(self):
        if self.stages[-1]:
            self.stages.append([])

    def emit(self):
        nc = self.nc
        self.bar()
        names = ["c_scalar", "c_vector", "c_gpsimd", "c_tensor", "d_sync", "d_scalar", "d_gpsimd"]
        cum = []
        cur = {n: 0 for n in names}
        for st in self.stages:
            cum.append(dict(cur))
            for (eng, dma, _) in st:
                if dma:
                    cur["d_" + eng] += 16
                else:
                    cur["c_" + eng] += 1
        final = dict(cur)
        with ExitStack() as es:
            sems = {n: es.enter_context(nc.semaphore(n)) for n in names}
            block = es.enter_context(nc.Block())

            def make(engname):
                def body(eng):
                    waited = {n: 0 for n in names}
                    for k, st in enumerate(self.stages):
                        mine = [o for o in st if o[0] == engname]
                        if not mine:
                            continue
                        for n in names:
                            if cum[k][n] > waited[n]:
                                eng.wait_ge(sems[n], cum[k][n])
                                waited[n] = cum[k][n]
                        for (_, dma, fn) in mine:
                            ins = fn(eng)
                            if dma:
                                ins.then_inc(sems["d_" + engname], 16)
                            else:
                                ins.then_inc(sems["c_" + engname], 1)
                    if engname == "sync":
                        for n in names:
                            if final[n] > waited[n]:
                                eng.wait_ge(sems[n], final[n])
                return body

            block.sync(make("sync"))
            block.scalar(make("scalar"))
            block.vector(make("vector"))
            block.gpsimd(make("gpsimd"))
            block.tensor(make("tensor"))


class _Stop(Exception):
    pass


def build_nc(n_layers, n_wl=4, stop=None, dbg=False):
    nc = bass.Bass("TRN2", target_bir_lowering=False)
    S = Sched(nc)
    W = n_wl

    def ckp(name):
        S.bar()
        if stop == name:
            raise _Stop()

    def din(name, shape):
        return nc.dram_tensor(name, list(shape), F32, kind="ExternalInput").ap()

    x_in = din("x", [LL, D])
    c_in = din("c", [D])
    ctx_in = din("ctx", [LC, D])
    cctx_in = din("c_ctx", [D])
    w_ada = din("w_ada", [W, D, 6 * D])
    b_ada = din("b_ada", [W, 6 * D])
    g_mix = din("g_mix", [W, D])
    w_in = din("w_in", [W, D, 1536])
    a_re = din("ssm_a_re", [W, 2, 32, 64])
    a_im = din("ssm_a_im", [W, 2, 32, 64])
    b_re = din("ssm_b_re", [W, 2, 32, 64, 16])
    b_im = din("ssm_b_im", [W, 2, 32, 64, 16])
    c_re = din("ssm_c_re", [W, 2, 32, 16, 64])
    c_im = din("ssm_c_im", [W, 2, 32, 16, 64])
    log_dt = din("ssm_log_dt", [W, 2, 32])
    ssm_d = din("ssm_d", [W, 512])
    w_glu = din("w_glu", [W, 512, 512])
    b_glu = din("b_glu", [W, 512])
    g_sgu = din("g_sgu", [W, 512])
    w_sp = din("w_spatial", [W, 4, 128, 128])
    b_sp = din("b_spatial", [W, 4, 128])
    w_out = din("w_out", [W, D, D])
    g_ffn = din("g_ffn", [W, D])
    w_up = din("w_up", [W, D, 2 * DFF])
    w_conv = din("w_conv", [W, 3, 3, 2 * DFF])
    w_down = din("w_down", [W, DFF, D])
    g_final = din("g_final", [D])
    ident_in = din("ident", [128, 128])
    mask_in = din("mask", [2, 128, 128])
    out = nc.dram_tensor("out", [LL, D], F32, kind="ExternalOutput").ap()

    SK = dict(kind="ExternalOutput") if dbg else {}
    XRES = nc.dram_tensor("XRES", [D, NT], F32, **SK).ap()
    USSMP = nc.dram_tensor("USSMP", [512, 8, NJ], BF16, **SK).ap()
    YSP = nc.dram_tensor("YSP", [128, 32, NJ], F32, **SK).ap()
    UG = nc.dram_tensor("UG", [512, NT], BF16, **SK).ap()
    VG = nc.dram_tensor("VG", [512, NT], F32, **SK).ap()
    MIX = nc.dram_tensor("MIX", [D, NT], BF16, **SK).ap()
    GD = nc.dram_tensor("GD", [DFF, NT], BF16, **SK).ap()

    es = ExitStack()

    def sb(name, shape, dt=F32):
        return es.enter_context(nc.sbuf_tensor(name, list(shape), dt))

    ps = [es.enter_context(nc.psum_tensor("ps%d" % i, [128, 512], F32)) for i in range(8)]

    IDENT = sb("IDENT", [128, 128])
    ONESB = sb("ONESB", [128, 128], BF16)
    MASKS = sb("MASKS", [128, 2, 128])
    SIGN = sb("SIGN", [128, 1])
    NSIGN = sb("NSIGN", [128, 1])
    SC = sb("SC", [128, 8, 2])
    MOD = sb("MOD", [128, 6, 8, 2])
    BADA = sb("BADA", [128, 6, 8])
    GM = sb("GM", [128, 8])
    GF = sb("GF", [128, 8])
    GFIN = sb("GFIN", [128, 8])
    A1 = sb("A1", [128, 8, 2])
    A2 = sb("A2", [128, 8, 2])
    HRAW = sb("HRAW", [128, 9216])
    H = HRAW[:].bitcast(BF16).rearrange("p (k t) -> p k t", t=NT)
    YS = HRAW[:].rearrange("p (g j) -> p g j", j=NJ)
    STG = sb("STG", [128, 4096])
    SQ = STG[:, 0:2048].bitcast(BF16).rearrange("p (k t) -> p k t", t=512)
    WBt = sb("WB", [128, 22528], BF16)
    WB = WBt[:]
    TOEP = WB[:, 0:4096].rearrange("p (g n) -> p g n", n=128)
    ET = WB[:, 4096:8192].rearrange("p (g n) -> p g n", n=128)
    ESW = WB[:, 8192:12288].rearrange("p (g n) -> p g n", n=128)
    IM = WB[:, 12288:21504].rearrange("p (g j) -> p g j", j=NJ)
    XTt = sb("XT", [128, 4096])
    XT = XTt[:].rearrange("p (k t) -> p k t", t=512)
    LT = XTt[:].rearrange("p (g s c) -> p g s c", s=8, c=16)
    SS = XTt[:].rearrange("p (j a g) -> p j a g", a=2, g=32)
    TMPt = sb("TMP", [128, 4096])
    TMP = TMPt[:].rearrange("p (k t) -> p k t", t=512)
    RT = TMPt[:].rearrange("p (g s c) -> p g s c", s=8, c=16)
    RS = sb("RS", [128, 512])
    OBt = sb("OB", [128, 4096], BF16)
    OB = OBt[:].rearrange("p (k t) -> p k t", t=512)
    RB = OBt[:].rearrange("p (g n) -> p g n", n=128)
    ACTBt = sb("ACTB", [128, 11264], BF16)
    ACTB = ACTBt[:].rearrange("p (k t) -> p k t", t=512)
    USP = ACTBt[:, 0:9216].rearrange("p (q s j) -> p q s j", s=8, j=NJ)
    HH = ACTBt[:, 0:9216].rearrange("p (j g) -> p j g", g=32)
    FB = sb("FB", [128, 9216])
    UPG = FB[:, 0:2304]; UPV = FB[:, 2304:4608]; CG = FB[:, 4608:6912]; CV = FB[:, 6912:9216]
    def ftab(i):
        return FB[:, i * 512:(i + 1) * 512].rearrange("p (g k) -> p g k", k=16)
    ARG, ARGC, EARG, NF, NFC, PRE, PIM, MAG, BX1, BX2, CX1, CX2 = [ftab(i) for i in range(12)]
    def qtab(i):
        return FB[:, 6144 + i * 256:6144 + (i + 1) * 256].rearrange("p (g k) -> p g k", k=8)
    QR, QI, QT, PA, PB = [qtab(i) for i in range(5)]
    NI = FB[:, 7424:7936].bitcast(I32).rearrange("p (g k) -> p g k", k=16)
    NIC = FB[:, 7936:8448].bitcast(I32).rearrange("p (g k) -> p g k", k=16)
    BS = FB[:, 0:512].rearrange("p (h q) -> p h q", q=128)
    VT = STG[:, 2048:4096].bitcast(BF16).rearrange("p (a n) -> p a n", n=128)
    W3 = sb("W3", [128, 2, 32])
    G3 = sb("G3", [128, 3, 32])
    T1 = sb("T1", [128, 2, 32])
    T2 = sb("T2", [128, 2, 32])
    ARp = sb("ARp", [128, 32]); AIp = sb("AIp", [128, 32]); LDT = sb("LDT", [128, 32])
    LR = sb("LR", [128, 32]); LI = sb("LI", [128, 32])
    CR = sb("CR", [128, 32]); CI = sb("CI", [128, 32]); NR = sb("NR", [128, 32]); DEN = sb("DEN", [128, 32])
    TA = sb("TA", [128, 32]); TB = sb("TB", [128, 32])
    ARC = sb("ARC", [128, 2, 32]); AIC = sb("AIC", [128, 2, 32])
    DV = sb("DV", [128, 32])
    BGLU = sb("BGLU", [128, 4]); GSGU = sb("GSGU", [128, 4])
    WST = sb("WST", [128, 4, 128], BF16)
    WC = sb("WC", [128, 9, 44])

    S.dma("sync", IDENT[:], ident_in[:, :])
    S.dma("sync", MASKS[:], mask_in.rearrange("m p q -> p m q"))
    S.op("vector", lambda e: e.memset(ONESB[:], 1.0))
    S.op("vector", lambda e: e.memset(SIGN[0:64, :], -1.0))
    S.op("vector", lambda e: e.memset(SIGN[64:128, :], 1.0))
    S.op("vector", lambda e: e.memset(NSIGN[0:64, :], 1.0))
    S.op("vector", lambda e: e.memset(NSIGN[64:128, :], -1.0))
    S.dma("sync", STG[0:8, 0:128], c_in.rearrange("(k p) -> k p", p=128))
    S.dma("sync", STG[8:16, 0:128], cctx_in.rearrange("(k p) -> k p", p=128))
    S.dma("sync", STG[16:24, 0:128], g_final.rearrange("(k p) -> k p", p=128))
    S.bar()
    S.op("tensor", lambda e: e.transpose(ps[7][:, 0:24], STG[0:24, 0:128], IDENT[0:24, 0:24]))
    S.bar()
    S.op("vector", lambda e: e.tensor_copy(out=SC[:, :, 0], in_=ps[7][:, 0:8]))
    S.op("vector", lambda e: e.tensor_copy(out=SC[:, :, 1], in_=ps[7][:, 8:16]))
    S.op("vector", lambda e: e.tensor_copy(out=GFIN[:], in_=ps[7][:, 16:24]))
    S.bar()
    S.op("scalar", lambda e: e.activation(out=SC[:], in_=SC[:], func=AF.Silu))
    S.bar()

    def in_transpose(src, nblk, tok0):
        for b in range(nblk):
            S.dma("sync", TMP[:, :, 0:128], src[b * 128:(b + 1) * 128, :].rearrange("t (k d) -> t k d", d=128))
            S.bar()
            for k in range(8):
                S.op("tensor", lambda e, k=k: e.transpose(ps[k // 4][:, (k % 4) * 128:(k % 4 + 1) * 128],
                                                           TMP[:, k, 0:128], IDENT[:]))
            S.bar()
            S.op("vector", lambda e: e.tensor_copy(out=XT[:, 0:4, 0:128], in_=ps[0][:].rearrange("p (k t) -> p k t", t=128)))
            S.op("scalar", lambda e: e.copy(out=XT[:, 4:8, 0:128], in_=ps[1][:].rearrange("p (k t) -> p k t", t=128)))
            S.bar()
            t0 = tok0 + b * 128
            S.dma("sync", XRES[:, t0:t0 + 128].rearrange("(k p) t -> p k t", p=128), XT[:, :, 0:128])
            S.bar()

    in_transpose(ctx_in, 2, 0)
    in_transpose(x_in, 16, 256)

    def load_weight(dst, wap, kch, ncols, cb):
        for c0 in range(0, ncols, cb):
            stg = STG[:, 0:kch * cb].rearrange("p (k n) -> p k n", n=cb)
            S.dma("sync", stg, wap[:, c0:c0 + cb].rearrange("(k p) n -> p k n", p=128))
            S.bar()
            S.op("vector", lambda e, stg=stg, c0=c0: e.tensor_copy(out=dst[:, :, c0:c0 + cb], in_=stg))
            S.bar()

    def rstd_from(src_sq, nk, w, inv_n):
        for k in range(nk):
            S.op("tensor", lambda e, k=k: e.matmul(ps[0][:, 0:w], lhsT=ONESB[:], rhs=src_sq[:, k, 0:w],
                                                    start=(k == 0), stop=(k == nk - 1)))
        S.bar()
        S.op("scalar", lambda e: e.activation(out=RS[:, 0:w], in_=ps[0][:, 0:w], func=AF.Sqrt, bias=EPS, scale=inv_n))
        S.bar()
        S.op("vector", lambda e: e.reciprocal(out=RS[:, 0:w], in_=RS[:, 0:w]))
        S.bar()

    def norm_mod(Acoef, which_shift):
        for (t0, w) in TILES:
            sel = 1 if t0 == 0 else 0
            S.dma("sync", XT[:, :, 0:w], XRES[:, t0:t0 + w].rearrange("(k p) t -> p k t", p=128))
            S.bar()
            S.op("scalar", lambda e, w=w: e.activation(out=SQ[:, :, 0:w], in_=XT[:, :, 0:w], func=AF.Square))
            S.bar()
            rstd_from(SQ, 8, w, 1.0 / D)
            for k in range(8):
                S.op("vector", lambda e, k=k, w=w, sel=sel: e.scalar_tensor_tensor(
                    out=TMP[:, k, 0:w], in0=XT[:, k, 0:w], scalar=Acoef[:, k, sel:sel + 1], in1=RS[:, 0:w],
                    op0=ALU.mult, op1=ALU.mult))
            S.bar()
            for k in range(8):
                S.op("scalar", lambda e, k=k, w=w, sel=sel, t0=t0: e.activation(
                    out=H[:, k, t0:t0 + w], in_=TMP[:, k, 0:w], func=AF.Identity,
                    bias=MOD[:, which_shift, k, sel:sel + 1], scale=1.0))
            S.bar()

    def resid_linear(src_dram, kch, gate_idx):
        Wv = WB[:, 0:kch * D].rearrange("p (k n) -> p k n", n=D)
        for (t0, w) in TILES:
            sel = 1 if t0 == 0 else 0
            S.dma("sync", ACTB[:, 0:kch, 0:w], src_dram[:, t0:t0 + w].rearrange("(k p) t -> p k t", p=128))
            S.dma("gpsimd", XT[:, :, 0:w], XRES[:, t0:t0 + w].rearrange("(k p) t -> p k t", p=128))
            S.bar()
            for m in range(8):
                for k in range(kch):
                    S.op("tensor", lambda e, m=m, k=k, w=w: e.matmul(
                        ps[m][:, 0:w], lhsT=Wv[:, k, m * 128:(m + 1) * 128], rhs=ACTB[:, k, 0:w],
                        start=(k == 0), stop=(k == kch - 1)))
            S.bar()
            for m in range(8):
                S.op("vector", lambda e, m=m, w=w, sel=sel: e.scalar_tensor_tensor(
                    out=XT[:, m, 0:w], in0=ps[m][:, 0:w], scalar=MOD[:, gate_idx, m, sel:sel + 1], in1=XT[:, m, 0:w],
                    op0=ALU.mult, op1=ALU.add))
            S.bar()
            S.dma("sync", XRES[:, t0:t0 + w].rearrange("(k p) t -> p k t", p=128), XT[:, :, 0:w])
            S.bar()

    try:
      for li in range(n_layers):
        S.dma("sync", STG[0:48, 0:128], b_ada[li].rearrange("(k p) -> k p", p=128))
        S.dma("sync", STG[48:56, 0:128], g_mix[li].rearrange("(k p) -> k p", p=128))
        S.dma("sync", STG[56:64, 0:128], g_ffn[li].rearrange("(k p) -> k p", p=128))
        S.dma("sync", STG[64:68, 0:128], b_glu[li].rearrange("(k p) -> k p", p=128))
        S.dma("sync", STG[68:72, 0:128], g_sgu[li].rearrange("(k p) -> k p", p=128))
        S.bar()
        S.op("tensor", lambda e: e.transpose(ps[7][:, 0:72], STG[0:72, 0:128], IDENT[0:72, 0:72]))
        S.bar()
        S.op("vector", lambda e: e.tensor_copy(out=BADA[:].rearrange("p q k -> p (q k)"), in_=ps[7][:, 0:48]))
        S.op("vector", lambda e: e.tensor_copy(out=GM[:], in_=ps[7][:, 48:56]))
        S.op("vector", lambda e: e.tensor_copy(out=GF[:], in_=ps[7][:, 56:64]))
        S.op("vector", lambda e: e.tensor_copy(out=BGLU[:], in_=ps[7][:, 64:68]))
        S.op("vector", lambda e: e.tensor_copy(out=GSGU[:], in_=ps[7][:, 68:72]))
        S.bar()
        for blk in range(12):
            q, mh = blk // 2, blk % 2
            WA = STG[:].rearrange("p (k n) -> p k n", n=512)
            S.dma("sync", WA, w_ada[li, :, blk * 512:(blk + 1) * 512].rearrange("(k p) n -> p k n", p=128))
            S.bar()
            for mm in range(4):
                m = mh * 4 + mm
                for k in range(8):
                    S.op("tensor", lambda e, m=m, mm=mm, k=k, q=q: e.matmul(
                        ps[0][:, (q * 8 + m) * 2:(q * 8 + m) * 2 + 2], lhsT=WA[:, k, mm * 128:(mm + 1) * 128],
                        rhs=SC[:, k, :], start=(k == 0), stop=(k == 7)))
            S.bar()
        S.op("vector", lambda e: e.tensor_tensor(
            out=MOD[:].rearrange("p q k n -> p (q k) n"), in0=ps[0][:, 0:96].rearrange("p (a n) -> p a n", n=2),
            in1=BADA[:].rearrange("p q k -> p (q k)").unsqueeze(2).to_broadcast([128, 48, 2]), op=ALU.add))
        S.bar()
        S.op("vector", lambda e: e.scalar_tensor_tensor(
            out=A1[:], in0=MOD[:, 1, :, :], scalar=1.0, in1=GM[:].unsqueeze(2).to_broadcast([128, 8, 2]),
            op0=ALU.add, op1=ALU.mult))
        S.op("vector", lambda e: e.scalar_tensor_tensor(
            out=A2[:], in0=MOD[:, 4, :, :], scalar=1.0, in1=GF[:].unsqueeze(2).to_broadcast([128, 8, 2]),
            op0=ALU.add, op1=ALU.mult))
        S.bar()

        ckp("ada")
        norm_mod(A1, 0)

        ckp("norm1")
        WIN = WB[:, 0:8 * 1536].rearrange("p (k n) -> p k n", n=1536)
        load_weight(WIN, w_in[li], 8, 1536, 512)
        for (t0, w) in TILES:
            j0, nj = t0 // 8, w // 8
            for grp in range(2):
                ms = list(range(8)) if grp == 0 else list(range(8, 12))
                for bi, m in enumerate(ms):
                    for k in range(8):
                        S.op("tensor", lambda e, bi=bi, m=m, k=k, w=w, t0=t0: e.matmul(
                            ps[bi][:, 0:w], lhsT=WIN[:, k, m * 128:(m + 1) * 128], rhs=H[:, k, t0:t0 + w],
                            start=(k == 0), stop=(k == 7)))
                S.bar()
                for bi, m in enumerate(ms):
                    if m < 4:
                        S.op("vector", lambda e, bi=bi, m=m, w=w, j0=j0, nj=nj: e.tensor_copy(
                            out=USP[:, m, :, j0:j0 + nj].rearrange("p s j -> p j s"),
                            in_=ps[bi][:, 0:w].rearrange("p (j s) -> p j s", s=8)))
                    elif m < 8:
                        S.op("scalar", lambda e, bi=bi, m=m, w=w: e.activation(
                            out=OB[:, m - 4, 0:w], in_=ps[bi][:, 0:w], func=AF.Gelu_apprx_tanh))
                    else:
                        S.op("scalar", lambda e, bi=bi, m=m, w=w: e.activation(
                            out=TMP[:, m - 8, 0:w], in_=ps[bi][:, 0:w], func=AF.Gelu_apprx_tanh))
                S.bar()
            S.dma("sync", UG[:, t0:t0 + w].rearrange("(k p) t -> p k t", p=128), OB[:, 0:4, 0:w])
            S.dma("gpsimd", VG[:, t0:t0 + w].rearrange("(k p) t -> p k t", p=128), TMP[:, 0:4, 0:w])
            S.bar()

        ckp("win")
        S.dma("sync", USSMP.rearrange("(q p) s j -> p q s j", p=128), USP)
        S.dma("gpsimd", STG[0:32, 0:16], ssm_d[li].rearrange("(g c) -> g c", c=16))
        S.bar()
        for s_ in range(8):
            S.dma("sync" if s_ % 2 == 0 else "gpsimd", IM[s_ * 16:(s_ + 1) * 16, :, :],
                  USSMP[:, s_, :].rearrange("(g c) j -> c g j", c=16))
        S.bar()

        S.op("vector", lambda e: e.tensor_copy(out=STG[0:32, 128:256].rearrange("p (s c) -> p s c", c=16),
                                               in_=STG[0:32, 0:16].unsqueeze(1).to_broadcast([32, 8, 16])))
        S.bar()
        S.op("tensor", lambda e: e.transpose(ps[7][:, 0:32], STG[0:32, 128:256], IDENT[0:32, 0:32]))
        S.bar()
        S.op("vector", lambda e: e.tensor_copy(out=DV[:], in_=ps[7][:, 0:32]))
        ckp("im2col")
        for dr in range(2):
            CIN1 = STG[:, 0:512].rearrange("p (q n) -> p q n", n=128)
            CIN2 = STG[:, 512:1024].rearrange("p (q n) -> p q n", n=128)
            for hf in range(2):
                lo = slice(hf * 64, hf * 64 + 64)
                csrc = [c_re, c_im] if hf == 0 else [c_im, c_re]
                S.dma("sync", CIN1[:, :, lo], csrc[0][li, dr].rearrange("(q g) c p -> (g c) q p", q=4))
                S.dma("gpsimd", CIN2[:, :, lo], csrc[1][li, dr].rearrange("(q g) c p -> (g c) q p", q=4))
            S.bar()
            for q in range(4):
                S.op("tensor", lambda e, q=q: e.transpose(ps[0][:, q * 128:(q + 1) * 128], CIN1[:, q, :], IDENT[:]))
                S.op("tensor", lambda e, q=q: e.transpose(ps[1][:, q * 128:(q + 1) * 128], CIN2[:, q, :], IDENT[:]))
            S.bar()
            S.op("vector", lambda e: e.tensor_copy(out=CX1.rearrange("p g c -> p (g c)"), in_=ps[0][:]))
            S.op("scalar", lambda e: e.copy(out=CX2.rearrange("p g c -> p (g c)"), in_=ps[1][:]))
            S.bar()
            BIN1 = STG[0:32, 0:2048].rearrange("p (h q c) -> p h q c", h=2, c=16)
            BIN2 = STG[0:32, 2048:4096].rearrange("p (h q c) -> p h q c", h=2, c=16)
            S.dma("sync", BIN1[:, 0, :, :], b_re[li, dr])
            S.dma("gpsimd", BIN1[:, 1, :, :], b_im[li, dr])
            S.dma("sync", BIN2[:, 0, :, :], b_im[li, dr])
            S.dma("gpsimd", BIN2[:, 1, :, :], b_re[li, dr])
            AIN = RS[0:32, 0:256].rearrange("p (a q) -> p a q", q=64)
            S.dma("sync", AIN[:, 0, :], a_re[li, dr])
            S.dma("gpsimd", AIN[:, 1, :], a_re[li, dr])
            S.dma("sync", AIN[:, 2, :], a_im[li, dr])
            S.dma("gpsimd", AIN[:, 3, :], a_im[li, dr])
            S.bar()
            for c_ in range(16):
                S.op("tensor", lambda e, c_=c_: e.transpose(ps[2][:, c_ * 32:(c_ + 1) * 32], BIN1[:, :, :, c_], IDENT[0:32, 0:32]))
                S.op("tensor", lambda e, c_=c_: e.transpose(ps[3][:, c_ * 32:(c_ + 1) * 32], BIN2[:, :, :, c_], IDENT[0:32, 0:32]))
            S.op("tensor", lambda e: e.transpose(ps[4][:, 0:32], RS[0:32, 0:128], IDENT[0:32, 0:32]))
            S.op("tensor", lambda e: e.transpose(ps[4][:, 32:64], RS[0:32, 128:256], IDENT[0:32, 0:32]))
            S.bar()
            S.op("vector", lambda e: e.tensor_copy(out=BX1.rearrange("p g c -> p c g"), in_=ps[2][:].rearrange("p (c g) -> p c g", g=32)))
            S.op("scalar", lambda e: e.copy(out=BX2.rearrange("p g c -> p c g"), in_=ps[3][:].rearrange("p (c g) -> p c g", g=32)))
            S.op("vector", lambda e: e.tensor_copy(out=ARp[:], in_=ps[4][:, 0:32]))
            S.op("vector", lambda e: e.tensor_copy(out=AIp[:], in_=ps[4][:, 32:64]))
            S.dma("sync", LDT[:], log_dt[li, dr].partition_broadcast(128))
            S.bar()
            ckp("pl%d" % dr)
            S.op("scalar", lambda e: e.activation(out=LDT[:], in_=LDT[:], func=AF.Exp))
            S.bar()
            S.op("vector", lambda e: e.tensor_tensor(out=LR[:], in0=ARp[:], in1=LDT[:], op=ALU.mult))
            S.op("gpsimd", lambda e: e.tensor_tensor(out=LI[:], in0=AIp[:], in1=LDT[:], op=ALU.mult))
            S.bar()
            ckp("pb%d" % dr)
            ks = list(range(-8, 0)) + list(range(1, 9))
            for idx, kk in enumerate(ks):
                S.op("vector", lambda e, idx=idx, kk=kk: e.tensor_scalar(
                    out=ARG[:, :, idx], in0=LI[:], scalar1=float(kk), scalar2=None, op0=ALU.mult))
                S.op("gpsimd", lambda e, idx=idx, kk=kk: e.tensor_scalar(
                    out=EARG[:, :, idx], in0=LR[:], scalar1=float(kk), scalar2=None, op0=ALU.mult))
            S.bar()
            ckp("pc%d" % dr)
            S.op("vector", lambda e: e.tensor_scalar(out=ARGC, in0=ARG, scalar1=TWO_PI / 4, scalar2=None, op0=ALU.add))
            S.op("scalar", lambda e: e.activation(out=MAG, in_=EARG, func=AF.Exp))
            S.bar()
            ckp("pd%d" % dr)
            S.op("vector", lambda e: e.tensor_scalar(out=NI, in0=ARG, scalar1=1.0 / TWO_PI, scalar2=None, op0=ALU.mult))
            S.op("gpsimd", lambda e: e.tensor_scalar(out=NIC, in0=ARGC, scalar1=1.0 / TWO_PI, scalar2=None, op0=ALU.mult))
            S.bar()
            ckp("pe%d" % dr)
            S.op("vector", lambda e: e.tensor_copy(out=NF, in_=NI))
            S.op("gpsimd", lambda e: e.tensor_copy(out=NFC, in_=NIC))
            S.bar()
            ckp("pf%d" % dr)
            S.op("vector", lambda e: e.scalar_tensor_tensor(out=ARG, in0=NF, scalar=-TWO_PI, in1=ARG, op0=ALU.mult, op1=ALU.add))
            S.op("vector", lambda e: e.scalar_tensor_tensor(out=ARGC, in0=NFC, scalar=-TWO_PI, in1=ARGC, op0=ALU.mult, op1=ALU.add))
            S.bar()
            S.op("vector", lambda e: e.tensor_scalar(out=ARG, in0=ARG, scalar1=3.1415925, scalar2=-3.1415925, op0=ALU.min, op1=ALU.max))
            S.op("gpsimd", lambda e: e.tensor_scalar(out=ARGC, in0=ARGC, scalar1=3.1415925, scalar2=-3.1415925, op0=ALU.min, op1=ALU.max))
            S.bar()
            ckp("pg%d" % dr)
            S.op("scalar", lambda e: e.activation(out=PIM, in_=ARG, func=AF.Sin))
            S.op("scalar", lambda e: e.activation(out=PRE, in_=ARGC, func=AF.Sin))
            S.bar()
            S.op("vector", lambda e: e.tensor_tensor(out=PIM, in0=PIM, in1=MAG, op=ALU.mult))
            S.op("gpsimd", lambda e: e.tensor_tensor(out=PRE, in0=PRE, in1=MAG, op=ALU.mult))
            S.bar()
            ckp("ph%d" % dr)
            S.op("vector", lambda e: e.tensor_scalar(out=NR[:], in0=PRE[:, :, 8], scalar1=-1.0, scalar2=None, op0=ALU.add))
            S.op("gpsimd", lambda e: e.tensor_tensor(out=DEN[:], in0=ARp[:], in1=ARp[:], op=ALU.mult))
            ckp("c0")
            S.op("vector", lambda e: e.tensor_tensor(out=TA[:], in0=AIp[:], in1=AIp[:], op=ALU.mult))
            ckp("c1")
            S.op("vector", lambda e: e.tensor_tensor(out=DEN[:], in0=DEN[:], in1=TA[:], op=ALU.add))
            ckp("c2")
            S.op("vector", lambda e: e.reciprocal(out=DEN[:], in_=DEN[:]))
            ckp("c3")
            S.op("vector", lambda e: e.tensor_tensor(out=TA[:], in0=NR[:], in1=ARp[:], op=ALU.mult))
            S.op("gpsimd", lambda e: e.tensor_tensor(out=TB[:], in0=PIM[:, :, 8], in1=AIp[:], op=ALU.mult))
            ckp("c4")
            S.op("vector", lambda e: e.tensor_tensor(out=CR[:], in0=TA[:], in1=TB[:], op=ALU.add))
            ckp("c5")
            S.op("vector", lambda e: e.tensor_tensor(out=TA[:], in0=PIM[:, :, 8], in1=ARp[:], op=ALU.mult))
            S.op("gpsimd", lambda e: e.tensor_tensor(out=TB[:], in0=NR[:], in1=AIp[:], op=ALU.mult))
            ckp("c6")
            S.op("vector", lambda e: e.tensor_tensor(out=CI[:], in0=TA[:], in1=TB[:], op=ALU.subtract))
            ckp("c7")
            S.op("vector", lambda e: e.tensor_tensor(out=CR[:], in0=CR[:], in1=DEN[:], op=ALU.mult))
            S.op("gpsimd", lambda e: e.tensor_tensor(out=CI[:], in0=CI[:], in1=DEN[:], op=ALU.mult))
            ckp("c8")
            ckp("pi%d" % dr)
            CRb = CR[:].unsqueeze(2).to_broadcast([128, 32, 8])
            CIb = CI[:].unsqueeze(2).to_broadcast([128, 32, 8])
            S.op("vector", lambda e: e.tensor_tensor(out=QR, in0=PRE[:, :, 0:8], in1=CRb, op=ALU.mult))
            S.op("gpsimd", lambda e: e.tensor_tensor(out=QT, in0=PIM[:, :, 0:8], in1=CIb, op=ALU.mult))
            S.bar()
            S.op("vector", lambda e: e.tensor_tensor(out=QR, in0=QR, in1=QT, op=ALU.subtract))
            S.bar()
            S.op("vector", lambda e: e.tensor_tensor(out=QI, in0=PRE[:, :, 0:8], in1=CIb, op=ALU.mult))
            S.op("gpsimd", lambda e: e.tensor_tensor(out=QT, in0=PIM[:, :, 0:8], in1=CRb, op=ALU.mult))
            S.bar()
            S.op("vector", lambda e: e.tensor_tensor(out=QI, in0=QI, in1=QT, op=ALU.add))
            S.bar()
            S.op("vector", lambda e: e.tensor_scalar(out=QI, in0=QI, scalar1=SIGN[:, 0:1], scalar2=None, op0=ALU.mult))
            S.op("gpsimd", lambda e: e.tensor_scalar(out=PA, in0=PRE[:, :, 8:16], scalar1=NSIGN[:, 0:1], scalar2=None, op0=ALU.mult))
            S.op("scalar", lambda e: e.mul(out=PB, in_=PIM[:, :, 8:16], mul=-1.0))
            S.bar()
            S.op("vector", lambda e: e.tensor_copy(out=ARC[:, 0, :], in_=PRE[:, :, 15]))
            S.op("vector", lambda e: e.tensor_copy(out=ARC[:, 1, :], in_=PRE[:, :, 15]))
            S.op("gpsimd", lambda e: e.tensor_scalar(out=AIC[:, 0, :], in0=PIM[:, :, 15], scalar1=SIGN[:, 0:1], scalar2=None, op0=ALU.mult))
            S.op("gpsimd", lambda e: e.tensor_scalar(out=AIC[:, 1, :], in0=PIM[:, :, 15], scalar1=NSIGN[:, 0:1], scalar2=None, op0=ALU.mult))
            S.bar()
            ckp("prep%d" % dr + "")
            for s_ in range(8):
                qi = (7 - s_) if dr == 0 else s_
                ri = s_ if dr == 0 else (7 - s_)
                S.op("vector", lambda e, s_=s_, qi=qi: e.tensor_tensor(
                    out=LT[:, :, s_, :], in0=BX1, in1=QR[:, :, qi:qi + 1].to_broadcast([128, 32, 16]), op=ALU.mult))
                S.op("gpsimd", lambda e, s_=s_, ri=ri: e.tensor_tensor(
                    out=RT[:, :, s_, :], in0=CX1, in1=PA[:, :, ri:ri + 1].to_broadcast([128, 32, 16]), op=ALU.mult))
            S.bar()
            TL = STG[:].rearrange("p (g s c) -> p g s c", s=8, c=16)
            for s_ in range(8):
                qi = (7 - s_) if dr == 0 else s_
                S.op("vector" if s_ % 2 == 0 else "gpsimd", lambda e, s_=s_, qi=qi: e.tensor_tensor(
                    out=TL[:, :, s_, :], in0=BX2, in1=QI[:, :, qi:qi + 1].to_broadcast([128, 32, 16]), op=ALU.mult))
            S.bar()
            S.op("vector", lambda e: e.tensor_tensor(out=LT, in0=LT, in1=TL, op=ALU.add))
            S.bar()
            for s_ in range(8):
                ri = s_ if dr == 0 else (7 - s_)
                S.op("vector" if s_ % 2 == 0 else "gpsimd", lambda e, s_=s_, ri=ri: e.tensor_tensor(
                    out=TL[:, :, s_, :], in0=CX2, in1=PB[:, :, ri:ri + 1].to_broadcast([128, 32, 16]), op=ALU.mult))
            S.bar()
            S.op("vector", lambda e: e.tensor_tensor(out=RT, in0=RT, in1=TL, op=ALU.add))
            S.bar()
            ckp("lr%d" % dr + "")
            for rnd in range(4):
                for gi in range(8):
                    g = rnd * 8 + gi
                    Lg = LT[:, g, :, :].rearrange("p s c -> p (s c)")
                    Rg = RT[:, g, :, :].rearrange("p s c -> p (s c)")
                    S.op("tensor", lambda e, gi=gi, Lg=Lg, Rg=Rg: e.matmul(
                        ps[gi // 4][:, (gi % 4) * 128:(gi % 4 + 1) * 128], lhsT=Lg, rhs=Rg, start=True, stop=True))
                    S.op("tensor", lambda e, gi=gi, Lg=Lg: e.transpose(
                        ps[2 + gi // 4][:, (gi % 4) * 128:(gi % 4 + 1) * 128], Lg, IDENT[:]))
                S.bar()
                for bk in range(2):
                    g0 = rnd * 8 + bk * 4
                    S.op("vector", lambda e, bk=bk, g0=g0, dr=dr: e.tensor_tensor(
                        out=TOEP[:, g0:g0 + 4, :], in0=ps[bk][:].rearrange("p (g n) -> p g n", n=128),
                        in1=MASKS[:, dr:dr + 1, :].to_broadcast([128, 4, 128]), op=ALU.mult))
                    S.op("scalar", lambda e, bk=bk, g0=g0: e.copy(
                        out=ET[:, g0:g0 + 4, :], in_=ps[2 + bk][:].rearrange("p (g n) -> p g n", n=128)))
                S.bar()
                for bk in range(2):
                    g0 = rnd * 8 + bk * 4
                    S.op("vector", lambda e, bk=bk, g0=g0: e.tensor_copy(
                        out=ESW[:, g0:g0 + 4, 0:64], in_=ps[2 + bk][:].rearrange("p (g n) -> p g n", n=128)[:, :, 64:128]))
                    S.op("vector", lambda e, bk=bk, g0=g0: e.tensor_copy(
                        out=ESW[:, g0:g0 + 4, 64:128], in_=ps[2 + bk][:].rearrange("p (g n) -> p g n", n=128)[:, :, 0:64]))
                S.bar()
            S.op("scalar", lambda e: e.copy(out=RB.rearrange("p g n -> p (g n)"), in_=RT.rearrange("p g s c -> p (g s c)")))
            if dr == 0:
                for g in range(32):
                    S.op("vector", lambda e, g=g: e.scalar_tensor_tensor(
                        out=TOEP[:, g, :], in0=IDENT[:], scalar=DV[:, g:g + 1], in1=TOEP[:, g, :],
                        op0=ALU.mult, op1=ALU.add))
            S.op("gpsimd", lambda e: e.memset(W3[:], 0.0))
            S.bar()
            ckp("tiles%d" % dr + "")
            blocks = [(0, 32)] + [(32 + 64 * b, 64) for b in range(4)]
            order = blocks if dr == 0 else [blocks[0]] + blocks[:0:-1]
            for (jb, nb) in order:
                for half in range(2):
                    for gi in range(16):
                        g = half * 16 + gi
                        for arr in range(2):
                            ii = gi * 2 + arr
                            Em = ET if arr == 0 else ESW
                            S.op("tensor", lambda e, ii=ii, g=g, Em=Em, jb=jb, nb=nb: e.matmul(
                                ps[ii // 8][:, (ii % 8) * 64:(ii % 8) * 64 + nb], lhsT=Em[:, g, :],
                                rhs=IM[:, g, jb:jb + nb], start=True, stop=True))
                    S.bar()
                    for bk in range(4):
                        g0 = half * 16 + bk * 4
                        S.op("vector" if bk % 2 == 0 else "scalar", (lambda e, bk=bk, g0=g0, nb=nb: e.tensor_copy(
                            out=SS[:, 0:nb, :, g0:g0 + 4].rearrange("p j a g -> p g a j"),
                            in_=ps[bk][:].rearrange("p (g a j) -> p g a j", a=2, j=64)[:, :, :, 0:nb]))
                            if bk % 2 == 0 else (lambda e, bk=bk, g0=g0, nb=nb: e.copy(
                            out=SS[:, 0:nb, :, g0:g0 + 4].rearrange("p j a g -> p g a j"),
                            in_=ps[bk][:].rearrange("p (g a j) -> p g a j", a=2, j=64)[:, :, :, 0:nb])))
                    S.bar()
                js = list(range(jb, jb + nb)) if dr == 0 else list(range(jb + nb - 1, jb - 1, -1))
                for j in js:
                    jl = j - jb
                    S.op("scalar", lambda e, j=j: e.copy(out=HH[:, j, :], in_=W3[:, 0, :]))
                    S.op("vector", lambda e, jl=jl: e.tensor_tensor(out=G3[:, 0:2, :], in0=W3[:], in1=SS[:, jl, :, :], op=ALU.add))
                    S.op("gpsimd", lambda e, jl=jl: e.tensor_tensor(out=G3[:, 2, :], in0=W3[:, 0, :], in1=SS[:, jl, 0, :], op=ALU.add))
                    S.bar()
                    S.op("vector", lambda e: e.tensor_tensor(out=T1[:], in0=G3[:, 0:2, :], in1=ARC[:], op=ALU.mult))
                    S.op("gpsimd", lambda e: e.tensor_tensor(out=T2[:], in0=G3[:, 1:3, :], in1=AIC[:], op=ALU.mult))
                    S.bar()
                    S.op("vector", lambda e: e.tensor_tensor(out=W3[:], in0=T1[:], in1=T2[:], op=ALU.add))
                    S.bar()
            ckp("rec%d" % dr + "")
            for rnd in range(4):
                for gi in range(8):
                    g = rnd * 8 + gi
                    S.op("tensor", lambda e, gi=gi, g=g: e.matmul(
                        ps[gi][:, 0:NJ], lhsT=TOEP[:, g, :], rhs=IM[:, g, :], start=True, stop=False))
                    S.op("tensor", lambda e, gi=gi, g=g: e.matmul(
                        ps[gi][:, 0:NJ], lhsT=RB[:, g, :], rhs=HH[:, :, g], start=False, stop=True))
                S.bar()
                for gi in range(8):
                    g = rnd * 8 + gi
                    if dr == 0:
                        S.op("vector" if gi % 2 == 0 else "scalar", (lambda e, gi=gi, g=g: e.tensor_copy(out=YS[:, g, :], in_=ps[gi][:, 0:NJ]))
                             if gi % 2 == 0 else (lambda e, gi=gi, g=g: e.copy(out=YS[:, g, :], in_=ps[gi][:, 0:NJ])))
                    else:
                        S.op("vector", lambda e, gi=gi, g=g: e.tensor_tensor(out=YS[:, g, :], in0=YS[:, g, :], in1=ps[gi][:, 0:NJ], op=ALU.add))
                S.bar()
        ckp("read")
        S.dma("sync", YSP[:, :, :], YS)
        S.bar()
        YV = YS.rearrange("p (q s) j -> p q s j", s=8)
        YSPv = YSP.rearrange("(s c) (q g) j -> g c q s j", c=16, g=8)
        for g8 in range(8):
            for q in range(4):
                S.dma(["sync", "gpsimd"][q % 2], YV[g8 * 16:(g8 + 1) * 16, q, :, :], YSPv[g8, :, q, :, :])
            S.bar()
        S.bar()

        ckp("unim")
        WG = WB[:, 0:4 * 512].rearrange("p (k n) -> p k n", n=512)
        load_weight(WG, w_glu[li], 4, 512, 512)
        for (t0, w) in TILES:
            j0, nj = t0 // 8, w // 8
            for q in range(4):
                S.op("scalar", lambda e, q=q, w=w, j0=j0, nj=nj: e.activation(
                    out=TMP[:, q, 0:w].rearrange("p (j s) -> p j s", s=8),
                    in_=YV[:, q, :, j0:j0 + nj].rearrange("p s j -> p j s"), func=AF.Gelu_apprx_tanh))
            S.bar()
            S.op("vector", lambda e, w=w: e.tensor_copy(out=OB[:, 0:4, 0:w], in_=TMP[:, 0:4, 0:w]))
            S.bar()
            for m in range(4):
                for k in range(4):
                    S.op("tensor", lambda e, m=m, k=k, w=w: e.matmul(
                        ps[m][:, 0:w], lhsT=WG[:, k, m * 128:(m + 1) * 128], rhs=OB[:, k, 0:w],
                        start=(k == 0), stop=(k == 3)))
            S.bar()
            for m in range(4):
                S.op("scalar", lambda e, m=m, w=w: e.activation(
                    out=XT[:, m, 0:w], in_=ps[m][:, 0:w], func=AF.Sigmoid, bias=BGLU[:, m:m + 1], scale=1.0))
            S.bar()
            S.op("vector", lambda e, w=w: e.tensor_tensor(out=OB[:, 4:8, 0:w], in0=TMP[:, 0:4, 0:w], in1=XT[:, 0:4, 0:w], op=ALU.mult))
            S.bar()
            S.dma("sync", MIX[0:512, t0:t0 + w].rearrange("(k p) t -> p k t", p=128), OB[:, 4:8, 0:w])
            S.bar()

        ckp("glu")
        S.dma("sync", TMP[:, 0:4, 0:128], w_sp[li].rearrange("h p q -> p h q"))
        S.dma("gpsimd", BS.rearrange("p h q -> p (h q)"), b_sp[li].rearrange("h q -> (h q)").partition_broadcast(128))
        S.bar()
        for h in range(4):
            S.op("tensor", lambda e, h=h: e.transpose(ps[0][:, h * 128:(h + 1) * 128], TMP[:, h, 0:128], IDENT[:]))
        S.bar()
        S.op("vector", lambda e: e.tensor_copy(out=WST[:], in_=ps[0][:].rearrange("p (h n) -> p h n", n=128)))
        S.bar()
        for (t0, w) in TILES:
            nchk = w // 128
            S.dma("sync", XT[:, 0:4, 0:w], VG[:, t0:t0 + w].rearrange("(k p) t -> p k t", p=128))
            S.dma("gpsimd", OB[:, 0:4, 0:w], UG[:, t0:t0 + w].rearrange("(k p) t -> p k t", p=128))
            S.bar()
            S.op("scalar", lambda e, w=w: e.activation(out=SQ[:, 0:4, 0:w], in_=XT[:, 0:4, 0:w], func=AF.Square))
            S.bar()
            rstd_from(SQ, 4, w, 1.0 / 512)
            for k in range(4):
                S.op("vector", lambda e, k=k, w=w: e.scalar_tensor_tensor(
                    out=TMP[:, k, 0:w], in0=XT[:, k, 0:w], scalar=GSGU[:, k:k + 1], in1=RS[:, 0:w],
                    op0=ALU.mult, op1=ALU.mult))
            S.bar()
            for ck in range(nchk):
                for h in range(4):
                    ii = ck * 4 + h
                    S.op("tensor", lambda e, ii=ii, ck=ck, h=h: e.transpose(
                        ps[ii // 4][:, (ii % 4) * 128:(ii % 4 + 1) * 128], TMP[:, h, ck * 128:(ck + 1) * 128], IDENT[:]))
            S.bar()
            for ck in range(nchk):
                S.op("vector" if ck % 2 == 0 else "scalar", (lambda e, ck=ck: e.tensor_copy(
                    out=VT[:, ck * 4:(ck + 1) * 4, :], in_=ps[ck][:].rearrange("p (h n) -> p h n", n=128)))
                    if ck % 2 == 0 else (lambda e, ck=ck: e.copy(
                    out=VT[:, ck * 4:(ck + 1) * 4, :], in_=ps[ck][:].rearrange("p (h n) -> p h n", n=128))))
            S.bar()
            for ck in range(nchk):
                for h in range(4):
                    ii = ck * 4 + h
                    S.op("tensor", lambda e, ii=ii, ck=ck, h=h: e.matmul(
                        ps[4 + ck][:, h * 128:(h + 1) * 128], lhsT=VT[:, ii, :], rhs=WST[:, h, :], start=True, stop=True))
            S.bar()
            for ck in range(nchk):
                S.op("vector", lambda e, ck=ck: e.tensor_tensor(
                    out=TMP[:, 4:8, ck * 128:(ck + 1) * 128], in0=ps[4 + ck][:].rearrange("p (h n) -> p h n", n=128),
                    in1=BS, op=ALU.add))
            S.bar()
            S.op("vector", lambda e, w=w: e.tensor_tensor(out=OB[:, 4:8, 0:w], in0=TMP[:, 4:8, 0:w], in1=OB[:, 0:4, 0:w], op=ALU.mult))
            S.bar()
            S.dma("sync", MIX[512:1024, t0:t0 + w].rearrange("(k p) t -> p k t", p=128), OB[:, 4:8, 0:w])
            S.bar()

        ckp("sgu")
        load_weight(WB[:, 0:8 * D].rearrange("p (k n) -> p k n", n=D), w_out[li], 8, D, 512)
        resid_linear(MIX, 8, 2)

        ckp("wout")
        norm_mod(A2, 3)

        ckp("norm2")
        S.dma("sync", FB[0:9, 0:2 * DFF], w_conv[li].rearrange("a b n -> (a b) n"))
        S.bar()
        for ch in range(44):
            S.op("tensor", lambda e, ch=ch: e.transpose(ps[7][:, ch * 9:(ch + 1) * 9], FB[0:9, ch * 128:(ch + 1) * 128], IDENT[0:9, 0:9]))
        S.bar()
        S.op("vector", lambda e: e.tensor_copy(out=WC[:].rearrange("p t c -> p c t"), in_=ps[7][:, 0:396].rearrange("p (c t) -> p c t", t=9)))
        S.bar()
        WU = WB[:, 0:8 * 256].rearrange("p (k n) -> p k n", n=256)
        for m in range(22):
            stg = STG[:, 0:2048].rearrange("p (k n) -> p k n", n=256)
            S.dma("sync", stg[:, :, 0:128], w_up[li, :, m * 128:(m + 1) * 128].rearrange("(k p) n -> p k n", p=128))
            S.dma("gpsimd", stg[:, :, 128:256], w_up[li, :, DFF + m * 128:DFF + (m + 1) * 128].rearrange("(k p) n -> p k n", p=128))
            S.bar()
            S.op("vector", lambda e, stg=stg: e.tensor_copy(out=WU, in_=stg))
            S.bar()
            for part in range(2):
                dst = UPG if part == 0 else UPV
                for ti, (t0, w) in enumerate(TILES):
                    for k in range(8):
                        S.op("tensor", lambda e, ti=ti, k=k, t0=t0, w=w, part=part: e.matmul(
                            ps[ti][:, 0:w], lhsT=WU[:, k, part * 128:(part + 1) * 128], rhs=H[:, k, t0:t0 + w],
                            start=(k == 0), stop=(k == 7)))
                S.bar()
                for ti, (t0, w) in enumerate(TILES):
                    S.op("vector" if ti % 2 == 0 else "scalar", (lambda e, ti=ti, t0=t0, w=w, dst=dst: e.tensor_copy(
                        out=dst[:, t0:t0 + w], in_=ps[ti][:, 0:w])) if ti % 2 == 0 else (lambda e, ti=ti, t0=t0, w=w, dst=dst: e.copy(
                        out=dst[:, t0:t0 + w], in_=ps[ti][:, 0:w])))
                S.bar()
            for part, (eng, src, acc) in enumerate([("vector", UPG, CG), ("vector", UPV, CV)]):
                ch = part * 22 + m
                S.op(eng, lambda e, src=src, acc=acc, ch=ch: e.tensor_scalar(
                    out=acc[:], in0=src[:], scalar1=WC[:, 4, ch:ch + 1], scalar2=None, op0=ALU.mult))
            S.bar()
            taps = [(ky, kx) for ky in range(3) for kx in range(3) if not (ky == 1 and kx == 1)]
            for (ky, kx) in taps:
                dy, dx = ky - 1, kx - 1
                r0, r1 = max(0, -dy), 32 - max(0, dy)
                c0, c1 = max(0, -dx), 64 - max(0, dx)
                for part, (eng, src, acc) in enumerate([("vector", UPG, CG), ("vector", UPV, CV)]):
                    ch = part * 22 + m
                    sv = src[:, LC:NT].rearrange("p (r c) -> p r c", c=64)
                    av = acc[:, LC:NT].rearrange("p (r c) -> p r c", c=64)
                    S.op(eng, lambda e, sv=sv, av=av, ch=ch, ky=ky, kx=kx, r0=r0, r1=r1, c0=c0, c1=c1, dy=dy, dx=dx:
                         e.scalar_tensor_tensor(
                             out=av[:, r0:r1, c0:c1], in0=sv[:, r0 + dy:r1 + dy, c0 + dx:c1 + dx],
                             scalar=WC[:, ky * 3 + kx, ch:ch + 1], in1=av[:, r0:r1, c0:c1], op0=ALU.mult, op1=ALU.add))
                S.bar()
                if ky == 1:
                    for part, (eng, src, acc) in enumerate([("vector", UPG, CG), ("vector", UPV, CV)]):
                        ch = part * 22 + m
                        a0, a1 = max(0, -dx), LC - max(0, dx)
                        S.op(eng, lambda e, src=src, acc=acc, ch=ch, kx=kx, a0=a0, a1=a1, dx=dx:
                             e.scalar_tensor_tensor(
                                 out=acc[:, a0:a1], in0=src[:, a0 + dx:a1 + dx],
                                 scalar=WC[:, 3 + kx, ch:ch + 1], in1=acc[:, a0:a1],
                                 op0=ALU.mult, op1=ALU.add))
                    S.bar()
            S.op("scalar", lambda e: e.activation(out=CG, in_=CG, func=AF.Silu))
            S.bar()
            GB = ACTB[:, 0:5, :].rearrange("p a t -> p (a t)")[:, 0:NT]
            S.op("vector", lambda e, GB=GB: e.tensor_tensor(out=GB, in0=CG, in1=CV, op=ALU.mult))
            S.bar()
            S.dma("sync", GD[m * 128:(m + 1) * 128, :], GB)
            S.bar()

        ckp("ffnup")
        load_weight(WB[:, 0:22 * D].rearrange("p (k n) -> p k n", n=D), w_down[li], 22, D, 128)
        resid_linear(GD, 22, 5)

    except _Stop:
        pass
    S.bar()
    if dbg:
        DF = nc.dram_tensor("DBGF", [128, 32768], F32, kind="ExternalOutput").ap()
        DB = nc.dram_tensor("DBGB", [128, 40960], BF16, kind="ExternalOutput").ap()
        off = 0
        for t_, n_ in [(HRAW[:], 9216), (XTt[:], 4096), (TMPt[:], 4096), (FB[:], 9216), (STG[:], 4096), (RS[:], 512),
                       (MOD[:].rearrange("p q k n -> p (q k n)"), 96), (A1[:].rearrange("p k n -> p (k n)"), 16),
                       (A2[:].rearrange("p k n -> p (k n)"), 16), (W3[:].rearrange("p a g -> p (a g)"), 64),
                       (ARC[:].rearrange("p a g -> p (a g)"), 64), (AIC[:].rearrange("p a g -> p (a g)"), 64),
                       (CR[:], 32), (CI[:], 32), (DV[:], 32), (LR[:], 32), (LI[:], 32)]:
            S.dma("sync", DF[:, off:off + n_], t_)
            off += n_
        offb = 0
        for t_, n_ in [(WB, 22528), (OBt[:], 4096), (ACTBt[:], 11264)]:
            S.dma("gpsimd", DB[:, offb:offb + n_], t_)
            offb += n_
        S.bar()
    for b in range(16):
        t0 = LC + b * 128
        S.dma("sync", XT[:, :, 0:128], XRES[:, t0:t0 + 128].rearrange("(k p) t -> p k t", p=128))
        S.bar()
        S.op("scalar", lambda e: e.activation(out=SQ[:, :, 0:128], in_=XT[:, :, 0:128], func=AF.Square))
        S.bar()
        rstd_from(SQ, 8, 128, 1.0 / D)
        for k in range(8):
            S.op("vector", lambda e, k=k: e.scalar_tensor_tensor(
                out=TMP[:, k, 0:128], in0=XT[:, k, 0:128], scalar=GFIN[:, k:k + 1], in1=RS[:, 0:128],
                op0=ALU.mult, op1=ALU.mult))
        S.bar()
        for k in range(8):
            S.op("tensor", lambda e, k=k: e.transpose(ps[k // 4][:, (k % 4) * 128:(k % 4 + 1) * 128],
                                                       TMP[:, k, 0:128], IDENT[:]))
        S.bar()
        S.op("vector", lambda e: e.tensor_copy(out=XT[:, 0:4, 0:128], in_=ps[0][:].rearrange("p (k t) -> p k t", t=128)))
        S.op("scalar", lambda e: e.copy(out=XT[:, 4:8, 0:128], in_=ps[1][:].rearrange("p (k t) -> p k t", t=128)))
        S.bar()
        S.dma("sync", out[b * 128:(b + 1) * 128, :].rearrange("t (k d) -> t k d", d=128), XT[:, :, 0:128])
        S.bar()

    S.emit()
    es.close()
    return nc


_CONST = None


def _consts():
    ident = np.eye(128, dtype=np.float32)
    sp = np.arange(128) // 16
    m0 = (sp[None, :] >= sp[:, None]).astype(np.float32)
    m1 = (sp[None, :] <= sp[:, None]).astype(np.float32)
    return ident, np.stack([m0, m1])


def kernel(n_layers=4, **inputs):
    nc = build_nc(n_layers)
    ident, mask = _consts()
    in_maps = []
    for b in range(8):
        m = {}
        for k, v in inputs.items():
            v = np.asarray(v)
            if k in ("x", "c", "ctx"):
                m[k] = np.ascontiguousarray(v[b], dtype=np.float32)
            else:
                m[k] = np.ascontiguousarray(v, dtype=np.float32)
        m["ident"] = ident
        m["mask"] = mask
        in_maps.append(m)
    res = run_bass_kernel_spmd(nc, in_maps, core_ids=list(range(8)))
    return np.stack([np.asarray(r["out"], dtype=np.float32) for r in res.results], axis=0)
```

```python
import numpy as np
from contextlib import ExitStack
import concourse.bass as bass
import concourse.mybir as mybir
from concourse.bass_utils import run_bass_kernel_spmd

F32, BF16, I32 = mybir.dt.float32, mybir.dt.bfloat16, mybir.dt.int32
AF = mybir.ActivationFunctionType
ALU = mybir.AluOpType

D = 1024
NT = 2304
LC = 256
LL = 2048
DFF = 2816
EPS = 1e-6
NJ = 288
TILES = [(0, 256)] + [(256 + 512 * i, 512) for i in range(4)]
TWO_PI = 6.283185307179586


class Sched:
    def __init__(self, nc):
        self.nc = nc
        self.stages = [[]]

    def op(self, eng, fn, dma=False):
        self.stages[-1].append((eng, dma, fn))

    def dma(self, eng, out, in_, slow=False):
        if slow:
            self.op(eng, lambda e, o=out, i=in_: e.dma_start(out=o, in_=i, allow_slow_non_contiguous=True), dma=True)
        else:
            self.op(eng, lambda e, o=out, i=in_: e.dma_start(out=o, in_=i), dma=True)

    def bar(self):
        if self.stages[-1]:
            self.stages.append([])

    def emit(self):
        nc = self.nc
        self.bar()
        names = ["c_scalar", "c_vector", "c_gpsimd", "c_tensor", "d_sync", "d_scalar", "d_gpsimd"]
        cum = []
        cur = {n: 0 for n in names}
        for st in self.stages:
            cum.append(dict(cur))
            for (eng, dma, _) in st:
                if dma:
                    cur["d_" + eng] += 16
                else:
                    cur["c_" + eng] += 1
        final = dict(cur)
        with ExitStack() as es:
            sems = {n: es.enter_context(nc.semaphore(n)) for n in names}
            block = es.enter_context(nc.Block())

            def make(engname):
                def body(eng):
                    waited = {n: 0 for n in names}
                    for k, st in enumerate(self.stages):
                        mine = [o for o in st if o[0] == engname]
                        if not mine:
                            continue
                        for n in names:
                            if cum[k][n] > waited[n]:
                                eng.wait_ge(sems[n], cum[k][n])
                                waited[n] = cum[k][n]
                        for (_, dma, fn) in mine:
                            ins = fn(eng)
                            if dma:
                                ins.then_inc(sems["d_" + engname], 16)
                            else:
                                ins.then_inc(sems["c_" + engname], 1)
                    if engname == "sync":
                        for n in names:
                            if final[n] > waited[n]:
                                eng.wait_ge(sems[n], final[n])
                return body

            block.sync(make("sync"))
            block.scalar(make("scalar"))
            block.vector(make("vector"))
            block.gpsimd(make("gpsimd"))
            block.tensor(make("tensor"))


class _Stop(Exception):
    pass


def build_nc(n_layers, n_wl=4, stop=None, dbg=False):
    nc = bass.Bass("TRN2", target_bir_lowering=False)
    S = Sched(nc)
    W = n_wl

    def ckp(name):
        S.bar()
        if stop == name:
            raise _Stop()

    def din(name, shape):
        return nc.dram_tensor(name, list(shape), F32, kind="ExternalInput").ap()

    x_in = din("x", [LL, D])
    c_in = din("c", [D])
    ctx_in = din("ctx", [LC, D])
    cctx_in = din("c_ctx", [D])
    w_ada = din("w_ada", [W, D, 6 * D])
    b_ada = din("b_ada", [W, 6 * D])
    g_mix = din("g_mix", [W, D])
    w_in = din("w_in", [W, D, 1536])
    a_re = din("ssm_a_re", [W, 2, 32, 64])
    a_im = din("ssm_a_im", [W, 2, 32, 64])
    b_re = din("ssm_b_re", [W, 2, 32, 64, 16])
    b_im = din("ssm_b_im", [W, 2, 32, 64, 16])
    c_re = din("ssm_c_re", [W, 2, 32, 16, 64])
    c_im = din("ssm_c_im", [W, 2, 32, 16, 64])
    log_dt = din("ssm_log_dt", [W, 2, 32])
    ssm_d = din("ssm_d", [W, 512])
    w_glu = din("w_glu", [W, 512, 512])
    b_glu = din("b_glu", [W, 512])
    g_sgu = din("g_sgu", [W, 512])
    w_sp = din("w_spatial", [W, 4, 128, 128])
    b_sp = din("b_spatial", [W, 4, 128])
    w_out = din("w_out", [W, D, D])
    g_ffn = din("g_ffn", [W, D])
    w_up = din("w_up", [W, D, 2 * DFF])
    w_conv = din("w_conv", [W, 3, 3, 2 * DFF])
    w_down = din("w_down", [W, DFF, D])
    g_final = din("g_final", [D])
    ident_in = din("ident", [128, 128])
    mask_in = din("mask", [2, 128, 128])
    out = nc.dram_tensor("out", [LL, D], F32, kind="ExternalOutput").ap()

    SK = dict(kind="ExternalOutput") if dbg else {}
    XRES = nc.dram_tensor("XRES", [D, NT], F32, **SK).ap()
    USSMP = nc.dram_tensor("USSMP", [512, 8, NJ], BF16, **SK).ap()
    YSP = nc.dram_tensor("YSP", [128, 32, NJ], F32, **SK).ap()
    UG = nc.dram_tensor("UG", [512, NT], BF16, **SK).ap()
    VG = nc.dram_tensor("VG", [512, NT], F32, **SK).ap()
    MIX = nc.dram_tensor("MIX", [D, NT], BF16, **SK).ap()
    GD = nc.dram_tensor("GD", [DFF, NT], BF16, **SK).ap()

    es = ExitStack()

    def sb(name, shape, dt=F32):
        return es.enter_context(nc.sbuf_tensor(name, list(shape), dt))

    ps = [es.enter_context(nc.psum_tensor("ps%d" % i, [128, 512], F32)) for i in range(8)]

    IDENT = sb("IDENT", [128, 128])
    ONESB = sb("ONESB", [128, 128], BF16)
    MASKS = sb("MASKS", [128, 2, 128])
    SIGN = sb("SIGN", [128, 1])
    NSIGN = sb("NSIGN", [128, 1])
    SC = sb("SC", [128, 8, 2])
    MOD = sb("MOD", [128, 6, 8, 2])
    BADA = sb("BADA", [128, 6, 8])
    GM = sb("GM", [128, 8])
    GF = sb("GF", [128, 8])
    GFIN = sb("GFIN", [128, 8])
    A1 = sb("A1", [128, 8, 2])
    A2 = sb("A2", [128, 8, 2])
    HRAW = sb("HRAW", [128, 9216])
    H = HRAW[:].bitcast(BF16).rearrange("p (k t) -> p k t", t=NT)
    YS = HRAW[:].rearrange("p (g j) -> p g j", j=NJ)
    STG = sb("STG", [128, 4096])
    SQ = STG[:, 0:2048].bitcast(BF16).rearrange("p (k t) -> p k t", t=512)
    WBt = sb("WB", [128, 22528], BF16)
    WB = WBt[:]
    TOEP = WB[:, 0:4096].rearrange("p (g n) -> p g n", n=128)
    ET = WB[:, 4096:8192].rearrange("p (g n) -> p g n", n=128)
    ESW = WB[:, 8192:12288].rearrange("p (g n) -> p g n", n=128)
    IM = WB[:, 12288:21504].rearrange("p (g j) -> p g j", j=NJ)
    XTt = sb("XT", [128, 4096])
    XT = XTt[:].rearrange("p (k t) -> p k t", t=512)
    LT = XTt[:].rearrange("p (g s c) -> p g s c", s=8, c=16)
    SS = XTt[:].rearrange("p (j a g) -> p j a g", a=2, g=32)
    TMPt = sb("TMP", [128, 4096])
    TMP = TMPt[:].rearrange("p (k t) -> p k t", t=512)
    RT = TMPt[:].rearrange("p (g s c) -> p g s c", s=8, c=16)
    RS = sb("RS", [128, 512])
    OBt = sb("OB", [128, 4096], BF16)
    OB = OBt[:].rearrange("p (k t) -> p k t", t=512)
    RB = OBt[:].rearrange("p (g n) -> p g n", n=128)
    ACTBt = sb("ACTB", [128, 11264], BF16)
    ACTB = ACTBt[:].rearrange("p (k t) -> p k t", t=512)
    USP = ACTBt[:, 0:9216].rearrange("p (q s j) -> p q s j", s=8, j=NJ)
    HH = ACTBt[:, 0:9216].rearrange("p (j g) -> p j g", g=32)
    FB = sb("FB", [128, 9216])
    UPG = FB[:, 0:2304]; UPV = FB[:, 2304:4608]; CG = FB[:, 4608:6912]; CV = FB[:, 6912:9216]
    def ftab(i):
        return FB[:, i * 512:(i + 1) * 512].rearrange("p (g k) -> p g k", k=16)
    ARG, ARGC, EARG, NF, NFC, PRE, PIM, MAG, BX1, BX2, CX1, CX2 = [ftab(i) for i in range(12)]
    def qtab(i):
        return FB[:, 6144 + i * 256:6144 + (i + 1) * 256].rearrange("p (g k) -> p g k", k=8)
    QR, QI, QT, PA, PB = [qtab(i) for i in range(5)]
    NI = FB[:, 7424:7936].bitcast(I32).rearrange("p (g k) -> p g k", k=16)
    NIC = FB[:, 7936:8448].bitcast(I32).rearrange("p (g k) -> p g k", k=16)
    BS = FB[:, 0:512].rearrange("p (h q) -> p h q", q=128)
    VT = STG[:, 2048:4096].bitcast(BF16).rearrange("p (a n) -> p a n", n=128)
    W3 = sb("W3", [128, 2, 32])
    G3 = sb("G3", [128, 3, 32])
    T1 = sb("T1", [128, 2, 32])
    T2 = sb("T2", [128, 2, 32])
    ARp = sb("ARp", [128, 32]); AIp = sb("AIp", [128, 32]); LDT = sb("LDT", [128, 32])
    LR = sb("LR", [128, 32]); LI = sb("LI", [128, 32])
    CR = sb("CR", [128, 32]); CI = sb("CI", [128, 32]); NR = sb("NR", [128, 32]); DEN = sb("DEN", [128, 32])
    TA = sb("TA", [128, 32]); TB = sb("TB", [128, 32])
    ARC = sb("ARC", [128, 2, 32]); AIC = sb("AIC", [128, 2, 32])
    DV = sb("DV", [128, 32])
    BGLU = sb("BGLU", [128, 4]); GSGU = sb("GSGU", [128, 4])
    WST = sb("WST", [128, 4, 128], BF16)
    WC = sb("WC", [128, 9, 44])

    S.dma("sync", IDENT[:], ident_in[:, :])
    S.dma("sync", MASKS[:], mask_in.rearrange("m p q -> p m q"))
    S.op("vector", lambda e: e.memset(ONESB[:], 1.0))
    S.op("vector", lambda e: e.memset(SIGN[0:64, :], -1.0))
    S.op("vector", lambda e: e.memset(SIGN[64:128, :], 1.0))
    S.op("vector", lambda e: e.memset(NSIGN[0:64, :], 1.0))
    S.op("vector", lambda e: e.memset(NSIGN[64:128, :], -1.0))
    S.dma("sync", STG[0:8, 0:128], c_in.rearrange("(k p) -> k p", p=128))
    S.dma("sync", STG[8:16, 0:128], cctx_in.rearrange("(k p) -> k p", p=128))
    S.dma("sync", STG[16:24, 0:128], g_final.rearrange("(k p) -> k p", p=128))
    S.bar()
    S.op("tensor", lambda e: e.transpose(ps[7][:, 0:24], STG[0:24, 0:128], IDENT[0:24, 0:24]))
    S.bar()
    S.op("vector", lambda e: e.tensor_copy(out=SC[:, :, 0], in_=ps[7][:, 0:8]))
    S.op("vector", lambda e: e.tensor_copy(out=SC[:, :, 1], in_=ps[7][:, 8:16]))
    S.op("vector", lambda e: e.tensor_copy(out=GFIN[:], in_=ps[7][:, 16:24]))
    S.bar()
    S.op("scalar", lambda e: e.activation(out=SC[:], in_=SC[:], func=AF.Silu))
    S.bar()

    def in_transpose(src, nblk, tok0):
        for b in range(nblk):
            S.dma("sync", TMP[:, :, 0:128], src[b * 128:(b + 1) * 128, :].rearrange("t (k d) -> t k d", d=128))
            S.bar()
            for k in range(8):
                S.op("tensor", lambda e, k=k: e.transpose(ps[k // 4][:, (k % 4) * 128:(k % 4 + 1) * 128],
                                                           TMP[:, k, 0:128], IDENT[:]))
            S.bar()
            S.op("vector", lambda e: e.tensor_copy(out=XT[:, 0:4, 0:128], in_=ps[0][:].rearrange("p (k t) -> p k t", t=128)))
            S.op("scalar", lambda e: e.copy(out=XT[:, 4:8, 0:128], in_=ps[1][:].rearrange("p (k t) -> p k t", t=128)))
            S.bar()
            t0 = tok0 + b * 128
            S.dma("sync", XRES[:, t0:t0 + 128].rearrange("(k p) t -> p k t", p=128), XT[:, :, 0:128])
            S.bar()

    in_transpose(ctx_in, 2, 0)
    in_transpose(x_in, 16, 256)

    def load_weight(dst, wap, kch, ncols, cb):
        for c0 in range(0, ncols, cb):
            stg = STG[:, 0:kch * cb].rearrange("p (k n) -> p k n", n=cb)
            S.dma("sync", stg, wap[:, c0:c0 + cb].rearrange("(k p) n -> p k n", p=128))
            S.bar()
            S.op("vector", lambda e, stg=stg, c0=c0: e.tensor_copy(out=dst[:, :, c0:c0 + cb], in_=stg))
            S.bar()

    def rstd_from(src_sq, nk, w, inv_n):
        for k in range(nk):
            S.op("tensor", lambda e, k=k: e.matmul(ps[0][:, 0:w], lhsT=ONESB[:], rhs=src_sq[:, k, 0:w],
                                                    start=(k == 0), stop=(k == nk - 1)))
        S.bar()
        S.op("scalar", lambda e: e.activation(out=RS[:, 0:w], in_=ps[0][:, 0:w], func=AF.Sqrt, bias=EPS, scale=inv_n))
        S.bar()
        S.op("vector", lambda e: e.reciprocal(out=RS[:, 0:w], in_=RS[:, 0:w]))
        S.bar()

    def norm_mod(Acoef, which_shift):
        for (t0, w) in TILES:
            sel = 1 if t0 == 0 else 0
            S.dma("sync", XT[:, :, 0:w], XRES[:, t0:t0 + w].rearrange("(k p) t -> p k t", p=128))
            S.bar()
            S.op("scalar", lambda e, w=w: e.activation(out=SQ[:, :, 0:w], in_=XT[:, :, 0:w], func=AF.Square))
            S.bar()
            rstd_from(SQ, 8, w, 1.0 / D)
            for k in range(8):
                S.op("vector", lambda e, k=k, w=w, sel=sel: e.scalar_tensor_tensor(
                    out=TMP[:, k, 0:w], in0=XT[:, k, 0:w], scalar=Acoef[:, k, sel:sel + 1], in1=RS[:, 0:w],
                    op0=ALU.mult, op1=ALU.mult))
            S.bar()
            for k in range(8):
                S.op("scalar", lambda e, k=k, w=w, sel=sel, t0=t0: e.activation(
                    out=H[:, k, t0:t0 + w], in_=TMP[:, k, 0:w], func=AF.Identity,
                    bias=MOD[:, which_shift, k, sel:sel + 1], scale=1.0))
            S.bar()

    def resid_linear(src_dram, kch, gate_idx):
        Wv = WB[:, 0:kch * D].rearrange("p (k n) -> p k n", n=D)
        for (t0, w) in TILES:
            sel = 1 if t0 == 0 else 0
            S.dma("sync", ACTB[:, 0:kch, 0:w], src_dram[:, t0:t0 + w].rearrange("(k p) t -> p k t", p=128))
            S.dma("gpsimd", XT[:, :, 0:w], XRES[:, t0:t0 + w].rearrange("(k p) t -> p k t", p=128))
            S.bar()
            for m in range(8):
                for k in range(kch):
                    S.op("tensor", lambda e, m=m, k=k, w=w: e.matmul(
                        ps[m][:, 0:w], lhsT=Wv[:, k, m * 128:(m + 1) * 128], rhs=ACTB[:, k, 0:w],
                        start=(k == 0), stop=(k == kch - 1)))
            S.bar()
            for m in range(8):
                S.op("vector", lambda e, m=m, w=w, sel=sel: e.scalar_tensor_tensor(
                    out=XT[:, m, 0:w], in0=ps[m][:, 0:w], scalar=MOD[:, gate_idx, m, sel:sel + 1], in1=XT[:, m, 0:w],
                    op0=ALU.mult, op1=ALU.add))
            S.bar()
            S.dma("sync", XRES[:, t0:t0 + w].rearrange("(k p) t -> p k t", p=128), XT[:, :, 0:w])
            S.bar()

    try:
      for li in range(n_layers):
        S.dma("sync", STG[0:48, 0:128], b_ada[li].rearrange("(k p) -> k p", p=128))
        S.dma("sync", STG[48:56, 0:128], g_mix[li].rearrange("(k p) -> k p", p=128))
        S.dma("sync", STG[56:64, 0:128], g_ffn[li].rearrange("(k p) -> k p", p=128))
        S.dma("sync", STG[64:68, 0:128], b_glu[li].rearrange("(k p) -> k p", p=128))
        S.dma("sync", STG[68:72, 0:128], g_sgu[li].rearrange("(k p) -> k p", p=128))
        S.bar()
        S.op("tensor", lambda e: e.transpose(ps[7][:, 0:72], STG[0:72, 0:128], IDENT[0:72, 0:72]))
        S.bar()
        S.op("vector", lambda e: e.tensor_copy(out=BADA[:].rearrange("p q k -> p (q k)"), in_=ps[7][:, 0:48]))
        S.op("vector", lambda e: e.tensor_copy(out=GM[:], in_=ps[7][:, 48:56]))
        S.op("vector", lambda e: e.tensor_copy(out=GF[:], in_=ps[7][:, 56:64]))
        S.op("vector", lambda e: e.tensor_copy(out=BGLU[:], in_=ps[7][:, 64:68]))
        S.op("vector", lambda e: e.tensor_copy(out=GSGU[:], in_=ps[7][:, 68:72]))
        S.bar()
        for blk in range(12):
            q, mh = blk // 2, blk % 2
            WA = STG[:].rearrange("p (k n) -> p k n", n=512)
            S.dma("sync", WA, w_ada[li, :, blk * 512:(blk + 1) * 512].rearrange("(k p) n -> p k n", p=128))
            S.bar()
            for mm in range(4):
                m = mh * 4 + mm
                for k in range(8):
                    S.op("tensor", lambda e, m=m, mm=mm, k=k, q=q: e.matmul(
                        ps[0][:, (q * 8 + m) * 2:(q * 8 + m) * 2 + 2], lhsT=WA[:, k, mm * 128:(mm + 1) * 128],
                        rhs=SC[:, k, :], start=(k == 0), stop=(k == 7)))
            S.bar()
        S.op("vector", lambda e: e.tensor_tensor(
            out=MOD[:].rearrange("p q k n -> p (q k) n"), in0=ps[0][:, 0:96].rearrange("p (a n) -> p a n", n=2),
            in1=BADA[:].rearrange("p q k -> p (q k)").unsqueeze(2).to_broadcast([128, 48, 2]), op=ALU.add))
        S.bar()
        S.op("vector", lambda e: e.scalar_tensor_tensor(
            out=A1[:], in0=MOD[:, 1, :, :], scalar=1.0, in1=GM[:].unsqueeze(2).to_broadcast([128, 8, 2]),
            op0=ALU.add, op1=ALU.mult))
        S.op("vector", lambda e: e.scalar_tensor_tensor(
            out=A2[:], in0=MOD[:, 4, :, :], scalar=1.0, in1=GF[:].unsqueeze(2).to_broadcast([128, 8, 2]),
            op0=ALU.add, op1=ALU.mult))
        S.bar()

        ckp("ada")
        norm_mod(A1, 0)

        ckp("norm1")
        WIN = WB[:, 0:8 * 1536].rearrange("p (k n) -> p k n", n=1536)
        load_weight(WIN, w_in[li], 8, 1536, 512)
        for (t0, w) in TILES:
            j0, nj = t0 // 8, w // 8
            for grp in range(2):
                ms = list(range(8)) if grp == 0 else list(range(8, 12))
                for bi, m in enumerate(ms):
                    for k in range(8):
                        S.op("tensor", lambda e, bi=bi, m=m, k=k, w=w, t0=t0: e.matmul(
                            ps[bi][:, 0:w], lhsT=WIN[:, k, m * 128:(m + 1) * 128], rhs=H[:, k, t0:t0 + w],
                            start=(k == 0), stop=(k == 7)))
                S.bar()
                for bi, m in enumerate(ms):
                    if m < 4:
                        S.op("vector", lambda e, bi=bi, m=m, w=w, j0=j0, nj=nj: e.tensor_copy(
                            out=USP[:, m, :, j0:j0 + nj].rearrange("p s j -> p j s"),
                            in_=ps[bi][:, 0:w].rearrange("p (j s) -> p j s", s=8)))
                    elif m < 8:
                        S.op("scalar", lambda e, bi=bi, m=m, w=w: e.activation(
                            out=OB[:, m - 4, 0:w], in_=ps[bi][:, 0:w], func=AF.Gelu_apprx_tanh))
                    else:
                        S.op("scalar", lambda e, bi=bi, m=m, w=w: e.activation(
                            out=TMP[:, m - 8, 0:w], in_=ps[bi][:, 0:w], func=AF.Gelu_apprx_tanh))
                S.bar()
            S.dma("sync", UG[:, t0:t0 + w].rearrange("(k p) t -> p k t", p=128), OB[:, 0:4, 0:w])
            S.dma("gpsimd", VG[:, t0:t0 + w].rearrange("(k p) t -> p k t", p=128), TMP[:, 0:4, 0:w])
            S.bar()

        ckp("win")
        S.dma("sync", USSMP.rearrange("(q p) s j -> p q s j", p=128), USP)
        S.dma("gpsimd", STG[0:32, 0:16], ssm_d[li].rearrange("(g c) -> g c", c=16))
        S.bar()
        for s_ in range(8):
            S.dma("sync" if s_ % 2 == 0 else "gpsimd", IM[s_ * 16:(s_ + 1) * 16, :, :],
                  USSMP[:, s_, :].rearrange("(g c) j -> c g j", c=16))
        S.bar()

        S.op("vector", lambda e: e.tensor_copy(out=STG[0:32, 128:256].rearrange("p (s c) -> p s c", c=16),
                                               in_=STG[0:32, 0:16].unsqueeze(1).to_broadcast([32, 8, 16])))
        S.bar()
        S.op("tensor", lambda e: e.transpose(ps[7][:, 0:32], STG[0:32, 128:256], IDENT[0:32, 0:32]))
        S.bar()
        S.op("vector", lambda e: e.tensor_copy(out=DV[:], in_=ps[7][:, 0:32]))
        ckp("im2col")
        for dr in range(2):
            CIN1 = STG[:, 0:512].rearrange("p (q n) -> p q n", n=128)
            CIN2 = STG[:, 512:1024].rearrange("p (q n) -> p q n", n=128)
            for hf in range(2):
                lo = slice(hf * 64, hf * 64 + 64)
                csrc = [c_re, c_im] if hf == 0 else [c_im, c_re]
                S.dma("sync", CIN1[:, :, lo], csrc[0][li, dr].rearrange("(q g) c p -> (g c) q p", q=4))
                S.dma("gpsimd", CIN2[:, :, lo], csrc[1][li, dr].rearrange("(q g) c p -> (g c) q p", q=4))
            S.bar()
            for q in range(4):
                S.op("tensor", lambda e, q=q: e.transpose(ps[0][:, q * 128:(q + 1) * 128], CIN1[:, q, :], IDENT[:]))
                S.op("tensor", lambda e, q=q: e.transpose(ps[1][:, q * 128:(q + 1) * 128], CIN2[:, q, :], IDENT[:]))
            S.bar()
            S.op("vector", lambda e: e.tensor_copy(out=CX1.rearrange("p g c -> p (g c)"), in_=ps[0][:]))
            S.op("scalar", lambda e: e.copy(out=CX2.rearrange("p g c -> p (g c)"), in_=ps[1][:]))
            S.bar()
            BIN1 = STG[0:32, 0:2048].rearrange("p (h q c) -> p h q c", h=2, c=16)
            BIN2 = STG[0:32, 2048:4096].rearrange("p (h q c) -> p h q c", h=2, c=16)
            S.dma("sync", BIN1[:, 0, :, :], b_re[li, dr])
            S.dma("gpsimd", BIN1[:, 1, :, :], b_im[li, dr])
            S.dma("sync", BIN2[:, 0, :, :], b_im[li, dr])
            S.dma("gpsimd", BIN2[:, 1, :, :], b_re[li, dr])
            AIN = RS[0:32, 0:256].rearrange("p (a q) -> p a q", q=64)
            S.dma("sync", AIN[:, 0, :], a_re[li, dr])
            S.dma("gpsimd", AIN[:, 1, :], a_re[li, dr])
            S.dma("sync", AIN[:, 2, :], a_im[li, dr])
            S.dma("gpsimd", AIN[:, 3, :], a_im[li, dr])
            S.bar()
            for c_ in range(16):
                S.op("tensor", lambda e, c_=c_: e.transpose(ps[2][:, c_ * 32:(c_ + 1) * 32], BIN1[:, :, :, c_], IDENT[0:32, 0:32]))
                S.op("tensor", lambda e, c_=c_: e.transpose(ps[3][:, c_ * 32:(c_ + 1) * 32], BIN2[:, :, :, c_], IDENT[0:32, 0:32]))
            S.op("tensor", lambda e: e.transpose(ps[4][:, 0:32], RS[0:32, 0:128], IDENT[0:32, 0:32]))
            S.op("tensor", lambda e: e.transpose(ps[4][:, 32:64], RS[0:32, 128:256], IDENT[0:32, 0:32]))
            S.bar()
            S.op("vector", lambda e: e.tensor_copy(out=BX1.rearrange("p g c -> p c g"), in_=ps[2][:].rearrange("p (c g) -> p c g", g=32)))
            S.op("scalar", lambda e: e.copy(out=BX2.rearrange("p g c -> p c g"), in_=ps[3][:].rearrange("p (c g) -> p c g", g=32)))
            S.op("vector", lambda e: e.tensor_copy(out=ARp[:], in_=ps[4][:, 0:32]))
            S.op("vector", lambda e: e.tensor_copy(out=AIp[:], in_=ps[4][:, 32:64]))
            S.dma("sync", LDT[:], log_dt[li, dr].partition_broadcast(128))
            S.bar()
            ckp("pl%d" % dr)
            S.op("scalar", lambda e: e.activation(out=LDT[:], in_=LDT[:], func=AF.Exp))
            S.bar()
            S.op("vector", lambda e: e.tensor_tensor(out=LR[:], in0=ARp[:], in1=LDT[:], op=ALU.mult))
            S.op("gpsimd", lambda e: e.tensor_tensor(out=LI[:], in0=AIp[:], in1=LDT[:], op=ALU.mult))
            S.bar()
            ckp("pb%d" % dr)
            ks = list(range(-8, 0)) + list(range(1, 9))
            for idx, kk in enumerate(ks):
                S.op("vector", lambda e, idx=idx, kk=kk: e.tensor_scalar(
                    out=ARG[:, :, idx], in0=LI[:], scalar1=float(kk), scalar2=None, op0=ALU.mult))
                S.op("gpsimd", lambda e, idx=idx, kk=kk: e.tensor_scalar(
                    out=EARG[:, :, idx], in0=LR[:], scalar1=float(kk), scalar2=None, op0=ALU.mult))
            S.bar()
            ckp("pc%d" % dr)
            S.op("vector", lambda e: e.tensor_scalar(out=ARGC, in0=ARG, scalar1=TWO_PI / 4, scalar2=None, op0=ALU.add))
            S.op("scalar", lambda e: e.activation(out=MAG, in_=EARG, func=AF.Exp))
            S.bar()
            ckp("pd%d" % dr)
            S.op("vector", lambda e: e.tensor_scalar(out=NI, in0=ARG, scalar1=1.0 / TWO_PI, scalar2=None, op0=ALU.mult))
            S.op("gpsimd", lambda e: e.tensor_scalar(out=NIC, in0=ARGC, scalar1=1.0 / TWO_PI, scalar2=None, op0=ALU.mult))
            S.bar()
            ckp("pe%d" % dr)
            S.op("vector", lambda e: e.tensor_copy(out=NF, in_=NI))
            S.op("gpsimd", lambda e: e.tensor_copy(out=NFC, in_=NIC))
            S.bar()
            ckp("pf%d" % dr)
            S.op("vector", lambda e: e.scalar_tensor_tensor(out=ARG, in0=NF, scalar=-TWO_PI, in1=ARG, op0=ALU.mult, op1=ALU.add))
            S.op("vector", lambda e: e.scalar_tensor_tensor(out=ARGC, in0=NFC, scalar=-TWO_PI, in1=ARGC, op0=ALU.mult, op1=ALU.add))
            S.bar()
            S.op("vector", lambda e: e.tensor_scalar(out=ARG, in0=ARG, scalar1=3.1415925, scalar2=-3.1415925, op0=ALU.min, op1=ALU.max))
            S.op("gpsimd", lambda e: e.tensor_scalar(out=ARGC, in0=ARGC, scalar1=3.1415925, scalar2=-3.1415925, op0=ALU.min, op1=ALU.max))
            S.bar()
            ckp("pg%d" % dr)
            S.op("scalar", lambda e: e.activation(out=PIM, in_=ARG, func=AF.Sin))
            S.op("scalar", lambda e: e.activation(out=PRE, in_=ARGC, func=AF.Sin))
            S.bar()
            S.op("vector", lambda e: e.tensor_tensor(out=PIM, in0=PIM, in1=MAG, op=ALU.mult))
            S.op("gpsimd", lambda e: e.tensor_tensor(out=PRE, in0=PRE, in1=MAG, op=ALU.mult))
            S.bar()
            ckp("ph%d" % dr)
            S.op("vector", lambda e: e.tensor_scalar(out=NR[:], in0=PRE[:, :, 8], scalar1=-1.0, scalar2=None, op0=ALU.add))
            S.op("gpsimd", lambda e: e.tensor_tensor(out=DEN[:], in0=ARp[:], in1=ARp[:], op=ALU.mult))
            ckp("c0")
            S.op("vector", lambda e: e.tensor_tensor(out=TA[:], in0=AIp[:], in1=AIp[:], op=ALU.mult))
            ckp("c1")
            S.op("vector", lambda e: e.tensor_tensor(out=DEN[:], in0=DEN[:], in1=TA[:], op=ALU.add))
            ckp("c2")
            S.op("vector", lambda e: e.reciprocal(out=DEN[:], in_=DEN[:]))
            ckp("c3")
            S.op("vector", lambda e: e.tensor_tensor(out=TA[:], in0=NR[:], in1=ARp[:], op=ALU.mult))
            S.op("gpsimd", lambda e: e.tensor_tensor(out=TB[:], in0=PIM[:, :, 8], in1=AIp[:], op=ALU.mult))
            ckp("c4")
            S.op("vector", lambda e: e.tensor_tensor(out=CR[:], in0=TA[:], in1=TB[:], op=ALU.add))
            ckp("c5")
            S.op("vector", lambda e: e.tensor_tensor(out=TA[:], in0=PIM[:, :, 8], in1=ARp[:], op=ALU.mult))
            S.op("gpsimd", lambda e: e.tensor_tensor(out=TB[:], in0=NR[:], in1=AIp[:], op=ALU.mult))
            ckp("c6")
            S.op("vector", lambda e: e.tensor_tensor(out=CI[:], in0=TA[:], in1=TB[:], op=ALU.subtract))
            ckp("c7")
            S.op("vector", lambda e: e.tensor_tensor(out=CR[:], in0=CR[:], in1=DEN[:], op=ALU.mult))
            S.op("gpsimd", lambda e: e.tensor_tensor(out=CI[:], in0=CI[:], in1=DEN[:], op=ALU.mult))
            ckp("c8")
            ckp("pi%d" % dr)
            CRb = CR[:].unsqueeze(2).to_broadcast([128, 32, 8])
            CIb = CI[:].unsqueeze(2).to_broadcast([128, 32, 8])
            S.op("vector", lambda e: e.tensor_tensor(out=QR, in0=PRE[:, :, 0:8], in1=CRb, op=ALU.mult))
            S.op("gpsimd", lambda e: e.tensor_tensor(out=QT, in0=PIM[:, :, 0:8], in1=CIb, op=ALU.mult))
            S.bar()
            S.op("vector", lambda e: e.tensor_tensor(out=QR, in0=QR, in1=QT, op=ALU.subtract))
            S.bar()
            S.op("vector", lambda e: e.tensor_tensor(out=QI, in0=PRE[:, :, 0:8], in1=CIb, op=ALU.mult))
            S.op("gpsimd", lambda e: e.tensor_tensor(out=QT, in0=PIM[:, :, 0:8], in1=CRb, op=ALU.mult))
            S.bar()
            S.op("vector", lambda e: e.tensor_tensor(out=QI, in0=QI, in1=QT, op=ALU.add))
            S.bar()
            S.op("vector", lambda e: e.tensor_scalar(out=QI, in0=QI, scalar1=SIGN[:, 0:1], scalar2=None, op0=ALU.mult))
            S.op("gpsimd", lambda e: e.tensor_scalar(out=PA, in0=PRE[:, :, 8:16], scalar1=NSIGN[:, 0:1], scalar2=None, op0=ALU.mult))
            S.op("scalar", lambda e: e.mul(out=PB, in_=PIM[:, :, 8:16], mul=-1.0))
            S.bar()
            S.op("vector", lambda e: e.tensor_copy(out=ARC[:, 0, :], in_=PRE[:, :, 15]))
            S.op("vector", lambda e: e.tensor_copy(out=ARC[:, 1, :], in_=PRE[:, :, 15]))
            S.op("gpsimd", lambda e: e.tensor_scalar(out=AIC[:, 0, :], in0=PIM[:, :, 15], scalar1=SIGN[:, 0:1], scalar2=None, op0=ALU.mult))
            S.op("gpsimd", lambda e: e.tensor_scalar(out=AIC[:, 1, :], in0=PIM[:, :, 15], scalar1=NSIGN[:, 0:1], scalar2=None, op0=ALU.mult))
            S.bar()
            ckp("prep%d" % dr + "")
            for s_ in range(8):
                qi = (7 - s_) if dr == 0 else s_
                ri = s_ if dr == 0 else (7 - s_)
                S.op("vector", lambda e, s_=s_, qi=qi: e.tensor_tensor(
                    out=LT[:, :, s_, :], in0=BX1, in1=QR[:, :, qi:qi + 1].to_broadcast([128, 32, 16]), op=ALU.mult))
                S.op("gpsimd", lambda e, s_=s_, ri=ri: e.tensor_tensor(
                    out=RT[:, :, s_, :], in0=CX1, in1=PA[:, :, ri:ri + 1].to_broadcast([128, 32, 16]), op=ALU.mult))
            S.bar()
            TL = STG[:].rearrange("p (g s c) -> p g s c", s=8, c=16)
            for s_ in range(8):
                qi = (7 - s_) if dr == 0 else s_
                S.op("vector" if s_ % 2 == 0 else "gpsimd", lambda e, s_=s_, qi=qi: e.tensor_tensor(
                    out=TL[:, :, s_, :], in0=BX2, in1=QI[:, :, qi:qi + 1].to_broadcast([128, 32, 16]), op=ALU.mult))
            S.bar()
            S.op("vector", lambda e: e.tensor_tensor(out=LT, in0=LT, in1=TL, op=ALU.add))
            S.bar()
            for s_ in range(8):
                ri = s_ if dr == 0 else (7 - s_)
                S.op("vector" if s_ % 2 == 0 else "gpsimd", lambda e, s_=s_, ri=ri: e.tensor_tensor(
                    out=TL[:, :, s_, :], in0=CX2, in1=PB[:, :, ri:ri + 1].to_broadcast([128, 32, 16]), op=ALU.mult))
            S.bar()
            S.op("vector", lambda e: e.tensor_tensor(out=RT, in0=RT, in1=TL, op=ALU.add))
            S.bar()
            ckp("lr%d" % dr + "")
            for rnd in range(4):
                for gi in range(8):
                    g = rnd * 8 + gi
                    Lg = LT[:, g, :, :].rearrange("p s c -> p (s c)")
                    Rg = RT[:, g, :, :].rearrange("p s c -> p (s c)")
                    S.op("tensor", lambda e, gi=gi, Lg=Lg, Rg=Rg: e.matmul(
                        ps[gi // 4][:, (gi % 4) * 128:(gi % 4 + 1) * 128], lhsT=Lg, rhs=Rg, start=True, stop=True))
                    S.op("tensor", lambda e, gi=gi, Lg=Lg: e.transpose(
                        ps[2 + gi // 4][:, (gi % 4) * 128:(gi % 4 + 1) * 128], Lg, IDENT[:]))
                S.bar()
                for bk in range(2):
                    g0 = rnd * 8 + bk * 4
                    S.op("vector", lambda e, bk=bk, g0=g0, dr=dr: e.tensor_tensor(
                        out=TOEP[:, g0:g0 + 4, :], in0=ps[bk][:].rearrange("p (g n) -> p g n", n=128),
                        in1=MASKS[:, dr:dr + 1, :].to_broadcast([128, 4, 128]), op=ALU.mult))
                    S.op("scalar", lambda e, bk=bk, g0=g0: e.copy(
                        out=ET[:, g0:g0 + 4, :], in_=ps[2 + bk][:].rearrange("p (g n) -> p g n", n=128)))
                S.bar()
                for bk in range(2):
                    g0 = rnd * 8 + bk * 4
                    S.op("vector", lambda e, bk=bk, g0=g0: e.tensor_copy(
                        out=ESW[:, g0:g0 + 4, 0:64], in_=ps[2 + bk][:].rearrange("p (g n) -> p g n", n=128)[:, :, 64:128]))
                    S.op("vector", lambda e, bk=bk, g0=g0: e.tensor_copy(
                        out=ESW[:, g0:g0 + 4, 64:128], in_=ps[2 + bk][:].rearrange("p (g n) -> p g n", n=128)[:, :, 0:64]))
                S.bar()
            S.op("scalar", lambda e: e.copy(out=RB.rearrange("p g n -> p (g n)"), in_=RT.rearrange("p g s c -> p (g s c)")))
            if dr == 0:
                for g in range(32):
                    S.op("vector", lambda e, g=g: e.scalar_tensor_tensor(
                        out=TOEP[:, g, :], in0=IDENT[:], scalar=DV[:, g:g + 1], in1=TOEP[:, g, :],
                        op0=ALU.mult, op1=ALU.add))
            S.op("gpsimd", lambda e: e.memset(W3[:], 0.0))
            S.bar()
            ckp("tiles%d" % dr + "")
            blocks = [(0, 32)] + [(32 + 64 * b, 64) for b in range(4)]
            order = blocks if dr == 0 else [blocks[0]] + blocks[:0:-1]
            for (jb, nb) in order:
                for half in range(2):
                    for gi in range(16):
                        g = half * 16 + gi
                        for arr in range(2):
                            ii = gi * 2 + arr
                            Em = ET if arr == 0 else ESW
                            S.op("tensor", lambda e, ii=ii, g=g, Em=Em, jb=jb, nb=nb: e.matmul(
                                ps[ii // 8][:, (ii % 8) * 64:(ii % 8) * 64 + nb], lhsT=Em[:, g, :],
                                rhs=IM[:, g, jb:jb + nb], start=True, stop=True))
                    S.bar()
                    for bk in range(4):
                        g0 = half * 16 + bk * 4
                        S.op("vector" if bk % 2 == 0 else "scalar", (lambda e, bk=bk, g0=g0, nb=nb: e.tensor_copy(
                            out=SS[:, 0:nb, :, g0:g0 + 4].rearrange("p j a g -> p g a j"),
                            in_=ps[bk][:].rearrange("p (g a j) -> p g a j", a=2, j=64)[:, :, :, 0:nb]))
                            if bk % 2 == 0 else (lambda e, bk=bk, g0=g0, nb=nb: e.copy(
                            out=SS[:, 0:nb, :, g0:g0 + 4].rearrange("p j a g -> p g a j"),
                            in_=ps[bk][:].rearrange("p (g a j) -> p g a j", a=2, j=64)[:, :, :, 0:nb])))
                    S.bar()
                js = list(range(jb, jb + nb)) if dr == 0 else list(range(jb + nb - 1, jb - 1, -1))
                pjl = None
                for j in js:
                    jl = j - jb
                    Wc = W3[:] if pjl is None else SS[:, pjl, :, :]
                    S.op("vector", lambda e, jl=jl, Wc=Wc: e.tensor_tensor(out=G3[:, 0:2, :], in0=Wc, in1=SS[:, jl, :, :], op=ALU.add))
                    S.op("vector", lambda e: e.tensor_tensor(out=T1[:], in0=G3[:, 0:2, :], in1=ARC[:], op=ALU.mult))
                    S.op("vector", lambda e: e.tensor_tensor(out=T2[:, 0, :], in0=G3[:, 1, :], in1=AIC[:, 0, :], op=ALU.mult))
                    S.op("vector", lambda e: e.tensor_tensor(out=T2[:, 1, :], in0=G3[:, 0, :], in1=AIC[:, 1, :], op=ALU.mult))
                    S.op("vector", lambda e, jl=jl: e.tensor_tensor(out=SS[:, jl, :, :], in0=T1[:], in1=T2[:], op=ALU.add))
                    pjl = jl
                S.bar()
                if dr == 0:
                    S.op("scalar", lambda e, jb=jb: e.copy(out=HH[:, jb, :], in_=W3[:, 0, :]))
                    S.op("scalar", lambda e, jb=jb, nb=nb: e.copy(out=HH[:, jb + 1:jb + nb, :], in_=SS[:, 0:nb - 1, 0, :]))
                else:
                    S.op("scalar", lambda e, jb=jb, nb=nb: e.copy(out=HH[:, jb + nb - 1, :], in_=W3[:, 0, :]))
                    S.op("scalar", lambda e, jb=jb, nb=nb: e.copy(out=HH[:, jb:jb + nb - 1, :], in_=SS[:, 1:nb, 0, :]))
                S.bar()
                S.op("vector", lambda e, pjl=pjl: e.tensor_copy(out=W3[:], in_=SS[:, pjl, :, :]))
                S.bar()
            ckp("rec%d" % dr + "")
            for rnd in range(4):
                for gi in range(8):
                    g = rnd * 8 + gi
                    S.op("tensor", lambda e, gi=gi, g=g: e.matmul(
                        ps[gi][:, 0:NJ], lhsT=TOEP[:, g, :], rhs=IM[:, g, :], start=True, stop=False))
                    S.op("tensor", lambda e, gi=gi, g=g: e.matmul(
                        ps[gi][:, 0:NJ], lhsT=RB[:, g, :], rhs=HH[:, :, g], start=False, stop=True))
                S.bar()
                for gi in range(8):
                    g = rnd * 8 + gi
                    if dr == 0:
                        S.op("vector" if gi % 2 == 0 else "scalar", (lambda e, gi=gi, g=g: e.tensor_copy(out=YS[:, g, :], in_=ps[gi][:, 0:NJ]))
                             if gi % 2 == 0 else (lambda e, gi=gi, g=g: e.copy(out=YS[:, g, :], in_=ps[gi][:, 0:NJ])))
                    else:
                        S.op("vector", lambda e, gi=gi, g=g: e.tensor_tensor(out=YS[:, g, :], in0=YS[:, g, :], in1=ps[gi][:, 0:NJ], op=ALU.add))
                S.bar()
        ckp("read")
        S.dma("sync", YSP[:, :, :], YS)
        S.bar()
        YV = YS.rearrange("p (q s) j -> p q s j", s=8)
        YSPv = YSP.rearrange("(s c) (q g) j -> g c q s j", c=16, g=8)
        for g8 in range(8):
            for q in range(4):
                S.dma(["sync", "gpsimd"][q % 2], YV[g8 * 16:(g8 + 1) * 16, q, :, :], YSPv[g8, :, q, :, :])
            S.bar()
        S.bar()

        ckp("unim")
        WG = WB[:, 0:4 * 512].rearrange("p (k n) -> p k n", n=512)
        load_weight(WG, w_glu[li], 4, 512, 512)
        for (t0, w) in TILES:
            j0, nj = t0 // 8, w // 8
            for q in range(4):
                S.op("scalar", lambda e, q=q, w=w, j0=j0, nj=nj: e.activation(
                    out=TMP[:, q, 0:w].rearrange("p (j s) -> p j s", s=8),
                    in_=YV[:, q, :, j0:j0 + nj].rearrange("p s j -> p j s"), func=AF.Gelu_apprx_tanh))
            S.bar()
            S.op("vector", lambda e, w=w: e.tensor_copy(out=OB[:, 0:4, 0:w], in_=TMP[:, 0:4, 0:w]))
            S.bar()
            for m in range(4):
                for k in range(4):
                    S.op("tensor", lambda e, m=m, k=k, w=w: e.matmul(
                        ps[m][:, 0:w], lhsT=WG[:, k, m * 128:(m + 1) * 128], rhs=OB[:, k, 0:w],
                        start=(k == 0), stop=(k == 3)))
            S.bar()
            for m in range(4):
                S.op("scalar", lambda e, m=m, w=w: e.activation(
                    out=XT[:, m, 0:w], in_=ps[m][:, 0:w], func=AF.Sigmoid, bias=BGLU[:, m:m + 1], scale=1.0))
            S.bar()
            S.op("vector", lambda e, w=w: e.tensor_tensor(out=OB[:, 4:8, 0:w], in0=TMP[:, 0:4, 0:w], in1=XT[:, 0:4, 0:w], op=ALU.mult))
            S.bar()
            S.dma("sync", MIX[0:512, t0:t0 + w].rearrange("(k p) t -> p k t", p=128), OB[:, 4:8, 0:w])
            S.bar()

        ckp("glu")
        S.dma("sync", TMP[:, 0:4, 0:128], w_sp[li].rearrange("h p q -> p h q"))
        S.dma("gpsimd", BS.rearrange("p h q -> p (h q)"), b_sp[li].rearrange("h q -> (h q)").partition_broadcast(128))
        S.bar()
        for h in range(4):
            S.op("tensor", lambda e, h=h: e.transpose(ps[0][:, h * 128:(h + 1) * 128], TMP[:, h, 0:128], IDENT[:]))
        S.bar()
        S.op("vector", lambda e: e.tensor_copy(out=WST[:], in_=ps[0][:].rearrange("p (h n) -> p h n", n=128)))
        S.bar()
        for (t0, w) in TILES:
            nchk = w // 128
            S.dma("sync", XT[:, 0:4, 0:w], VG[:, t0:t0 + w].rearrange("(k p) t -> p k t", p=128))
            S.dma("gpsimd", OB[:, 0:4, 0:w], UG[:, t0:t0 + w].rearrange("(k p) t -> p k t", p=128))
            S.bar()
            S.op("scalar", lambda e, w=w: e.activation(out=SQ[:, 0:4, 0:w], in_=XT[:, 0:4, 0:w], func=AF.Square))
            S.bar()
            rstd_from(SQ, 4, w, 1.0 / 512)
            for k in range(4):
                S.op("vector", lambda e, k=k, w=w: e.scalar_tensor_tensor(
                    out=TMP[:, k, 0:w], in0=XT[:, k, 0:w], scalar=GSGU[:, k:k + 1], in1=RS[:, 0:w],
                    op0=ALU.mult, op1=ALU.mult))
            S.bar()
            for ck in range(nchk):
                for h in range(4):
                    ii = ck * 4 + h
                    S.op("tensor", lambda e, ii=ii, ck=ck, h=h: e.transpose(
                        ps[ii // 4][:, (ii % 4) * 128:(ii % 4 + 1) * 128], TMP[:, h, ck * 128:(ck + 1) * 128], IDENT[:]))
            S.bar()
            for ck in range(nchk):
                S.op("vector" if ck % 2 == 0 else "scalar", (lambda e, ck=ck: e.tensor_copy(
                    out=VT[:, ck * 4:(ck + 1) * 4, :], in_=ps[ck][:].rearrange("p (h n) -> p h n", n=128)))
                    if ck % 2 == 0 else (lambda e, ck=ck: e.copy(
                    out=VT[:, ck * 4:(ck + 1) * 4, :], in_=ps[ck][:].rearrange("p (h n) -> p h n", n=128))))
            S.bar()
            for ck in range(nchk):
                for h in range(4):
                    ii = ck * 4 + h
                    S.op("tensor", lambda e, ii=ii, ck=ck, h=h: e.matmul(
                        ps[4 + ck][:, h * 128:(h + 1) * 128], lhsT=VT[:, ii, :], rhs=WST[:, h, :], start=True, stop=True))
            S.bar()
            for ck in range(nchk):
                S.op("vector", lambda e, ck=ck: e.tensor_tensor(
                    out=TMP[:, 4:8, ck * 128:(ck + 1) * 128], in0=ps[4 + ck][:].rearrange("p (h n) -> p h n", n=128),
                    in1=BS, op=ALU.add))
            S.bar()
            S.op("vector", lambda e, w=w: e.tensor_tensor(out=OB[:, 4:8, 0:w], in0=TMP[:, 4:8, 0:w], in1=OB[:, 0:4, 0:w], op=ALU.mult))
            S.bar()
            S.dma("sync", MIX[512:1024, t0:t0 + w].rearrange("(k p) t -> p k t", p=128), OB[:, 4:8, 0:w])
            S.bar()

        ckp("sgu")
        load_weight(WB[:, 0:8 * D].rearrange("p (k n) -> p k n", n=D), w_out[li], 8, D, 512)
        resid_linear(MIX, 8, 2)

        ckp("wout")
        norm_mod(A2, 3)

        ckp("norm2")
        S.dma("sync", FB[0:9, 0:2 * DFF], w_conv[li].rearrange("a b n -> (a b) n"))
        S.bar()
        for ch in range(44):
            S.op("tensor", lambda e, ch=ch: e.transpose(ps[7][:, ch * 9:(ch + 1) * 9], FB[0:9, ch * 128:(ch + 1) * 128], IDENT[0:9, 0:9]))
        S.bar()
        S.op("vector", lambda e: e.tensor_copy(out=WC[:].rearrange("p t c -> p c t"), in_=ps[7][:, 0:396].rearrange("p (c t) -> p c t", t=9)))
        S.bar()
        WU = WB[:, 0:8 * 256].rearrange("p (k n) -> p k n", n=256)
        FBb = FB[:].bitcast(BF16)
        UPb = [FBb[:, 0:2304], FBb[:, 2304:4608]]
        DG = FBb[:, 4608:6912].rearrange("p (t n) -> p t n", n=128)
        SG = FB[:, 4608:6912]
        GB = ACTB[:, 0:5, :].rearrange("p a t -> p (a t)")[:, 0:NT]
        stgs = [STG[:, 0:2048].rearrange("p (k n) -> p k n", n=256), STG[:, 2048:4096].rearrange("p (k n) -> p k n", n=256)]

        def wup_dma(m):
            st = stgs[m % 2]
            S.dma("sync", st[:, :, 0:128], w_up[li, :, m * 128:(m + 1) * 128].rearrange("(k p) n -> p k n", p=128))
            S.dma("gpsimd", st[:, :, 128:256], w_up[li, :, DFF + m * 128:DFF + (m + 1) * 128].rearrange("(k p) n -> p k n", p=128))

        def conv_mm(part):
            src = UPb[part]
            d0 = part * 9
            S.op("tensor", lambda e: e.matmul(ps[0][:, 0:256], lhsT=DG[:, d0 + 4, :], rhs=src[:, 0:256], start=True, stop=False))
            S.op("tensor", lambda e: e.matmul(ps[0][:, 1:256], lhsT=DG[:, d0 + 3, :], rhs=src[:, 0:255], start=False, stop=False))
            S.op("tensor", lambda e: e.matmul(ps[0][:, 0:255], lhsT=DG[:, d0 + 5, :], rhs=src[:, 1:256], start=False, stop=True))
            sv = src[:, LC:NT].rearrange("p (r c) -> p r c", c=64)
            taps = [(1, 1)] + [(ky, kx) for ky in range(3) for kx in range(3) if not (ky == 1 and kx == 1)]
            for ti in range(1, 5):
                R0 = 8 * (ti - 1)
                pv = ps[ti][:, 0:512].rearrange("p (r c) -> p r c", c=64)
                for n_, (ky, kx) in enumerate(taps):
                    dy, dx = ky - 1, kx - 1
                    ra, rb = max(R0, -dy, 0), min(R0 + 8, 32 - max(0, dy))
                    ra = max(ra, 0 - min(0, dy))
                    c0, c1 = max(0, -dx), 64 - max(0, dx)
                    S.op("tensor", lambda e, pv=pv, ra=ra, rb=rb, c0=c0, c1=c1, dy=dy, dx=dx, R0=R0, ky=ky, kx=kx, n_=n_:
                         e.matmul(pv[:, ra - R0:rb - R0, c0:c1], lhsT=DG[:, d0 + ky * 3 + kx, :],
                                  rhs=sv[:, ra + dy:rb + dy, c0 + dx:c1 + dx], start=(n_ == 0), stop=(n_ == 8)))

        wup_dma(0)
        S.bar()
        for m in range(22):
            S.op("vector", lambda e, m=m: e.tensor_copy(out=WU, in_=stgs[m % 2]))
            for part in range(2):
                for tap in range(9):
                    ch = part * 22 + m
                    S.op("gpsimd", lambda e, part=part, tap=tap, ch=ch: e.tensor_scalar(
                        out=DG[:, part * 9 + tap, :], in0=IDENT[:], scalar1=WC[:, tap, ch:ch + 1], scalar2=None, op0=ALU.mult))
            S.bar()
            for part in range(2):
                for ti, (t0, w) in enumerate(TILES):
                    for k in range(8):
                        S.op("tensor", lambda e, ti=ti, k=k, t0=t0, w=w, part=part: e.matmul(
                            ps[ti][:, 0:w], lhsT=WU[:, k, part * 128:(part + 1) * 128], rhs=H[:, k, t0:t0 + w],
                            start=(k == 0), stop=(k == 7)))
                if part == 0:
                    if m + 1 < 22:
                        wup_dma(m + 1)
                    if m > 0:
                        S.dma("sync", GD[(m - 1) * 128:m * 128, :], GB)
                S.bar()
                dst = UPb[part]
                for ti, (t0, w) in enumerate(TILES):
                    S.op("vector" if ti % 2 == 0 else "scalar", (lambda e, ti=ti, t0=t0, w=w, dst=dst: e.tensor_copy(
                        out=dst[:, t0:t0 + w], in_=ps[ti][:, 0:w])) if ti % 2 == 0 else (lambda e, ti=ti, t0=t0, w=w, dst=dst: e.copy(
                        out=dst[:, t0:t0 + w], in_=ps[ti][:, 0:w])))
                S.bar()
            conv_mm(0)
            S.bar()
            for ti, (t0, w) in enumerate(TILES):
                S.op("scalar", lambda e, ti=ti, t0=t0, w=w: e.activation(out=SG[:, t0:t0 + w], in_=ps[ti][:, 0:w], func=AF.Silu))
            S.bar()
            conv_mm(1)
            S.bar()
            for ti, (t0, w) in enumerate(TILES):
                S.op("vector", lambda e, ti=ti, t0=t0, w=w: e.tensor_tensor(out=GB[:, t0:t0 + w], in0=ps[ti][:, 0:w], in1=SG[:, t0:t0 + w], op=ALU.mult))
            S.bar()
        S.dma("sync", GD[21 * 128:22 * 128, :], GB)
        S.bar()

        ckp("ffnup")
        load_weight(WB[:, 0:22 * D].rearrange("p (k n) -> p k n", n=D), w_down[li], 22, D, 128)
        resid_linear(GD, 22, 5)

    except _Stop:
        pass
    S.bar()
    if dbg:
        DF = nc.dram_tensor("DBGF", [128, 32768], F32, kind="ExternalOutput").ap()
        DB = nc.dram_tensor("DBGB", [128, 40960], BF16, kind="ExternalOutput").ap()
        off = 0
        for t_, n_ in [(HRAW[:], 9216), (XTt[:], 4096), (TMPt[:], 4096), (FB[:], 9216), (STG[:], 4096), (RS[:], 512),
                       (MOD[:].rearrange("p q k n -> p (q k n)"), 96), (A1[:].rearrange("p k n -> p (k n)"), 16),
                       (A2[:].rearrange("p k n -> p (k n)"), 16), (W3[:].rearrange("p a g -> p (a g)"), 64),
                       (ARC[:].rearrange("p a g -> p (a g)"), 64), (AIC[:].rearrange("p a g -> p (a g)"), 64),
                       (CR[:], 32), (CI[:], 32), (DV[:], 32), (LR[:], 32), (LI[:], 32)]:
            S.dma("sync", DF[:, off:off + n_], t_)
            off += n_
        offb = 0
        for t_, n_ in [(WB, 22528), (OBt[:], 4096), (ACTBt[:], 11264)]:
            S.dma("gpsimd", DB[:, offb:offb + n_], t_)
            offb += n_
        S.bar()
    for b in range(16):
        t0 = LC + b * 128
        S.dma("sync", XT[:, :, 0:128], XRES[:, t0:t0 + 128].rearrange("(k p) t -> p k t", p=128))
        S.bar()
        S.op("scalar", lambda e: e.activation(out=SQ[:, :, 0:128], in_=XT[:, :, 0:128], func=AF.Square))
        S.bar()
        rstd_from(SQ, 8, 128, 1.0 / D)
        for k in range(8):
            S.op("vector", lambda e, k=k: e.scalar_tensor_tensor(
                out=TMP[:, k, 0:128], in0=XT[:, k, 0:128], scalar=GFIN[:, k:k + 1], in1=RS[:, 0:128],
                op0=ALU.mult, op1=ALU.mult))
        S.bar()
        for k in range(8):
            S.op("tensor", lambda e, k=k: e.transpose(ps[k // 4][:, (k % 4) * 128:(k % 4 + 1) * 128],
                                                       TMP[:, k, 0:128], IDENT[:]))
        S.bar()
        S.op("vector", lambda e: e.tensor_copy(out=XT[:, 0:4, 0:128], in_=ps[0][:].rearrange("p (k t) -> p k t", t=128)))
        S.op("scalar", lambda e: e.copy(out=XT[:, 4:8, 0:128], in_=ps[1][:].rearrange("p (k t) -> p k t", t=128)))
        S.bar()
        S.dma("sync", out[b * 128:(b + 1) * 128, :].rearrange("t (k d) -> t k d", d=128), XT[:, :, 0:128])
        S.bar()

    S.emit()
    es.close()
    return nc


_CONST = None


def _consts():
    ident = np.eye(128, dtype=np.float32)
    sp = np.arange(128) // 16
    m0 = (sp[None, :] >= sp[:, None]).astype(np.float32)
    m1 = (sp[None, :] <= sp[:, None]).astype(np.float32)
    return ident, np.stack([m0, m1])


def kernel(n_layers=4, **inputs):
    nc = build_nc(n_layers)
    ident, mask = _consts()
    in_maps = []
    for b in range(8):
        m = {}
        for k, v in inputs.items():
            v = np.asarray(v)
            if k in ("x", "c", "ctx"):
                m[k] = np.ascontiguousarray(v[b], dtype=np.float32)
            else:
                m[k] = np.ascontiguousarray(v, dtype=np.float32)
        m["ident"] = ident
        m["mask"] = mask
        in_maps.append(m)
    res = run_bass_kernel_spmd(nc, in_maps, core_ids=list(range(8)))
    return np.stack([np.asarray(r["out"], dtype=np.float32) for r in res.results], axis=0)
```

```python
import numpy as np
from contextlib import ExitStack
import concourse.bass as bass
import concourse.mybir as mybir
from concourse.bass_utils import run_bass_kernel_spmd

F32, BF16, I32 = mybir.dt.float32, mybir.dt.bfloat16, mybir.dt.int32
AF = mybir.ActivationFunctionType
ALU = mybir.AluOpType

D = 1024
NT = 2304
LC = 256
LL = 2048
DFF = 2816
EPS = 1e-6
NJ = 288
TILES = [(0, 256)] + [(256 + 512 * i, 512) for i in range(4)]
TWO_PI = 6.283185307179586


class Sched:
    def __init__(self, nc):
        self.nc = nc
        self.stages = [[]]

    def op(self, eng, fn, dma=False):
        self.stages[-1].append((eng, dma, fn))

    def dma(self, eng, out, in_, slow=False):
        if slow:
            self.op(eng, lambda e, o=out, i=in_: e.dma_start(out=o, in_=i, allow_slow_non_contiguous=True), dma=True)
        else:
            self.op(eng, lambda e, o=out, i=in_: e.dma_start(out=o, in_=i), dma=True)

    def bar(self):
        if self.stages[-1]:
            self.stages.append([])

    def emit(self):
        nc = self.nc
        self.bar()
        names = ["c_scalar", "c_vector", "c_gpsimd", "c_tensor", "d_sync", "d_scalar", "d_gpsimd"]
        cum = []
        cur = {n: 0 for n in names}
        for st in self.stages:
            cum.append(dict(cur))
            for (eng, dma, _) in st:
                if dma:
                    cur["d_" + eng] += 16
                else:
                    cur["c_" + eng] += 1
        final = dict(cur)
        with ExitStack() as es:
            sems = {n: es.enter_context(nc.semaphore(n)) for n in names}
            block = es.enter_context(nc.Block())

            def make(engname):
                def body(eng):
                    waited = {n: 0 for n in names}
                    for k, st in enumerate(self.stages):
                        mine = [o for o in st if o[0] == engname]
                        if not mine:
                            continue
                        for n in names:
                            if cum[k][n] > waited[n]:
                                eng.wait_ge(sems[n], cum[k][n])
                                waited[n] = cum[k][n]
                        for (_, dma, fn) in mine:
                            ins = fn(eng)
                            if dma:
                                ins.then_inc(sems["d_" + engname], 16)
                            else:
                                ins.then_inc(sems["c_" + engname], 1)
                    if engname == "sync":
                        for n in names:
                            if final[n] > waited[n]:
                                eng.wait_ge(sems[n], final[n])
                return body

            block.sync(make("sync"))
            block.scalar(make("scalar"))
            block.vector(make("vector"))
            block.gpsimd(make("gpsimd"))
            block.tensor(make("tensor"))


class _Stop(Exception):
    pass


def build_nc(n_layers, n_wl=4, stop=None, dbg=False):
    nc = bass.Bass("TRN2", target_bir_lowering=False)
    S = Sched(nc)
    W = n_wl

    def ckp(name):
        S.bar()
        if stop == name:
            raise _Stop()

    def din(name, shape):
        return nc.dram_tensor(name, list(shape), F32, kind="ExternalInput").ap()

    x_in = din("x", [LL, D])
    c_in = din("c", [D])
    ctx_in = din("ctx", [LC, D])
    cctx_in = din("c_ctx", [D])
    w_ada = din("w_ada", [W, D, 6 * D])
    b_ada = din("b_ada", [W, 6 * D])
    g_mix = din("g_mix", [W, D])
    w_in = din("w_in", [W, D, 1536])
    a_re = din("ssm_a_re", [W, 2, 32, 64])
    a_im = din("ssm_a_im", [W, 2, 32, 64])
    b_re = din("ssm_b_re", [W, 2, 32, 64, 16])
    b_im = din("ssm_b_im", [W, 2, 32, 64, 16])
    c_re = din("ssm_c_re", [W, 2, 32, 16, 64])
    c_im = din("ssm_c_im", [W, 2, 32, 16, 64])
    log_dt = din("ssm_log_dt", [W, 2, 32])
    ssm_d = din("ssm_d", [W, 512])
    w_glu = din("w_glu", [W, 512, 512])
    b_glu = din("b_glu", [W, 512])
    g_sgu = din("g_sgu", [W, 512])
    w_sp = din("w_spatial", [W, 4, 128, 128])
    b_sp = din("b_spatial", [W, 4, 128])
    w_out = din("w_out", [W, D, D])
    g_ffn = din("g_ffn", [W, D])
    w_up = din("w_up", [W, D, 2 * DFF])
    w_conv = din("w_conv", [W, 3, 3, 2 * DFF])
    w_down = din("w_down", [W, DFF, D])
    g_final = din("g_final", [D])
    ident_in = din("ident", [128, 128])
    mask_in = din("mask", [2, 128, 128])
    out = nc.dram_tensor("out", [LL, D], F32, kind="ExternalOutput").ap()

    SK = dict(kind="ExternalOutput") if dbg else {}
    XRES = nc.dram_tensor("XRES", [D, NT], F32, **SK).ap()
    USSMP = nc.dram_tensor("USSMP", [512, 8, NJ], BF16, **SK).ap()
    YSP = nc.dram_tensor("YSP", [128, 32, NJ], F32, **SK).ap()
    UG = nc.dram_tensor("UG", [512, NT], BF16, **SK).ap()
    VG = nc.dram_tensor("VG", [512, NT], F32, **SK).ap()
    MIX = nc.dram_tensor("MIX", [D, NT], BF16, **SK).ap()
    GD = nc.dram_tensor("GD", [DFF, NT], BF16, **SK).ap()

    es = ExitStack()

    def sb(name, shape, dt=F32):
        return es.enter_context(nc.sbuf_tensor(name, list(shape), dt))

    ps = [es.enter_context(nc.psum_tensor("ps%d" % i, [128, 512], F32)) for i in range(8)]

    IDENT = sb("IDENT", [128, 128])
    ONESB = sb("ONESB", [128, 128], BF16)
    MASKS = sb("MASKS", [128, 2, 128])
    SIGN = sb("SIGN", [128, 1])
    NSIGN = sb("NSIGN", [128, 1])
    SC = sb("SC", [128, 8, 2])
    MOD = sb("MOD", [128, 6, 8, 2])
    BADA = sb("BADA", [128, 6, 8])
    GM = sb("GM", [128, 8])
    GF = sb("GF", [128, 8])
    GFIN = sb("GFIN", [128, 8])
    A1 = sb("A1", [128, 8, 2])
    A2 = sb("A2", [128, 8, 2])
    HRAW = sb("HRAW", [128, 9216])
    H = HRAW[:].bitcast(BF16).rearrange("p (k t) -> p k t", t=NT)
    YS = HRAW[:].rearrange("p (g j) -> p g j", j=NJ)
    STG = sb("STG", [128, 4096])
    SQ = STG[:, 0:2048].bitcast(BF16).rearrange("p (k t) -> p k t", t=512)
    WBt = sb("WB", [128, 22528], BF16)
    WB = WBt[:]
    TOEP = WB[:, 0:4096].rearrange("p (g n) -> p g n", n=128)
    ET = WB[:, 4096:8192].rearrange("p (g n) -> p g n", n=128)
    ESW = WB[:, 8192:12288].rearrange("p (g n) -> p g n", n=128)
    IM = WB[:, 12288:21504].rearrange("p (g j) -> p g j", j=NJ)
    XTt = sb("XT", [128, 4096])
    XT = XTt[:].rearrange("p (k t) -> p k t", t=512)
    LT = XTt[:].rearrange("p (g s c) -> p g s c", s=8, c=16)
    SS = XTt[:].rearrange("p (j a g) -> p j a g", a=2, g=32)
    TMPt = sb("TMP", [128, 4096])
    TMP = TMPt[:].rearrange("p (k t) -> p k t", t=512)
    RT = TMPt[:].rearrange("p (g s c) -> p g s c", s=8, c=16)
    RS = sb("RS", [128, 512])
    OBt = sb("OB", [128, 4096], BF16)
    OB = OBt[:].rearrange("p (k t) -> p k t", t=512)
    RB = OBt[:].rearrange("p (g n) -> p g n", n=128)
    ACTBt = sb("ACTB", [128, 11264], BF16)
    ACTB = ACTBt[:].rearrange("p (k t) -> p k t", t=512)
    USP = ACTBt[:, 0:9216].rearrange("p (q s j) -> p q s j", s=8, j=NJ)
    HH = ACTBt[:, 0:9216].rearrange("p (j g) -> p j g", g=32)
    FB = sb("FB", [128, 9216])
    UPG = FB[:, 0:2304]; UPV = FB[:, 2304:4608]; CG = FB[:, 4608:6912]; CV = FB[:, 6912:9216]
    def ftab(i):
        return FB[:, i * 512:(i + 1) * 512].rearrange("p (g k) -> p g k", k=16)
    ARG, ARGC, EARG, NF, NFC, PRE, PIM, MAG, BX1, BX2, CX1, CX2 = [ftab(i) for i in range(12)]
    def qtab(i):
        return FB[:, 6144 + i * 256:6144 + (i + 1) * 256].rearrange("p (g k) -> p g k", k=8)
    QR, QI, QT, PA, PB = [qtab(i) for i in range(5)]
    NI = FB[:, 7424:7936].bitcast(I32).rearrange("p (g k) -> p g k", k=16)
    NIC = FB[:, 7936:8448].bitcast(I32).rearrange("p (g k) -> p g k", k=16)
    BS = FB[:, 0:512].rearrange("p (h q) -> p h q", q=128)
    VT = STG[:, 2048:4096].bitcast(BF16).rearrange("p (a n) -> p a n", n=128)
    W3 = sb("W3", [128, 2, 32])
    G3 = sb("G3", [128, 3, 32])
    T1 = sb("T1", [128, 2, 32])
    T2 = sb("T2", [128, 2, 32])
    ARp = sb("ARp", [128, 32]); AIp = sb("AIp", [128, 32]); LDT = sb("LDT", [128, 32])
    LR = sb("LR", [128, 32]); LI = sb("LI", [128, 32])
    CR = sb("CR", [128, 32]); CI = sb("CI", [128, 32]); NR = sb("NR", [128, 32]); DEN = sb("DEN", [128, 32])
    TA = sb("TA", [128, 32]); TB = sb("TB", [128, 32])
    ARC = sb("ARC", [128, 2, 32]); AIC = sb("AIC", [128, 2, 32])
    DV = sb("DV", [128, 32])
    BGLU = sb("BGLU", [128, 4]); GSGU = sb("GSGU", [128, 4])
    WST = sb("WST", [128, 4, 128], BF16)
    WC = sb("WC", [128, 9, 44])

    S.dma("sync", IDENT[:], ident_in[:, :])
    S.dma("sync", MASKS[:], mask_in.rearrange("m p q -> p m q"))
    S.op("vector", lambda e: e.memset(ONESB[:], 1.0))
    S.op("vector", lambda e: e.memset(SIGN[0:64, :], -1.0))
    S.op("vector", lambda e: e.memset(SIGN[64:128, :], 1.0))
    S.op("vector", lambda e: e.memset(NSIGN[0:64, :], 1.0))
    S.op("vector", lambda e: e.memset(NSIGN[64:128, :], -1.0))
    S.dma("sync", STG[0:8, 0:128], c_in.rearrange("(k p) -> k p", p=128))
    S.dma("sync", STG[8:16, 0:128], cctx_in.rearrange("(k p) -> k p", p=128))
    S.dma("sync", STG[16:24, 0:128], g_final.rearrange("(k p) -> k p", p=128))
    S.bar()
    S.op("tensor", lambda e: e.transpose(ps[7][:, 0:24], STG[0:24, 0:128], IDENT[0:24, 0:24]))
    S.bar()
    S.op("vector", lambda e: e.tensor_copy(out=SC[:, :, 0], in_=ps[7][:, 0:8]))
    S.op("vector", lambda e: e.tensor_copy(out=SC[:, :, 1], in_=ps[7][:, 8:16]))
    S.op("vector", lambda e: e.tensor_copy(out=GFIN[:], in_=ps[7][:, 16:24]))
    S.bar()
    S.op("scalar", lambda e: e.activation(out=SC[:], in_=SC[:], func=AF.Silu))
    S.bar()

    def in_transpose(src, nblk, tok0):
        for b in range(nblk):
            S.dma("sync", TMP[:, :, 0:128], src[b * 128:(b + 1) * 128, :].rearrange("t (k d) -> t k d", d=128))
            S.bar()
            for k in range(8):
                S.op("tensor", lambda e, k=k: e.transpose(ps[k // 4][:, (k % 4) * 128:(k % 4 + 1) * 128],
                                                           TMP[:, k, 0:128], IDENT[:]))
            S.bar()
            S.op("vector", lambda e: e.tensor_copy(out=XT[:, 0:4, 0:128], in_=ps[0][:].rearrange("p (k t) -> p k t", t=128)))
            S.op("scalar", lambda e: e.copy(out=XT[:, 4:8, 0:128], in_=ps[1][:].rearrange("p (k t) -> p k t", t=128)))
            S.bar()
            t0 = tok0 + b * 128
            S.dma("sync", XRES[:, t0:t0 + 128].rearrange("(k p) t -> p k t", p=128), XT[:, :, 0:128])
            S.bar()

    in_transpose(ctx_in, 2, 0)
    in_transpose(x_in, 16, 256)

    def load_weight(dst, wap, kch, ncols, cb):
        for c0 in range(0, ncols, cb):
            stg = STG[:, 0:kch * cb].rearrange("p (k n) -> p k n", n=cb)
            S.dma("sync", stg, wap[:, c0:c0 + cb].rearrange("(k p) n -> p k n", p=128))
            S.bar()
            S.op("vector", lambda e, stg=stg, c0=c0: e.tensor_copy(out=dst[:, :, c0:c0 + cb], in_=stg))
            S.bar()

    def rstd_from(src_sq, nk, w, inv_n):
        for k in range(nk):
            S.op("tensor", lambda e, k=k: e.matmul(ps[0][:, 0:w], lhsT=ONESB[:], rhs=src_sq[:, k, 0:w],
                                                    start=(k == 0), stop=(k == nk - 1)))
        S.bar()
        S.op("scalar", lambda e: e.activation(out=RS[:, 0:w], in_=ps[0][:, 0:w], func=AF.Sqrt, bias=EPS, scale=inv_n))
        S.bar()
        S.op("vector", lambda e: e.reciprocal(out=RS[:, 0:w], in_=RS[:, 0:w]))
        S.bar()

    def norm_mod(Acoef, which_shift):
        for (t0, w) in TILES:
            sel = 1 if t0 == 0 else 0
            S.dma("sync", XT[:, :, 0:w], XRES[:, t0:t0 + w].rearrange("(k p) t -> p k t", p=128))
            S.bar()
            S.op("scalar", lambda e, w=w: e.activation(out=SQ[:, :, 0:w], in_=XT[:, :, 0:w], func=AF.Square))
            S.bar()
            rstd_from(SQ, 8, w, 1.0 / D)
            for k in range(8):
                S.op("vector", lambda e, k=k, w=w, sel=sel: e.scalar_tensor_tensor(
                    out=TMP[:, k, 0:w], in0=XT[:, k, 0:w], scalar=Acoef[:, k, sel:sel + 1], in1=RS[:, 0:w],
                    op0=ALU.mult, op1=ALU.mult))
            S.bar()
            for k in range(8):
                S.op("scalar", lambda e, k=k, w=w, sel=sel, t0=t0: e.activation(
                    out=H[:, k, t0:t0 + w], in_=TMP[:, k, 0:w], func=AF.Identity,
                    bias=MOD[:, which_shift, k, sel:sel + 1], scale=1.0))
            S.bar()

    def resid_linear(src_dram, kch, gate_idx):
        Wv = WB[:, 0:kch * D].rearrange("p (k n) -> p k n", n=D)
        for (t0, w) in TILES:
            sel = 1 if t0 == 0 else 0
            S.dma("sync", ACTB[:, 0:kch, 0:w], src_dram[:, t0:t0 + w].rearrange("(k p) t -> p k t", p=128))
            S.dma("gpsimd", XT[:, :, 0:w], XRES[:, t0:t0 + w].rearrange("(k p) t -> p k t", p=128))
            S.bar()
            for m in range(8):
                for k in range(kch):
                    S.op("tensor", lambda e, m=m, k=k, w=w: e.matmul(
                        ps[m][:, 0:w], lhsT=Wv[:, k, m * 128:(m + 1) * 128], rhs=ACTB[:, k, 0:w],
                        start=(k == 0), stop=(k == kch - 1)))
            S.bar()
            for m in range(8):
                S.op("vector", lambda e, m=m, w=w, sel=sel: e.scalar_tensor_tensor(
                    out=XT[:, m, 0:w], in0=ps[m][:, 0:w], scalar=MOD[:, gate_idx, m, sel:sel + 1], in1=XT[:, m, 0:w],
                    op0=ALU.mult, op1=ALU.add))
            S.bar()
            S.dma("sync", XRES[:, t0:t0 + w].rearrange("(k p) t -> p k t", p=128), XT[:, :, 0:w])
            S.bar()

    try:
      for li in range(n_layers):
        S.dma("sync", STG[0:48, 0:128], b_ada[li].rearrange("(k p) -> k p", p=128))
        S.dma("sync", STG[48:56, 0:128], g_mix[li].rearrange("(k p) -> k p", p=128))
        S.dma("sync", STG[56:64, 0:128], g_ffn[li].rearrange("(k p) -> k p", p=128))
        S.dma("sync", STG[64:68, 0:128], b_glu[li].rearrange("(k p) -> k p", p=128))
        S.dma("sync", STG[68:72, 0:128], g_sgu[li].rearrange("(k p) -> k p", p=128))
        S.bar()
        S.op("tensor", lambda e: e.transpose(ps[7][:, 0:72], STG[0:72, 0:128], IDENT[0:72, 0:72]))
        S.bar()
        S.op("vector", lambda e: e.tensor_copy(out=BADA[:].rearrange("p q k -> p (q k)"), in_=ps[7][:, 0:48]))
        S.op("vector", lambda e: e.tensor_copy(out=GM[:], in_=ps[7][:, 48:56]))
        S.op("vector", lambda e: e.tensor_copy(out=GF[:], in_=ps[7][:, 56:64]))
        S.op("vector", lambda e: e.tensor_copy(out=BGLU[:], in_=ps[7][:, 64:68]))
        S.op("vector", lambda e: e.tensor_copy(out=GSGU[:], in_=ps[7][:, 68:72]))
        S.bar()
        WAs = [STG[:, 0:2048].rearrange("p (k n) -> p k n", n=256), STG[:, 2048:4096].rearrange("p (k n) -> p k n", n=256)]

        def wada_dma(blk):
            WA = WAs[blk % 2]
            src = w_ada[li, :, blk * 256:(blk + 1) * 256].rearrange("(k p) n -> p k n", p=128)
            S.dma("sync", WA[:, 0:4, :], src[:, 0:4, :])
            S.dma("gpsimd", WA[:, 4:8, :], src[:, 4:8, :])

        wada_dma(0)
        S.bar()
        for blk in range(24):
            q, mq = blk // 4, blk % 4
            WA = WAs[blk % 2]
            if blk + 1 < 24:
                wada_dma(blk + 1)
            for mm in range(2):
                m = mq * 2 + mm
                for k in range(8):
                    S.op("tensor", lambda e, m=m, mm=mm, k=k, q=q, WA=WA: e.matmul(
                        ps[0][:, (q * 8 + m) * 2:(q * 8 + m) * 2 + 2], lhsT=WA[:, k, mm * 128:(mm + 1) * 128],
                        rhs=SC[:, k, :], start=(k == 0), stop=(k == 7)))
            S.bar()
        S.op("vector", lambda e: e.tensor_tensor(
            out=MOD[:].rearrange("p q k n -> p (q k) n"), in0=ps[0][:, 0:96].rearrange("p (a n) -> p a n", n=2),
            in1=BADA[:].rearrange("p q k -> p (q k)").unsqueeze(2).to_broadcast([128, 48, 2]), op=ALU.add))
        S.bar()
        S.op("vector", lambda e: e.scalar_tensor_tensor(
            out=A1[:], in0=MOD[:, 1, :, :], scalar=1.0, in1=GM[:].unsqueeze(2).to_broadcast([128, 8, 2]),
            op0=ALU.add, op1=ALU.mult))
        S.op("vector", lambda e: e.scalar_tensor_tensor(
            out=A2[:], in0=MOD[:, 4, :, :], scalar=1.0, in1=GF[:].unsqueeze(2).to_broadcast([128, 8, 2]),
            op0=ALU.add, op1=ALU.mult))
        S.bar()

        ckp("ada")
        norm_mod(A1, 0)

        ckp("norm1")
        WIN = WB[:, 0:8 * 1536].rearrange("p (k n) -> p k n", n=1536)
        load_weight(WIN, w_in[li], 8, 1536, 512)
        for (t0, w) in TILES:
            j0, nj = t0 // 8, w // 8
            for grp in range(2):
                ms = list(range(8)) if grp == 0 else list(range(8, 12))
                for bi, m in enumerate(ms):
                    for k in range(8):
                        S.op("tensor", lambda e, bi=bi, m=m, k=k, w=w, t0=t0: e.matmul(
                            ps[bi][:, 0:w], lhsT=WIN[:, k, m * 128:(m + 1) * 128], rhs=H[:, k, t0:t0 + w],
                            start=(k == 0), stop=(k == 7)))
                S.bar()
                for bi, m in enumerate(ms):
                    if m < 4:
                        S.op("vector", lambda e, bi=bi, m=m, w=w, j0=j0, nj=nj: e.tensor_copy(
                            out=USP[:, m, :, j0:j0 + nj].rearrange("p s j -> p j s"),
                            in_=ps[bi][:, 0:w].rearrange("p (j s) -> p j s", s=8)))
                    elif m < 8:
                        S.op("scalar", lambda e, bi=bi, m=m, w=w: e.activation(
                            out=OB[:, m - 4, 0:w], in_=ps[bi][:, 0:w], func=AF.Gelu_apprx_tanh))
                    else:
                        S.op("scalar", lambda e, bi=bi, m=m, w=w: e.activation(
                            out=TMP[:, m - 8, 0:w], in_=ps[bi][:, 0:w], func=AF.Gelu_apprx_tanh))
                S.bar()
            S.dma("sync", UG[:, t0:t0 + w].rearrange("(k p) t -> p k t", p=128), OB[:, 0:4, 0:w])
            S.dma("gpsimd", VG[:, t0:t0 + w].rearrange("(k p) t -> p k t", p=128), TMP[:, 0:4, 0:w])
            S.bar()

        ckp("win")
        S.dma("sync", USSMP.rearrange("(q p) s j -> p q s j", p=128), USP)
        S.dma("gpsimd", STG[0:32, 0:16], ssm_d[li].rearrange("(g c) -> g c", c=16))
        S.bar()
        for s_ in range(8):
            S.dma("sync" if s_ % 2 == 0 else "gpsimd", IM[s_ * 16:(s_ + 1) * 16, :, :],
                  USSMP[:, s_, :].rearrange("(g c) j -> c g j", c=16))
        S.bar()

        S.op("vector", lambda e: e.tensor_copy(out=STG[0:32, 128:256].rearrange("p (s c) -> p s c", c=16),
                                               in_=STG[0:32, 0:16].unsqueeze(1).to_broadcast([32, 8, 16])))
        S.bar()
        S.op("tensor", lambda e: e.transpose(ps[7][:, 0:32], STG[0:32, 128:256], IDENT[0:32, 0:32]))
        S.bar()
        S.op("vector", lambda e: e.tensor_copy(out=DV[:], in_=ps[7][:, 0:32]))
        ckp("im2col")
        for dr in range(2):
            CIN1 = STG[:, 0:512].rearrange("p (q n) -> p q n", n=128)
            CIN2 = STG[:, 512:1024].rearrange("p (q n) -> p q n", n=128)
            for hf in range(2):
                lo = slice(hf * 64, hf * 64 + 64)
                csrc = [c_re, c_im] if hf == 0 else [c_im, c_re]
                S.dma("sync", CIN1[:, :, lo], csrc[0][li, dr].rearrange("(q g) c p -> (g c) q p", q=4))
                S.dma("gpsimd", CIN2[:, :, lo], csrc[1][li, dr].rearrange("(q g) c p -> (g c) q p", q=4))
            S.bar()
            for q in range(4):
                S.op("tensor", lambda e, q=q: e.transpose(ps[0][:, q * 128:(q + 1) * 128], CIN1[:, q, :], IDENT[:]))
                S.op("tensor", lambda e, q=q: e.transpose(ps[1][:, q * 128:(q + 1) * 128], CIN2[:, q, :], IDENT[:]))
            S.bar()
            S.op("vector", lambda e: e.tensor_copy(out=CX1.rearrange("p g c -> p (g c)"), in_=ps[0][:]))
            S.op("scalar", lambda e: e.copy(out=CX2.rearrange("p g c -> p (g c)"), in_=ps[1][:]))
            S.bar()
            BIN1 = STG[0:32, 0:2048].rearrange("p (h q c) -> p h q c", h=2, c=16)
            BIN2 = STG[0:32, 2048:4096].rearrange("p (h q c) -> p h q c", h=2, c=16)
            S.dma("sync", BIN1[:, 0, :, :], b_re[li, dr])
            S.dma("gpsimd", BIN1[:, 1, :, :], b_im[li, dr])
            S.dma("sync", BIN2[:, 0, :, :], b_im[li, dr])
            S.dma("gpsimd", BIN2[:, 1, :, :], b_re[li, dr])
            AIN = RS[0:32, 0:256].rearrange("p (a q) -> p a q", q=64)
            S.dma("sync", AIN[:, 0, :], a_re[li, dr])
            S.dma("gpsimd", AIN[:, 1, :], a_re[li, dr])
            S.dma("sync", AIN[:, 2, :], a_im[li, dr])
            S.dma("gpsimd", AIN[:, 3, :], a_im[li, dr])
            S.bar()
            for c_ in range(16):
                S.op("tensor", lambda e, c_=c_: e.transpose(ps[2][:, c_ * 32:(c_ + 1) * 32], BIN1[:, :, :, c_], IDENT[0:32, 0:32]))
                S.op("tensor", lambda e, c_=c_: e.transpose(ps[3][:, c_ * 32:(c_ + 1) * 32], BIN2[:, :, :, c_], IDENT[0:32, 0:32]))
            S.op("tensor", lambda e: e.transpose(ps[4][:, 0:32], RS[0:32, 0:128], IDENT[0:32, 0:32]))
            S.op("tensor", lambda e: e.transpose(ps[4][:, 32:64], RS[0:32, 128:256], IDENT[0:32, 0:32]))
            S.bar()
            S.op("vector", lambda e: e.tensor_copy(out=BX1.rearrange("p g c -> p c g"), in_=ps[2][:].rearrange("p (c g) -> p c g", g=32)))
            S.op("scalar", lambda e: e.copy(out=BX2.rearrange("p g c -> p c g"), in_=ps[3][:].rearrange("p (c g) -> p c g", g=32)))
            S.op("vector", lambda e: e.tensor_copy(out=ARp[:], in_=ps[4][:, 0:32]))
            S.op("vector", lambda e: e.tensor_copy(out=AIp[:], in_=ps[4][:, 32:64]))
            S.dma("sync", LDT[:], log_dt[li, dr].partition_broadcast(128))
            S.bar()
            ckp("pl%d" % dr)
            S.op("scalar", lambda e: e.activation(out=LDT[:], in_=LDT[:], func=AF.Exp))
            S.bar()
            S.op("vector", lambda e: e.tensor_tensor(out=LR[:], in0=ARp[:], in1=LDT[:], op=ALU.mult))
            S.op("gpsimd", lambda e: e.tensor_tensor(out=LI[:], in0=AIp[:], in1=LDT[:], op=ALU.mult))
            S.bar()
            ckp("pb%d" % dr)
            ks = list(range(-8, 0)) + list(range(1, 9))
            for idx, kk in enumerate(ks):
                S.op("vector", lambda e, idx=idx, kk=kk: e.tensor_scalar(
                    out=ARG[:, :, idx], in0=LI[:], scalar1=float(kk), scalar2=None, op0=ALU.mult))
                S.op("gpsimd", lambda e, idx=idx, kk=kk: e.tensor_scalar(
                    out=EARG[:, :, idx], in0=LR[:], scalar1=float(kk), scalar2=None, op0=ALU.mult))
            S.bar()
            ckp("pc%d" % dr)
            S.op("vector", lambda e: e.tensor_scalar(out=ARGC, in0=ARG, scalar1=TWO_PI / 4, scalar2=None, op0=ALU.add))
            S.op("scalar", lambda e: e.activation(out=MAG, in_=EARG, func=AF.Exp))
            S.bar()
            ckp("pd%d" % dr)
            S.op("vector", lambda e: e.tensor_scalar(out=NI, in0=ARG, scalar1=1.0 / TWO_PI, scalar2=None, op0=ALU.mult))
            S.op("gpsimd", lambda e: e.tensor_scalar(out=NIC, in0=ARGC, scalar1=1.0 / TWO_PI, scalar2=None, op0=ALU.mult))
            S.bar()
            ckp("pe%d" % dr)
            S.op("vector", lambda e: e.tensor_copy(out=NF, in_=NI))
            S.op("gpsimd", lambda e: e.tensor_copy(out=NFC, in_=NIC))
            S.bar()
            ckp("pf%d" % dr)
            S.op("vector", lambda e: e.scalar_tensor_tensor(out=ARG, in0=NF, scalar=-TWO_PI, in1=ARG, op0=ALU.mult, op1=ALU.add))
            S.op("vector", lambda e: e.scalar_tensor_tensor(out=ARGC, in0=NFC, scalar=-TWO_PI, in1=ARGC, op0=ALU.mult, op1=ALU.add))
            S.bar()
            S.op("vector", lambda e: e.tensor_scalar(out=ARG, in0=ARG, scalar1=3.1415925, scalar2=-3.1415925, op0=ALU.min, op1=ALU.max))
            S.op("gpsimd", lambda e: e.tensor_scalar(out=ARGC, in0=ARGC, scalar1=3.1415925, scalar2=-3.1415925, op0=ALU.min, op1=ALU.max))
            S.bar()
            ckp("pg%d" % dr)
            S.op("scalar", lambda e: e.activation(out=PIM, in_=ARG, func=AF.Sin))
            S.op("scalar", lambda e: e.activation(out=PRE, in_=ARGC, func=AF.Sin))
            S.bar()
            S.op("vector", lambda e: e.tensor_tensor(out=PIM, in0=PIM, in1=MAG, op=ALU.mult))
            S.op("gpsimd", lambda e: e.tensor_tensor(out=PRE, in0=PRE, in1=MAG, op=ALU.mult))
            S.bar()
            ckp("ph%d" % dr)
            S.op("vector", lambda e: e.tensor_scalar(out=NR[:], in0=PRE[:, :, 8], scalar1=-1.0, scalar2=None, op0=ALU.add))
            S.op("gpsimd", lambda e: e.tensor_tensor(out=DEN[:], in0=ARp[:], in1=ARp[:], op=ALU.mult))
            ckp("c0")
            S.op("vector", lambda e: e.tensor_tensor(out=TA[:], in0=AIp[:], in1=AIp[:], op=ALU.mult))
            ckp("c1")
            S.op("vector", lambda e: e.tensor_tensor(out=DEN[:], in0=DEN[:], in1=TA[:], op=ALU.add))
            ckp("c2")
            S.op("vector", lambda e: e.reciprocal(out=DEN[:], in_=DEN[:]))
            ckp("c3")
            S.op("vector", lambda e: e.tensor_tensor(out=TA[:], in0=NR[:], in1=ARp[:], op=ALU.mult))
            S.op("gpsimd", lambda e: e.tensor_tensor(out=TB[:], in0=PIM[:, :, 8], in1=AIp[:], op=ALU.mult))
            ckp("c4")
            S.op("vector", lambda e: e.tensor_tensor(out=CR[:], in0=TA[:], in1=TB[:], op=ALU.add))
            ckp("c5")
            S.op("vector", lambda e: e.tensor_tensor(out=TA[:], in0=PIM[:, :, 8], in1=ARp[:], op=ALU.mult))
            S.op("gpsimd", lambda e: e.tensor_tensor(out=TB[:], in0=NR[:], in1=AIp[:], op=ALU.mult))
            ckp("c6")
            S.op("vector", lambda e: e.tensor_tensor(out=CI[:], in0=TA[:], in1=TB[:], op=ALU.subtract))
            ckp("c7")
            S.op("vector", lambda e: e.tensor_tensor(out=CR[:], in0=CR[:], in1=DEN[:], op=ALU.mult))
            S.op("gpsimd", lambda e: e.tensor_tensor(out=CI[:], in0=CI[:], in1=DEN[:], op=ALU.mult))
            ckp("c8")
            ckp("pi%d" % dr)
            CRb = CR[:].unsqueeze(2).to_broadcast([128, 32, 8])
            CIb = CI[:].unsqueeze(2).to_broadcast([128, 32, 8])
            S.op("vector", lambda e: e.tensor_tensor(out=QR, in0=PRE[:, :, 0:8], in1=CRb, op=ALU.mult))
            S.op("gpsimd", lambda e: e.tensor_tensor(out=QT, in0=PIM[:, :, 0:8], in1=CIb, op=ALU.mult))
            S.bar()
            S.op("vector", lambda e: e.tensor_tensor(out=QR, in0=QR, in1=QT, op=ALU.subtract))
            S.bar()
            S.op("vector", lambda e: e.tensor_tensor(out=QI, in0=PRE[:, :, 0:8], in1=CIb, op=ALU.mult))
            S.op("gpsimd", lambda e: e.tensor_tensor(out=QT, in0=PIM[:, :, 0:8], in1=CRb, op=ALU.mult))
            S.bar()
            S.op("vector", lambda e: e.tensor_tensor(out=QI, in0=QI, in1=QT, op=ALU.add))
            S.bar()
            S.op("vector", lambda e: e.tensor_scalar(out=QI, in0=QI, scalar1=SIGN[:, 0:1], scalar2=None, op0=ALU.mult))
            S.op("gpsimd", lambda e: e.tensor_scalar(out=PA, in0=PRE[:, :, 8:16], scalar1=NSIGN[:, 0:1], scalar2=None, op0=ALU.mult))
            S.op("scalar", lambda e: e.mul(out=PB, in_=PIM[:, :, 8:16], mul=-1.0))
            S.bar()
            S.op("vector", lambda e: e.tensor_copy(out=ARC[:, 0, :], in_=PRE[:, :, 15]))
            S.op("vector", lambda e: e.tensor_copy(out=ARC[:, 1, :], in_=PRE[:, :, 15]))
            S.op("gpsimd", lambda e: e.tensor_scalar(out=AIC[:, 0, :], in0=PIM[:, :, 15], scalar1=SIGN[:, 0:1], scalar2=None, op0=ALU.mult))
            S.op("gpsimd", lambda e: e.tensor_scalar(out=AIC[:, 1, :], in0=PIM[:, :, 15], scalar1=NSIGN[:, 0:1], scalar2=None, op0=ALU.mult))
            S.bar()
            ckp("prep%d" % dr + "")
            for s_ in range(8):
                qi = (7 - s_) if dr == 0 else s_
                ri = s_ if dr == 0 else (7 - s_)
                S.op("vector", lambda e, s_=s_, qi=qi: e.tensor_tensor(
                    out=LT[:, :, s_, :], in0=BX1, in1=QR[:, :, qi:qi + 1].to_broadcast([128, 32, 16]), op=ALU.mult))
                S.op("gpsimd", lambda e, s_=s_, ri=ri: e.tensor_tensor(
                    out=RT[:, :, s_, :], in0=CX1, in1=PA[:, :, ri:ri + 1].to_broadcast([128, 32, 16]), op=ALU.mult))
            S.bar()
            TL = STG[:].rearrange("p (g s c) -> p g s c", s=8, c=16)
            for s_ in range(8):
                qi = (7 - s_) if dr == 0 else s_
                S.op("vector" if s_ % 2 == 0 else "gpsimd", lambda e, s_=s_, qi=qi: e.tensor_tensor(
                    out=TL[:, :, s_, :], in0=BX2, in1=QI[:, :, qi:qi + 1].to_broadcast([128, 32, 16]), op=ALU.mult))
            S.bar()
            S.op("vector", lambda e: e.tensor_tensor(out=LT, in0=LT, in1=TL, op=ALU.add))
            S.bar()
            for s_ in range(8):
                ri = s_ if dr == 0 else (7 - s_)
                S.op("vector" if s_ % 2 == 0 else "gpsimd", lambda e, s_=s_, ri=ri: e.tensor_tensor(
                    out=TL[:, :, s_, :], in0=CX2, in1=PB[:, :, ri:ri + 1].to_broadcast([128, 32, 16]), op=ALU.mult))
            S.bar()
            S.op("vector", lambda e: e.tensor_tensor(out=RT, in0=RT, in1=TL, op=ALU.add))
            S.bar()
            ckp("lr%d" % dr + "")
            for rnd in range(4):
                for gi in range(8):
                    g = rnd * 8 + gi
                    Lg = LT[:, g, :, :].rearrange("p s c -> p (s c)")
                    Rg = RT[:, g, :, :].rearrange("p s c -> p (s c)")
                    S.op("tensor", lambda e, gi=gi, Lg=Lg, Rg=Rg: e.matmul(
                        ps[gi // 4][:, (gi % 4) * 128:(gi % 4 + 1) * 128], lhsT=Lg, rhs=Rg, start=True, stop=True))
                    S.op("tensor", lambda e, gi=gi, Lg=Lg: e.transpose(
                        ps[2 + gi // 4][:, (gi % 4) * 128:(gi % 4 + 1) * 128], Lg, IDENT[:]))
                S.bar()
                for bk in range(2):
                    g0 = rnd * 8 + bk * 4
                    S.op("vector", lambda e, bk=bk, g0=g0, dr=dr: e.tensor_tensor(
                        out=TOEP[:, g0:g0 + 4, :], in0=ps[bk][:].rearrange("p (g n) -> p g n", n=128),
                        in1=MASKS[:, dr:dr + 1, :].to_broadcast([128, 4, 128]), op=ALU.mult))
                    S.op("scalar", lambda e, bk=bk, g0=g0: e.copy(
                        out=ET[:, g0:g0 + 4, :], in_=ps[2 + bk][:].rearrange("p (g n) -> p g n", n=128)))
                S.bar()
                for bk in range(2):
                    g0 = rnd * 8 + bk * 4
                    S.op("vector", lambda e, bk=bk, g0=g0: e.tensor_copy(
                        out=ESW[:, g0:g0 + 4, 0:64], in_=ps[2 + bk][:].rearrange("p (g n) -> p g n", n=128)[:, :, 64:128]))
                    S.op("vector", lambda e, bk=bk, g0=g0: e.tensor_copy(
                        out=ESW[:, g0:g0 + 4, 64:128], in_=ps[2 + bk][:].rearrange("p (g n) -> p g n", n=128)[:, :, 0:64]))
                S.bar()
            S.op("scalar", lambda e: e.copy(out=RB.rearrange("p g n -> p (g n)"), in_=RT.rearrange("p g s c -> p (g s c)")))
            if dr == 0:
                for g in range(32):
                    S.op("vector", lambda e, g=g: e.scalar_tensor_tensor(
                        out=TOEP[:, g, :], in0=IDENT[:], scalar=DV[:, g:g + 1], in1=TOEP[:, g, :],
                        op0=ALU.mult, op1=ALU.add))
            S.op("gpsimd", lambda e: e.memset(W3[:], 0.0))
            S.bar()
            ckp("tiles%d" % dr + "")
            blocks = [(0, 32)] + [(32 + 64 * b, 64) for b in range(4)]
            order = blocks if dr == 0 else [blocks[0]] + blocks[:0:-1]
            for (jb, nb) in order:
                for half in range(2):
                    for gi in range(16):
                        g = half * 16 + gi
                        for arr in range(2):
                            ii = gi * 2 + arr
                            Em = ET if arr == 0 else ESW
                            S.op("tensor", lambda e, ii=ii, g=g, Em=Em, jb=jb, nb=nb: e.matmul(
                                ps[ii // 8][:, (ii % 8) * 64:(ii % 8) * 64 + nb], lhsT=Em[:, g, :],
                                rhs=IM[:, g, jb:jb + nb], start=True, stop=True))
                    S.bar()
                    for bk in range(4):
                        g0 = half * 16 + bk * 4
                        S.op("vector" if bk % 2 == 0 else "scalar", (lambda e, bk=bk, g0=g0, nb=nb: e.tensor_copy(
                            out=SS[:, 0:nb, :, g0:g0 + 4].rearrange("p j a g -> p g a j"),
                            in_=ps[bk][:].rearrange("p (g a j) -> p g a j", a=2, j=64)[:, :, :, 0:nb]))
                            if bk % 2 == 0 else (lambda e, bk=bk, g0=g0, nb=nb: e.copy(
                            out=SS[:, 0:nb, :, g0:g0 + 4].rearrange("p j a g -> p g a j"),
                            in_=ps[bk][:].rearrange("p (g a j) -> p g a j", a=2, j=64)[:, :, :, 0:nb])))
                    S.bar()
                js = list(range(jb, jb + nb)) if dr == 0 else list(range(jb + nb - 1, jb - 1, -1))
                pjl = None
                for j in js:
                    jl = j - jb
                    Wc = W3[:] if pjl is None else SS[:, pjl, :, :]
                    S.op("vector", lambda e, jl=jl, Wc=Wc: e.tensor_tensor(out=G3[:, 0:2, :], in0=Wc, in1=SS[:, jl, :, :], op=ALU.add))
                    S.op("vector", lambda e: e.tensor_tensor(out=T1[:], in0=G3[:, 0:2, :], in1=ARC[:], op=ALU.mult))
                    S.op("vector", lambda e: e.tensor_tensor(out=T2[:, 0, :], in0=G3[:, 1, :], in1=AIC[:, 0, :], op=ALU.mult))
                    S.op("vector", lambda e: e.tensor_tensor(out=T2[:, 1, :], in0=G3[:, 0, :], in1=AIC[:, 1, :], op=ALU.mult))
                    S.op("vector", lambda e, jl=jl: e.tensor_tensor(out=SS[:, jl, :, :], in0=T1[:], in1=T2[:], op=ALU.add))
                    pjl = jl
                S.bar()
                if dr == 0:
                    S.op("scalar", lambda e, jb=jb: e.copy(out=HH[:, jb, :], in_=W3[:, 0, :]))
                    S.op("scalar", lambda e, jb=jb, nb=nb: e.copy(out=HH[:, jb + 1:jb + nb, :], in_=SS[:, 0:nb - 1, 0, :]))
                else:
                    S.op("scalar", lambda e, jb=jb, nb=nb: e.copy(out=HH[:, jb + nb - 1, :], in_=W3[:, 0, :]))
                    S.op("scalar", lambda e, jb=jb, nb=nb: e.copy(out=HH[:, jb:jb + nb - 1, :], in_=SS[:, 1:nb, 0, :]))
                S.bar()
                S.op("vector", lambda e, pjl=pjl: e.tensor_copy(out=W3[:], in_=SS[:, pjl, :, :]))
                S.bar()
            ckp("rec%d" % dr + "")
            for rnd in range(4):
                for gi in range(8):
                    g = rnd * 8 + gi
                    S.op("tensor", lambda e, gi=gi, g=g: e.matmul(
                        ps[gi][:, 0:NJ], lhsT=TOEP[:, g, :], rhs=IM[:, g, :], start=True, stop=False))
                    S.op("tensor", lambda e, gi=gi, g=g: e.matmul(
                        ps[gi][:, 0:NJ], lhsT=RB[:, g, :], rhs=HH[:, :, g], start=False, stop=True))
                S.bar()
                for gi in range(8):
                    g = rnd * 8 + gi
                    if dr == 0:
                        S.op("vector" if gi % 2 == 0 else "scalar", (lambda e, gi=gi, g=g: e.tensor_copy(out=YS[:, g, :], in_=ps[gi][:, 0:NJ]))
                             if gi % 2 == 0 else (lambda e, gi=gi, g=g: e.copy(out=YS[:, g, :], in_=ps[gi][:, 0:NJ])))
                    else:
                        S.op("vector", lambda e, gi=gi, g=g: e.tensor_tensor(out=YS[:, g, :], in0=YS[:, g, :], in1=ps[gi][:, 0:NJ], op=ALU.add))
                S.bar()
        ckp("read")
        S.dma("sync", YSP[:, :, :], YS)
        S.bar()
        YV = YS.rearrange("p (q s) j -> p q s j", s=8)
        YSPv = YSP.rearrange("(s c) (q g) j -> g c q s j", c=16, g=8)
        for g8 in range(8):
            for q in range(4):
                S.dma(["sync", "gpsimd"][q % 2], YV[g8 * 16:(g8 + 1) * 16, q, :, :], YSPv[g8, :, q, :, :])
            S.bar()
        S.bar()

        ckp("unim")
        WG = WB[:, 0:4 * 512].rearrange("p (k n) -> p k n", n=512)
        load_weight(WG, w_glu[li], 4, 512, 512)
        for (t0, w) in TILES:
            j0, nj = t0 // 8, w // 8
            for q in range(4):
                S.op("scalar", lambda e, q=q, w=w, j0=j0, nj=nj: e.activation(
                    out=TMP[:, q, 0:w].rearrange("p (j s) -> p j s", s=8),
                    in_=YV[:, q, :, j0:j0 + nj].rearrange("p s j -> p j s"), func=AF.Gelu_apprx_tanh))
            S.bar()
            S.op("vector", lambda e, w=w: e.tensor_copy(out=OB[:, 0:4, 0:w], in_=TMP[:, 0:4, 0:w]))
            S.bar()
            for m in range(4):
                for k in range(4):
                    S.op("tensor", lambda e, m=m, k=k, w=w: e.matmul(
                        ps[m][:, 0:w], lhsT=WG[:, k, m * 128:(m + 1) * 128], rhs=OB[:, k, 0:w],
                        start=(k == 0), stop=(k == 3)))
            S.bar()
            for m in range(4):
                S.op("scalar", lambda e, m=m, w=w: e.activation(
                    out=XT[:, m, 0:w], in_=ps[m][:, 0:w], func=AF.Sigmoid, bias=BGLU[:, m:m + 1], scale=1.0))
            S.bar()
            S.op("vector", lambda e, w=w: e.tensor_tensor(out=OB[:, 4:8, 0:w], in0=TMP[:, 0:4, 0:w], in1=XT[:, 0:4, 0:w], op=ALU.mult))
            S.bar()
            S.dma("sync", MIX[0:512, t0:t0 + w].rearrange("(k p) t -> p k t", p=128), OB[:, 4:8, 0:w])
            S.bar()

        ckp("glu")
        S.dma("sync", TMP[:, 0:4, 0:128], w_sp[li].rearrange("h p q -> p h q"))
        S.dma("gpsimd", BS.rearrange("p h q -> p (h q)"), b_sp[li].rearrange("h q -> (h q)").partition_broadcast(128))
        S.bar()
        for h in range(4):
            S.op("tensor", lambda e, h=h: e.transpose(ps[0][:, h * 128:(h + 1) * 128], TMP[:, h, 0:128], IDENT[:]))
        S.bar()
        S.op("vector", lambda e: e.tensor_copy(out=WST[:], in_=ps[0][:].rearrange("p (h n) -> p h n", n=128)))
        S.bar()
        for (t0, w) in TILES:
            nchk = w // 128
            S.dma("sync", XT[:, 0:4, 0:w], VG[:, t0:t0 + w].rearrange("(k p) t -> p k t", p=128))
            S.dma("gpsimd", OB[:, 0:4, 0:w], UG[:, t0:t0 + w].rearrange("(k p) t -> p k t", p=128))
            S.bar()
            S.op("scalar", lambda e, w=w: e.activation(out=SQ[:, 0:4, 0:w], in_=XT[:, 0:4, 0:w], func=AF.Square))
            S.bar()
            rstd_from(SQ, 4, w, 1.0 / 512)
            for k in range(4):
                S.op("vector", lambda e, k=k, w=w: e.scalar_tensor_tensor(
                    out=TMP[:, k, 0:w], in0=XT[:, k, 0:w], scalar=GSGU[:, k:k + 1], in1=RS[:, 0:w],
                    op0=ALU.mult, op1=ALU.mult))
            S.bar()
            for ck in range(nchk):
                for h in range(4):
                    ii = ck * 4 + h
                    S.op("tensor", lambda e, ii=ii, ck=ck, h=h: e.transpose(
                        ps[ii // 4][:, (ii % 4) * 128:(ii % 4 + 1) * 128], TMP[:, h, ck * 128:(ck + 1) * 128], IDENT[:]))
            S.bar()
            for ck in range(nchk):
                S.op("vector" if ck % 2 == 0 else "scalar", (lambda e, ck=ck: e.tensor_copy(
                    out=VT[:, ck * 4:(ck + 1) * 4, :], in_=ps[ck][:].rearrange("p (h n) -> p h n", n=128)))
                    if ck % 2 == 0 else (lambda e, ck=ck: e.copy(
                    out=VT[:, ck * 4:(ck + 1) * 4, :], in_=ps[ck][:].rearrange("p (h n) -> p h n", n=128))))
            S.bar()
            for ck in range(nchk):
                for h in range(4):
                    ii = ck * 4 + h
                    S.op("tensor", lambda e, ii=ii, ck=ck, h=h: e.matmul(
                        ps[4 + ck][:, h * 128:(h + 1) * 128], lhsT=VT[:, ii, :], rhs=WST[:, h, :], start=True, stop=True))
            S.bar()
            for ck in range(nchk):
                S.op("vector", lambda e, ck=ck: e.tensor_tensor(
                    out=TMP[:, 4:8, ck * 128:(ck + 1) * 128], in0=ps[4 + ck][:].rearrange("p (h n) -> p h n", n=128),
                    in1=BS, op=ALU.add))
            S.bar()
            S.op("vector", lambda e, w=w: e.tensor_tensor(out=OB[:, 4:8, 0:w], in0=TMP[:, 4:8, 0:w], in1=OB[:, 0:4, 0:w], op=ALU.mult))
            S.bar()
            S.dma("sync", MIX[512:1024, t0:t0 + w].rearrange("(k p) t -> p k t", p=128), OB[:, 4:8, 0:w])
            S.bar()

        ckp("sgu")
        load_weight(WB[:, 0:8 * D].rearrange("p (k n) -> p k n", n=D), w_out[li], 8, D, 512)
        resid_linear(MIX, 8, 2)

        ckp("wout")
        norm_mod(A2, 3)

        ckp("norm2")
        S.dma("sync", FB[0:9, 0:2 * DFF], w_conv[li].rearrange("a b n -> (a b) n"))
        S.bar()
        for ch in range(44):
            S.op("tensor", lambda e, ch=ch: e.transpose(ps[7][:, ch * 9:(ch + 1) * 9], FB[0:9, ch * 128:(ch + 1) * 128], IDENT[0:9, 0:9]))
        S.bar()
        S.op("vector", lambda e: e.tensor_copy(out=WC[:].rearrange("p t c -> p c t"), in_=ps[7][:, 0:396].rearrange("p (c t) -> p c t", t=9)))
        S.bar()
        WU = WB[:, 0:8 * 256].rearrange("p (k n) -> p k n", n=256)
        FBb = FB[:].bitcast(BF16)
        UPb = [FBb[:, 0:2304], FBb[:, 2304:4608]]
        DG = FBb[:, 4608:6912].rearrange("p (t n) -> p t n", n=128)
        SG = FB[:, 4608:6912]
        GB = ACTB[:, 0:5, :].rearrange("p a t -> p (a t)")[:, 0:NT]
        stgs = [STG[:, 0:2048].rearrange("p (k n) -> p k n", n=256), STG[:, 2048:4096].rearrange("p (k n) -> p k n", n=256)]

        def wup_dma(m):
            st = stgs[m % 2]
            S.dma("sync", st[:, :, 0:128], w_up[li, :, m * 128:(m + 1) * 128].rearrange("(k p) n -> p k n", p=128))
            S.dma("gpsimd", st[:, :, 128:256], w_up[li, :, DFF + m * 128:DFF + (m + 1) * 128].rearrange("(k p) n -> p k n", p=128))

        def conv_mm(part):
            src = UPb[part]
            d0 = part * 9
            S.op("tensor", lambda e: e.matmul(ps[0][:, 0:256], lhsT=DG[:, d0 + 4, :], rhs=src[:, 0:256], start=True, stop=False))
            S.op("tensor", lambda e: e.matmul(ps[0][:, 1:256], lhsT=DG[:, d0 + 3, :], rhs=src[:, 0:255], start=False, stop=False))
            S.op("tensor", lambda e: e.matmul(ps[0][:, 0:255], lhsT=DG[:, d0 + 5, :], rhs=src[:, 1:256], start=False, stop=True))
            sv = src[:, LC:NT].rearrange("p (r c) -> p r c", c=64)
            taps = [(1, 1)] + [(ky, kx) for ky in range(3) for kx in range(3) if not (ky == 1 and kx == 1)]
            for ti in range(1, 5):
                R0 = 8 * (ti - 1)
                pv = ps[ti][:, 0:512].rearrange("p (r c) -> p r c", c=64)
                for n_, (ky, kx) in enumerate(taps):
                    dy, dx = ky - 1, kx - 1
                    ra, rb = max(R0, -dy, 0), min(R0 + 8, 32 - max(0, dy))
                    ra = max(ra, 0 - min(0, dy))
                    c0, c1 = max(0, -dx), 64 - max(0, dx)
                    S.op("tensor", lambda e, pv=pv, ra=ra, rb=rb, c0=c0, c1=c1, dy=dy, dx=dx, R0=R0, ky=ky, kx=kx, n_=n_:
                         e.matmul(pv[:, ra - R0:rb - R0, c0:c1], lhsT=DG[:, d0 + ky * 3 + kx, :],
                                  rhs=sv[:, ra + dy:rb + dy, c0 + dx:c1 + dx], start=(n_ == 0), stop=(n_ == 8)))

        wup_dma(0)
        S.bar()
        for m in range(22):
            S.op("vector", lambda e, m=m: e.tensor_copy(out=WU, in_=stgs[m % 2]))
            for part in range(2):
                for tap in range(9):
                    ch = part * 22 + m
                    if (part * 9 + tap) % 2 == 0:
                        S.op("vector", lambda e, part=part, tap=tap, ch=ch: e.tensor_scalar(
                            out=DG[:, part * 9 + tap, :], in0=IDENT[:], scalar1=WC[:, tap, ch:ch + 1], scalar2=None, op0=ALU.mult))
                    else:
                        S.op("scalar", lambda e, part=part, tap=tap, ch=ch: e.activation(
                            out=DG[:, part * 9 + tap, :], in_=IDENT[:], func=AF.Copy, scale=WC[:, tap, ch:ch + 1]))
            S.bar()
            for part in range(2):
                for ti, (t0, w) in enumerate(TILES):
                    for k in range(8):
                        S.op("tensor", lambda e, ti=ti, k=k, t0=t0, w=w, part=part: e.matmul(
                            ps[ti][:, 0:w], lhsT=WU[:, k, part * 128:(part + 1) * 128], rhs=H[:, k, t0:t0 + w],
                            start=(k == 0), stop=(k == 7)))
                if part == 0:
                    if m + 1 < 22:
                        wup_dma(m + 1)
                    if m > 0:
                        S.dma("sync", GD[(m - 1) * 128:m * 128, :], GB)
                S.bar()
                dst = UPb[part]
                for ti, (t0, w) in enumerate(TILES):
                    S.op("vector" if ti % 2 == 0 else "scalar", (lambda e, ti=ti, t0=t0, w=w, dst=dst: e.tensor_copy(
                        out=dst[:, t0:t0 + w], in_=ps[ti][:, 0:w])) if ti % 2 == 0 else (lambda e, ti=ti, t0=t0, w=w, dst=dst: e.copy(
                        out=dst[:, t0:t0 + w], in_=ps[ti][:, 0:w])))
                S.bar()
            conv_mm(0)
            S.bar()
            for ti, (t0, w) in enumerate(TILES):
                S.op("scalar", lambda e, ti=ti, t0=t0, w=w: e.activation(out=SG[:, t0:t0 + w], in_=ps[ti][:, 0:w], func=AF.Silu))
            S.bar()
            conv_mm(1)
            S.bar()
            for ti, (t0, w) in enumerate(TILES):
                S.op("vector", lambda e, ti=ti, t0=t0, w=w: e.tensor_tensor(out=GB[:, t0:t0 + w], in0=ps[ti][:, 0:w], in1=SG[:, t0:t0 + w], op=ALU.mult))
            S.bar()
        S.dma("sync", GD[21 * 128:22 * 128, :], GB)
        S.bar()

        ckp("ffnup")
        load_weight(WB[:, 0:22 * D].rearrange("p (k n) -> p k n", n=D), w_down[li], 22, D, 128)
        resid_linear(GD, 22, 5)

    except _Stop:
        pass
    S.bar()
    if dbg:
        DF = nc.dram_tensor("DBGF", [128, 32768], F32, kind="ExternalOutput").ap()
        DB = nc.dram_tensor("DBGB", [128, 40960], BF16, kind="ExternalOutput").ap()
        off = 0
        for t_, n_ in [(HRAW[:], 9216), (XTt[:], 4096), (TMPt[:], 4096), (FB[:], 9216), (STG[:], 4096), (RS[:], 512),
                       (MOD[:].rearrange("p q k n -> p (q k n)"), 96), (A1[:].rearrange("p k n -> p (k n)"), 16),
                       (A2[:].rearrange("p k n -> p (k n)"), 16), (W3[:].rearrange("p a g -> p (a g)"), 64),
                       (ARC[:].rearrange("p a g -> p (a g)"), 64), (AIC[:].rearrange("p a g -> p (a g)"), 64),
                       (CR[:], 32), (CI[:], 32), (DV[:], 32), (LR[:], 32), (LI[:], 32)]:
            S.dma("sync", DF[:, off:off + n_], t_)
            off += n_
        offb = 0
        for t_, n_ in [(WB, 22528), (OBt[:], 4096), (ACTBt[:], 11264)]:
            S.dma("gpsimd", DB[:, offb:offb + n_], t_)
            offb += n_
        S.bar()
    for b in range(16):
        t0 = LC + b * 128
        S.dma("sync", XT[:, :, 0:128], XRES[:, t0:t0 + 128].rearrange("(k p) t -> p k t", p=128))
        S.bar()
        S.op("scalar", lambda e: e.activation(out=SQ[:, :, 0:128], in_=XT[:, :, 0:128], func=AF.Square))
        S.bar()
        rstd_from(SQ, 8, 128, 1.0 / D)
        for k in range(8):
            S.op("vector", lambda e, k=k: e.scalar_tensor_tensor(
                out=TMP[:, k, 0:128], in0=XT[:, k, 0:128], scalar=GFIN[:, k:k + 1], in1=RS[:, 0:128],
                op0=ALU.mult, op1=ALU.mult))
        S.bar()
        for k in range(8):
            S.op("tensor", lambda e, k=k: e.transpose(ps[k // 4][:, (k % 4) * 128:(k % 4 + 1) * 128],
                                                       TMP[:, k, 0:128], IDENT[:]))
        S.bar()
        S.op("vector", lambda e: e.tensor_copy(out=XT[:, 0:4, 0:128], in_=ps[0][:].rearrange("p (k t) -> p k t", t=128)))
        S.op("scalar", lambda e: e.copy(out=XT[:, 4:8, 0:128], in_=ps[1][:].rearrange("p (k t) -> p k t", t=128)))
        S.bar()
        S.dma("sync", out[b * 128:(b + 1) * 128, :].rearrange("t (k d) -> t k d", d=128), XT[:, :, 0:128])
        S.bar()

    S.emit()
    es.close()
    return nc


_CONST = None


def _consts():
    ident = np.eye(128, dtype=np.float32)
    sp = np.arange(128) // 16
    m0 = (sp[None, :] >= sp[:, None]).astype(np.float32)
    m1 = (sp[None, :] <= sp[:, None]).astype(np.float32)
    return ident, np.stack([m0, m1])


def kernel(n_layers=4, **inputs):
    nc = build_nc(n_layers)
    ident, mask = _consts()
    in_maps = []
    for b in range(8):
        m = {}
        for k, v in inputs.items():
            v = np.asarray(v)
            if k in ("x", "c", "ctx"):
                m[k] = np.ascontiguousarray(v[b], dtype=np.float32)
            else:
                m[k] = np.ascontiguousarray(v, dtype=np.float32)
        m["ident"] = ident
        m["mask"] = mask
        in_maps.append(m)
    res = run_bass_kernel_spmd(nc, in_maps, core_ids=list(range(8)))
    return np.stack([np.asarray(r["out"], dtype=np.float32) for r in res.results], axis=0)
```

```python
import numpy as np
from contextlib import ExitStack
import concourse.bass as bass
import concourse.mybir as mybir
from concourse.bass_utils import run_bass_kernel_spmd

F32, BF16, I32 = mybir.dt.float32, mybir.dt.bfloat16, mybir.dt.int32
AF = mybir.ActivationFunctionType
ALU = mybir.AluOpType

D = 1024
NT = 2304
LC = 256
LL = 2048
DFF = 2816
EPS = 1e-6
NJ = 288
TILES = [(0, 256)] + [(256 + 512 * i, 512) for i in range(4)]
TWO_PI = 6.283185307179586


class Sched:
    def __init__(self, nc):
        self.nc = nc
        self.stages = [[]]

    def op(self, eng, fn, dma=False):
        self.stages[-1].append((eng, dma, fn))

    def dma(self, eng, out, in_, slow=False):
        if slow:
            self.op(eng, lambda e, o=out, i=in_: e.dma_start(out=o, in_=i, allow_slow_non_contiguous=True), dma=True)
        else:
            self.op(eng, lambda e, o=out, i=in_: e.dma_start(out=o, in_=i), dma=True)

    def bar(self):
        if self.stages[-1]:
            self.stages.append([])

    def emit(self):
        nc = self.nc
        self.bar()
        names = ["c_scalar", "c_vector", "c_gpsimd", "c_tensor", "d_sync", "d_scalar", "d_gpsimd"]
        cum = []
        cur = {n: 0 for n in names}
        for st in self.stages:
            cum.append(dict(cur))
            for (eng, dma, _) in st:
                if dma:
                    cur["d_" + eng] += 16
                else:
                    cur["c_" + eng] += 1
        final = dict(cur)
        with ExitStack() as es:
            sems = {n: es.enter_context(nc.semaphore(n)) for n in names}
            block = es.enter_context(nc.Block())

            def make(engname):
                def body(eng):
                    waited = {n: 0 for n in names}
                    for k, st in enumerate(self.stages):
                        mine = [o for o in st if o[0] == engname]
                        if not mine:
                            continue
                        for n in names:
                            if cum[k][n] > waited[n]:
                                eng.wait_ge(sems[n], cum[k][n])
                                waited[n] = cum[k][n]
                        for (_, dma, fn) in mine:
                            ins = fn(eng)
                            if dma:
                                ins.then_inc(sems["d_" + engname], 16)
                            else:
                                ins.then_inc(sems["c_" + engname], 1)
                    if engname == "sync":
                        for n in names:
                            if final[n] > waited[n]:
                                eng.wait_ge(sems[n], final[n])
                return body

            block.sync(make("sync"))
            block.scalar(make("scalar"))
            block.vector(make("vector"))
            block.gpsimd(make("gpsimd"))
            block.tensor(make("tensor"))


class _Stop(Exception):
    pass


def build_nc(n_layers, n_wl=4, stop=None, dbg=False):
    nc = bass.Bass("TRN2", target_bir_lowering=False)
    S = Sched(nc)
    W = n_wl

    def ckp(name):
        S.bar()
        if stop == name:
            raise _Stop()

    def din(name, shape):
        return nc.dram_tensor(name, list(shape), F32, kind="ExternalInput").ap()

    x_in = din("x", [LL, D])
    c_in = din("c", [D])
    ctx_in = din("ctx", [LC, D])
    cctx_in = din("c_ctx", [D])
    w_ada = din("w_ada", [W, D, 6 * D])
    b_ada = din("b_ada", [W, 6 * D])
    g_mix = din("g_mix", [W, D])
    w_in = din("w_in", [W, D, 1536])
    a_re = din("ssm_a_re", [W, 2, 32, 64])
    a_im = din("ssm_a_im", [W, 2, 32, 64])
    b_re = din("ssm_b_re", [W, 2, 32, 64, 16])
    b_im = din("ssm_b_im", [W, 2, 32, 64, 16])
    c_re = din("ssm_c_re", [W, 2, 32, 16, 64])
    c_im = din("ssm_c_im", [W, 2, 32, 16, 64])
    log_dt = din("ssm_log_dt", [W, 2, 32])
    ssm_d = din("ssm_d", [W, 512])
    w_glu = din("w_glu", [W, 512, 512])
    b_glu = din("b_glu", [W, 512])
    g_sgu = din("g_sgu", [W, 512])
    w_sp = din("w_spatial", [W, 4, 128, 128])
    b_sp = din("b_spatial", [W, 4, 128])
    w_out = din("w_out", [W, D, D])
    g_ffn = din("g_ffn", [W, D])
    w_up = din("w_up", [W, D, 2 * DFF])
    w_conv = din("w_conv", [W, 3, 3, 2 * DFF])
    w_down = din("w_down", [W, DFF, D])
    g_final = din("g_final", [D])
    ident_in = din("ident", [128, 128])
    mask_in = din("mask", [2, 128, 128])
    out = nc.dram_tensor("out", [LL, D], F32, kind="ExternalOutput").ap()

    SK = dict(kind="ExternalOutput") if dbg else {}
    XRES = nc.dram_tensor("XRES", [D, NT], F32, **SK).ap()
    USSMP = nc.dram_tensor("USSMP", [512, 8, NJ], BF16, **SK).ap()
    YSP = nc.dram_tensor("YSP", [128, 32, NJ], F32, **SK).ap()
    UG = nc.dram_tensor("UG", [512, NT], BF16, **SK).ap()
    VG = nc.dram_tensor("VG", [512, NT], F32, **SK).ap()
    MIX = nc.dram_tensor("MIX", [D, NT], BF16, **SK).ap()
    GD = nc.dram_tensor("GD", [DFF, NT], BF16, **SK).ap()

    es = ExitStack()

    def sb(name, shape, dt=F32):
        return es.enter_context(nc.sbuf_tensor(name, list(shape), dt))

    ps = [es.enter_context(nc.psum_tensor("ps%d" % i, [128, 512], F32)) for i in range(8)]

    IDENT = sb("IDENT", [128, 128])
    ONESB = sb("ONESB", [128, 128], BF16)
    MASKS = sb("MASKS", [128, 2, 128])
    SIGN = sb("SIGN", [128, 1])
    NSIGN = sb("NSIGN", [128, 1])
    SC = sb("SC", [128, 8, 2])
    MOD = sb("MOD", [128, 6, 8, 2])
    BADA = sb("BADA", [128, 6, 8])
    GM = sb("GM", [128, 8])
    GF = sb("GF", [128, 8])
    GFIN = sb("GFIN", [128, 8])
    A1 = sb("A1", [128, 8, 2])
    A2 = sb("A2", [128, 8, 2])
    HRAW = sb("HRAW", [128, 9216])
    H = HRAW[:].bitcast(BF16).rearrange("p (k t) -> p k t", t=NT)
    YS = HRAW[:].rearrange("p (g j) -> p g j", j=NJ)
    STG = sb("STG", [128, 4096])
    SQ = STG[:, 0:2048].bitcast(BF16).rearrange("p (k t) -> p k t", t=512)
    WBt = sb("WB", [128, 22528], BF16)
    WB = WBt[:]
    TOEP = WB[:, 0:4096].rearrange("p (g n) -> p g n", n=128)
    ET = WB[:, 4096:8192].rearrange("p (g n) -> p g n", n=128)
    ESW = WB[:, 8192:12288].rearrange("p (g n) -> p g n", n=128)
    IM = WB[:, 12288:21504].rearrange("p (g j) -> p g j", j=NJ)
    XTt = sb("XT", [128, 4096])
    XT = XTt[:].rearrange("p (k t) -> p k t", t=512)
    LT = XTt[:].rearrange("p (g s c) -> p g s c", s=8, c=16)
    SS = XTt[:].rearrange("p (j a g) -> p j a g", a=2, g=32)
    TMPt = sb("TMP", [128, 4096])
    TMP = TMPt[:].rearrange("p (k t) -> p k t", t=512)
    RT = TMPt[:].rearrange("p (g s c) -> p g s c", s=8, c=16)
    RS = sb("RS", [128, 512])
    OBt = sb("OB", [128, 4096], BF16)
    OB = OBt[:].rearrange("p (k t) -> p k t", t=512)
    RB = OBt[:].rearrange("p (g n) -> p g n", n=128)
    ACTBt = sb("ACTB", [128, 11264], BF16)
    ACTB = ACTBt[:].rearrange("p (k t) -> p k t", t=512)
    USP = ACTBt[:, 0:9216].rearrange("p (q s j) -> p q s j", s=8, j=NJ)
    HH = ACTBt[:, 0:9216].rearrange("p (j g) -> p j g", g=32)
    FB = sb("FB", [128, 9216])
    UPG = FB[:, 0:2304]; UPV = FB[:, 2304:4608]; CG = FB[:, 4608:6912]; CV = FB[:, 6912:9216]
    def ftab(i):
        return FB[:, i * 512:(i + 1) * 512].rearrange("p (g k) -> p g k", k=16)
    ARG, ARGC, EARG, NF, NFC, PRE, PIM, MAG, BX1, BX2, CX1, CX2 = [ftab(i) for i in range(12)]
    def qtab(i):
        return FB[:, 6144 + i * 256:6144 + (i + 1) * 256].rearrange("p (g k) -> p g k", k=8)
    QR, QI, QT, PA, PB = [qtab(i) for i in range(5)]
    NI = FB[:, 7424:7936].bitcast(I32).rearrange("p (g k) -> p g k", k=16)
    NIC = FB[:, 7936:8448].bitcast(I32).rearrange("p (g k) -> p g k", k=16)
    BS = FB[:, 0:512].rearrange("p (h q) -> p h q", q=128)
    VT = STG[:, 2048:4096].bitcast(BF16).rearrange("p (a n) -> p a n", n=128)
    W3 = sb("W3", [128, 2, 32])
    G3 = sb("G3", [128, 3, 32])
    T1 = sb("T1", [128, 2, 32])
    T2 = sb("T2", [128, 2, 32])
    ARp = sb("ARp", [128, 32]); AIp = sb("AIp", [128, 32]); LDT = sb("LDT", [128, 32])
    LR = sb("LR", [128, 32]); LI = sb("LI", [128, 32])
    CR = sb("CR", [128, 32]); CI = sb("CI", [128, 32]); NR = sb("NR", [128, 32]); DEN = sb("DEN", [128, 32])
    TA = sb("TA", [128, 32]); TB = sb("TB", [128, 32])
    ARC = sb("ARC", [128, 2, 32]); AIC = sb("AIC", [128, 2, 32])
    DV = sb("DV", [128, 32])
    BGLU = sb("BGLU", [128, 4]); GSGU = sb("GSGU", [128, 4])
    WST = sb("WST", [128, 4, 128], BF16)
    WC = sb("WC", [128, 9, 44])

    S.dma("sync", IDENT[:], ident_in[:, :])
    S.dma("sync", MASKS[:], mask_in.rearrange("m p q -> p m q"))
    S.op("vector", lambda e: e.memset(ONESB[:], 1.0))
    S.op("vector", lambda e: e.memset(SIGN[0:64, :], -1.0))
    S.op("vector", lambda e: e.memset(SIGN[64:128, :], 1.0))
    S.op("vector", lambda e: e.memset(NSIGN[0:64, :], 1.0))
    S.op("vector", lambda e: e.memset(NSIGN[64:128, :], -1.0))
    S.dma("sync", STG[0:8, 0:128], c_in.rearrange("(k p) -> k p", p=128))
    S.dma("sync", STG[8:16, 0:128], cctx_in.rearrange("(k p) -> k p", p=128))
    S.dma("sync", STG[16:24, 0:128], g_final.rearrange("(k p) -> k p", p=128))
    S.bar()
    S.op("tensor", lambda e: e.transpose(ps[7][:, 0:24], STG[0:24, 0:128], IDENT[0:24, 0:24]))
    S.bar()
    S.op("vector", lambda e: e.tensor_copy(out=SC[:, :, 0], in_=ps[7][:, 0:8]))
    S.op("vector", lambda e: e.tensor_copy(out=SC[:, :, 1], in_=ps[7][:, 8:16]))
    S.op("vector", lambda e: e.tensor_copy(out=GFIN[:], in_=ps[7][:, 16:24]))
    S.bar()
    S.op("scalar", lambda e: e.activation(out=SC[:], in_=SC[:], func=AF.Silu))
    S.bar()

    def in_transpose(src, nblk, tok0):
        for b in range(nblk):
            S.dma("sync", TMP[:, :, 0:128], src[b * 128:(b + 1) * 128, :].rearrange("t (k d) -> t k d", d=128))
            S.bar()
            for k in range(8):
                S.op("tensor", lambda e, k=k: e.transpose(ps[k // 4][:, (k % 4) * 128:(k % 4 + 1) * 128],
                                                           TMP[:, k, 0:128], IDENT[:]))
            S.bar()
            S.op("vector", lambda e: e.tensor_copy(out=XT[:, 0:4, 0:128], in_=ps[0][:].rearrange("p (k t) -> p k t", t=128)))
            S.op("scalar", lambda e: e.copy(out=XT[:, 4:8, 0:128], in_=ps[1][:].rearrange("p (k t) -> p k t", t=128)))
            S.bar()
            t0 = tok0 + b * 128
            S.dma("sync", XRES[:, t0:t0 + 128].rearrange("(k p) t -> p k t", p=128), XT[:, :, 0:128])
            S.bar()

    in_transpose(ctx_in, 2, 0)
    in_transpose(x_in, 16, 256)

    def load_weight(dst, wap, kch, ncols, cb):
        for c0 in range(0, ncols, cb):
            stg = STG[:, 0:kch * cb].rearrange("p (k n) -> p k n", n=cb)
            S.dma("sync", stg, wap[:, c0:c0 + cb].rearrange("(k p) n -> p k n", p=128))
            S.bar()
            S.op("vector", lambda e, stg=stg, c0=c0: e.tensor_copy(out=dst[:, :, c0:c0 + cb], in_=stg))
            S.bar()

    def rstd_from(src_sq, nk, w, inv_n):
        for k in range(nk):
            S.op("tensor", lambda e, k=k: e.matmul(ps[0][:, 0:w], lhsT=ONESB[:], rhs=src_sq[:, k, 0:w],
                                                    start=(k == 0), stop=(k == nk - 1)))
        S.bar()
        S.op("scalar", lambda e: e.activation(out=RS[:, 0:w], in_=ps[0][:, 0:w], func=AF.Sqrt, bias=EPS, scale=inv_n))
        S.bar()
        S.op("vector", lambda e: e.reciprocal(out=RS[:, 0:w], in_=RS[:, 0:w]))
        S.bar()

    def norm_mod(Acoef, which_shift):
        for (t0, w) in TILES:
            sel = 1 if t0 == 0 else 0
            S.dma("sync", XT[:, :, 0:w], XRES[:, t0:t0 + w].rearrange("(k p) t -> p k t", p=128))
            S.bar()
            S.op("scalar", lambda e, w=w: e.activation(out=SQ[:, :, 0:w], in_=XT[:, :, 0:w], func=AF.Square))
            S.bar()
            rstd_from(SQ, 8, w, 1.0 / D)
            for k in range(8):
                S.op("vector", lambda e, k=k, w=w, sel=sel: e.scalar_tensor_tensor(
                    out=TMP[:, k, 0:w], in0=XT[:, k, 0:w], scalar=Acoef[:, k, sel:sel + 1], in1=RS[:, 0:w],
                    op0=ALU.mult, op1=ALU.mult))
            S.bar()
            for k in range(8):
                S.op("scalar", lambda e, k=k, w=w, sel=sel, t0=t0: e.activation(
                    out=H[:, k, t0:t0 + w], in_=TMP[:, k, 0:w], func=AF.Identity,
                    bias=MOD[:, which_shift, k, sel:sel + 1], scale=1.0))
            S.bar()

    def resid_linear(src_dram, kch, gate_idx, Abufs):
        Wv = WB[:, 0:kch * D].rearrange("p (k n) -> p k n", n=D)
        Xb = [XT, TMP, STG[:].rearrange("p (k t) -> p k t", t=512)]

        def loads(i):
            t0, w = TILES[i]
            S.dma("sync", Abufs[i % 2][:, 0:kch, 0:w], src_dram[:, t0:t0 + w].rearrange("(k p) t -> p k t", p=128))
            S.dma("gpsimd", Xb[i % 3][:, :, 0:w], XRES[:, t0:t0 + w].rearrange("(k p) t -> p k t", p=128))

        def store(i):
            t0, w = TILES[i]
            S.dma("gpsimd", XRES[:, t0:t0 + w].rearrange("(k p) t -> p k t", p=128), Xb[i % 3][:, :, 0:w])

        loads(0)
        S.bar()
        nt = len(TILES)
        for i, (t0, w) in enumerate(TILES):
            sel = 1 if t0 == 0 else 0
            Ab = Abufs[i % 2]
            for m in range(8):
                for k in range(kch):
                    S.op("tensor", lambda e, m=m, k=k, w=w, Ab=Ab: e.matmul(
                        ps[m][:, 0:w], lhsT=Wv[:, k, m * 128:(m + 1) * 128], rhs=Ab[:, k, 0:w],
                        start=(k == 0), stop=(k == kch - 1)))
            if i + 1 < nt:
                loads(i + 1)
            if i >= 1:
                store(i - 1)
            S.bar()
            Xi = Xb[i % 3]
            for m in range(8):
                S.op("vector", lambda e, m=m, w=w, sel=sel, Xi=Xi: e.scalar_tensor_tensor(
                    out=Xi[:, m, 0:w], in0=ps[m][:, 0:w], scalar=MOD[:, gate_idx, m, sel:sel + 1], in1=Xi[:, m, 0:w],
                    op0=ALU.mult, op1=ALU.add))
            S.bar()
        store(nt - 1)
        S.bar()

    try:
      for li in range(n_layers):
        S.dma("sync", STG[0:48, 0:128], b_ada[li].rearrange("(k p) -> k p", p=128))
        S.dma("sync", STG[48:56, 0:128], g_mix[li].rearrange("(k p) -> k p", p=128))
        S.dma("sync", STG[56:64, 0:128], g_ffn[li].rearrange("(k p) -> k p", p=128))
        S.dma("sync", STG[64:68, 0:128], b_glu[li].rearrange("(k p) -> k p", p=128))
        S.dma("sync", STG[68:72, 0:128], g_sgu[li].rearrange("(k p) -> k p", p=128))
        S.bar()
        S.op("tensor", lambda e: e.transpose(ps[7][:, 0:72], STG[0:72, 0:128], IDENT[0:72, 0:72]))
        S.bar()
        S.op("vector", lambda e: e.tensor_copy(out=BADA[:].rearrange("p q k -> p (q k)"), in_=ps[7][:, 0:48]))
        S.op("vector", lambda e: e.tensor_copy(out=GM[:], in_=ps[7][:, 48:56]))
        S.op("vector", lambda e: e.tensor_copy(out=GF[:], in_=ps[7][:, 56:64]))
        S.op("vector", lambda e: e.tensor_copy(out=BGLU[:], in_=ps[7][:, 64:68]))
        S.op("vector", lambda e: e.tensor_copy(out=GSGU[:], in_=ps[7][:, 68:72]))
        S.bar()
        WAs = [STG[:, 0:2048].rearrange("p (k n) -> p k n", n=256), STG[:, 2048:4096].rearrange("p (k n) -> p k n", n=256)]

        def wada_dma(blk):
            WA = WAs[blk % 2]
            src = w_ada[li, :, blk * 256:(blk + 1) * 256].rearrange("(k p) n -> p k n", p=128)
            S.dma("sync", WA[:, 0:4, :], src[:, 0:4, :])
            S.dma("gpsimd", WA[:, 4:8, :], src[:, 4:8, :])

        wada_dma(0)
        S.bar()
        for blk in range(24):
            q, mq = blk // 4, blk % 4
            WA = WAs[blk % 2]
            if blk + 1 < 24:
                wada_dma(blk + 1)
            for mm in range(2):
                m = mq * 2 + mm
                for k in range(8):
                    S.op("tensor", lambda e, m=m, mm=mm, k=k, q=q, WA=WA: e.matmul(
                        ps[0][:, (q * 8 + m) * 2:(q * 8 + m) * 2 + 2], lhsT=WA[:, k, mm * 128:(mm + 1) * 128],
                        rhs=SC[:, k, :], start=(k == 0), stop=(k == 7)))
            S.bar()
        S.op("vector", lambda e: e.tensor_tensor(
            out=MOD[:].rearrange("p q k n -> p (q k) n"), in0=ps[0][:, 0:96].rearrange("p (a n) -> p a n", n=2),
            in1=BADA[:].rearrange("p q k -> p (q k)").unsqueeze(2).to_broadcast([128, 48, 2]), op=ALU.add))
        S.bar()
        S.op("vector", lambda e: e.scalar_tensor_tensor(
            out=A1[:], in0=MOD[:, 1, :, :], scalar=1.0, in1=GM[:].unsqueeze(2).to_broadcast([128, 8, 2]),
            op0=ALU.add, op1=ALU.mult))
        S.op("vector", lambda e: e.scalar_tensor_tensor(
            out=A2[:], in0=MOD[:, 4, :, :], scalar=1.0, in1=GF[:].unsqueeze(2).to_broadcast([128, 8, 2]),
            op0=ALU.add, op1=ALU.mult))
        S.bar()

        ckp("ada")
        norm_mod(A1, 0)

        ckp("norm1")
        WIN = WB[:, 0:8 * 1536].rearrange("p (k n) -> p k n", n=1536)
        load_weight(WIN, w_in[li], 8, 1536, 512)
        for (t0, w) in TILES:
            j0, nj = t0 // 8, w // 8
            for grp in range(2):
                ms = list(range(8)) if grp == 0 else list(range(8, 12))
                for bi, m in enumerate(ms):
                    for k in range(8):
                        S.op("tensor", lambda e, bi=bi, m=m, k=k, w=w, t0=t0: e.matmul(
                            ps[bi][:, 0:w], lhsT=WIN[:, k, m * 128:(m + 1) * 128], rhs=H[:, k, t0:t0 + w],
                            start=(k == 0), stop=(k == 7)))
                S.bar()
                for bi, m in enumerate(ms):
                    if m < 4:
                        S.op("vector", lambda e, bi=bi, m=m, w=w, j0=j0, nj=nj: e.tensor_copy(
                            out=USP[:, m, :, j0:j0 + nj].rearrange("p s j -> p j s"),
                            in_=ps[bi][:, 0:w].rearrange("p (j s) -> p j s", s=8)))
                    elif m < 8:
                        S.op("scalar", lambda e, bi=bi, m=m, w=w: e.activation(
                            out=OB[:, m - 4, 0:w], in_=ps[bi][:, 0:w], func=AF.Gelu_apprx_tanh))
                    else:
                        S.op("scalar", lambda e, bi=bi, m=m, w=w: e.activation(
                            out=TMP[:, m - 8, 0:w], in_=ps[bi][:, 0:w], func=AF.Gelu_apprx_tanh))
                S.bar()
            S.dma("sync", UG[:, t0:t0 + w].rearrange("(k p) t -> p k t", p=128), OB[:, 0:4, 0:w])
            S.dma("gpsimd", VG[:, t0:t0 + w].rearrange("(k p) t -> p k t", p=128), TMP[:, 0:4, 0:w])
            S.bar()

        ckp("win")
        S.dma("sync", USSMP.rearrange("(q p) s j -> p q s j", p=128), USP)
        S.dma("gpsimd", STG[0:32, 0:16], ssm_d[li].rearrange("(g c) -> g c", c=16))
        S.bar()
        for s_ in range(8):
            S.dma("sync" if s_ % 2 == 0 else "gpsimd", IM[s_ * 16:(s_ + 1) * 16, :, :],
                  USSMP[:, s_, :].rearrange("(g c) j -> c g j", c=16))
        S.bar()

        S.op("vector", lambda e: e.tensor_copy(out=STG[0:32, 128:256].rearrange("p (s c) -> p s c", c=16),
                                               in_=STG[0:32, 0:16].unsqueeze(1).to_broadcast([32, 8, 16])))
        S.bar()
        S.op("tensor", lambda e: e.transpose(ps[7][:, 0:32], STG[0:32, 128:256], IDENT[0:32, 0:32]))
        S.bar()
        S.op("vector", lambda e: e.tensor_copy(out=DV[:], in_=ps[7][:, 0:32]))
        ckp("im2col")
        for dr in range(2):
            CIN1 = STG[:, 0:512].rearrange("p (q n) -> p q n", n=128)
            CIN2 = STG[:, 512:1024].rearrange("p (q n) -> p q n", n=128)
            for hf in range(2):
                lo = slice(hf * 64, hf * 64 + 64)
                csrc = [c_re, c_im] if hf == 0 else [c_im, c_re]
                S.dma("sync", CIN1[:, :, lo], csrc[0][li, dr].rearrange("(q g) c p -> (g c) q p", q=4))
                S.dma("gpsimd", CIN2[:, :, lo], csrc[1][li, dr].rearrange("(q g) c p -> (g c) q p", q=4))
            S.bar()
            for q in range(4):
                S.op("tensor", lambda e, q=q: e.transpose(ps[0][:, q * 128:(q + 1) * 128], CIN1[:, q, :], IDENT[:]))
                S.op("tensor", lambda e, q=q: e.transpose(ps[1][:, q * 128:(q + 1) * 128], CIN2[:, q, :], IDENT[:]))
            S.bar()
            S.op("vector", lambda e: e.tensor_copy(out=CX1.rearrange("p g c -> p (g c)"), in_=ps[0][:]))
            S.op("scalar", lambda e: e.copy(out=CX2.rearrange("p g c -> p (g c)"), in_=ps[1][:]))
            S.bar()
            BIN1 = STG[0:32, 0:2048].rearrange("p (h q c) -> p h q c", h=2, c=16)
            BIN2 = STG[0:32, 2048:4096].rearrange("p (h q c) -> p h q c", h=2, c=16)
            S.dma("sync", BIN1[:, 0, :, :], b_re[li, dr])
            S.dma("gpsimd", BIN1[:, 1, :, :], b_im[li, dr])
            S.dma("sync", BIN2[:, 0, :, :], b_im[li, dr])
            S.dma("gpsimd", BIN2[:, 1, :, :], b_re[li, dr])
            AIN = RS[0:32, 0:256].rearrange("p (a q) -> p a q", q=64)
            S.dma("sync", AIN[:, 0, :], a_re[li, dr])
            S.dma("gpsimd", AIN[:, 1, :], a_re[li, dr])
            S.dma("sync", AIN[:, 2, :], a_im[li, dr])
            S.dma("gpsimd", AIN[:, 3, :], a_im[li, dr])
            S.bar()
            for c_ in range(16):
                S.op("tensor", lambda e, c_=c_: e.transpose(ps[2][:, c_ * 32:(c_ + 1) * 32], BIN1[:, :, :, c_], IDENT[0:32, 0:32]))
                S.op("tensor", lambda e, c_=c_: e.transpose(ps[3][:, c_ * 32:(c_ + 1) * 32], BIN2[:, :, :, c_], IDENT[0:32, 0:32]))
            S.op("tensor", lambda e: e.transpose(ps[4][:, 0:32], RS[0:32, 0:128], IDENT[0:32, 0:32]))
            S.op("tensor", lambda e: e.transpose(ps[4][:, 32:64], RS[0:32, 128:256], IDENT[0:32, 0:32]))
            S.bar()
            S.op("vector", lambda e: e.tensor_copy(out=BX1.rearrange("p g c -> p c g"), in_=ps[2][:].rearrange("p (c g) -> p c g", g=32)))
            S.op("scalar", lambda e: e.copy(out=BX2.rearrange("p g c -> p c g"), in_=ps[3][:].rearrange("p (c g) -> p c g", g=32)))
            S.op("vector", lambda e: e.tensor_copy(out=ARp[:], in_=ps[4][:, 0:32]))
            S.op("vector", lambda e: e.tensor_copy(out=AIp[:], in_=ps[4][:, 32:64]))
            S.dma("sync", LDT[:], log_dt[li, dr].partition_broadcast(128))
            S.bar()
            ckp("pl%d" % dr)
            S.op("scalar", lambda e: e.activation(out=LDT[:], in_=LDT[:], func=AF.Exp))
            S.bar()
            S.op("vector", lambda e: e.tensor_tensor(out=LR[:], in0=ARp[:], in1=LDT[:], op=ALU.mult))
            S.op("gpsimd", lambda e: e.tensor_tensor(out=LI[:], in0=AIp[:], in1=LDT[:], op=ALU.mult))
            S.bar()
            ckp("pb%d" % dr)
            ks = list(range(-8, 0)) + list(range(1, 9))
            for idx, kk in enumerate(ks):
                S.op("vector", lambda e, idx=idx, kk=kk: e.tensor_scalar(
                    out=ARG[:, :, idx], in0=LI[:], scalar1=float(kk), scalar2=None, op0=ALU.mult))
                S.op("gpsimd", lambda e, idx=idx, kk=kk: e.tensor_scalar(
                    out=EARG[:, :, idx], in0=LR[:], scalar1=float(kk), scalar2=None, op0=ALU.mult))
            S.bar()
            ckp("pc%d" % dr)
            S.op("vector", lambda e: e.tensor_scalar(out=ARGC, in0=ARG, scalar1=TWO_PI / 4, scalar2=None, op0=ALU.add))
            S.op("scalar", lambda e: e.activation(out=MAG, in_=EARG, func=AF.Exp))
            S.bar()
            ckp("pd%d" % dr)
            S.op("vector", lambda e: e.tensor_scalar(out=NI, in0=ARG, scalar1=1.0 / TWO_PI, scalar2=None, op0=ALU.mult))
            S.op("gpsimd", lambda e: e.tensor_scalar(out=NIC, in0=ARGC, scalar1=1.0 / TWO_PI, scalar2=None, op0=ALU.mult))
            S.bar()
            ckp("pe%d" % dr)
            S.op("vector", lambda e: e.tensor_copy(out=NF, in_=NI))
            S.op("gpsimd", lambda e: e.tensor_copy(out=NFC, in_=NIC))
            S.bar()
            ckp("pf%d" % dr)
            S.op("vector", lambda e: e.scalar_tensor_tensor(out=ARG, in0=NF, scalar=-TWO_PI, in1=ARG, op0=ALU.mult, op1=ALU.add))
            S.op("vector", lambda e: e.scalar_tensor_tensor(out=ARGC, in0=NFC, scalar=-TWO_PI, in1=ARGC, op0=ALU.mult, op1=ALU.add))
            S.bar()
            S.op("vector", lambda e: e.tensor_scalar(out=ARG, in0=ARG, scalar1=3.1415925, scalar2=-3.1415925, op0=ALU.min, op1=ALU.max))
            S.op("gpsimd", lambda e: e.tensor_scalar(out=ARGC, in0=ARGC, scalar1=3.1415925, scalar2=-3.1415925, op0=ALU.min, op1=ALU.max))
            S.bar()
            ckp("pg%d" % dr)
            S.op("scalar", lambda e: e.activation(out=PIM, in_=ARG, func=AF.Sin))
            S.op("scalar", lambda e: e.activation(out=PRE, in_=ARGC, func=AF.Sin))
            S.bar()
            S.op("vector", lambda e: e.tensor_tensor(out=PIM, in0=PIM, in1=MAG, op=ALU.mult))
            S.op("gpsimd", lambda e: e.tensor_tensor(out=PRE, in0=PRE, in1=MAG, op=ALU.mult))
            S.bar()
            ckp("ph%d" % dr)
            S.op("vector", lambda e: e.tensor_scalar(out=NR[:], in0=PRE[:, :, 8], scalar1=-1.0, scalar2=None, op0=ALU.add))
            S.op("gpsimd", lambda e: e.tensor_tensor(out=DEN[:], in0=ARp[:], in1=ARp[:], op=ALU.mult))
            ckp("c0")
            S.op("vector", lambda e: e.tensor_tensor(out=TA[:], in0=AIp[:], in1=AIp[:], op=ALU.mult))
            ckp("c1")
            S.op("vector", lambda e: e.tensor_tensor(out=DEN[:], in0=DEN[:], in1=TA[:], op=ALU.add))
            ckp("c2")
            S.op("vector", lambda e: e.reciprocal(out=DEN[:], in_=DEN[:]))
            ckp("c3")
            S.op("vector", lambda e: e.tensor_tensor(out=TA[:], in0=NR[:], in1=ARp[:], op=ALU.mult))
            S.op("gpsimd", lambda e: e.tensor_tensor(out=TB[:], in0=PIM[:, :, 8], in1=AIp[:], op=ALU.mult))
            ckp("c4")
            S.op("vector", lambda e: e.tensor_tensor(out=CR[:], in0=TA[:], in1=TB[:], op=ALU.add))
            ckp("c5")
            S.op("vector", lambda e: e.tensor_tensor(out=TA[:], in0=PIM[:, :, 8], in1=ARp[:], op=ALU.mult))
            S.op("gpsimd", lambda e: e.tensor_tensor(out=TB[:], in0=NR[:], in1=AIp[:], op=ALU.mult))
            ckp("c6")
            S.op("vector", lambda e: e.tensor_tensor(out=CI[:], in0=TA[:], in1=TB[:], op=ALU.subtract))
            ckp("c7")
            S.op("vector", lambda e: e.tensor_tensor(out=CR[:], in0=CR[:], in1=DEN[:], op=ALU.mult))
            S.op("gpsimd", lambda e: e.tensor_tensor(out=CI[:], in0=CI[:], in1=DEN[:], op=ALU.mult))
            ckp("c8")
            ckp("pi%d" % dr)
            CRb = CR[:].unsqueeze(2).to_broadcast([128, 32, 8])
            CIb = CI[:].unsqueeze(2).to_broadcast([128, 32, 8])
            S.op("vector", lambda e: e.tensor_tensor(out=QR, in0=PRE[:, :, 0:8], in1=CRb, op=ALU.mult))
            S.op("gpsimd", lambda e: e.tensor_tensor(out=QT, in0=PIM[:, :, 0:8], in1=CIb, op=ALU.mult))
            S.bar()
            S.op("vector", lambda e: e.tensor_tensor(out=QR, in0=QR, in1=QT, op=ALU.subtract))
            S.bar()
            S.op("vector", lambda e: e.tensor_tensor(out=QI, in0=PRE[:, :, 0:8], in1=CIb, op=ALU.mult))
            S.op("gpsimd", lambda e: e.tensor_tensor(out=QT, in0=PIM[:, :, 0:8], in1=CRb, op=ALU.mult))
            S.bar()
            S.op("vector", lambda e: e.tensor_tensor(out=QI, in0=QI, in1=QT, op=ALU.add))
            S.bar()
            S.op("vector", lambda e: e.tensor_scalar(out=QI, in0=QI, scalar1=SIGN[:, 0:1], scalar2=None, op0=ALU.mult))
            S.op("gpsimd", lambda e: e.tensor_scalar(out=PA, in0=PRE[:, :, 8:16], scalar1=NSIGN[:, 0:1], scalar2=None, op0=ALU.mult))
            S.op("scalar", lambda e: e.mul(out=PB, in_=PIM[:, :, 8:16], mul=-1.0))
            S.bar()
            S.op("vector", lambda e: e.tensor_copy(out=ARC[:, 0, :], in_=PRE[:, :, 15]))
            S.op("vector", lambda e: e.tensor_copy(out=ARC[:, 1, :], in_=PRE[:, :, 15]))
            S.op("gpsimd", lambda e: e.tensor_scalar(out=AIC[:, 0, :], in0=PIM[:, :, 15], scalar1=SIGN[:, 0:1], scalar2=None, op0=ALU.mult))
            S.op("gpsimd", lambda e: e.tensor_scalar(out=AIC[:, 1, :], in0=PIM[:, :, 15], scalar1=NSIGN[:, 0:1], scalar2=None, op0=ALU.mult))
            S.bar()
            ckp("prep%d" % dr + "")
            for s_ in range(8):
                qi = (7 - s_) if dr == 0 else s_
                ri = s_ if dr == 0 else (7 - s_)
                S.op("vector", lambda e, s_=s_, qi=qi: e.tensor_tensor(
                    out=LT[:, :, s_, :], in0=BX1, in1=QR[:, :, qi:qi + 1].to_broadcast([128, 32, 16]), op=ALU.mult))
                S.op("gpsimd", lambda e, s_=s_, ri=ri: e.tensor_tensor(
                    out=RT[:, :, s_, :], in0=CX1, in1=PA[:, :, ri:ri + 1].to_broadcast([128, 32, 16]), op=ALU.mult))
            S.bar()
            TL = STG[:].rearrange("p (g s c) -> p g s c", s=8, c=16)
            for s_ in range(8):
                qi = (7 - s_) if dr == 0 else s_
                S.op("vector" if s_ % 2 == 0 else "gpsimd", lambda e, s_=s_, qi=qi: e.tensor_tensor(
                    out=TL[:, :, s_, :], in0=BX2, in1=QI[:, :, qi:qi + 1].to_broadcast([128, 32, 16]), op=ALU.mult))
            S.bar()
            S.op("vector", lambda e: e.tensor_tensor(out=LT, in0=LT, in1=TL, op=ALU.add))
            S.bar()
            for s_ in range(8):
                ri = s_ if dr == 0 else (7 - s_)
                S.op("vector" if s_ % 2 == 0 else "gpsimd", lambda e, s_=s_, ri=ri: e.tensor_tensor(
                    out=TL[:, :, s_, :], in0=CX2, in1=PB[:, :, ri:ri + 1].to_broadcast([128, 32, 16]), op=ALU.mult))
            S.bar()
            S.op("vector", lambda e: e.tensor_tensor(out=RT, in0=RT, in1=TL, op=ALU.add))
            S.bar()
            ckp("lr%d" % dr + "")
            for rnd in range(4):
                for gi in range(8):
                    g = rnd * 8 + gi
                    Lg = LT[:, g, :, :].rearrange("p s c -> p (s c)")
                    Rg = RT[:, g, :, :].rearrange("p s c -> p (s c)")
                    S.op("tensor", lambda e, gi=gi, Lg=Lg, Rg=Rg: e.matmul(
                        ps[gi // 4][:, (gi % 4) * 128:(gi % 4 + 1) * 128], lhsT=Lg, rhs=Rg, start=True, stop=True))
                    S.op("tensor", lambda e, gi=gi, Lg=Lg: e.transpose(
                        ps[2 + gi // 4][:, (gi % 4) * 128:(gi % 4 + 1) * 128], Lg, IDENT[:]))
                S.bar()
                for bk in range(2):
                    g0 = rnd * 8 + bk * 4
                    S.op("vector", lambda e, bk=bk, g0=g0, dr=dr: e.tensor_tensor(
                        out=TOEP[:, g0:g0 + 4, :], in0=ps[bk][:].rearrange("p (g n) -> p g n", n=128),
                        in1=MASKS[:, dr:dr + 1, :].to_broadcast([128, 4, 128]), op=ALU.mult))
                    S.op("scalar", lambda e, bk=bk, g0=g0: e.copy(
                        out=ET[:, g0:g0 + 4, :], in_=ps[2 + bk][:].rearrange("p (g n) -> p g n", n=128)))
                S.bar()
                for bk in range(2):
                    g0 = rnd * 8 + bk * 4
                    S.op("vector", lambda e, bk=bk, g0=g0: e.tensor_copy(
                        out=ESW[:, g0:g0 + 4, 0:64], in_=ps[2 + bk][:].rearrange("p (g n) -> p g n", n=128)[:, :, 64:128]))
                    S.op("vector", lambda e, bk=bk, g0=g0: e.tensor_copy(
                        out=ESW[:, g0:g0 + 4, 64:128], in_=ps[2 + bk][:].rearrange("p (g n) -> p g n", n=128)[:, :, 0:64]))
                S.bar()
            S.op("scalar", lambda e: e.copy(out=RB.rearrange("p g n -> p (g n)"), in_=RT.rearrange("p g s c -> p (g s c)")))
            if dr == 0:
                for g in range(32):
                    S.op("vector", lambda e, g=g: e.scalar_tensor_tensor(
                        out=TOEP[:, g, :], in0=IDENT[:], scalar=DV[:, g:g + 1], in1=TOEP[:, g, :],
                        op0=ALU.mult, op1=ALU.add))
            S.op("gpsimd", lambda e: e.memset(W3[:], 0.0))
            S.bar()
            ckp("tiles%d" % dr + "")
            blocks = [(0, 32)] + [(32 + 64 * b, 64) for b in range(4)]
            order = blocks if dr == 0 else [blocks[0]] + blocks[:0:-1]
            for (jb, nb) in order:
                for half in range(2):
                    for gi in range(16):
                        g = half * 16 + gi
                        for arr in range(2):
                            ii = gi * 2 + arr
                            Em = ET if arr == 0 else ESW
                            S.op("tensor", lambda e, ii=ii, g=g, Em=Em, jb=jb, nb=nb: e.matmul(
                                ps[ii // 8][:, (ii % 8) * 64:(ii % 8) * 64 + nb], lhsT=Em[:, g, :],
                                rhs=IM[:, g, jb:jb + nb], start=True, stop=True))
                    S.bar()
                    for bk in range(4):
                        g0 = half * 16 + bk * 4
                        S.op("vector" if bk % 2 == 0 else "scalar", (lambda e, bk=bk, g0=g0, nb=nb: e.tensor_copy(
                            out=SS[:, 0:nb, :, g0:g0 + 4].rearrange("p j a g -> p g a j"),
                            in_=ps[bk][:].rearrange("p (g a j) -> p g a j", a=2, j=64)[:, :, :, 0:nb]))
                            if bk % 2 == 0 else (lambda e, bk=bk, g0=g0, nb=nb: e.copy(
                            out=SS[:, 0:nb, :, g0:g0 + 4].rearrange("p j a g -> p g a j"),
                            in_=ps[bk][:].rearrange("p (g a j) -> p g a j", a=2, j=64)[:, :, :, 0:nb])))
                    S.bar()
                js = list(range(jb, jb + nb)) if dr == 0 else list(range(jb + nb - 1, jb - 1, -1))
                pjl = None
                for j in js:
                    jl = j - jb
                    Wc = W3[:] if pjl is None else SS[:, pjl, :, :]
                    S.op("vector", lambda e, jl=jl, Wc=Wc: e.tensor_tensor(out=G3[:, 0:2, :], in0=Wc, in1=SS[:, jl, :, :], op=ALU.add))
                    S.op("vector", lambda e: e.tensor_tensor(out=T1[:], in0=G3[:, 0:2, :], in1=ARC[:], op=ALU.mult))
                    S.op("vector", lambda e: e.tensor_tensor(out=T2[:, 0, :], in0=G3[:, 1, :], in1=AIC[:, 0, :], op=ALU.mult))
                    S.op("vector", lambda e: e.tensor_tensor(out=T2[:, 1, :], in0=G3[:, 0, :], in1=AIC[:, 1, :], op=ALU.mult))
                    S.op("vector", lambda e, jl=jl: e.tensor_tensor(out=SS[:, jl, :, :], in0=T1[:], in1=T2[:], op=ALU.add))
                    pjl = jl
                S.bar()
                if dr == 0:
                    S.op("scalar", lambda e, jb=jb: e.copy(out=HH[:, jb, :], in_=W3[:, 0, :]))
                    S.op("scalar", lambda e, jb=jb, nb=nb: e.copy(out=HH[:, jb + 1:jb + nb, :], in_=SS[:, 0:nb - 1, 0, :]))
                else:
                    S.op("scalar", lambda e, jb=jb, nb=nb: e.copy(out=HH[:, jb + nb - 1, :], in_=W3[:, 0, :]))
                    S.op("scalar", lambda e, jb=jb, nb=nb: e.copy(out=HH[:, jb:jb + nb - 1, :], in_=SS[:, 1:nb, 0, :]))
                S.bar()
                S.op("vector", lambda e, pjl=pjl: e.tensor_copy(out=W3[:], in_=SS[:, pjl, :, :]))
                S.bar()
            ckp("rec%d" % dr + "")
            for rnd in range(4):
                for gi in range(8):
                    g = rnd * 8 + gi
                    S.op("tensor", lambda e, gi=gi, g=g: e.matmul(
                        ps[gi][:, 0:NJ], lhsT=TOEP[:, g, :], rhs=IM[:, g, :], start=True, stop=False))
                    S.op("tensor", lambda e, gi=gi, g=g: e.matmul(
                        ps[gi][:, 0:NJ], lhsT=RB[:, g, :], rhs=HH[:, :, g], start=False, stop=True))
                S.bar()
                for gi in range(8):
                    g = rnd * 8 + gi
                    if dr == 0:
                        S.op("vector" if gi % 2 == 0 else "scalar", (lambda e, gi=gi, g=g: e.tensor_copy(out=YS[:, g, :], in_=ps[gi][:, 0:NJ]))
                             if gi % 2 == 0 else (lambda e, gi=gi, g=g: e.copy(out=YS[:, g, :], in_=ps[gi][:, 0:NJ])))
                    else:
                        S.op("vector", lambda e, gi=gi, g=g: e.tensor_tensor(out=YS[:, g, :], in0=YS[:, g, :], in1=ps[gi][:, 0:NJ], op=ALU.add))
                S.bar()
        ckp("read")
        S.dma("sync", YSP[:, :, :], YS)
        S.bar()
        YV = YS.rearrange("p (q s) j -> p q s j", s=8)
        YSPv = YSP.rearrange("(s c) (q g) j -> g c q s j", c=16, g=8)
        for g8 in range(8):
            for q in range(4):
                S.dma(["sync", "gpsimd"][q % 2], YV[g8 * 16:(g8 + 1) * 16, q, :, :], YSPv[g8, :, q, :, :])
            S.bar()
        S.bar()

        ckp("unim")
        WG = WB[:, 0:4 * 512].rearrange("p (k n) -> p k n", n=512)
        load_weight(WG, w_glu[li], 4, 512, 512)
        for (t0, w) in TILES:
            j0, nj = t0 // 8, w // 8
            for q in range(4):
                S.op("scalar", lambda e, q=q, w=w, j0=j0, nj=nj: e.activation(
                    out=TMP[:, q, 0:w].rearrange("p (j s) -> p j s", s=8),
                    in_=YV[:, q, :, j0:j0 + nj].rearrange("p s j -> p j s"), func=AF.Gelu_apprx_tanh))
            S.bar()
            S.op("vector", lambda e, w=w: e.tensor_copy(out=OB[:, 0:4, 0:w], in_=TMP[:, 0:4, 0:w]))
            S.bar()
            for m in range(4):
                for k in range(4):
                    S.op("tensor", lambda e, m=m, k=k, w=w: e.matmul(
                        ps[m][:, 0:w], lhsT=WG[:, k, m * 128:(m + 1) * 128], rhs=OB[:, k, 0:w],
                        start=(k == 0), stop=(k == 3)))
            S.bar()
            for m in range(4):
                S.op("scalar", lambda e, m=m, w=w: e.activation(
                    out=XT[:, m, 0:w], in_=ps[m][:, 0:w], func=AF.Sigmoid, bias=BGLU[:, m:m + 1], scale=1.0))
            S.bar()
            S.op("vector", lambda e, w=w: e.tensor_tensor(out=OB[:, 4:8, 0:w], in0=TMP[:, 0:4, 0:w], in1=XT[:, 0:4, 0:w], op=ALU.mult))
            S.bar()
            S.dma("sync", MIX[0:512, t0:t0 + w].rearrange("(k p) t -> p k t", p=128), OB[:, 4:8, 0:w])
            S.bar()

        ckp("glu")
        S.dma("sync", TMP[:, 0:4, 0:128], w_sp[li].rearrange("h p q -> p h q"))
        S.dma("gpsimd", BS.rearrange("p h q -> p (h q)"), b_sp[li].rearrange("h q -> (h q)").partition_broadcast(128))
        S.bar()
        for h in range(4):
            S.op("tensor", lambda e, h=h: e.transpose(ps[0][:, h * 128:(h + 1) * 128], TMP[:, h, 0:128], IDENT[:]))
        S.bar()
        S.op("vector", lambda e: e.tensor_copy(out=WST[:], in_=ps[0][:].rearrange("p (h n) -> p h n", n=128)))
        S.bar()
        for (t0, w) in TILES:
            nchk = w // 128
            S.dma("sync", XT[:, 0:4, 0:w], VG[:, t0:t0 + w].rearrange("(k p) t -> p k t", p=128))
            S.dma("gpsimd", OB[:, 0:4, 0:w], UG[:, t0:t0 + w].rearrange("(k p) t -> p k t", p=128))
            S.bar()
            S.op("scalar", lambda e, w=w: e.activation(out=SQ[:, 0:4, 0:w], in_=XT[:, 0:4, 0:w], func=AF.Square))
            S.bar()
            rstd_from(SQ, 4, w, 1.0 / 512)
            for k in range(4):
                S.op("vector", lambda e, k=k, w=w: e.scalar_tensor_tensor(
                    out=TMP[:, k, 0:w], in0=XT[:, k, 0:w], scalar=GSGU[:, k:k + 1], in1=RS[:, 0:w],
                    op0=ALU.mult, op1=ALU.mult))
            S.bar()
            for ck in range(nchk):
                for h in range(4):
                    ii = ck * 4 + h
                    S.op("tensor", lambda e, ii=ii, ck=ck, h=h: e.transpose(
                        ps[ii // 4][:, (ii % 4) * 128:(ii % 4 + 1) * 128], TMP[:, h, ck * 128:(ck + 1) * 128], IDENT[:]))
            S.bar()
            for ck in range(nchk):
                S.op("vector" if ck % 2 == 0 else "scalar", (lambda e, ck=ck: e.tensor_copy(
                    out=VT[:, ck * 4:(ck + 1) * 4, :], in_=ps[ck][:].rearrange("p (h n) -> p h n", n=128)))
                    if ck % 2 == 0 else (lambda e, ck=ck: e.copy(
                    out=VT[:, ck * 4:(ck + 1) * 4, :], in_=ps[ck][:].rearrange("p (h n) -> p h n", n=128))))
            S.bar()
            for ck in range(nchk):
                for h in range(4):
                    ii = ck * 4 + h
                    S.op("tensor", lambda e, ii=ii, ck=ck, h=h: e.matmul(
                        ps[4 + ck][:, h * 128:(h + 1) * 128], lhsT=VT[:, ii, :], rhs=WST[:, h, :], start=True, stop=True))
            S.bar()
            for ck in range(nchk):
                S.op("vector", lambda e, ck=ck: e.tensor_tensor(
                    out=TMP[:, 4:8, ck * 128:(ck + 1) * 128], in0=ps[4 + ck][:].rearrange("p (h n) -> p h n", n=128),
                    in1=BS, op=ALU.add))
            S.bar()
            S.op("vector", lambda e, w=w: e.tensor_tensor(out=OB[:, 4:8, 0:w], in0=TMP[:, 4:8, 0:w], in1=OB[:, 0:4, 0:w], op=ALU.mult))
            S.bar()
            S.dma("sync", MIX[512:1024, t0:t0 + w].rearrange("(k p) t -> p k t", p=128), OB[:, 4:8, 0:w])
            S.bar()

        ckp("sgu")
        load_weight(WB[:, 0:8 * D].rearrange("p (k n) -> p k n", n=D), w_out[li], 8, D, 512)
        resid_linear(MIX, 8, 2, [ACTB[:, 0:8, :], ACTB[:, 8:16, :]])

        ckp("wout")
        norm_mod(A2, 3)

        ckp("norm2")
        S.dma("sync", FB[0:9, 0:2 * DFF], w_conv[li].rearrange("a b n -> (a b) n"))
        S.bar()
        for ch in range(44):
            S.op("tensor", lambda e, ch=ch: e.transpose(ps[7][:, ch * 9:(ch + 1) * 9], FB[0:9, ch * 128:(ch + 1) * 128], IDENT[0:9, 0:9]))
        S.bar()
        S.op("vector", lambda e: e.tensor_copy(out=WC[:].rearrange("p t c -> p c t"), in_=ps[7][:, 0:396].rearrange("p (c t) -> p c t", t=9)))
        S.bar()
        WU = WB[:, 0:8 * 256].rearrange("p (k n) -> p k n", n=256)
        FBb = FB[:].bitcast(BF16)
        UPb = [FBb[:, 0:2304], FBb[:, 2304:4608]]
        DG = FBb[:, 4608:6912].rearrange("p (t n) -> p t n", n=128)
        SG = FB[:, 4608:6912]
        GB = ACTB[:, 0:5, :].rearrange("p a t -> p (a t)")[:, 0:NT]
        stgs = [STG[:, 0:2048].rearrange("p (k n) -> p k n", n=256), STG[:, 2048:4096].rearrange("p (k n) -> p k n", n=256)]

        def wup_dma(m):
            st = stgs[m % 2]
            S.dma("sync", st[:, :, 0:128], w_up[li, :, m * 128:(m + 1) * 128].rearrange("(k p) n -> p k n", p=128))
            S.dma("gpsimd", st[:, :, 128:256], w_up[li, :, DFF + m * 128:DFF + (m + 1) * 128].rearrange("(k p) n -> p k n", p=128))

        def conv_mm(part):
            src = UPb[part]
            d0 = part * 9
            S.op("tensor", lambda e: e.matmul(ps[0][:, 0:256], lhsT=DG[:, d0 + 4, :], rhs=src[:, 0:256], start=True, stop=False))
            S.op("tensor", lambda e: e.matmul(ps[0][:, 1:256], lhsT=DG[:, d0 + 3, :], rhs=src[:, 0:255], start=False, stop=False))
            S.op("tensor", lambda e: e.matmul(ps[0][:, 0:255], lhsT=DG[:, d0 + 5, :], rhs=src[:, 1:256], start=False, stop=True))
            sv = src[:, LC:NT].rearrange("p (r c) -> p r c", c=64)
            taps = [(1, 1)] + [(ky, kx) for ky in range(3) for kx in range(3) if not (ky == 1 and kx == 1)]
            for ti in range(1, 5):
                R0 = 8 * (ti - 1)
                pv = ps[ti][:, 0:512].rearrange("p (r c) -> p r c", c=64)
                for n_, (ky, kx) in enumerate(taps):
                    dy, dx = ky - 1, kx - 1
                    ra, rb = max(R0, -dy, 0), min(R0 + 8, 32 - max(0, dy))
                    ra = max(ra, 0 - min(0, dy))
                    c0, c1 = max(0, -dx), 64 - max(0, dx)
                    S.op("tensor", lambda e, pv=pv, ra=ra, rb=rb, c0=c0, c1=c1, dy=dy, dx=dx, R0=R0, ky=ky, kx=kx, n_=n_:
                         e.matmul(pv[:, ra - R0:rb - R0, c0:c1], lhsT=DG[:, d0 + ky * 3 + kx, :],
                                  rhs=sv[:, ra + dy:rb + dy, c0 + dx:c1 + dx], start=(n_ == 0), stop=(n_ == 8)))

        wup_dma(0)
        S.bar()
        for m in range(22):
            S.op("vector", lambda e, m=m: e.tensor_copy(out=WU, in_=stgs[m % 2]))
            for part in range(2):
                for tap in range(9):
                    ch = part * 22 + m
                    if (part * 9 + tap) % 2 == 0:
                        S.op("vector", lambda e, part=part, tap=tap, ch=ch: e.tensor_scalar(
                            out=DG[:, part * 9 + tap, :], in0=IDENT[:], scalar1=WC[:, tap, ch:ch + 1], scalar2=None, op0=ALU.mult))
                    else:
                        S.op("scalar", lambda e, part=part, tap=tap, ch=ch: e.activation(
                            out=DG[:, part * 9 + tap, :], in_=IDENT[:], func=AF.Copy, scale=WC[:, tap, ch:ch + 1]))
            S.bar()
            for part in range(2):
                for ti, (t0, w) in enumerate(TILES):
                    for k in range(8):
                        S.op("tensor", lambda e, ti=ti, k=k, t0=t0, w=w, part=part: e.matmul(
                            ps[ti][:, 0:w], lhsT=WU[:, k, part * 128:(part + 1) * 128], rhs=H[:, k, t0:t0 + w],
                            start=(k == 0), stop=(k == 7)))
                if part == 0:
                    if m + 1 < 22:
                        wup_dma(m + 1)
                    if m > 0:
                        S.dma("sync", GD[(m - 1) * 128:m * 128, :], GB)
                S.bar()
                dst = UPb[part]
                for ti, (t0, w) in enumerate(TILES):
                    S.op("vector" if ti % 2 == 0 else "scalar", (lambda e, ti=ti, t0=t0, w=w, dst=dst: e.tensor_copy(
                        out=dst[:, t0:t0 + w], in_=ps[ti][:, 0:w])) if ti % 2 == 0 else (lambda e, ti=ti, t0=t0, w=w, dst=dst: e.copy(
                        out=dst[:, t0:t0 + w], in_=ps[ti][:, 0:w])))
                S.bar()
            conv_mm(0)
            S.bar()
            for ti, (t0, w) in enumerate(TILES):
                S.op("scalar", lambda e, ti=ti, t0=t0, w=w: e.activation(out=SG[:, t0:t0 + w], in_=ps[ti][:, 0:w], func=AF.Silu))
            S.bar()
            conv_mm(1)
            S.bar()
            for ti, (t0, w) in enumerate(TILES):
                S.op("vector", lambda e, ti=ti, t0=t0, w=w: e.tensor_tensor(out=GB[:, t0:t0 + w], in0=ps[ti][:, 0:w], in1=SG[:, t0:t0 + w], op=ALU.mult))
            S.bar()
        S.dma("sync", GD[21 * 128:22 * 128, :], GB)
        S.bar()

        ckp("ffnup")
        load_weight(WB[:, 0:22 * D].rearrange("p (k n) -> p k n", n=D), w_down[li], 22, D, 128)
        resid_linear(GD, 22, 5, [ACTB, FB[:].bitcast(BF16)[:, 0:11264].rearrange("p (k t) -> p k t", t=512)])

    except _Stop:
        pass
    S.bar()
    if dbg:
        DF = nc.dram_tensor("DBGF", [128, 32768], F32, kind="ExternalOutput").ap()
        DB = nc.dram_tensor("DBGB", [128, 40960], BF16, kind="ExternalOutput").ap()
        off = 0
        for t_, n_ in [(HRAW[:], 9216), (XTt[:], 4096), (TMPt[:], 4096), (FB[:], 9216), (STG[:], 4096), (RS[:], 512),
                       (MOD[:].rearrange("p q k n -> p (q k n)"), 96), (A1[:].rearrange("p k n -> p (k n)"), 16),
                       (A2[:].rearrange("p k n -> p (k n)"), 16), (W3[:].rearrange("p a g -> p (a g)"), 64),
                       (ARC[:].rearrange("p a g -> p (a g)"), 64), (AIC[:].rearrange("p a g -> p (a g)"), 64),
                       (CR[:], 32), (CI[:], 32), (DV[:], 32), (LR[:], 32), (LI[:], 32)]:
            S.dma("sync", DF[:, off:off + n_], t_)
            off += n_
        offb = 0
        for t_, n_ in [(WB, 22528), (OBt[:], 4096), (ACTBt[:], 11264)]:
            S.dma("gpsimd", DB[:, offb:offb + n_], t_)
            offb += n_
        S.bar()
    for b in range(16):
        t0 = LC + b * 128
        S.dma("sync", XT[:, :, 0:128], XRES[:, t0:t0 + 128].rearrange("(k p) t -> p k t", p=128))
        S.bar()
        S.op("scalar", lambda e: e.activation(out=SQ[:, :, 0:128], in_=XT[:, :, 0:128], func=AF.Square))
        S.bar()
        rstd_from(SQ, 8, 128, 1.0 / D)
        for k in range(8):
            S.op("vector", lambda e, k=k: e.scalar_tensor_tensor(
                out=TMP[:, k, 0:128], in0=XT[:, k, 0:128], scalar=GFIN[:, k:k + 1], in1=RS[:, 0:128],
                op0=ALU.mult, op1=ALU.mult))
        S.bar()
        for k in range(8):
            S.op("tensor", lambda e, k=k: e.transpose(ps[k // 4][:, (k % 4) * 128:(k % 4 + 1) * 128],
                                                       TMP[:, k, 0:128], IDENT[:]))
        S.bar()
        S.op("vector", lambda e: e.tensor_copy(out=XT[:, 0:4, 0:128], in_=ps[0][:].rearrange("p (k t) -> p k t", t=128)))
        S.op("scalar", lambda e: e.copy(out=XT[:, 4:8, 0:128], in_=ps[1][:].rearrange("p (k t) -> p k t", t=128)))
        S.bar()
        S.dma("sync", out[b * 128:(b + 1) * 128, :].rearrange("t (k d) -> t k d", d=128), XT[:, :, 0:128])
        S.bar()

    S.emit()
    es.close()
    return nc


_CONST = None


def _consts():
    ident = np.eye(128, dtype=np.float32)
    sp = np.arange(128) // 16
    m0 = (sp[None, :] >= sp[:, None]).astype(np.float32)
    m1 = (sp[None, :] <= sp[:, None]).astype(np.float32)
    return ident, np.stack([m0, m1])


def kernel(n_layers=4, **inputs):
    nc = build_nc(n_layers)
    ident, mask = _consts()
    in_maps = []
    for b in range(8):
        m = {}
        for k, v in inputs.items():
            v = np.asarray(v)
            if k in ("x", "c", "ctx"):
                m[k] = np.ascontiguousarray(v[b], dtype=np.float32)
            else:
                m[k] = np.ascontiguousarray(v, dtype=np.float32)
        m["ident"] = ident
        m["mask"] = mask
        in_maps.append(m)
    res = run_bass_kernel_spmd(nc, in_maps, core_ids=list(range(8)))
    return np.stack([np.asarray(r["out"], dtype=np.float32) for r in res.results], axis=0)
```

```python
import numpy as np
from contextlib import ExitStack
import concourse.bass as bass
import concourse.mybir as mybir
from concourse.bass_utils import run_bass_kernel_spmd

F32, BF16, I32 = mybir.dt.float32, mybir.dt.bfloat16, mybir.dt.int32
AF = mybir.ActivationFunctionType
ALU = mybir.AluOpType

D = 1024
NT = 2304
LC = 256
LL = 2048
DFF = 2816
EPS = 1e-6
NJ = 288
TILES = [(0, 256)] + [(256 + 512 * i, 512) for i in range(4)]
TWO_PI = 6.283185307179586


class Sched:
    def __init__(self, nc):
        self.nc = nc
        self.stages = [[]]
        self.join_at = set()

    def op(self, eng, fn, dma=False):
        self.stages[-1].append((eng, dma, fn))

    def dma(self, eng, out, in_, slow=False):
        self.op(eng, lambda e, o=out, i=in_: e.dma_start(out=o, in_=i), dma=True)

    def dma_async(self, eng, out, in_):
        self.op(eng, lambda e, o=out, i=in_: e.dma_start(out=o, in_=i), dma="async")

    def bar(self):
        if self.stages[-1]:
            self.stages.append([])

    def join(self):
        self.bar()
        self.join_at.add(len(self.stages) - 1)

    def emit(self):
        nc = self.nc
        self.bar()
        names = ["c_scalar", "c_vector", "c_gpsimd", "c_tensor", "d_sync", "d_scalar", "d_gpsimd", "a_sync", "a_gpsimd"]

        def semname(eng, dma):
            if dma == "async":
                return "a_" + eng
            return ("d_" if dma else "c_") + eng

        cum = []
        cur = {n: 0 for n in names}
        for st in self.stages:
            cum.append(dict(cur))
            for (eng, dma, _) in st:
                cur[semname(eng, dma)] += 16 if dma else 1
        final = dict(cur)
        with ExitStack() as es:
            sems = {n: es.enter_context(nc.semaphore(n)) for n in names}
            block = es.enter_context(nc.Block())

            def make(engname):
                def body(eng):
                    waited = {n: 0 for n in names}
                    joined = {n: 0 for n in names}
                    for k, st in enumerate(self.stages):
                        if k in self.join_at:
                            for n in names:
                                if n.startswith("a_"):
                                    joined[n] = cum[k][n]
                        mine = [o for o in st if o[0] == engname]
                        if not mine:
                            continue
                        for n in names:
                            tgt = joined[n] if n.startswith("a_") else cum[k][n]
                            if tgt > waited[n]:
                                eng.wait_ge(sems[n], tgt)
                                waited[n] = tgt
                        for (_, dma, fn) in mine:
                            ins = fn(eng)
                            ins.then_inc(sems[semname(engname, dma)], 16 if dma else 1)
                    if engname == "sync":
                        for n in names:
                            if final[n] > waited[n]:
                                eng.wait_ge(sems[n], final[n])
                return body

            block.sync(make("sync"))
            block.scalar(make("scalar"))
            block.vector(make("vector"))
            block.gpsimd(make("gpsimd"))
            block.tensor(make("tensor"))


class _Stop(Exception):
    pass


def build_nc(n_layers, n_wl=4, stop=None, dbg=False):
    nc = bass.Bass("TRN2", target_bir_lowering=False)
    S = Sched(nc)
    W = n_wl

    def ckp(name):
        S.bar()
        if stop == name:
            raise _Stop()

    def din(name, shape):
        return nc.dram_tensor(name, list(shape), F32, kind="ExternalInput").ap()

    x_in = din("x", [LL, D])
    c_in = din("c", [D])
    ctx_in = din("ctx", [LC, D])
    cctx_in = din("c_ctx", [D])
    w_ada = din("w_ada", [W, D, 6 * D])
    b_ada = din("b_ada", [W, 6 * D])
    g_mix = din("g_mix", [W, D])
    w_in = din("w_in", [W, D, 1536])
    a_re = din("ssm_a_re", [W, 2, 32, 64])
    a_im = din("ssm_a_im", [W, 2, 32, 64])
    b_re = din("ssm_b_re", [W, 2, 32, 64, 16])
    b_im = din("ssm_b_im", [W, 2, 32, 64, 16])
    c_re = din("ssm_c_re", [W, 2, 32, 16, 64])
    c_im = din("ssm_c_im", [W, 2, 32, 16, 64])
    log_dt = din("ssm_log_dt", [W, 2, 32])
    ssm_d = din("ssm_d", [W, 512])
    w_glu = din("w_glu", [W, 512, 512])
    b_glu = din("b_glu", [W, 512])
    g_sgu = din("g_sgu", [W, 512])
    w_sp = din("w_spatial", [W, 4, 128, 128])
    b_sp = din("b_spatial", [W, 4, 128])
    w_out = din("w_out", [W, D, D])
    g_ffn = din("g_ffn", [W, D])
    w_up = din("w_up", [W, D, 2 * DFF])
    w_conv = din("w_conv", [W, 3, 3, 2 * DFF])
    w_down = din("w_down", [W, DFF, D])
    g_final = din("g_final", [D])
    ident_in = din("ident", [128, 128])
    mask_in = din("mask", [2, 128, 128])
    out = nc.dram_tensor("out", [LL, D], F32, kind="ExternalOutput").ap()

    SK = dict(kind="ExternalOutput") if dbg else {}
    XRES = nc.dram_tensor("XRES", [D, NT], F32, **SK).ap()
    USSMP = nc.dram_tensor("USSMP", [512, 8, NJ], BF16, **SK).ap()
    YSP = nc.dram_tensor("YSP", [128, 32, NJ], F32, **SK).ap()
    UG = nc.dram_tensor("UG", [512, NT], BF16, **SK).ap()
    VG = nc.dram_tensor("VG", [512, NT], F32, **SK).ap()
    MIX = nc.dram_tensor("MIX", [D, NT], BF16, **SK).ap()
    GD = nc.dram_tensor("GD", [DFF, NT], BF16, **SK).ap()

    es = ExitStack()

    def sb(name, shape, dt=F32):
        return es.enter_context(nc.sbuf_tensor(name, list(shape), dt))

    ps = [es.enter_context(nc.psum_tensor("ps%d" % i, [128, 512], F32)) for i in range(8)]

    IDENT = sb("IDENT", [128, 128])
    ONESB = sb("ONESB", [128, 128], BF16)
    MASKS = sb("MASKS", [128, 2, 128])
    SIGN = sb("SIGN", [128, 1])
    NSIGN = sb("NSIGN", [128, 1])
    SC = sb("SC", [128, 8, 2])
    MOD = sb("MOD", [128, 6, 8, 2])
    BADA = sb("BADA", [128, 6, 8])
    GM = sb("GM", [128, 8])
    GF = sb("GF", [128, 8])
    GFIN = sb("GFIN", [128, 8])
    A1 = sb("A1", [128, 8, 2])
    A2 = sb("A2", [128, 8, 2])
    HRAW = sb("HRAW", [128, 9216])
    H = HRAW[:].bitcast(BF16).rearrange("p (k t) -> p k t", t=NT)
    YS = HRAW[:].rearrange("p (g j) -> p g j", j=NJ)
    STG = sb("STG", [128, 4096])
    SQ = STG[:, 0:2048].bitcast(BF16).rearrange("p (k t) -> p k t", t=512)
    WBt = sb("WB", [128, 22528], BF16)
    WB = WBt[:]
    TOEP = WB[:, 0:4096].rearrange("p (g n) -> p g n", n=128)
    ET = WB[:, 4096:8192].rearrange("p (g n) -> p g n", n=128)
    ESW = WB[:, 8192:12288].rearrange("p (g n) -> p g n", n=128)
    IM = WB[:, 12288:21504].rearrange("p (g j) -> p g j", j=NJ)
    XTt = sb("XT", [128, 4096])
    XT = XTt[:].rearrange("p (k t) -> p k t", t=512)
    LT = XTt[:].rearrange("p (g s c) -> p g s c", s=8, c=16)
    SS = XTt[:].rearrange("p (j a g) -> p j a g", a=2, g=32)
    TMPt = sb("TMP", [128, 4096])
    TMP = TMPt[:].rearrange("p (k t) -> p k t", t=512)
    RT = TMPt[:].rearrange("p (g s c) -> p g s c", s=8, c=16)
    RS = sb("RS", [128, 512])
    OBt = sb("OB", [128, 4096], BF16)
    OB = OBt[:].rearrange("p (k t) -> p k t", t=512)
    RB = OBt[:].rearrange("p (g n) -> p g n", n=128)
    ACTBt = sb("ACTB", [128, 11264], BF16)
    ACTB = ACTBt[:].rearrange("p (k t) -> p k t", t=512)
    USP = ACTBt[:, 0:9216].rearrange("p (q s j) -> p q s j", s=8, j=NJ)
    HH = ACTBt[:, 0:9216].rearrange("p (j g) -> p j g", g=32)
    FB = sb("FB", [128, 9216])
    UPG = FB[:, 0:2304]; UPV = FB[:, 2304:4608]; CG = FB[:, 4608:6912]; CV = FB[:, 6912:9216]
    def ftab(i):
        return FB[:, i * 512:(i + 1) * 512].rearrange("p (g k) -> p g k", k=16)
    ARG, ARGC, EARG, NF, NFC, PRE, PIM, MAG, BX1, BX2, CX1, CX2 = [ftab(i) for i in range(12)]
    def qtab(i):
        return FB[:, 6144 + i * 256:6144 + (i + 1) * 256].rearrange("p (g k) -> p g k", k=8)
    QR, QI, QT, PA, PB = [qtab(i) for i in range(5)]
    NI = FB[:, 7424:7936].bitcast(I32).rearrange("p (g k) -> p g k", k=16)
    NIC = FB[:, 7936:8448].bitcast(I32).rearrange("p (g k) -> p g k", k=16)
    BS = FB[:, 0:512].rearrange("p (h q) -> p h q", q=128)
    VT = STG[:, 2048:4096].bitcast(BF16).rearrange("p (a n) -> p a n", n=128)
    W3 = sb("W3", [128, 2, 32])
    G3 = sb("G3", [128, 3, 32])
    T1 = sb("T1", [128, 2, 32])
    T2 = sb("T2", [128, 2, 32])
    ARp = sb("ARp", [128, 32]); AIp = sb("AIp", [128, 32]); LDT = sb("LDT", [128, 32])
    LR = sb("LR", [128, 32]); LI = sb("LI", [128, 32])
    CR = sb("CR", [128, 32]); CI = sb("CI", [128, 32]); NR = sb("NR", [128, 32]); DEN = sb("DEN", [128, 32])
    TA = sb("TA", [128, 32]); TB = sb("TB", [128, 32])
    ARC = sb("ARC", [128, 2, 32]); AIC = sb("AIC", [128, 2, 32])
    DV = sb("DV", [128, 32])
    BGLU = sb("BGLU", [128, 4]); GSGU = sb("GSGU", [128, 4])
    WST = sb("WST", [128, 4, 128], BF16)
    WC = sb("WC", [128, 9, 44])

    S.dma("sync", IDENT[:], ident_in[:, :])
    S.dma("sync", MASKS[:], mask_in.rearrange("m p q -> p m q"))
    S.op("vector", lambda e: e.memset(ONESB[:], 1.0))
    S.op("vector", lambda e: e.memset(SIGN[0:64, :], -1.0))
    S.op("vector", lambda e: e.memset(SIGN[64:128, :], 1.0))
    S.op("vector", lambda e: e.memset(NSIGN[0:64, :], 1.0))
    S.op("vector", lambda e: e.memset(NSIGN[64:128, :], -1.0))
    S.dma("sync", STG[0:8, 0:128], c_in.rearrange("(k p) -> k p", p=128))
    S.dma("sync", STG[8:16, 0:128], cctx_in.rearrange("(k p) -> k p", p=128))
    S.dma("sync", STG[16:24, 0:128], g_final.rearrange("(k p) -> k p", p=128))
    S.bar()
    S.op("tensor", lambda e: e.transpose(ps[7][:, 0:24], STG[0:24, 0:128], IDENT[0:24, 0:24]))
    S.bar()
    S.op("vector", lambda e: e.tensor_copy(out=SC[:, :, 0], in_=ps[7][:, 0:8]))
    S.op("vector", lambda e: e.tensor_copy(out=SC[:, :, 1], in_=ps[7][:, 8:16]))
    S.op("vector", lambda e: e.tensor_copy(out=GFIN[:], in_=ps[7][:, 16:24]))
    S.bar()
    S.op("scalar", lambda e: e.activation(out=SC[:], in_=SC[:], func=AF.Silu))
    S.bar()

    def in_transpose(src, nblk, tok0):
        for b in range(nblk):
            S.dma("sync", TMP[:, :, 0:128], src[b * 128:(b + 1) * 128, :].rearrange("t (k d) -> t k d", d=128))
            S.bar()
            for k in range(8):
                S.op("tensor", lambda e, k=k: e.transpose(ps[k // 4][:, (k % 4) * 128:(k % 4 + 1) * 128],
                                                           TMP[:, k, 0:128], IDENT[:]))
            S.bar()
            S.op("vector", lambda e: e.tensor_copy(out=XT[:, 0:4, 0:128], in_=ps[0][:].rearrange("p (k t) -> p k t", t=128)))
            S.op("scalar", lambda e: e.copy(out=XT[:, 4:8, 0:128], in_=ps[1][:].rearrange("p (k t) -> p k t", t=128)))
            S.bar()
            t0 = tok0 + b * 128
            S.dma("sync", XRES[:, t0:t0 + 128].rearrange("(k p) t -> p k t", p=128), XT[:, :, 0:128])
            S.bar()

    in_transpose(ctx_in, 2, 0)
    in_transpose(x_in, 16, 256)

    def load_weight(dst, wap, kch, ncols, cb):
        for c0 in range(0, ncols, cb):
            stg = STG[:, 0:kch * cb].rearrange("p (k n) -> p k n", n=cb)
            S.dma("sync", stg, wap[:, c0:c0 + cb].rearrange("(k p) n -> p k n", p=128))
            S.bar()
            S.op("vector", lambda e, stg=stg, c0=c0: e.tensor_copy(out=dst[:, :, c0:c0 + cb], in_=stg))
            S.bar()

    def rstd_from(src_sq, nk, w, inv_n):
        for k in range(nk):
            S.op("tensor", lambda e, k=k: e.matmul(ps[0][:, 0:w], lhsT=ONESB[:], rhs=src_sq[:, k, 0:w],
                                                    start=(k == 0), stop=(k == nk - 1)))
        S.bar()
        S.op("scalar", lambda e: e.activation(out=RS[:, 0:w], in_=ps[0][:, 0:w], func=AF.Sqrt, bias=EPS, scale=inv_n))
        S.bar()
        S.op("vector", lambda e: e.reciprocal(out=RS[:, 0:w], in_=RS[:, 0:w]))
        S.bar()

    def norm_mod(Acoef, which_shift):
        Xn = [XT, STG[:].rearrange("p (k t) -> p k t", t=512)]

        def xload(i):
            t0, w = TILES[i]
            S.dma_async("sync", Xn[i % 2][:, 0:4, 0:w], XRES[0:512, t0:t0 + w].rearrange("(k p) t -> p k t", p=128))
            S.dma_async("gpsimd", Xn[i % 2][:, 4:8, 0:w], XRES[512:1024, t0:t0 + w].rearrange("(k p) t -> p k t", p=128))

        xload(0)
        for i, (t0, w) in enumerate(TILES):
            sel = 1 if t0 == 0 else 0
            Xi = Xn[i % 2]
            S.join()
            if i + 1 < len(TILES):
                xload(i + 1)
            S.op("scalar", lambda e, w=w, Xi=Xi: e.activation(out=OB[:, :, 0:w], in_=Xi[:, :, 0:w], func=AF.Square))
            S.bar()
            rstd_from(OB, 8, w, 1.0 / D)
            for k in range(8):
                S.op("vector", lambda e, k=k, w=w, sel=sel, Xi=Xi: e.scalar_tensor_tensor(
                    out=TMP[:, k, 0:w], in0=Xi[:, k, 0:w], scalar=Acoef[:, k, sel:sel + 1], in1=RS[:, 0:w],
                    op0=ALU.mult, op1=ALU.mult))
            S.bar()
            for k in range(8):
                S.op("scalar", lambda e, k=k, w=w, sel=sel, t0=t0: e.activation(
                    out=H[:, k, t0:t0 + w], in_=TMP[:, k, 0:w], func=AF.Identity,
                    bias=MOD[:, which_shift, k, sel:sel + 1], scale=1.0))
            S.bar()

    def resid_linear(src_dram, kch, gate_idx, Abufs):
        Wv = WB[:, 0:kch * D].rearrange("p (k n) -> p k n", n=D)
        Xb = [XT, TMP, STG[:].rearrange("p (k t) -> p k t", t=512)]

        def loads(i):
            t0, w = TILES[i]
            S.dma("sync", Abufs[i % 2][:, 0:kch, 0:w], src_dram[:, t0:t0 + w].rearrange("(k p) t -> p k t", p=128))
            S.dma("gpsimd", Xb[i % 3][:, :, 0:w], XRES[:, t0:t0 + w].rearrange("(k p) t -> p k t", p=128))

        def store(i):
            t0, w = TILES[i]
            S.dma("gpsimd", XRES[:, t0:t0 + w].rearrange("(k p) t -> p k t", p=128), Xb[i % 3][:, :, 0:w])

        loads(0)
        S.bar()
        nt = len(TILES)
        for i, (t0, w) in enumerate(TILES):
            sel = 1 if t0 == 0 else 0
            Ab = Abufs[i % 2]
            for m in range(8):
                for k in range(kch):
                    S.op("tensor", lambda e, m=m, k=k, w=w, Ab=Ab: e.matmul(
                        ps[m][:, 0:w], lhsT=Wv[:, k, m * 128:(m + 1) * 128], rhs=Ab[:, k, 0:w],
                        start=(k == 0), stop=(k == kch - 1)))
            if i + 1 < nt:
                loads(i + 1)
            if i >= 1:
                store(i - 1)
            S.bar()
            Xi = Xb[i % 3]
            for m in range(8):
                S.op("vector", lambda e, m=m, w=w, sel=sel, Xi=Xi: e.scalar_tensor_tensor(
                    out=Xi[:, m, 0:w], in0=ps[m][:, 0:w], scalar=MOD[:, gate_idx, m, sel:sel + 1], in1=Xi[:, m, 0:w],
                    op0=ALU.mult, op1=ALU.add))
            S.bar()
        store(nt - 1)
        S.bar()

    try:
      for li in range(n_layers):
        S.dma("sync", STG[0:48, 0:128], b_ada[li].rearrange("(k p) -> k p", p=128))
        S.dma("sync", STG[48:56, 0:128], g_mix[li].rearrange("(k p) -> k p", p=128))
        S.dma("sync", STG[56:64, 0:128], g_ffn[li].rearrange("(k p) -> k p", p=128))
        S.dma("sync", STG[64:68, 0:128], b_glu[li].rearrange("(k p) -> k p", p=128))
        S.dma("sync", STG[68:72, 0:128], g_sgu[li].rearrange("(k p) -> k p", p=128))
        S.bar()
        S.op("tensor", lambda e: e.transpose(ps[7][:, 0:72], STG[0:72, 0:128], IDENT[0:72, 0:72]))
        S.bar()
        S.op("vector", lambda e: e.tensor_copy(out=BADA[:].rearrange("p q k -> p (q k)"), in_=ps[7][:, 0:48]))
        S.op("vector", lambda e: e.tensor_copy(out=GM[:], in_=ps[7][:, 48:56]))
        S.op("vector", lambda e: e.tensor_copy(out=GF[:], in_=ps[7][:, 56:64]))
        S.op("vector", lambda e: e.tensor_copy(out=BGLU[:], in_=ps[7][:, 64:68]))
        S.op("vector", lambda e: e.tensor_copy(out=GSGU[:], in_=ps[7][:, 68:72]))
        S.bar()
        WAs = [STG[:, 0:2048].rearrange("p (k n) -> p k n", n=256), STG[:, 2048:4096].rearrange("p (k n) -> p k n", n=256)]

        def wada_dma(blk):
            WA = WAs[blk % 2]
            src = w_ada[li, :, blk * 256:(blk + 1) * 256].rearrange("(k p) n -> p k n", p=128)
            S.dma("sync", WA[:, 0:4, :], src[:, 0:4, :])
            S.dma("gpsimd", WA[:, 4:8, :], src[:, 4:8, :])

        wada_dma(0)
        S.bar()
        for blk in range(24):
            q, mq = blk // 4, blk % 4
            WA = WAs[blk % 2]
            if blk + 1 < 24:
                wada_dma(blk + 1)
            for mm in range(2):
                m = mq * 2 + mm
                for k in range(8):
                    S.op("tensor", lambda e, m=m, mm=mm, k=k, q=q, WA=WA: e.matmul(
                        ps[0][:, (q * 8 + m) * 2:(q * 8 + m) * 2 + 2], lhsT=WA[:, k, mm * 128:(mm + 1) * 128],
                        rhs=SC[:, k, :], start=(k == 0), stop=(k == 7)))
            S.bar()
        S.op("vector", lambda e: e.tensor_tensor(
            out=MOD[:].rearrange("p q k n -> p (q k) n"), in0=ps[0][:, 0:96].rearrange("p (a n) -> p a n", n=2),
            in1=BADA[:].rearrange("p q k -> p (q k)").unsqueeze(2).to_broadcast([128, 48, 2]), op=ALU.add))
        S.bar()
        S.op("vector", lambda e: e.scalar_tensor_tensor(
            out=A1[:], in0=MOD[:, 1, :, :], scalar=1.0, in1=GM[:].unsqueeze(2).to_broadcast([128, 8, 2]),
            op0=ALU.add, op1=ALU.mult))
        S.op("vector", lambda e: e.scalar_tensor_tensor(
            out=A2[:], in0=MOD[:, 4, :, :], scalar=1.0, in1=GF[:].unsqueeze(2).to_broadcast([128, 8, 2]),
            op0=ALU.add, op1=ALU.mult))
        S.bar()

        ckp("ada")
        norm_mod(A1, 0)

        ckp("norm1")
        WIN = WB[:, 0:8 * 1536].rearrange("p (k n) -> p k n", n=1536)
        load_weight(WIN, w_in[li], 8, 1536, 512)
        for (t0, w) in TILES:
            j0, nj = t0 // 8, w // 8
            for grp in range(2):
                ms = list(range(8)) if grp == 0 else list(range(8, 12))
                for bi, m in enumerate(ms):
                    for k in range(8):
                        S.op("tensor", lambda e, bi=bi, m=m, k=k, w=w, t0=t0: e.matmul(
                            ps[bi][:, 0:w], lhsT=WIN[:, k, m * 128:(m + 1) * 128], rhs=H[:, k, t0:t0 + w],
                            start=(k == 0), stop=(k == 7)))
                S.bar()
                for bi, m in enumerate(ms):
                    if m < 4:
                        S.op("vector", lambda e, bi=bi, m=m, w=w, j0=j0, nj=nj: e.tensor_copy(
                            out=USP[:, m, :, j0:j0 + nj].rearrange("p s j -> p j s"),
                            in_=ps[bi][:, 0:w].rearrange("p (j s) -> p j s", s=8)))
                    elif m < 8:
                        S.op("scalar", lambda e, bi=bi, m=m, w=w: e.activation(
                            out=OB[:, m - 4, 0:w], in_=ps[bi][:, 0:w], func=AF.Gelu_apprx_tanh))
                    else:
                        S.op("scalar", lambda e, bi=bi, m=m, w=w: e.activation(
                            out=TMP[:, m - 8, 0:w], in_=ps[bi][:, 0:w], func=AF.Gelu_apprx_tanh))
                S.bar()
            S.dma("sync", UG[:, t0:t0 + w].rearrange("(k p) t -> p k t", p=128), OB[:, 0:4, 0:w])
            S.dma("gpsimd", VG[:, t0:t0 + w].rearrange("(k p) t -> p k t", p=128), TMP[:, 0:4, 0:w])
            S.bar()

        ckp("win")
        S.dma("sync", USSMP.rearrange("(q p) s j -> p q s j", p=128), USP)
        S.dma("gpsimd", STG[0:32, 0:16], ssm_d[li].rearrange("(g c) -> g c", c=16))
        S.bar()
        for s_ in range(8):
            S.dma("sync" if s_ % 2 == 0 else "gpsimd", IM[s_ * 16:(s_ + 1) * 16, :, :],
                  USSMP[:, s_, :].rearrange("(g c) j -> c g j", c=16))
        S.bar()

        S.op("vector", lambda e: e.tensor_copy(out=STG[0:32, 128:256].rearrange("p (s c) -> p s c", c=16),
                                               in_=STG[0:32, 0:16].unsqueeze(1).to_broadcast([32, 8, 16])))
        S.bar()
        S.op("tensor", lambda e: e.transpose(ps[7][:, 0:32], STG[0:32, 128:256], IDENT[0:32, 0:32]))
        S.bar()
        S.op("vector", lambda e: e.tensor_copy(out=DV[:], in_=ps[7][:, 0:32]))
        ckp("im2col")
        for dr in range(2):
            CIN1 = STG[:, 0:512].rearrange("p (q n) -> p q n", n=128)
            CIN2 = STG[:, 512:1024].rearrange("p (q n) -> p q n", n=128)
            for hf in range(2):
                lo = slice(hf * 64, hf * 64 + 64)
                csrc = [c_re, c_im] if hf == 0 else [c_im, c_re]
                S.dma("sync", CIN1[:, :, lo], csrc[0][li, dr].rearrange("(q g) c p -> (g c) q p", q=4))
                S.dma("gpsimd", CIN2[:, :, lo], csrc[1][li, dr].rearrange("(q g) c p -> (g c) q p", q=4))
            S.bar()
            for q in range(4):
                S.op("tensor", lambda e, q=q: e.transpose(ps[0][:, q * 128:(q + 1) * 128], CIN1[:, q, :], IDENT[:]))
                S.op("tensor", lambda e, q=q: e.transpose(ps[1][:, q * 128:(q + 1) * 128], CIN2[:, q, :], IDENT[:]))
            S.bar()
            S.op("vector", lambda e: e.tensor_copy(out=CX1.rearrange("p g c -> p (g c)"), in_=ps[0][:]))
            S.op("scalar", lambda e: e.copy(out=CX2.rearrange("p g c -> p (g c)"), in_=ps[1][:]))
            S.bar()
            BIN1 = STG[0:32, 0:2048].rearrange("p (h q c) -> p h q c", h=2, c=16)
            BIN2 = STG[0:32, 2048:4096].rearrange("p (h q c) -> p h q c", h=2, c=16)
            S.dma("sync", BIN1[:, 0, :, :], b_re[li, dr])
            S.dma("gpsimd", BIN1[:, 1, :, :], b_im[li, dr])
            S.dma("sync", BIN2[:, 0, :, :], b_im[li, dr])
            S.dma("gpsimd", BIN2[:, 1, :, :], b_re[li, dr])
            AIN = RS[0:32, 0:256].rearrange("p (a q) -> p a q", q=64)
            S.dma("sync", AIN[:, 0, :], a_re[li, dr])
            S.dma("gpsimd", AIN[:, 1, :], a_re[li, dr])
            S.dma("sync", AIN[:, 2, :], a_im[li, dr])
            S.dma("gpsimd", AIN[:, 3, :], a_im[li, dr])
            S.bar()
            for c_ in range(16):
                S.op("tensor", lambda e, c_=c_: e.transpose(ps[2][:, c_ * 32:(c_ + 1) * 32], BIN1[:, :, :, c_], IDENT[0:32, 0:32]))
                S.op("tensor", lambda e, c_=c_: e.transpose(ps[3][:, c_ * 32:(c_ + 1) * 32], BIN2[:, :, :, c_], IDENT[0:32, 0:32]))
            S.op("tensor", lambda e: e.transpose(ps[4][:, 0:32], RS[0:32, 0:128], IDENT[0:32, 0:32]))
            S.op("tensor", lambda e: e.transpose(ps[4][:, 32:64], RS[0:32, 128:256], IDENT[0:32, 0:32]))
            S.bar()
            S.op("vector", lambda e: e.tensor_copy(out=BX1.rearrange("p g c -> p c g"), in_=ps[2][:].rearrange("p (c g) -> p c g", g=32)))
            S.op("scalar", lambda e: e.copy(out=BX2.rearrange("p g c -> p c g"), in_=ps[3][:].rearrange("p (c g) -> p c g", g=32)))
            S.op("vector", lambda e: e.tensor_copy(out=ARp[:], in_=ps[4][:, 0:32]))
            S.op("vector", lambda e: e.tensor_copy(out=AIp[:], in_=ps[4][:, 32:64]))
            S.dma("sync", LDT[:], log_dt[li, dr].partition_broadcast(128))
            S.bar()
            ckp("pl%d" % dr)
            S.op("scalar", lambda e: e.activation(out=LDT[:], in_=LDT[:], func=AF.Exp))
            S.bar()
            S.op("vector", lambda e: e.tensor_tensor(out=LR[:], in0=ARp[:], in1=LDT[:], op=ALU.mult))
            S.op("gpsimd", lambda e: e.tensor_tensor(out=LI[:], in0=AIp[:], in1=LDT[:], op=ALU.mult))
            S.bar()
            ckp("pb%d" % dr)
            ks = list(range(-8, 0)) + list(range(1, 9))
            for idx, kk in enumerate(ks):
                S.op("vector", lambda e, idx=idx, kk=kk: e.tensor_scalar(
                    out=ARG[:, :, idx], in0=LI[:], scalar1=float(kk), scalar2=None, op0=ALU.mult))
                S.op("gpsimd", lambda e, idx=idx, kk=kk: e.tensor_scalar(
                    out=EARG[:, :, idx], in0=LR[:], scalar1=float(kk), scalar2=None, op0=ALU.mult))
            S.bar()
            ckp("pc%d" % dr)
            S.op("vector", lambda e: e.tensor_scalar(out=ARGC, in0=ARG, scalar1=TWO_PI / 4, scalar2=None, op0=ALU.add))
            S.op("scalar", lambda e: e.activation(out=MAG, in_=EARG, func=AF.Exp))
            S.bar()
            ckp("pd%d" % dr)
            S.op("vector", lambda e: e.tensor_scalar(out=NI, in0=ARG, scalar1=1.0 / TWO_PI, scalar2=None, op0=ALU.mult))
            S.op("gpsimd", lambda e: e.tensor_scalar(out=NIC, in0=ARGC, scalar1=1.0 / TWO_PI, scalar2=None, op0=ALU.mult))
            S.bar()
            ckp("pe%d" % dr)
            S.op("vector", lambda e: e.tensor_copy(out=NF, in_=NI))
            S.op("gpsimd", lambda e: e.tensor_copy(out=NFC, in_=NIC))
            S.bar()
            ckp("pf%d" % dr)
            S.op("vector", lambda e: e.scalar_tensor_tensor(out=ARG, in0=NF, scalar=-TWO_PI, in1=ARG, op0=ALU.mult, op1=ALU.add))
            S.op("vector", lambda e: e.scalar_tensor_tensor(out=ARGC, in0=NFC, scalar=-TWO_PI, in1=ARGC, op0=ALU.mult, op1=ALU.add))
            S.bar()
            S.op("vector", lambda e: e.tensor_scalar(out=ARG, in0=ARG, scalar1=3.1415925, scalar2=-3.1415925, op0=ALU.min, op1=ALU.max))
            S.op("gpsimd", lambda e: e.tensor_scalar(out=ARGC, in0=ARGC, scalar1=3.1415925, scalar2=-3.1415925, op0=ALU.min, op1=ALU.max))
            S.bar()
            ckp("pg%d" % dr)
            S.op("scalar", lambda e: e.activation(out=PIM, in_=ARG, func=AF.Sin))
            S.op("scalar", lambda e: e.activation(out=PRE, in_=ARGC, func=AF.Sin))
            S.bar()
            S.op("vector", lambda e: e.tensor_tensor(out=PIM, in0=PIM, in1=MAG, op=ALU.mult))
            S.op("gpsimd", lambda e: e.tensor_tensor(out=PRE, in0=PRE, in1=MAG, op=ALU.mult))
            S.bar()
            ckp("ph%d" % dr)
            S.op("vector", lambda e: e.tensor_scalar(out=NR[:], in0=PRE[:, :, 8], scalar1=-1.0, scalar2=None, op0=ALU.add))
            S.op("gpsimd", lambda e: e.tensor_tensor(out=DEN[:], in0=ARp[:], in1=ARp[:], op=ALU.mult))
            ckp("c0")
            S.op("vector", lambda e: e.tensor_tensor(out=TA[:], in0=AIp[:], in1=AIp[:], op=ALU.mult))
            ckp("c1")
            S.op("vector", lambda e: e.tensor_tensor(out=DEN[:], in0=DEN[:], in1=TA[:], op=ALU.add))
            ckp("c2")
            S.op("vector", lambda e: e.reciprocal(out=DEN[:], in_=DEN[:]))
            ckp("c3")
            S.op("vector", lambda e: e.tensor_tensor(out=TA[:], in0=NR[:], in1=ARp[:], op=ALU.mult))
            S.op("gpsimd", lambda e: e.tensor_tensor(out=TB[:], in0=PIM[:, :, 8], in1=AIp[:], op=ALU.mult))
            ckp("c4")
            S.op("vector", lambda e: e.tensor_tensor(out=CR[:], in0=TA[:], in1=TB[:], op=ALU.add))
            ckp("c5")
            S.op("vector", lambda e: e.tensor_tensor(out=TA[:], in0=PIM[:, :, 8], in1=ARp[:], op=ALU.mult))
            S.op("gpsimd", lambda e: e.tensor_tensor(out=TB[:], in0=NR[:], in1=AIp[:], op=ALU.mult))
            ckp("c6")
            S.op("vector", lambda e: e.tensor_tensor(out=CI[:], in0=TA[:], in1=TB[:], op=ALU.subtract))
            ckp("c7")
            S.op("vector", lambda e: e.tensor_tensor(out=CR[:], in0=CR[:], in1=DEN[:], op=ALU.mult))
            S.op("gpsimd", lambda e: e.tensor_tensor(out=CI[:], in0=CI[:], in1=DEN[:], op=ALU.mult))
            ckp("c8")
            ckp("pi%d" % dr)
            CRb = CR[:].unsqueeze(2).to_broadcast([128, 32, 8])
            CIb = CI[:].unsqueeze(2).to_broadcast([128, 32, 8])
            S.op("vector", lambda e: e.tensor_tensor(out=QR, in0=PRE[:, :, 0:8], in1=CRb, op=ALU.mult))
            S.op("gpsimd", lambda e: e.tensor_tensor(out=QT, in0=PIM[:, :, 0:8], in1=CIb, op=ALU.mult))
            S.bar()
            S.op("vector", lambda e: e.tensor_tensor(out=QR, in0=QR, in1=QT, op=ALU.subtract))
            S.bar()
            S.op("vector", lambda e: e.tensor_tensor(out=QI, in0=PRE[:, :, 0:8], in1=CIb, op=ALU.mult))
            S.op("gpsimd", lambda e: e.tensor_tensor(out=QT, in0=PIM[:, :, 0:8], in1=CRb, op=ALU.mult))
            S.bar()
            S.op("vector", lambda e: e.tensor_tensor(out=QI, in0=QI, in1=QT, op=ALU.add))
            S.bar()
            S.op("vector", lambda e: e.tensor_scalar(out=QI, in0=QI, scalar1=SIGN[:, 0:1], scalar2=None, op0=ALU.mult))
            S.op("gpsimd", lambda e: e.tensor_scalar(out=PA, in0=PRE[:, :, 8:16], scalar1=NSIGN[:, 0:1], scalar2=None, op0=ALU.mult))
            S.op("scalar", lambda e: e.mul(out=PB, in_=PIM[:, :, 8:16], mul=-1.0))
            S.bar()
            S.op("vector", lambda e: e.tensor_copy(out=ARC[:, 0, :], in_=PRE[:, :, 15]))
            S.op("vector", lambda e: e.tensor_copy(out=ARC[:, 1, :], in_=PRE[:, :, 15]))
            S.op("gpsimd", lambda e: e.tensor_scalar(out=AIC[:, 0, :], in0=PIM[:, :, 15], scalar1=SIGN[:, 0:1], scalar2=None, op0=ALU.mult))
            S.op("gpsimd", lambda e: e.tensor_scalar(out=AIC[:, 1, :], in0=PIM[:, :, 15], scalar1=NSIGN[:, 0:1], scalar2=None, op0=ALU.mult))
            S.bar()
            ckp("prep%d" % dr + "")
            for s_ in range(8):
                qi = (7 - s_) if dr == 0 else s_
                ri = s_ if dr == 0 else (7 - s_)
                S.op("vector", lambda e, s_=s_, qi=qi: e.tensor_tensor(
                    out=LT[:, :, s_, :], in0=BX1, in1=QR[:, :, qi:qi + 1].to_broadcast([128, 32, 16]), op=ALU.mult))
                S.op("gpsimd", lambda e, s_=s_, ri=ri: e.tensor_tensor(
                    out=RT[:, :, s_, :], in0=CX1, in1=PA[:, :, ri:ri + 1].to_broadcast([128, 32, 16]), op=ALU.mult))
            S.bar()
            TL = STG[:].rearrange("p (g s c) -> p g s c", s=8, c=16)
            for s_ in range(8):
                qi = (7 - s_) if dr == 0 else s_
                S.op("vector" if s_ % 2 == 0 else "gpsimd", lambda e, s_=s_, qi=qi: e.tensor_tensor(
                    out=TL[:, :, s_, :], in0=BX2, in1=QI[:, :, qi:qi + 1].to_broadcast([128, 32, 16]), op=ALU.mult))
            S.bar()
            S.op("vector", lambda e: e.tensor_tensor(out=LT, in0=LT, in1=TL, op=ALU.add))
            S.bar()
            for s_ in range(8):
                ri = s_ if dr == 0 else (7 - s_)
                S.op("vector" if s_ % 2 == 0 else "gpsimd", lambda e, s_=s_, ri=ri: e.tensor_tensor(
                    out=TL[:, :, s_, :], in0=CX2, in1=PB[:, :, ri:ri + 1].to_broadcast([128, 32, 16]), op=ALU.mult))
            S.bar()
            S.op("vector", lambda e: e.tensor_tensor(out=RT, in0=RT, in1=TL, op=ALU.add))
            S.bar()
            ckp("lr%d" % dr + "")
            for rnd in range(4):
                for gi in range(8):
                    g = rnd * 8 + gi
                    Lg = LT[:, g, :, :].rearrange("p s c -> p (s c)")
                    Rg = RT[:, g, :, :].rearrange("p s c -> p (s c)")
                    S.op("tensor", lambda e, gi=gi, Lg=Lg, Rg=Rg: e.matmul(
                        ps[gi // 4][:, (gi % 4) * 128:(gi % 4 + 1) * 128], lhsT=Lg, rhs=Rg, start=True, stop=True))
                    S.op("tensor", lambda e, gi=gi, Lg=Lg: e.transpose(
                        ps[2 + gi // 4][:, (gi % 4) * 128:(gi % 4 + 1) * 128], Lg, IDENT[:]))
                S.bar()
                for bk in range(2):
                    g0 = rnd * 8 + bk * 4
                    S.op("vector", lambda e, bk=bk, g0=g0, dr=dr: e.tensor_tensor(
                        out=TOEP[:, g0:g0 + 4, :], in0=ps[bk][:].rearrange("p (g n) -> p g n", n=128),
                        in1=MASKS[:, dr:dr + 1, :].to_broadcast([128, 4, 128]), op=ALU.mult))
                    S.op("scalar", lambda e, bk=bk, g0=g0: e.copy(
                        out=ET[:, g0:g0 + 4, :], in_=ps[2 + bk][:].rearrange("p (g n) -> p g n", n=128)))
                S.bar()
                for bk in range(2):
                    g0 = rnd * 8 + bk * 4
                    S.op("vector", lambda e, bk=bk, g0=g0: e.tensor_copy(
                        out=ESW[:, g0:g0 + 4, 0:64], in_=ps[2 + bk][:].rearrange("p (g n) -> p g n", n=128)[:, :, 64:128]))
                    S.op("vector", lambda e, bk=bk, g0=g0: e.tensor_copy(
                        out=ESW[:, g0:g0 + 4, 64:128], in_=ps[2 + bk][:].rearrange("p (g n) -> p g n", n=128)[:, :, 0:64]))
                S.bar()
            S.op("scalar", lambda e: e.copy(out=RB.rearrange("p g n -> p (g n)"), in_=RT.rearrange("p g s c -> p (g s c)")))
            if dr == 0:
                for g in range(32):
                    S.op("vector", lambda e, g=g: e.scalar_tensor_tensor(
                        out=TOEP[:, g, :], in0=IDENT[:], scalar=DV[:, g:g + 1], in1=TOEP[:, g, :],
                        op0=ALU.mult, op1=ALU.add))
            S.op("gpsimd", lambda e: e.memset(W3[:], 0.0))
            S.bar()
            ckp("tiles%d" % dr + "")
            blocks = [(0, 32)] + [(32 + 64 * b, 64) for b in range(4)]
            order = blocks if dr == 0 else [blocks[0]] + blocks[:0:-1]
            for (jb, nb) in order:
                for half in range(2):
                    for gi in range(16):
                        g = half * 16 + gi
                        for arr in range(2):
                            ii = gi * 2 + arr
                            Em = ET if arr == 0 else ESW
                            S.op("tensor", lambda e, ii=ii, g=g, Em=Em, jb=jb, nb=nb: e.matmul(
                                ps[ii // 8][:, (ii % 8) * 64:(ii % 8) * 64 + nb], lhsT=Em[:, g, :],
                                rhs=IM[:, g, jb:jb + nb], start=True, stop=True))
                    S.bar()
                    for bk in range(4):
                        g0 = half * 16 + bk * 4
                        S.op("vector" if bk % 2 == 0 else "scalar", (lambda e, bk=bk, g0=g0, nb=nb: e.tensor_copy(
                            out=SS[:, 0:nb, :, g0:g0 + 4].rearrange("p j a g -> p g a j"),
                            in_=ps[bk][:].rearrange("p (g a j) -> p g a j", a=2, j=64)[:, :, :, 0:nb]))
                            if bk % 2 == 0 else (lambda e, bk=bk, g0=g0, nb=nb: e.copy(
                            out=SS[:, 0:nb, :, g0:g0 + 4].rearrange("p j a g -> p g a j"),
                            in_=ps[bk][:].rearrange("p (g a j) -> p g a j", a=2, j=64)[:, :, :, 0:nb])))
                    S.bar()
                js = list(range(jb, jb + nb)) if dr == 0 else list(range(jb + nb - 1, jb - 1, -1))
                pjl = None
                for j in js:
                    jl = j - jb
                    Wc = W3[:] if pjl is None else SS[:, pjl, :, :]
                    S.op("vector", lambda e, jl=jl, Wc=Wc: e.tensor_tensor(out=G3[:, 0:2, :], in0=Wc, in1=SS[:, jl, :, :], op=ALU.add))
                    S.op("vector", lambda e: e.tensor_tensor(out=T1[:], in0=G3[:, 0:2, :], in1=ARC[:], op=ALU.mult))
                    S.op("vector", lambda e: e.tensor_tensor(out=T2[:, 0, :], in0=G3[:, 1, :], in1=AIC[:, 0, :], op=ALU.mult))
                    S.op("vector", lambda e: e.tensor_tensor(out=T2[:, 1, :], in0=G3[:, 0, :], in1=AIC[:, 1, :], op=ALU.mult))
                    S.op("vector", lambda e, jl=jl: e.tensor_tensor(out=SS[:, jl, :, :], in0=T1[:], in1=T2[:], op=ALU.add))
                    pjl = jl
                S.bar()
                if dr == 0:
                    S.op("scalar", lambda e, jb=jb: e.copy(out=HH[:, jb, :], in_=W3[:, 0, :]))
                    S.op("scalar", lambda e, jb=jb, nb=nb: e.copy(out=HH[:, jb + 1:jb + nb, :], in_=SS[:, 0:nb - 1, 0, :]))
                else:
                    S.op("scalar", lambda e, jb=jb, nb=nb: e.copy(out=HH[:, jb + nb - 1, :], in_=W3[:, 0, :]))
                    S.op("scalar", lambda e, jb=jb, nb=nb: e.copy(out=HH[:, jb:jb + nb - 1, :], in_=SS[:, 1:nb, 0, :]))
                S.bar()
                S.op("vector", lambda e, pjl=pjl: e.tensor_copy(out=W3[:], in_=SS[:, pjl, :, :]))
                S.bar()
            ckp("rec%d" % dr + "")
            for rnd in range(4):
                for gi in range(8):
                    g = rnd * 8 + gi
                    S.op("tensor", lambda e, gi=gi, g=g: e.matmul(
                        ps[gi][:, 0:NJ], lhsT=TOEP[:, g, :], rhs=IM[:, g, :], start=True, stop=False))
                    S.op("tensor", lambda e, gi=gi, g=g: e.matmul(
                        ps[gi][:, 0:NJ], lhsT=RB[:, g, :], rhs=HH[:, :, g], start=False, stop=True))
                S.bar()
                for gi in range(8):
                    g = rnd * 8 + gi
                    if dr == 0:
                        S.op("vector" if gi % 2 == 0 else "scalar", (lambda e, gi=gi, g=g: e.tensor_copy(out=YS[:, g, :], in_=ps[gi][:, 0:NJ]))
                             if gi % 2 == 0 else (lambda e, gi=gi, g=g: e.copy(out=YS[:, g, :], in_=ps[gi][:, 0:NJ])))
                    else:
                        S.op("vector", lambda e, gi=gi, g=g: e.tensor_tensor(out=YS[:, g, :], in0=YS[:, g, :], in1=ps[gi][:, 0:NJ], op=ALU.add))
                S.bar()
        ckp("read")
        S.dma("sync", YSP[:, :, :], YS)
        S.bar()
        YV = YS.rearrange("p (q s) j -> p q s j", s=8)
        YSPv = YSP.rearrange("(s c) (q g) j -> g c q s j", c=16, g=8)
        for g8 in range(8):
            for q in range(4):
                S.dma(["sync", "gpsimd"][q % 2], YV[g8 * 16:(g8 + 1) * 16, q, :, :], YSPv[g8, :, q, :, :])
            S.bar()
        S.bar()

        ckp("unim")
        WG = WB[:, 0:4 * 512].rearrange("p (k n) -> p k n", n=512)
        load_weight(WG, w_glu[li], 4, 512, 512)
        for (t0, w) in TILES:
            j0, nj = t0 // 8, w // 8
            for q in range(4):
                S.op("scalar", lambda e, q=q, w=w, j0=j0, nj=nj: e.activation(
                    out=TMP[:, q, 0:w].rearrange("p (j s) -> p j s", s=8),
                    in_=YV[:, q, :, j0:j0 + nj].rearrange("p s j -> p j s"), func=AF.Gelu_apprx_tanh))
            S.bar()
            S.op("vector", lambda e, w=w: e.tensor_copy(out=OB[:, 0:4, 0:w], in_=TMP[:, 0:4, 0:w]))
            S.bar()
            for m in range(4):
                for k in range(4):
                    S.op("tensor", lambda e, m=m, k=k, w=w: e.matmul(
                        ps[m][:, 0:w], lhsT=WG[:, k, m * 128:(m + 1) * 128], rhs=OB[:, k, 0:w],
                        start=(k == 0), stop=(k == 3)))
            S.bar()
            for m in range(4):
                S.op("scalar", lambda e, m=m, w=w: e.activation(
                    out=XT[:, m, 0:w], in_=ps[m][:, 0:w], func=AF.Sigmoid, bias=BGLU[:, m:m + 1], scale=1.0))
            S.bar()
            S.op("vector", lambda e, w=w: e.tensor_tensor(out=OB[:, 4:8, 0:w], in0=TMP[:, 0:4, 0:w], in1=XT[:, 0:4, 0:w], op=ALU.mult))
            S.bar()
            S.dma("sync", MIX[0:512, t0:t0 + w].rearrange("(k p) t -> p k t", p=128), OB[:, 4:8, 0:w])
            S.bar()

        ckp("glu")
        S.dma("sync", TMP[:, 0:4, 0:128], w_sp[li].rearrange("h p q -> p h q"))
        S.dma("gpsimd", BS.rearrange("p h q -> p (h q)"), b_sp[li].rearrange("h q -> (h q)").partition_broadcast(128))
        S.bar()
        for h in range(4):
            S.op("tensor", lambda e, h=h: e.transpose(ps[0][:, h * 128:(h + 1) * 128], TMP[:, h, 0:128], IDENT[:]))
        S.bar()
        S.op("vector", lambda e: e.tensor_copy(out=WST[:], in_=ps[0][:].rearrange("p (h n) -> p h n", n=128)))
        S.bar()
        for (t0, w) in TILES:
            nchk = w // 128
            S.dma("sync", XT[:, 0:4, 0:w], VG[:, t0:t0 + w].rearrange("(k p) t -> p k t", p=128))
            S.dma("gpsimd", OB[:, 0:4, 0:w], UG[:, t0:t0 + w].rearrange("(k p) t -> p k t", p=128))
            S.bar()
            S.op("scalar", lambda e, w=w: e.activation(out=SQ[:, 0:4, 0:w], in_=XT[:, 0:4, 0:w], func=AF.Square))
            S.bar()
            rstd_from(SQ, 4, w, 1.0 / 512)
            for k in range(4):
                S.op("vector", lambda e, k=k, w=w: e.scalar_tensor_tensor(
                    out=TMP[:, k, 0:w], in0=XT[:, k, 0:w], scalar=GSGU[:, k:k + 1], in1=RS[:, 0:w],
                    op0=ALU.mult, op1=ALU.mult))
            S.bar()
            for ck in range(nchk):
                for h in range(4):
                    ii = ck * 4 + h
                    S.op("tensor", lambda e, ii=ii, ck=ck, h=h: e.transpose(
                        ps[ii // 4][:, (ii % 4) * 128:(ii % 4 + 1) * 128], TMP[:, h, ck * 128:(ck + 1) * 128], IDENT[:]))
            S.bar()
            for ck in range(nchk):
                S.op("vector" if ck % 2 == 0 else "scalar", (lambda e, ck=ck: e.tensor_copy(
                    out=VT[:, ck * 4:(ck + 1) * 4, :], in_=ps[ck][:].rearrange("p (h n) -> p h n", n=128)))
                    if ck % 2 == 0 else (lambda e, ck=ck: e.copy(
                    out=VT[:, ck * 4:(ck + 1) * 4, :], in_=ps[ck][:].rearrange("p (h n) -> p h n", n=128))))
            S.bar()
            for ck in range(nchk):
                for h in range(4):
                    ii = ck * 4 + h
                    S.op("tensor", lambda e, ii=ii, ck=ck, h=h: e.matmul(
                        ps[4 + ck][:, h * 128:(h + 1) * 128], lhsT=VT[:, ii, :], rhs=WST[:, h, :], start=True, stop=True))
            S.bar()
            for ck in range(nchk):
                S.op("vector", lambda e, ck=ck: e.tensor_tensor(
                    out=TMP[:, 4:8, ck * 128:(ck + 1) * 128], in0=ps[4 + ck][:].rearrange("p (h n) -> p h n", n=128),
                    in1=BS, op=ALU.add))
            S.bar()
            S.op("vector", lambda e, w=w: e.tensor_tensor(out=OB[:, 4:8, 0:w], in0=TMP[:, 4:8, 0:w], in1=OB[:, 0:4, 0:w], op=ALU.mult))
            S.bar()
            S.dma("sync", MIX[512:1024, t0:t0 + w].rearrange("(k p) t -> p k t", p=128), OB[:, 4:8, 0:w])
            S.bar()

        ckp("sgu")
        load_weight(WB[:, 0:8 * D].rearrange("p (k n) -> p k n", n=D), w_out[li], 8, D, 512)
        resid_linear(MIX, 8, 2, [ACTB[:, 0:8, :], ACTB[:, 8:16, :]])

        ckp("wout")
        norm_mod(A2, 3)

        ckp("norm2")
        S.dma("sync", FB[0:9, 0:2 * DFF], w_conv[li].rearrange("a b n -> (a b) n"))
        S.bar()
        for ch in range(44):
            S.op("tensor", lambda e, ch=ch: e.transpose(ps[7][:, ch * 9:(ch + 1) * 9], FB[0:9, ch * 128:(ch + 1) * 128], IDENT[0:9, 0:9]))
        S.bar()
        S.op("vector", lambda e: e.tensor_copy(out=WC[:].rearrange("p t c -> p c t"), in_=ps[7][:, 0:396].rearrange("p (c t) -> p c t", t=9)))
        S.bar()
        WU = OBt[:, 0:2048].rearrange("p (k n) -> p k n", n=256)
        WDv = WB[:, 0:22 * D].rearrange("p (k n) -> p k n", n=D)
        wd_stage = [XTt[:, 0:2816].rearrange("p (k n) -> p k n", n=128), TMPt[:, 0:2816].rearrange("p (k n) -> p k n", n=128)]
        FBb = FB[:].bitcast(BF16)
        UPb = [FBb[:, 0:2304], FBb[:, 2304:4608]]
        DG = FBb[:, 4608:6912].rearrange("p (t n) -> p t n", n=128)
        SG = FB[:, 4608:6912]
        GB = ACTB[:, 0:5, :].rearrange("p a t -> p (a t)")[:, 0:NT]
        stgs = [STG[:, 0:2048].rearrange("p (k n) -> p k n", n=256), STG[:, 2048:4096].rearrange("p (k n) -> p k n", n=256)]

        def wup_dma(m):
            st = stgs[m % 2]
            S.dma("sync", st[:, :, 0:128], w_up[li, :, m * 128:(m + 1) * 128].rearrange("(k p) n -> p k n", p=128))
            S.dma("gpsimd", st[:, :, 128:256], w_up[li, :, DFF + m * 128:DFF + (m + 1) * 128].rearrange("(k p) n -> p k n", p=128))

        def conv_mm(part):
            src = UPb[part]
            d0 = part * 9
            S.op("tensor", lambda e: e.matmul(ps[0][:, 0:256], lhsT=DG[:, d0 + 4, :], rhs=src[:, 0:256], start=True, stop=False))
            S.op("tensor", lambda e: e.matmul(ps[0][:, 1:256], lhsT=DG[:, d0 + 3, :], rhs=src[:, 0:255], start=False, stop=False))
            S.op("tensor", lambda e: e.matmul(ps[0][:, 0:255], lhsT=DG[:, d0 + 5, :], rhs=src[:, 1:256], start=False, stop=True))
            sv = src[:, LC:NT].rearrange("p (r c) -> p r c", c=64)
            taps = [(1, 1)] + [(ky, kx) for ky in range(3) for kx in range(3) if not (ky == 1 and kx == 1)]
            for ti in range(1, 5):
                R0 = 8 * (ti - 1)
                pv = ps[ti][:, 0:512].rearrange("p (r c) -> p r c", c=64)
                for n_, (ky, kx) in enumerate(taps):
                    dy, dx = ky - 1, kx - 1
                    ra, rb = max(R0, -dy, 0), min(R0 + 8, 32 - max(0, dy))
                    ra = max(ra, 0 - min(0, dy))
                    c0, c1 = max(0, -dx), 64 - max(0, dx)
                    S.op("tensor", lambda e, pv=pv, ra=ra, rb=rb, c0=c0, c1=c1, dy=dy, dx=dx, R0=R0, ky=ky, kx=kx, n_=n_:
                         e.matmul(pv[:, ra - R0:rb - R0, c0:c1], lhsT=DG[:, d0 + ky * 3 + kx, :],
                                  rhs=sv[:, ra + dy:rb + dy, c0 + dx:c1 + dx], start=(n_ == 0), stop=(n_ == 8)))

        wup_dma(0)
        S.bar()
        for m in range(22):
            if m % 2 == 1 and m // 2 < 8:
                S.join()
                S.op("vector", lambda e, m=m: e.tensor_copy(out=WDv[:, :, (m // 2) * 128:(m // 2 + 1) * 128], in_=wd_stage[(m // 2) % 2]))
            if m % 2 == 0 and m // 2 < 8:
                S.dma_async("sync", wd_stage[(m // 2) % 2], w_down[li][:, (m // 2) * 128:(m // 2 + 1) * 128].rearrange("(k p) n -> p k n", p=128))
            S.op("vector", lambda e, m=m: e.tensor_copy(out=WU, in_=stgs[m % 2]))
            for part in range(2):
                for tap in range(9):
                    ch = part * 22 + m
                    if (part * 9 + tap) % 2 == 0:
                        S.op("vector", lambda e, part=part, tap=tap, ch=ch: e.tensor_scalar(
                            out=DG[:, part * 9 + tap, :], in0=IDENT[:], scalar1=WC[:, tap, ch:ch + 1], scalar2=None, op0=ALU.mult))
                    else:
                        S.op("scalar", lambda e, part=part, tap=tap, ch=ch: e.activation(
                            out=DG[:, part * 9 + tap, :], in_=IDENT[:], func=AF.Copy, scale=WC[:, tap, ch:ch + 1]))
            S.bar()
            for part in range(2):
                for ti, (t0, w) in enumerate(TILES):
                    for k in range(8):
                        S.op("tensor", lambda e, ti=ti, k=k, t0=t0, w=w, part=part: e.matmul(
                            ps[ti][:, 0:w], lhsT=WU[:, k, part * 128:(part + 1) * 128], rhs=H[:, k, t0:t0 + w],
                            start=(k == 0), stop=(k == 7)))
                if part == 0:
                    if m + 1 < 22:
                        wup_dma(m + 1)
                    if m > 0:
                        S.dma("sync", GD[(m - 1) * 128:m * 128, :], GB)
                S.bar()
                dst = UPb[part]
                for ti, (t0, w) in enumerate(TILES):
                    S.op("vector" if ti % 2 == 0 else "scalar", (lambda e, ti=ti, t0=t0, w=w, dst=dst: e.tensor_copy(
                        out=dst[:, t0:t0 + w], in_=ps[ti][:, 0:w])) if ti % 2 == 0 else (lambda e, ti=ti, t0=t0, w=w, dst=dst: e.copy(
                        out=dst[:, t0:t0 + w], in_=ps[ti][:, 0:w])))
                S.bar()
            conv_mm(0)
            S.bar()
            for ti, (t0, w) in enumerate(TILES):
                S.op("scalar", lambda e, ti=ti, t0=t0, w=w: e.activation(out=SG[:, t0:t0 + w], in_=ps[ti][:, 0:w], func=AF.Silu))
            S.bar()
            conv_mm(1)
            S.bar()
            for ti, (t0, w) in enumerate(TILES):
                S.op("vector", lambda e, ti=ti, t0=t0, w=w: e.tensor_tensor(out=GB[:, t0:t0 + w], in0=ps[ti][:, 0:w], in1=SG[:, t0:t0 + w], op=ALU.mult))
            S.bar()
        S.dma("sync", GD[21 * 128:22 * 128, :], GB)
        S.bar()

        ckp("ffnup")
        resid_linear(GD, 22, 5, [ACTB, FB[:].bitcast(BF16)[:, 0:11264].rearrange("p (k t) -> p k t", t=512)])

    except _Stop:
        pass
    S.bar()
    if dbg:
        DF = nc.dram_tensor("DBGF", [128, 32768], F32, kind="ExternalOutput").ap()
        DB = nc.dram_tensor("DBGB", [128, 40960], BF16, kind="ExternalOutput").ap()
        off = 0
        for t_, n_ in [(HRAW[:], 9216), (XTt[:], 4096), (TMPt[:], 4096), (FB[:], 9216), (STG[:], 4096), (RS[:], 512),
                       (MOD[:].rearrange("p q k n -> p (q k n)"), 96), (A1[:].rearrange("p k n -> p (k n)"), 16),
                       (A2[:].rearrange("p k n -> p (k n)"), 16), (W3[:].rearrange("p a g -> p (a g)"), 64),
                       (ARC[:].rearrange("p a g -> p (a g)"), 64), (AIC[:].rearrange("p a g -> p (a g)"), 64),
                       (CR[:], 32), (CI[:], 32), (DV[:], 32), (LR[:], 32), (LI[:], 32)]:
            S.dma("sync", DF[:, off:off + n_], t_)
            off += n_
        offb = 0
        for t_, n_ in [(WB, 22528), (OBt[:], 4096), (ACTBt[:], 11264)]:
            S.dma("gpsimd", DB[:, offb:offb + n_], t_)
            offb += n_
        S.bar()
    for b in range(16):
        t0 = LC + b * 128
        S.dma("sync", XT[:, :, 0:128], XRES[:, t0:t0 + 128].rearrange("(k p) t -> p k t", p=128))
        S.bar()
        S.op("scalar", lambda e: e.activation(out=SQ[:, :, 0:128], in_=XT[:, :, 0:128], func=AF.Square))
        S.bar()
        rstd_from(SQ, 8, 128, 1.0 / D)
        for k in range(8):
            S.op("vector", lambda e, k=k: e.scalar_tensor_tensor(
                out=TMP[:, k, 0:128], in0=XT[:, k, 0:128], scalar=GFIN[:, k:k + 1], in1=RS[:, 0:128],
                op0=ALU.mult, op1=ALU.mult))
        S.bar()
        for k in range(8):
            S.op("tensor", lambda e, k=k: e.transpose(ps[k // 4][:, (k % 4) * 128:(k % 4 + 1) * 128],
                                                       TMP[:, k, 0:128], IDENT[:]))
        S.bar()
        S.op("vector", lambda e: e.tensor_copy(out=XT[:, 0:4, 0:128], in_=ps[0][:].rearrange("p (k t) -> p k t", t=128)))
        S.op("scalar", lambda e: e.copy(out=XT[:, 4:8, 0:128], in_=ps[1][:].rearrange("p (k t) -> p k t", t=128)))
        S.bar()
        S.dma("sync", out[b * 128:(b + 1) * 128, :].rearrange("t (k d) -> t k d", d=128), XT[:, :, 0:128])
        S.bar()

    S.emit()
    es.close()
    return nc


_CONST = None


def _consts():
    ident = np.eye(128, dtype=np.float32)
    sp = np.arange(128) // 16
    m0 = (sp[None, :] >= sp[:, None]).astype(np.float32)
    m1 = (sp[None, :] <= sp[:, None]).astype(np.float32)
    return ident, np.stack([m0, m1])


def kernel(n_layers=4, **inputs):
    nc = build_nc(n_layers)
    ident, mask = _consts()
    in_maps = []
    for b in range(8):
        m = {}
        for k, v in inputs.items():
            v = np.asarray(v)
            if k in ("x", "c", "ctx"):
                m[k] = np.ascontiguousarray(v[b], dtype=np.float32)
            else:
                m[k] = np.ascontiguousarray(v, dtype=np.float32)
        m["ident"] = ident
        m["mask"] = mask
        in_maps.append(m)
    res = run_bass_kernel_spmd(nc, in_maps, core_ids=list(range(8)))
    return np.stack([np.asarray(r["out"], dtype=np.float32) for r in res.results], axis=0)
```

```python
import numpy as np
from contextlib import ExitStack
import concourse.bass as bass
import concourse.mybir as mybir
from concourse.bass_utils import run_bass_kernel_spmd

F32, BF16, I32 = mybir.dt.float32, mybir.dt.bfloat16, mybir.dt.int32
AF = mybir.ActivationFunctionType
ALU = mybir.AluOpType

D = 1024
NT = 2304
LC = 256
LL = 2048
DFF = 2816
EPS = 1e-6
NJ = 288
TILES = [(0, 256)] + [(256 + 512 * i, 512) for i in range(4)]
TWO_PI = 6.283185307179586


class Sched:
    def __init__(self, nc):
        self.nc = nc
        self.stages = [[]]
        self.join_at = set()

    def op(self, eng, fn, dma=False):
        self.stages[-1].append((eng, dma, fn))

    def dma(self, eng, out, in_, slow=False):
        self.op(eng, lambda e, o=out, i=in_: e.dma_start(out=o, in_=i), dma=True)

    def dma_async(self, eng, out, in_):
        self.op(eng, lambda e, o=out, i=in_: e.dma_start(out=o, in_=i), dma="async")

    def bar(self):
        if self.stages[-1]:
            self.stages.append([])

    def join(self):
        self.bar()
        self.join_at.add(len(self.stages) - 1)

    def emit(self):
        nc = self.nc
        self.bar()
        names = ["c_scalar", "c_vector", "c_gpsimd", "c_tensor", "d_sync", "d_scalar", "d_gpsimd", "a_sync", "a_gpsimd"]

        def semname(eng, dma):
            if dma == "async":
                return "a_" + eng
            return ("d_" if dma else "c_") + eng

        cum = []
        cur = {n: 0 for n in names}
        for st in self.stages:
            cum.append(dict(cur))
            for (eng, dma, _) in st:
                cur[semname(eng, dma)] += 16 if dma else 1
        final = dict(cur)
        with ExitStack() as es:
            sems = {n: es.enter_context(nc.semaphore(n)) for n in names}
            block = es.enter_context(nc.Block())

            def make(engname):
                def body(eng):
                    waited = {n: 0 for n in names}
                    joined = {n: 0 for n in names}
                    for k, st in enumerate(self.stages):
                        if k in self.join_at:
                            for n in names:
                                if n.startswith("a_"):
                                    joined[n] = cum[k][n]
                        mine = [o for o in st if o[0] == engname]
                        if not mine:
                            continue
                        for n in names:
                            tgt = joined[n] if n.startswith("a_") else cum[k][n]
                            if tgt > waited[n]:
                                eng.wait_ge(sems[n], tgt)
                                waited[n] = tgt
                        for (_, dma, fn) in mine:
                            ins = fn(eng)
                            ins.then_inc(sems[semname(engname, dma)], 16 if dma else 1)
                    if engname == "sync":
                        for n in names:
                            if final[n] > waited[n]:
                                eng.wait_ge(sems[n], final[n])
                return body

            block.sync(make("sync"))
            block.scalar(make("scalar"))
            block.vector(make("vector"))
            block.gpsimd(make("gpsimd"))
            block.tensor(make("tensor"))


class _Stop(Exception):
    pass


def build_nc(n_layers, n_wl=4, stop=None, dbg=False):
    nc = bass.Bass("TRN2", target_bir_lowering=False)
    S = Sched(nc)
    W = n_wl

    def ckp(name):
        S.bar()
        if stop == name:
            raise _Stop()

    def din(name, shape):
        return nc.dram_tensor(name, list(shape), F32, kind="ExternalInput").ap()

    x_in = din("x", [LL, D])
    c_in = din("c", [D])
    ctx_in = din("ctx", [LC, D])
    cctx_in = din("c_ctx", [D])
    w_ada = din("w_ada", [W, D, 6 * D])
    b_ada = din("b_ada", [W, 6 * D])
    g_mix = din("g_mix", [W, D])
    w_in = din("w_in", [W, D, 1536])
    a_re = din("ssm_a_re", [W, 2, 32, 64])
    a_im = din("ssm_a_im", [W, 2, 32, 64])
    b_re = din("ssm_b_re", [W, 2, 32, 64, 16])
    b_im = din("ssm_b_im", [W, 2, 32, 64, 16])
    c_re = din("ssm_c_re", [W, 2, 32, 16, 64])
    c_im = din("ssm_c_im", [W, 2, 32, 16, 64])
    log_dt = din("ssm_log_dt", [W, 2, 32])
    ssm_d = din("ssm_d", [W, 512])
    w_glu = din("w_glu", [W, 512, 512])
    b_glu = din("b_glu", [W, 512])
    g_sgu = din("g_sgu", [W, 512])
    w_sp = din("w_spatial", [W, 4, 128, 128])
    b_sp = din("b_spatial", [W, 4, 128])
    w_out = din("w_out", [W, D, D])
    g_ffn = din("g_ffn", [W, D])
    w_up = din("w_up", [W, D, 2 * DFF])
    w_conv = din("w_conv", [W, 3, 3, 2 * DFF])
    w_down = din("w_down", [W, DFF, D])
    g_final = din("g_final", [D])
    ident_in = din("ident", [128, 128])
    mask_in = din("mask", [2, 128, 128])
    out = nc.dram_tensor("out", [LL, D], F32, kind="ExternalOutput").ap()

    SK = dict(kind="ExternalOutput") if dbg else {}
    XRES = nc.dram_tensor("XRES", [D, NT], F32, **SK).ap()
    USSMP = nc.dram_tensor("USSMP", [512, 8, NJ], BF16, **SK).ap()
    YSP = nc.dram_tensor("YSP", [128, 32, NJ], F32, **SK).ap()
    UG = nc.dram_tensor("UG", [512, NT], BF16, **SK).ap()
    VG = nc.dram_tensor("VG", [512, NT], F32, **SK).ap()
    MIX = nc.dram_tensor("MIX", [D, NT], BF16, **SK).ap()
    GD = nc.dram_tensor("GD", [DFF, NT], BF16, **SK).ap()

    es = ExitStack()

    def sb(name, shape, dt=F32):
        return es.enter_context(nc.sbuf_tensor(name, list(shape), dt))

    ps = [es.enter_context(nc.psum_tensor("ps%d" % i, [128, 512], F32)) for i in range(8)]

    IDENT = sb("IDENT", [128, 128])
    ONESB = sb("ONESB", [128, 128], BF16)
    MASKS = sb("MASKS", [128, 2, 128])
    SIGN = sb("SIGN", [128, 1])
    NSIGN = sb("NSIGN", [128, 1])
    SC = sb("SC", [128, 8, 2])
    MOD = sb("MOD", [128, 6, 8, 2])
    BADA = sb("BADA", [128, 6, 8])
    GM = sb("GM", [128, 8])
    GF = sb("GF", [128, 8])
    GFIN = sb("GFIN", [128, 8])
    A1 = sb("A1", [128, 8, 2])
    A2 = sb("A2", [128, 8, 2])
    HRAW = sb("HRAW", [128, 9216])
    H = HRAW[:].bitcast(BF16).rearrange("p (k t) -> p k t", t=NT)
    YS = HRAW[:].rearrange("p (g j) -> p g j", j=NJ)
    STG = sb("STG", [128, 4096])
    SQ = STG[:, 0:2048].bitcast(BF16).rearrange("p (k t) -> p k t", t=512)
    WBt = sb("WB", [128, 22528], BF16)
    WB = WBt[:]
    TOEP = WB[:, 0:4096].rearrange("p (g n) -> p g n", n=128)
    ET = WB[:, 4096:8192].rearrange("p (g n) -> p g n", n=128)
    ESW = WB[:, 8192:12288].rearrange("p (g n) -> p g n", n=128)
    IM = WB[:, 12288:21504].rearrange("p (g j) -> p g j", j=NJ)
    XTt = sb("XT", [128, 4096])
    XT = XTt[:].rearrange("p (k t) -> p k t", t=512)
    LT = XTt[:].rearrange("p (g s c) -> p g s c", s=8, c=16)
    SS = XTt[:].rearrange("p (j a g) -> p j a g", a=2, g=32)
    TMPt = sb("TMP", [128, 4096])
    TMP = TMPt[:].rearrange("p (k t) -> p k t", t=512)
    RT = TMPt[:].rearrange("p (g s c) -> p g s c", s=8, c=16)
    RS = sb("RS", [128, 512])
    OBt = sb("OB", [128, 4096], BF16)
    OB = OBt[:].rearrange("p (k t) -> p k t", t=512)
    RB = OBt[:].rearrange("p (g n) -> p g n", n=128)
    ACTBt = sb("ACTB", [128, 11264], BF16)
    ACTB = ACTBt[:].rearrange("p (k t) -> p k t", t=512)
    USP = ACTBt[:, 0:9216].rearrange("p (q s j) -> p q s j", s=8, j=NJ)
    HH = ACTBt[:, 0:9216].rearrange("p (j g) -> p j g", g=32)
    FB = sb("FB", [128, 9216])
    UPG = FB[:, 0:2304]; UPV = FB[:, 2304:4608]; CG = FB[:, 4608:6912]; CV = FB[:, 6912:9216]
    def ftab(i):
        return FB[:, i * 512:(i + 1) * 512].rearrange("p (g k) -> p g k", k=16)
    ARG, ARGC, EARG, NF, NFC, PRE, PIM, MAG, BX1, BX2, CX1, CX2 = [ftab(i) for i in range(12)]
    def qtab(i):
        return FB[:, 6144 + i * 256:6144 + (i + 1) * 256].rearrange("p (g k) -> p g k", k=8)
    QR, QI, QT, PA, PB = [qtab(i) for i in range(5)]
    NI = FB[:, 7424:7936].bitcast(I32).rearrange("p (g k) -> p g k", k=16)
    NIC = FB[:, 7936:8448].bitcast(I32).rearrange("p (g k) -> p g k", k=16)
    BS = FB[:, 0:512].rearrange("p (h q) -> p h q", q=128)
    VT = STG[:, 2048:4096].bitcast(BF16).rearrange("p (a n) -> p a n", n=128)
    W3 = sb("W3", [128, 2, 32])
    G3 = sb("G3", [128, 3, 32])
    T1 = sb("T1", [128, 2, 32])
    T2 = sb("T2", [128, 2, 32])
    ARp = sb("ARp", [128, 32]); AIp = sb("AIp", [128, 32]); LDT = sb("LDT", [128, 32])
    LR = sb("LR", [128, 32]); LI = sb("LI", [128, 32])
    CR = sb("CR", [128, 32]); CI = sb("CI", [128, 32]); NR = sb("NR", [128, 32]); DEN = sb("DEN", [128, 32])
    TA = sb("TA", [128, 32]); TB = sb("TB", [128, 32])
    ARC = sb("ARC", [128, 2, 32]); AIC = sb("AIC", [128, 2, 32])
    DV = sb("DV", [128, 32])
    BGLU = sb("BGLU", [128, 4]); GSGU = sb("GSGU", [128, 4])
    WST = sb("WST", [128, 4, 128], BF16)
    WC = sb("WC", [128, 9, 44])

    S.dma("sync", IDENT[:], ident_in[:, :])
    S.dma("sync", MASKS[:], mask_in.rearrange("m p q -> p m q"))
    S.op("vector", lambda e: e.memset(ONESB[:], 1.0))
    S.op("vector", lambda e: e.memset(SIGN[0:64, :], -1.0))
    S.op("vector", lambda e: e.memset(SIGN[64:128, :], 1.0))
    S.op("vector", lambda e: e.memset(NSIGN[0:64, :], 1.0))
    S.op("vector", lambda e: e.memset(NSIGN[64:128, :], -1.0))
    S.dma("sync", STG[0:8, 0:128], c_in.rearrange("(k p) -> k p", p=128))
    S.dma("sync", STG[8:16, 0:128], cctx_in.rearrange("(k p) -> k p", p=128))
    S.dma("sync", STG[16:24, 0:128], g_final.rearrange("(k p) -> k p", p=128))
    S.bar()
    S.op("tensor", lambda e: e.transpose(ps[7][:, 0:24], STG[0:24, 0:128], IDENT[0:24, 0:24]))
    S.bar()
    S.op("vector", lambda e: e.tensor_copy(out=SC[:, :, 0], in_=ps[7][:, 0:8]))
    S.op("vector", lambda e: e.tensor_copy(out=SC[:, :, 1], in_=ps[7][:, 8:16]))
    S.op("vector", lambda e: e.tensor_copy(out=GFIN[:], in_=ps[7][:, 16:24]))
    S.bar()
    S.op("scalar", lambda e: e.activation(out=SC[:], in_=SC[:], func=AF.Silu))
    S.bar()

    def in_transpose(src, nblk, tok0):
        for b in range(nblk):
            S.dma("sync", TMP[:, :, 0:128], src[b * 128:(b + 1) * 128, :].rearrange("t (k d) -> t k d", d=128))
            S.bar()
            for k in range(8):
                S.op("tensor", lambda e, k=k: e.transpose(ps[k // 4][:, (k % 4) * 128:(k % 4 + 1) * 128],
                                                           TMP[:, k, 0:128], IDENT[:]))
            S.bar()
            S.op("vector", lambda e: e.tensor_copy(out=XT[:, 0:4, 0:128], in_=ps[0][:].rearrange("p (k t) -> p k t", t=128)))
            S.op("scalar", lambda e: e.copy(out=XT[:, 4:8, 0:128], in_=ps[1][:].rearrange("p (k t) -> p k t", t=128)))
            S.bar()
            t0 = tok0 + b * 128
            S.dma("sync", XRES[:, t0:t0 + 128].rearrange("(k p) t -> p k t", p=128), XT[:, :, 0:128])
            S.bar()

    in_transpose(ctx_in, 2, 0)
    in_transpose(x_in, 16, 256)

    def load_weight(dst, wap, kch, ncols, cb):
        for c0 in range(0, ncols, cb):
            stg = STG[:, 0:kch * cb].rearrange("p (k n) -> p k n", n=cb)
            S.dma("sync", stg, wap[:, c0:c0 + cb].rearrange("(k p) n -> p k n", p=128))
            S.bar()
            S.op("vector", lambda e, stg=stg, c0=c0: e.tensor_copy(out=dst[:, :, c0:c0 + cb], in_=stg))
            S.bar()

    def rstd_from(src_sq, nk, w, inv_n):
        for k in range(nk):
            S.op("tensor", lambda e, k=k: e.matmul(ps[0][:, 0:w], lhsT=ONESB[:], rhs=src_sq[:, k, 0:w],
                                                    start=(k == 0), stop=(k == nk - 1)))
        S.bar()
        S.op("scalar", lambda e: e.activation(out=RS[:, 0:w], in_=ps[0][:, 0:w], func=AF.Sqrt, bias=EPS, scale=inv_n))
        S.bar()
        S.op("vector", lambda e: e.reciprocal(out=RS[:, 0:w], in_=RS[:, 0:w]))
        S.bar()

    def norm_mod(Acoef, which_shift):
        Xn = [XT, STG[:].rearrange("p (k t) -> p k t", t=512)]

        def xload(i):
            t0, w = TILES[i]
            S.dma_async("sync", Xn[i % 2][:, 0:4, 0:w], XRES[0:512, t0:t0 + w].rearrange("(k p) t -> p k t", p=128))
            S.dma_async("gpsimd", Xn[i % 2][:, 4:8, 0:w], XRES[512:1024, t0:t0 + w].rearrange("(k p) t -> p k t", p=128))

        xload(0)
        for i, (t0, w) in enumerate(TILES):
            sel = 1 if t0 == 0 else 0
            Xi = Xn[i % 2]
            S.join()
            if i + 1 < len(TILES):
                xload(i + 1)
            S.op("scalar", lambda e, w=w, Xi=Xi: e.activation(out=OB[:, :, 0:w], in_=Xi[:, :, 0:w], func=AF.Square))
            S.bar()
            rstd_from(OB, 8, w, 1.0 / D)
            for k in range(8):
                S.op("vector", lambda e, k=k, w=w, sel=sel, Xi=Xi: e.scalar_tensor_tensor(
                    out=TMP[:, k, 0:w], in0=Xi[:, k, 0:w], scalar=Acoef[:, k, sel:sel + 1], in1=RS[:, 0:w],
                    op0=ALU.mult, op1=ALU.mult))
            S.bar()
            for k in range(8):
                S.op("scalar", lambda e, k=k, w=w, sel=sel, t0=t0: e.activation(
                    out=H[:, k, t0:t0 + w], in_=TMP[:, k, 0:w], func=AF.Identity,
                    bias=MOD[:, which_shift, k, sel:sel + 1], scale=1.0))
            S.bar()

    def resid_linear(src_dram, kch, gate_idx, Abufs):
        Wv = WB[:, 0:kch * D].rearrange("p (k n) -> p k n", n=D)
        Xb = [XT, TMP, STG[:].rearrange("p (k t) -> p k t", t=512)]

        def loads(i):
            t0, w = TILES[i]
            S.dma("sync", Abufs[i % 2][:, 0:kch, 0:w], src_dram[:, t0:t0 + w].rearrange("(k p) t -> p k t", p=128))
            S.dma("gpsimd", Xb[i % 3][:, :, 0:w], XRES[:, t0:t0 + w].rearrange("(k p) t -> p k t", p=128))

        def store(i):
            t0, w = TILES[i]
            S.dma("gpsimd", XRES[:, t0:t0 + w].rearrange("(k p) t -> p k t", p=128), Xb[i % 3][:, :, 0:w])

        loads(0)
        S.bar()
        nt = len(TILES)
        for i, (t0, w) in enumerate(TILES):
            sel = 1 if t0 == 0 else 0
            Ab = Abufs[i % 2]
            for m in range(8):
                for k in range(kch):
                    S.op("tensor", lambda e, m=m, k=k, w=w, Ab=Ab: e.matmul(
                        ps[m][:, 0:w], lhsT=Wv[:, k, m * 128:(m + 1) * 128], rhs=Ab[:, k, 0:w],
                        start=(k == 0), stop=(k == kch - 1)))
            if i + 1 < nt:
                loads(i + 1)
            if i >= 1:
                store(i - 1)
            S.bar()
            Xi = Xb[i % 3]
            for m in range(8):
                S.op("vector", lambda e, m=m, w=w, sel=sel, Xi=Xi: e.scalar_tensor_tensor(
                    out=Xi[:, m, 0:w], in0=ps[m][:, 0:w], scalar=MOD[:, gate_idx, m, sel:sel + 1], in1=Xi[:, m, 0:w],
                    op0=ALU.mult, op1=ALU.add))
            S.bar()
        store(nt - 1)
        S.bar()

    try:
      for li in range(n_layers):
        S.dma("sync", STG[0:48, 0:128], b_ada[li].rearrange("(k p) -> k p", p=128))
        S.dma("sync", STG[48:56, 0:128], g_mix[li].rearrange("(k p) -> k p", p=128))
        S.dma("sync", STG[56:64, 0:128], g_ffn[li].rearrange("(k p) -> k p", p=128))
        S.dma("sync", STG[64:68, 0:128], b_glu[li].rearrange("(k p) -> k p", p=128))
        S.dma("sync", STG[68:72, 0:128], g_sgu[li].rearrange("(k p) -> k p", p=128))
        S.bar()
        S.op("tensor", lambda e: e.transpose(ps[7][:, 0:72], STG[0:72, 0:128], IDENT[0:72, 0:72]))
        S.bar()
        S.op("vector", lambda e: e.tensor_copy(out=BADA[:].rearrange("p q k -> p (q k)"), in_=ps[7][:, 0:48]))
        S.op("vector", lambda e: e.tensor_copy(out=GM[:], in_=ps[7][:, 48:56]))
        S.op("vector", lambda e: e.tensor_copy(out=GF[:], in_=ps[7][:, 56:64]))
        S.op("vector", lambda e: e.tensor_copy(out=BGLU[:], in_=ps[7][:, 64:68]))
        S.op("vector", lambda e: e.tensor_copy(out=GSGU[:], in_=ps[7][:, 68:72]))
        S.bar()
        WAs = [STG[:, 0:2048].rearrange("p (k n) -> p k n", n=256), STG[:, 2048:4096].rearrange("p (k n) -> p k n", n=256)]

        def wada_dma(blk):
            WA = WAs[blk % 2]
            src = w_ada[li, :, blk * 256:(blk + 1) * 256].rearrange("(k p) n -> p k n", p=128)
            S.dma("sync", WA[:, 0:4, :], src[:, 0:4, :])
            S.dma("gpsimd", WA[:, 4:8, :], src[:, 4:8, :])

        wada_dma(0)
        S.bar()
        for blk in range(24):
            q, mq = blk // 4, blk % 4
            WA = WAs[blk % 2]
            if blk + 1 < 24:
                wada_dma(blk + 1)
            for mm in range(2):
                m = mq * 2 + mm
                for k in range(8):
                    S.op("tensor", lambda e, m=m, mm=mm, k=k, q=q, WA=WA: e.matmul(
                        ps[0][:, (q * 8 + m) * 2:(q * 8 + m) * 2 + 2], lhsT=WA[:, k, mm * 128:(mm + 1) * 128],
                        rhs=SC[:, k, :], start=(k == 0), stop=(k == 7)))
            S.bar()
        S.op("vector", lambda e: e.tensor_tensor(
            out=MOD[:].rearrange("p q k n -> p (q k) n"), in0=ps[0][:, 0:96].rearrange("p (a n) -> p a n", n=2),
            in1=BADA[:].rearrange("p q k -> p (q k)").unsqueeze(2).to_broadcast([128, 48, 2]), op=ALU.add))
        S.bar()
        S.op("vector", lambda e: e.scalar_tensor_tensor(
            out=A1[:], in0=MOD[:, 1, :, :], scalar=1.0, in1=GM[:].unsqueeze(2).to_broadcast([128, 8, 2]),
            op0=ALU.add, op1=ALU.mult))
        S.op("vector", lambda e: e.scalar_tensor_tensor(
            out=A2[:], in0=MOD[:, 4, :, :], scalar=1.0, in1=GF[:].unsqueeze(2).to_broadcast([128, 8, 2]),
            op0=ALU.add, op1=ALU.mult))
        S.bar()

        ckp("ada")
        norm_mod(A1, 0)

        ckp("norm1")
        WIN = WB[:, 0:8 * 1536].rearrange("p (k n) -> p k n", n=1536)
        load_weight(WIN, w_in[li], 8, 1536, 512)
        for (t0, w) in TILES:
            j0, nj = t0 // 8, w // 8
            for grp in range(2):
                ms = list(range(8)) if grp == 0 else list(range(8, 12))
                for bi, m in enumerate(ms):
                    for k in range(8):
                        S.op("tensor", lambda e, bi=bi, m=m, k=k, w=w, t0=t0: e.matmul(
                            ps[bi][:, 0:w], lhsT=WIN[:, k, m * 128:(m + 1) * 128], rhs=H[:, k, t0:t0 + w],
                            start=(k == 0), stop=(k == 7)))
                if grp == 0:
                    S.join()
                else:
                    S.bar()
                for bi, m in enumerate(ms):
                    if m < 4:
                        S.op("vector", lambda e, bi=bi, m=m, w=w, j0=j0, nj=nj: e.tensor_copy(
                            out=USP[:, m, :, j0:j0 + nj].rearrange("p s j -> p j s"),
                            in_=ps[bi][:, 0:w].rearrange("p (j s) -> p j s", s=8)))
                    elif m < 8:
                        S.op("scalar", lambda e, bi=bi, m=m, w=w: e.activation(
                            out=OB[:, m - 4, 0:w], in_=ps[bi][:, 0:w], func=AF.Gelu_apprx_tanh))
                    else:
                        S.op("scalar", lambda e, bi=bi, m=m, w=w: e.activation(
                            out=TMP[:, m - 8, 0:w], in_=ps[bi][:, 0:w], func=AF.Gelu_apprx_tanh))
                S.bar()
            S.dma_async("sync", UG[:, t0:t0 + w].rearrange("(k p) t -> p k t", p=128), OB[:, 0:4, 0:w])
            S.dma_async("gpsimd", VG[:, t0:t0 + w].rearrange("(k p) t -> p k t", p=128), TMP[:, 0:4, 0:w])
            S.bar()

        S.join()
        ckp("win")
        S.dma("sync", USSMP.rearrange("(q p) s j -> p q s j", p=128), USP)
        S.dma("gpsimd", STG[0:32, 0:16], ssm_d[li].rearrange("(g c) -> g c", c=16))
        S.bar()
        for s_ in range(8):
            S.dma("sync" if s_ % 2 == 0 else "gpsimd", IM[s_ * 16:(s_ + 1) * 16, :, :],
                  USSMP[:, s_, :].rearrange("(g c) j -> c g j", c=16))
        S.bar()

        S.op("vector", lambda e: e.tensor_copy(out=STG[0:32, 128:256].rearrange("p (s c) -> p s c", c=16),
                                               in_=STG[0:32, 0:16].unsqueeze(1).to_broadcast([32, 8, 16])))
        S.bar()
        S.op("tensor", lambda e: e.transpose(ps[7][:, 0:32], STG[0:32, 128:256], IDENT[0:32, 0:32]))
        S.bar()
        S.op("vector", lambda e: e.tensor_copy(out=DV[:], in_=ps[7][:, 0:32]))
        ckp("im2col")
        for dr in range(2):
            CIN1 = STG[:, 0:512].rearrange("p (q n) -> p q n", n=128)
            CIN2 = STG[:, 512:1024].rearrange("p (q n) -> p q n", n=128)
            for hf in range(2):
                lo = slice(hf * 64, hf * 64 + 64)
                csrc = [c_re, c_im] if hf == 0 else [c_im, c_re]
                S.dma("sync", CIN1[:, :, lo], csrc[0][li, dr].rearrange("(q g) c p -> (g c) q p", q=4))
                S.dma("gpsimd", CIN2[:, :, lo], csrc[1][li, dr].rearrange("(q g) c p -> (g c) q p", q=4))
            S.bar()
            for q in range(4):
                S.op("tensor", lambda e, q=q: e.transpose(ps[0][:, q * 128:(q + 1) * 128], CIN1[:, q, :], IDENT[:]))
                S.op("tensor", lambda e, q=q: e.transpose(ps[1][:, q * 128:(q + 1) * 128], CIN2[:, q, :], IDENT[:]))
            S.bar()
            S.op("vector", lambda e: e.tensor_copy(out=CX1.rearrange("p g c -> p (g c)"), in_=ps[0][:]))
            S.op("scalar", lambda e: e.copy(out=CX2.rearrange("p g c -> p (g c)"), in_=ps[1][:]))
            S.bar()
            BIN1 = STG[0:32, 0:2048].rearrange("p (h q c) -> p h q c", h=2, c=16)
            BIN2 = STG[0:32, 2048:4096].rearrange("p (h q c) -> p h q c", h=2, c=16)
            S.dma("sync", BIN1[:, 0, :, :], b_re[li, dr])
            S.dma("gpsimd", BIN1[:, 1, :, :], b_im[li, dr])
            S.dma("sync", BIN2[:, 0, :, :], b_im[li, dr])
            S.dma("gpsimd", BIN2[:, 1, :, :], b_re[li, dr])
            AIN = RS[0:32, 0:256].rearrange("p (a q) -> p a q", q=64)
            S.dma("sync", AIN[:, 0, :], a_re[li, dr])
            S.dma("gpsimd", AIN[:, 1, :], a_re[li, dr])
            S.dma("sync", AIN[:, 2, :], a_im[li, dr])
            S.dma("gpsimd", AIN[:, 3, :], a_im[li, dr])
            S.bar()
            for c_ in range(16):
                S.op("tensor", lambda e, c_=c_: e.transpose(ps[2][:, c_ * 32:(c_ + 1) * 32], BIN1[:, :, :, c_], IDENT[0:32, 0:32]))
                S.op("tensor", lambda e, c_=c_: e.transpose(ps[3][:, c_ * 32:(c_ + 1) * 32], BIN2[:, :, :, c_], IDENT[0:32, 0:32]))
            S.op("tensor", lambda e: e.transpose(ps[4][:, 0:32], RS[0:32, 0:128], IDENT[0:32, 0:32]))
            S.op("tensor", lambda e: e.transpose(ps[4][:, 32:64], RS[0:32, 128:256], IDENT[0:32, 0:32]))
            S.bar()
            S.op("vector", lambda e: e.tensor_copy(out=BX1.rearrange("p g c -> p c g"), in_=ps[2][:].rearrange("p (c g) -> p c g", g=32)))
            S.op("scalar", lambda e: e.copy(out=BX2.rearrange("p g c -> p c g"), in_=ps[3][:].rearrange("p (c g) -> p c g", g=32)))
            S.op("vector", lambda e: e.tensor_copy(out=ARp[:], in_=ps[4][:, 0:32]))
            S.op("vector", lambda e: e.tensor_copy(out=AIp[:], in_=ps[4][:, 32:64]))
            S.dma("sync", LDT[:], log_dt[li, dr].partition_broadcast(128))
            S.bar()
            ckp("pl%d" % dr)
            S.op("scalar", lambda e: e.activation(out=LDT[:], in_=LDT[:], func=AF.Exp))
            S.bar()
            S.op("vector", lambda e: e.tensor_tensor(out=LR[:], in0=ARp[:], in1=LDT[:], op=ALU.mult))
            S.op("gpsimd", lambda e: e.tensor_tensor(out=LI[:], in0=AIp[:], in1=LDT[:], op=ALU.mult))
            S.bar()
            ckp("pb%d" % dr)
            ks = list(range(-8, 0)) + list(range(1, 9))
            for idx, kk in enumerate(ks):
                S.op("vector", lambda e, idx=idx, kk=kk: e.tensor_scalar(
                    out=ARG[:, :, idx], in0=LI[:], scalar1=float(kk), scalar2=None, op0=ALU.mult))
                S.op("gpsimd", lambda e, idx=idx, kk=kk: e.tensor_scalar(
                    out=EARG[:, :, idx], in0=LR[:], scalar1=float(kk), scalar2=None, op0=ALU.mult))
            S.bar()
            ckp("pc%d" % dr)
            S.op("vector", lambda e: e.tensor_scalar(out=ARGC, in0=ARG, scalar1=TWO_PI / 4, scalar2=None, op0=ALU.add))
            S.op("scalar", lambda e: e.activation(out=MAG, in_=EARG, func=AF.Exp))
            S.bar()
            ckp("pd%d" % dr)
            S.op("vector", lambda e: e.tensor_scalar(out=NI, in0=ARG, scalar1=1.0 / TWO_PI, scalar2=None, op0=ALU.mult))
            S.op("gpsimd", lambda e: e.tensor_scalar(out=NIC, in0=ARGC, scalar1=1.0 / TWO_PI, scalar2=None, op0=ALU.mult))
            S.bar()
            ckp("pe%d" % dr)
            S.op("vector", lambda e: e.tensor_copy(out=NF, in_=NI))
            S.op("gpsimd", lambda e: e.tensor_copy(out=NFC, in_=NIC))
            S.bar()
            ckp("pf%d" % dr)
            S.op("vector", lambda e: e.scalar_tensor_tensor(out=ARG, in0=NF, scalar=-TWO_PI, in1=ARG, op0=ALU.mult, op1=ALU.add))
            S.op("vector", lambda e: e.scalar_tensor_tensor(out=ARGC, in0=NFC, scalar=-TWO_PI, in1=ARGC, op0=ALU.mult, op1=ALU.add))
            S.bar()
            S.op("vector", lambda e: e.tensor_scalar(out=ARG, in0=ARG, scalar1=3.1415925, scalar2=-3.1415925, op0=ALU.min, op1=ALU.max))
            S.op("gpsimd", lambda e: e.tensor_scalar(out=ARGC, in0=ARGC, scalar1=3.1415925, scalar2=-3.1415925, op0=ALU.min, op1=ALU.max))
            S.bar()
            ckp("pg%d" % dr)
            S.op("scalar", lambda e: e.activation(out=PIM, in_=ARG, func=AF.Sin))
            S.op("scalar", lambda e: e.activation(out=PRE, in_=ARGC, func=AF.Sin))
            S.bar()
            S.op("vector", lambda e: e.tensor_tensor(out=PIM, in0=PIM, in1=MAG, op=ALU.mult))
            S.op("gpsimd", lambda e: e.tensor_tensor(out=PRE, in0=PRE, in1=MAG, op=ALU.mult))
            S.bar()
            ckp("ph%d" % dr)
            S.op("vector", lambda e: e.tensor_scalar(out=NR[:], in0=PRE[:, :, 8], scalar1=-1.0, scalar2=None, op0=ALU.add))
            S.op("gpsimd", lambda e: e.tensor_tensor(out=DEN[:], in0=ARp[:], in1=ARp[:], op=ALU.mult))
            ckp("c0")
            S.op("vector", lambda e: e.tensor_tensor(out=TA[:], in0=AIp[:], in1=AIp[:], op=ALU.mult))
            ckp("c1")
            S.op("vector", lambda e: e.tensor_tensor(out=DEN[:], in0=DEN[:], in1=TA[:], op=ALU.add))
            ckp("c2")
            S.op("vector", lambda e: e.reciprocal(out=DEN[:], in_=DEN[:]))
            ckp("c3")
            S.op("vector", lambda e: e.tensor_tensor(out=TA[:], in0=NR[:], in1=ARp[:], op=ALU.mult))
            S.op("gpsimd", lambda e: e.tensor_tensor(out=TB[:], in0=PIM[:, :, 8], in1=AIp[:], op=ALU.mult))
            ckp("c4")
            S.op("vector", lambda e: e.tensor_tensor(out=CR[:], in0=TA[:], in1=TB[:], op=ALU.add))
            ckp("c5")
            S.op("vector", lambda e: e.tensor_tensor(out=TA[:], in0=PIM[:, :, 8], in1=ARp[:], op=ALU.mult))
            S.op("gpsimd", lambda e: e.tensor_tensor(out=TB[:], in0=NR[:], in1=AIp[:], op=ALU.mult))
            ckp("c6")
            S.op("vector", lambda e: e.tensor_tensor(out=CI[:], in0=TA[:], in1=TB[:], op=ALU.subtract))
            ckp("c7")
            S.op("vector", lambda e: e.tensor_tensor(out=CR[:], in0=CR[:], in1=DEN[:], op=ALU.mult))
            S.op("gpsimd", lambda e: e.tensor_tensor(out=CI[:], in0=CI[:], in1=DEN[:], op=ALU.mult))
            ckp("c8")
            ckp("pi%d" % dr)
            CRb = CR[:].unsqueeze(2).to_broadcast([128, 32, 8])
            CIb = CI[:].unsqueeze(2).to_broadcast([128, 32, 8])
            S.op("vector", lambda e: e.tensor_tensor(out=QR, in0=PRE[:, :, 0:8], in1=CRb, op=ALU.mult))
            S.op("gpsimd", lambda e: e.tensor_tensor(out=QT, in0=PIM[:, :, 0:8], in1=CIb, op=ALU.mult))
            S.bar()
            S.op("vector", lambda e: e.tensor_tensor(out=QR, in0=QR, in1=QT, op=ALU.subtract))
            S.bar()
            S.op("vector", lambda e: e.tensor_tensor(out=QI, in0=PRE[:, :, 0:8], in1=CIb, op=ALU.mult))
            S.op("gpsimd", lambda e: e.tensor_tensor(out=QT, in0=PIM[:, :, 0:8], in1=CRb, op=ALU.mult))
            S.bar()
            S.op("vector", lambda e: e.tensor_tensor(out=QI, in0=QI, in1=QT, op=ALU.add))
            S.bar()
            S.op("vector", lambda e: e.tensor_scalar(out=QI, in0=QI, scalar1=SIGN[:, 0:1], scalar2=None, op0=ALU.mult))
            S.op("gpsimd", lambda e: e.tensor_scalar(out=PA, in0=PRE[:, :, 8:16], scalar1=NSIGN[:, 0:1], scalar2=None, op0=ALU.mult))
            S.op("scalar", lambda e: e.mul(out=PB, in_=PIM[:, :, 8:16], mul=-1.0))
            S.bar()
            S.op("vector", lambda e: e.tensor_copy(out=ARC[:, 0, :], in_=PRE[:, :, 15]))
            S.op("vector", lambda e: e.tensor_copy(out=ARC[:, 1, :], in_=PRE[:, :, 15]))
            S.op("gpsimd", lambda e: e.tensor_scalar(out=AIC[:, 0, :], in0=PIM[:, :, 15], scalar1=SIGN[:, 0:1], scalar2=None, op0=ALU.mult))
            S.op("gpsimd", lambda e: e.tensor_scalar(out=AIC[:, 1, :], in0=PIM[:, :, 15], scalar1=NSIGN[:, 0:1], scalar2=None, op0=ALU.mult))
            S.bar()
            ckp("prep%d" % dr + "")
            for s_ in range(8):
                qi = (7 - s_) if dr == 0 else s_
                ri = s_ if dr == 0 else (7 - s_)
                S.op("vector", lambda e, s_=s_, qi=qi: e.tensor_tensor(
                    out=LT[:, :, s_, :], in0=BX1, in1=QR[:, :, qi:qi + 1].to_broadcast([128, 32, 16]), op=ALU.mult))
                S.op("gpsimd", lambda e, s_=s_, ri=ri: e.tensor_tensor(
                    out=RT[:, :, s_, :], in0=CX1, in1=PA[:, :, ri:ri + 1].to_broadcast([128, 32, 16]), op=ALU.mult))
            S.bar()
            TL = STG[:].rearrange("p (g s c) -> p g s c", s=8, c=16)
            for s_ in range(8):
                qi = (7 - s_) if dr == 0 else s_
                S.op("vector" if s_ % 2 == 0 else "gpsimd", lambda e, s_=s_, qi=qi: e.tensor_tensor(
                    out=TL[:, :, s_, :], in0=BX2, in1=QI[:, :, qi:qi + 1].to_broadcast([128, 32, 16]), op=ALU.mult))
            S.bar()
            S.op("vector", lambda e: e.tensor_tensor(out=LT, in0=LT, in1=TL, op=ALU.add))
            S.bar()
            for s_ in range(8):
                ri = s_ if dr == 0 else (7 - s_)
                S.op("vector" if s_ % 2 == 0 else "gpsimd", lambda e, s_=s_, ri=ri: e.tensor_tensor(
                    out=TL[:, :, s_, :], in0=CX2, in1=PB[:, :, ri:ri + 1].to_broadcast([128, 32, 16]), op=ALU.mult))
            S.bar()
            S.op("vector", lambda e: e.tensor_tensor(out=RT, in0=RT, in1=TL, op=ALU.add))
            S.bar()
            ckp("lr%d" % dr + "")
            for rnd in range(4):
                for gi in range(8):
                    g = rnd * 8 + gi
                    Lg = LT[:, g, :, :].rearrange("p s c -> p (s c)")
                    Rg = RT[:, g, :, :].rearrange("p s c -> p (s c)")
                    S.op("tensor", lambda e, gi=gi, Lg=Lg, Rg=Rg: e.matmul(
                        ps[gi // 4][:, (gi % 4) * 128:(gi % 4 + 1) * 128], lhsT=Lg, rhs=Rg, start=True, stop=True))
                    S.op("tensor", lambda e, gi=gi, Lg=Lg: e.transpose(
                        ps[2 + gi // 4][:, (gi % 4) * 128:(gi % 4 + 1) * 128], Lg, IDENT[:]))
                S.bar()
                for bk in range(2):
                    g0 = rnd * 8 + bk * 4
                    S.op("vector", lambda e, bk=bk, g0=g0, dr=dr: e.tensor_tensor(
                        out=TOEP[:, g0:g0 + 4, :], in0=ps[bk][:].rearrange("p (g n) -> p g n", n=128),
                        in1=MASKS[:, dr:dr + 1, :].to_broadcast([128, 4, 128]), op=ALU.mult))
                    S.op("scalar", lambda e, bk=bk, g0=g0: e.copy(
                        out=ET[:, g0:g0 + 4, :], in_=ps[2 + bk][:].rearrange("p (g n) -> p g n", n=128)))
                S.bar()
                for bk in range(2):
                    g0 = rnd * 8 + bk * 4
                    S.op("vector", lambda e, bk=bk, g0=g0: e.tensor_copy(
                        out=ESW[:, g0:g0 + 4, 0:64], in_=ps[2 + bk][:].rearrange("p (g n) -> p g n", n=128)[:, :, 64:128]))
                    S.op("vector", lambda e, bk=bk, g0=g0: e.tensor_copy(
                        out=ESW[:, g0:g0 + 4, 64:128], in_=ps[2 + bk][:].rearrange("p (g n) -> p g n", n=128)[:, :, 0:64]))
                S.bar()
            S.op("scalar", lambda e: e.copy(out=RB.rearrange("p g n -> p (g n)"), in_=RT.rearrange("p g s c -> p (g s c)")))
            if dr == 0:
                for g in range(32):
                    S.op("vector", lambda e, g=g: e.scalar_tensor_tensor(
                        out=TOEP[:, g, :], in0=IDENT[:], scalar=DV[:, g:g + 1], in1=TOEP[:, g, :],
                        op0=ALU.mult, op1=ALU.add))
            S.op("gpsimd", lambda e: e.memset(W3[:], 0.0))
            S.bar()
            ckp("tiles%d" % dr + "")
            blocks = [(0, 32)] + [(32 + 64 * b, 64) for b in range(4)]
            order = blocks if dr == 0 else [blocks[0]] + blocks[:0:-1]
            for (jb, nb) in order:
                for half in range(2):
                    for gi in range(16):
                        g = half * 16 + gi
                        for arr in range(2):
                            ii = gi * 2 + arr
                            Em = ET if arr == 0 else ESW
                            S.op("tensor", lambda e, ii=ii, g=g, Em=Em, jb=jb, nb=nb: e.matmul(
                                ps[ii // 8][:, (ii % 8) * 64:(ii % 8) * 64 + nb], lhsT=Em[:, g, :],
                                rhs=IM[:, g, jb:jb + nb], start=True, stop=True))
                    S.bar()
                    for bk in range(4):
                        g0 = half * 16 + bk * 4
                        S.op("vector" if bk % 2 == 0 else "scalar", (lambda e, bk=bk, g0=g0, nb=nb: e.tensor_copy(
                            out=SS[:, 0:nb, :, g0:g0 + 4].rearrange("p j a g -> p g a j"),
                            in_=ps[bk][:].rearrange("p (g a j) -> p g a j", a=2, j=64)[:, :, :, 0:nb]))
                            if bk % 2 == 0 else (lambda e, bk=bk, g0=g0, nb=nb: e.copy(
                            out=SS[:, 0:nb, :, g0:g0 + 4].rearrange("p j a g -> p g a j"),
                            in_=ps[bk][:].rearrange("p (g a j) -> p g a j", a=2, j=64)[:, :, :, 0:nb])))
                    S.bar()
                js = list(range(jb, jb + nb)) if dr == 0 else list(range(jb + nb - 1, jb - 1, -1))
                pjl = None
                for j in js:
                    jl = j - jb
                    Wc = W3[:] if pjl is None else SS[:, pjl, :, :]
                    S.op("vector", lambda e, jl=jl, Wc=Wc: e.tensor_tensor(out=G3[:, 0:2, :], in0=Wc, in1=SS[:, jl, :, :], op=ALU.add))
                    S.op("vector", lambda e: e.tensor_tensor(out=T1[:], in0=G3[:, 0:2, :], in1=ARC[:], op=ALU.mult))
                    S.op("vector", lambda e: e.tensor_tensor(out=T2[:, 0, :], in0=G3[:, 1, :], in1=AIC[:, 0, :], op=ALU.mult))
                    S.op("vector", lambda e: e.tensor_tensor(out=T2[:, 1, :], in0=G3[:, 0, :], in1=AIC[:, 1, :], op=ALU.mult))
                    S.op("vector", lambda e, jl=jl: e.tensor_tensor(out=SS[:, jl, :, :], in0=T1[:], in1=T2[:], op=ALU.add))
                    pjl = jl
                S.bar()
                if dr == 0:
                    S.op("scalar", lambda e, jb=jb: e.copy(out=HH[:, jb, :], in_=W3[:, 0, :]))
                    S.op("scalar", lambda e, jb=jb, nb=nb: e.copy(out=HH[:, jb + 1:jb + nb, :], in_=SS[:, 0:nb - 1, 0, :]))
                else:
                    S.op("scalar", lambda e, jb=jb, nb=nb: e.copy(out=HH[:, jb + nb - 1, :], in_=W3[:, 0, :]))
                    S.op("scalar", lambda e, jb=jb, nb=nb: e.copy(out=HH[:, jb:jb + nb - 1, :], in_=SS[:, 1:nb, 0, :]))
                S.bar()
                S.op("vector", lambda e, pjl=pjl: e.tensor_copy(out=W3[:], in_=SS[:, pjl, :, :]))
                S.bar()
            ckp("rec%d" % dr + "")
            for rnd in range(4):
                for gi in range(8):
                    g = rnd * 8 + gi
                    S.op("tensor", lambda e, gi=gi, g=g: e.matmul(
                        ps[gi][:, 0:NJ], lhsT=TOEP[:, g, :], rhs=IM[:, g, :], start=True, stop=False))
                    S.op("tensor", lambda e, gi=gi, g=g: e.matmul(
                        ps[gi][:, 0:NJ], lhsT=RB[:, g, :], rhs=HH[:, :, g], start=False, stop=True))
                S.bar()
                for gi in range(8):
                    g = rnd * 8 + gi
                    if dr == 0:
                        S.op("vector" if gi % 2 == 0 else "scalar", (lambda e, gi=gi, g=g: e.tensor_copy(out=YS[:, g, :], in_=ps[gi][:, 0:NJ]))
                             if gi % 2 == 0 else (lambda e, gi=gi, g=g: e.copy(out=YS[:, g, :], in_=ps[gi][:, 0:NJ])))
                    else:
                        S.op("vector", lambda e, gi=gi, g=g: e.tensor_tensor(out=YS[:, g, :], in0=YS[:, g, :], in1=ps[gi][:, 0:NJ], op=ALU.add))
                S.bar()
        ckp("read")
        S.dma("sync", YSP[:, :, :], YS)
        S.bar()
        YV = YS.rearrange("p (q s) j -> p q s j", s=8)
        YSPv = YSP.rearrange("(s c) (q g) j -> g c q s j", c=16, g=8)
        for g8 in range(8):
            for q in range(4):
                S.dma(["sync", "gpsimd"][q % 2], YV[g8 * 16:(g8 + 1) * 16, q, :, :], YSPv[g8, :, q, :, :])
            S.bar()
        S.bar()

        ckp("unim")
        WG = WB[:, 0:4 * 512].rearrange("p (k n) -> p k n", n=512)
        load_weight(WG, w_glu[li], 4, 512, 512)
        for (t0, w) in TILES:
            j0, nj = t0 // 8, w // 8
            for q in range(4):
                S.op("scalar", lambda e, q=q, w=w, j0=j0, nj=nj: e.activation(
                    out=TMP[:, q, 0:w].rearrange("p (j s) -> p j s", s=8),
                    in_=YV[:, q, :, j0:j0 + nj].rearrange("p s j -> p j s"), func=AF.Gelu_apprx_tanh))
            S.bar()
            S.op("vector", lambda e, w=w: e.tensor_copy(out=OB[:, 0:4, 0:w], in_=TMP[:, 0:4, 0:w]))
            S.bar()
            for m in range(4):
                for k in range(4):
                    S.op("tensor", lambda e, m=m, k=k, w=w: e.matmul(
                        ps[m][:, 0:w], lhsT=WG[:, k, m * 128:(m + 1) * 128], rhs=OB[:, k, 0:w],
                        start=(k == 0), stop=(k == 3)))
            S.bar()
            for m in range(4):
                S.op("scalar", lambda e, m=m, w=w: e.activation(
                    out=XT[:, m, 0:w], in_=ps[m][:, 0:w], func=AF.Sigmoid, bias=BGLU[:, m:m + 1], scale=1.0))
            S.bar()
            S.join()
            S.op("vector", lambda e, w=w: e.tensor_tensor(out=OB[:, 4:8, 0:w], in0=TMP[:, 0:4, 0:w], in1=XT[:, 0:4, 0:w], op=ALU.mult))
            S.bar()
            S.dma_async("sync", MIX[0:512, t0:t0 + w].rearrange("(k p) t -> p k t", p=128), OB[:, 4:8, 0:w])

        S.join()
        ckp("glu")
        S.dma("sync", TMP[:, 0:4, 0:128], w_sp[li].rearrange("h p q -> p h q"))
        S.dma("gpsimd", BS.rearrange("p h q -> p (h q)"), b_sp[li].rearrange("h q -> (h q)").partition_broadcast(128))
        S.bar()
        for h in range(4):
            S.op("tensor", lambda e, h=h: e.transpose(ps[0][:, h * 128:(h + 1) * 128], TMP[:, h, 0:128], IDENT[:]))
        S.bar()
        S.op("vector", lambda e: e.tensor_copy(out=WST[:], in_=ps[0][:].rearrange("p (h n) -> p h n", n=128)))
        S.bar()
        VGb = [XT[:, 0:4, :], XT[:, 4:8, :]]
        UGb = [OB[:, 0:4, :], ACTB[:, 0:4, :]]

        def sgu_loads(i):
            t0, w = TILES[i]
            S.dma_async("sync", VGb[i % 2][:, :, 0:w], VG[:, t0:t0 + w].rearrange("(k p) t -> p k t", p=128))
            S.dma_async("gpsimd", UGb[i % 2][:, :, 0:w], UG[:, t0:t0 + w].rearrange("(k p) t -> p k t", p=128))

        sgu_loads(0)
        for i, (t0, w) in enumerate(TILES):
            nchk = w // 128
            VGi, UGi = VGb[i % 2], UGb[i % 2]
            S.join()
            if i + 1 < len(TILES):
                sgu_loads(i + 1)
            S.op("scalar", lambda e, w=w, VGi=VGi: e.activation(out=SQ[:, 0:4, 0:w], in_=VGi[:, 0:4, 0:w], func=AF.Square))
            S.bar()
            rstd_from(SQ, 4, w, 1.0 / 512)
            for k in range(4):
                S.op("vector", lambda e, k=k, w=w, VGi=VGi: e.scalar_tensor_tensor(
                    out=TMP[:, k, 0:w], in0=VGi[:, k, 0:w], scalar=GSGU[:, k:k + 1], in1=RS[:, 0:w],
                    op0=ALU.mult, op1=ALU.mult))
            S.bar()
            for ck in range(nchk):
                for h in range(4):
                    ii = ck * 4 + h
                    S.op("tensor", lambda e, ii=ii, ck=ck, h=h: e.transpose(
                        ps[ii // 4][:, (ii % 4) * 128:(ii % 4 + 1) * 128], TMP[:, h, ck * 128:(ck + 1) * 128], IDENT[:]))
            S.bar()
            for ck in range(nchk):
                S.op("vector" if ck % 2 == 0 else "scalar", (lambda e, ck=ck: e.tensor_copy(
                    out=VT[:, ck * 4:(ck + 1) * 4, :], in_=ps[ck][:].rearrange("p (h n) -> p h n", n=128)))
                    if ck % 2 == 0 else (lambda e, ck=ck: e.copy(
                    out=VT[:, ck * 4:(ck + 1) * 4, :], in_=ps[ck][:].rearrange("p (h n) -> p h n", n=128))))
            S.bar()
            for ck in range(nchk):
                for h in range(4):
                    ii = ck * 4 + h
                    S.op("tensor", lambda e, ii=ii, ck=ck, h=h: e.matmul(
                        ps[4 + ck][:, h * 128:(h + 1) * 128], lhsT=VT[:, ii, :], rhs=WST[:, h, :], start=True, stop=True))
            S.bar()
            for ck in range(nchk):
                S.op("vector", lambda e, ck=ck: e.tensor_tensor(
                    out=TMP[:, 4:8, ck * 128:(ck + 1) * 128], in0=ps[4 + ck][:].rearrange("p (h n) -> p h n", n=128),
                    in1=BS, op=ALU.add))
            S.bar()
            S.op("vector", lambda e, w=w, UGi=UGi: e.tensor_tensor(out=OB[:, 4:8, 0:w], in0=TMP[:, 4:8, 0:w], in1=UGi[:, 0:4, 0:w], op=ALU.mult))
            S.bar()
            S.dma_async("sync", MIX[512:1024, t0:t0 + w].rearrange("(k p) t -> p k t", p=128), OB[:, 4:8, 0:w])

        S.join()
        ckp("sgu")
        load_weight(WB[:, 0:8 * D].rearrange("p (k n) -> p k n", n=D), w_out[li], 8, D, 512)
        resid_linear(MIX, 8, 2, [ACTB[:, 0:8, :], ACTB[:, 8:16, :]])

        ckp("wout")
        norm_mod(A2, 3)

        ckp("norm2")
        S.dma("sync", FB[0:9, 0:2 * DFF], w_conv[li].rearrange("a b n -> (a b) n"))
        S.bar()
        for ch in range(44):
            S.op("tensor", lambda e, ch=ch: e.transpose(ps[7][:, ch * 9:(ch + 1) * 9], FB[0:9, ch * 128:(ch + 1) * 128], IDENT[0:9, 0:9]))
        S.bar()
        S.op("vector", lambda e: e.tensor_copy(out=WC[:].rearrange("p t c -> p c t"), in_=ps[7][:, 0:396].rearrange("p (c t) -> p c t", t=9)))
        S.bar()
        WUb = [OBt[:, 0:2048].rearrange("p (k n) -> p k n", n=256), OBt[:, 2048:4096].rearrange("p (k n) -> p k n", n=256)]
        WDv = WB[:, 0:22 * D].rearrange("p (k n) -> p k n", n=D)
        wd_stage = [XTt[:, 0:2816].rearrange("p (k n) -> p k n", n=128), TMPt[:, 0:2816].rearrange("p (k n) -> p k n", n=128)]
        FBb = FB[:].bitcast(BF16)
        UPb = [FBb[:, 0:2304], FBb[:, 2304:4608]]
        DGb = [FBb[:, 4608:6912].rearrange("p (t n) -> p t n", n=128), FBb[:, 6912:9216].rearrange("p (t n) -> p t n", n=128)]
        SG = FB[:, 4608:6912]
        GB = ACTB[:, 0:5, :].rearrange("p a t -> p (a t)")[:, 0:NT]
        stgs = [STG[:, 0:2048].rearrange("p (k n) -> p k n", n=256), STG[:, 2048:4096].rearrange("p (k n) -> p k n", n=256)]

        def bank(m, sl):
            return ps[(sl + 4 * m) % 8]

        def wup_dma(m):
            st = stgs[m % 2]
            S.dma_async("sync", st[:, :, 0:128], w_up[li, :, m * 128:(m + 1) * 128].rearrange("(k p) n -> p k n", p=128))
            S.dma_async("gpsimd", st[:, :, 128:256], w_up[li, :, DFF + m * 128:DFF + (m + 1) * 128].rearrange("(k p) n -> p k n", p=128))

        def prep_w(m, which):
            if which == 0:
                S.op("vector", lambda e, m=m: e.tensor_copy(out=WUb[m % 2], in_=stgs[m % 2]))
            for idx in range(18):
                part, tap = divmod(idx, 9)
                ch = part * 22 + m
                if idx % 2 == 0 and which == 0:
                    S.op("vector", lambda e, idx=idx, tap=tap, ch=ch, m=m: e.tensor_scalar(
                        out=DGb[m % 2][:, idx, :], in0=IDENT[:], scalar1=WC[:, tap, ch:ch + 1], scalar2=None, op0=ALU.mult))
                if idx % 2 == 1 and which == 1:
                    S.op("scalar", lambda e, idx=idx, tap=tap, ch=ch, m=m: e.activation(
                        out=DGb[m % 2][:, idx, :], in_=IDENT[:], func=AF.Copy, scale=WC[:, tap, ch:ch + 1]))

        def up_mm(m, part, tiles, slots):
            for t, sl in zip(tiles, slots):
                t0, w = TILES[t]
                bk = bank(m, sl)
                for k in range(8):
                    S.op("tensor", lambda e, bk=bk, k=k, t0=t0, w=w, part=part, m=m: e.matmul(
                        bk[:, 0:w], lhsT=WUb[m % 2][:, k, part * 128:(part + 1) * 128], rhs=H[:, k, t0:t0 + w],
                        start=(k == 0), stop=(k == 7)))

        def evac_up(m, part, tiles, slots, engs):
            dst = UPb[part]
            for n_, (t, sl) in enumerate(zip(tiles, slots)):
                t0, w = TILES[t]
                bk = bank(m, sl)
                if engs[n_ % len(engs)] == "vector":
                    S.op("vector", lambda e, bk=bk, t0=t0, w=w, dst=dst: e.tensor_copy(out=dst[:, t0:t0 + w], in_=bk[:, 0:w]))
                else:
                    S.op("scalar", lambda e, bk=bk, t0=t0, w=w, dst=dst: e.copy(out=dst[:, t0:t0 + w], in_=bk[:, 0:w]))

        taps9 = [(1, 1)] + [(ky, kx) for ky in range(3) for kx in range(3) if not (ky == 1 and kx == 1)]

        def conv_mm(m, part, tiles, slots):
            src = UPb[part]
            d0 = part * 9
            DGm = DGb[m % 2]
            sv = src[:, LC:NT].rearrange("p (r c) -> p r c", c=64)
            for t, sl in zip(tiles, slots):
                bk = bank(m, sl)
                if t == 0:
                    S.op("tensor", lambda e, bk=bk, DGm=DGm: e.matmul(bk[:, 0:256], lhsT=DGm[:, d0 + 4, :], rhs=src[:, 0:256], start=True, stop=False))
                    S.op("tensor", lambda e, bk=bk, DGm=DGm: e.matmul(bk[:, 1:256], lhsT=DGm[:, d0 + 3, :], rhs=src[:, 0:255], start=False, stop=False))
                    S.op("tensor", lambda e, bk=bk, DGm=DGm: e.matmul(bk[:, 0:255], lhsT=DGm[:, d0 + 5, :], rhs=src[:, 1:256], start=False, stop=True))
                    continue
                R0 = 8 * (t - 1)
                pv = bk[:, 0:512].rearrange("p (r c) -> p r c", c=64)
                for n_, (ky, kx) in enumerate(taps9):
                    dy, dx = ky - 1, kx - 1
                    ra, rb = max(R0, -dy, 0), min(R0 + 8, 32 - max(0, dy))
                    c0, c1 = max(0, -dx), 64 - max(0, dx)
                    S.op("tensor", lambda e, pv=pv, ra=ra, rb=rb, c0=c0, c1=c1, dy=dy, dx=dx, R0=R0, ky=ky, kx=kx, n_=n_, DGm=DGm:
                         e.matmul(pv[:, ra - R0:rb - R0, c0:c1], lhsT=DGm[:, d0 + ky * 3 + kx, :],
                                  rhs=sv[:, ra + dy:rb + dy, c0 + dx:c1 + dx], start=(n_ == 0), stop=(n_ == 8)))

        def silu_ev(m, tiles, slots):
            for t, sl in zip(tiles, slots):
                t0, w = TILES[t]
                bk = bank(m, sl)
                S.op("scalar", lambda e, bk=bk, t0=t0, w=w: e.activation(out=SG[:, t0:t0 + w], in_=bk[:, 0:w], func=AF.Silu))

        def mult_ev(m, tiles, slots):
            for t, sl in zip(tiles, slots):
                t0, w = TILES[t]
                bk = bank(m, sl)
                S.op("vector", lambda e, bk=bk, t0=t0, w=w: e.tensor_tensor(out=GB[:, t0:t0 + w], in0=bk[:, 0:w], in1=SG[:, t0:t0 + w], op=ALU.mult))

        wup_dma(0)
        S.join()
        prep_w(0, 0)
        prep_w(0, 1)
        S.bar()
        wup_dma(1)
        for m in range(22):
            up_mm(m, 0, [0, 1, 2, 3, 4], [0, 1, 2, 3, 4])
            if m > 0:
                mult_ev(m - 1, [3, 4], [2, 3])
            if m % 2 == 0 and m // 2 < 8:
                S.dma_async("sync", wd_stage[(m // 2) % 2], w_down[li][:, (m // 2) * 128:(m // 2 + 1) * 128].rearrange("(k p) n -> p k n", p=128))
            S.bar()
            up_mm(m, 1, [0, 1, 2], [5, 6, 7])
            evac_up(m, 0, [0, 1, 2, 3, 4], [0, 1, 2, 3, 4], ["vector", "scalar"])
            if m > 0:
                S.dma_async("gpsimd", GD[(m - 1) * 128:m * 128, :], GB)
            S.bar()
            up_mm(m, 1, [3, 4], [0, 1])
            conv_mm(m, 0, [0, 1, 2], [2, 3, 4])
            evac_up(m, 1, [0, 1, 2], [5, 6, 7], ["vector", "scalar"])
            S.bar()
            conv_mm(m, 0, [3, 4], [5, 6])
            evac_up(m, 1, [3, 4], [0, 1], ["vector"])
            silu_ev(m, [0, 1, 2], [2, 3, 4])
            if m % 2 == 1 and m // 2 < 8:
                S.op("vector", lambda e, m=m: e.tensor_copy(out=WDv[:, :, (m // 2) * 128:(m // 2 + 1) * 128], in_=wd_stage[(m // 2) % 2]))
            S.join()
            conv_mm(m, 1, [0, 1, 2], [0, 1, 7])
            silu_ev(m, [3, 4], [5, 6])
            if m + 1 < 22:
                prep_w(m + 1, 0)
            S.bar()
            conv_mm(m, 1, [3, 4], [2, 3])
            mult_ev(m, [0, 1, 2], [0, 1, 7])
            if m + 1 < 22:
                prep_w(m + 1, 1)
            if m + 2 < 22:
                wup_dma(m + 2)
            S.bar()
        mult_ev(21, [3, 4], [2, 3])
        S.bar()
        S.dma_async("gpsimd", GD[21 * 128:22 * 128, :], GB)
        S.join()

        ckp("ffnup")
        resid_linear(GD, 22, 5, [ACTB, FB[:].bitcast(BF16)[:, 0:11264].rearrange("p (k t) -> p k t", t=512)])

    except _Stop:
        pass
    S.bar()
    if dbg:
        DF = nc.dram_tensor("DBGF", [128, 32768], F32, kind="ExternalOutput").ap()
        DB = nc.dram_tensor("DBGB", [128, 40960], BF16, kind="ExternalOutput").ap()
        off = 0
        for t_, n_ in [(HRAW[:], 9216), (XTt[:], 4096), (TMPt[:], 4096), (FB[:], 9216), (STG[:], 4096), (RS[:], 512),
                       (MOD[:].rearrange("p q k n -> p (q k n)"), 96), (A1[:].rearrange("p k n -> p (k n)"), 16),
                       (A2[:].rearrange("p k n -> p (k n)"), 16), (W3[:].rearrange("p a g -> p (a g)"), 64),
                       (ARC[:].rearrange("p a g -> p (a g)"), 64), (AIC[:].rearrange("p a g -> p (a g)"), 64),
                       (CR[:], 32), (CI[:], 32), (DV[:], 32), (LR[:], 32), (LI[:], 32)]:
            S.dma("sync", DF[:, off:off + n_], t_)
            off += n_
        offb = 0
        for t_, n_ in [(WB, 22528), (OBt[:], 4096), (ACTBt[:], 11264)]:
            S.dma("gpsimd", DB[:, offb:offb + n_], t_)
            offb += n_
        S.bar()
    for b in range(16):
        t0 = LC + b * 128
        S.dma("sync", XT[:, :, 0:128], XRES[:, t0:t0 + 128].rearrange("(k p) t -> p k t", p=128))
        S.bar()
        S.op("scalar", lambda e: e.activation(out=SQ[:, :, 0:128], in_=XT[:, :, 0:128], func=AF.Square))
        S.bar()
        rstd_from(SQ, 8, 128, 1.0 / D)
        for k in range(8):
            S.op("vector", lambda e, k=k: e.scalar_tensor_tensor(
                out=TMP[:, k, 0:128], in0=XT[:, k, 0:128], scalar=GFIN[:, k:k + 1], in1=RS[:, 0:128],
                op0=ALU.mult, op1=ALU.mult))
        S.bar()
        for k in range(8):
            S.op("tensor", lambda e, k=k: e.transpose(ps[k // 4][:, (k % 4) * 128:(k % 4 + 1) * 128],
                                                       TMP[:, k, 0:128], IDENT[:]))
        S.bar()
        S.op("vector", lambda e: e.tensor_copy(out=XT[:, 0:4, 0:128], in_=ps[0][:].rearrange("p (k t) -> p k t", t=128)))
        S.op("scalar", lambda e: e.copy(out=XT[:, 4:8, 0:128], in_=ps[1][:].rearrange("p (k t) -> p k t", t=128)))
        S.bar()
        S.dma("sync", out[b * 128:(b + 1) * 128, :].rearrange("t (k d) -> t k d", d=128), XT[:, :, 0:128])
        S.bar()

    S.emit()
    es.close()
    return nc


_CONST = None


def _consts():
    ident = np.eye(128, dtype=np.float32)
    sp = np.arange(128) // 16
    m0 = (sp[None, :] >= sp[:, None]).astype(np.float32)
    m1 = (sp[None, :] <= sp[:, None]).astype(np.float32)
    return ident, np.stack([m0, m1])


def kernel(n_layers=4, **inputs):
    nc = build_nc(n_layers)
    ident, mask = _consts()
    in_maps = []
    for b in range(8):
        m = {}
        for k, v in inputs.items():
            v = np.asarray(v)
            if k in ("x", "c", "ctx"):
                m[k] = np.ascontiguousarray(v[b], dtype=np.float32)
            else:
                m[k] = np.ascontiguousarray(v, dtype=np.float32)
        m["ident"] = ident
        m["mask"] = mask
        in_maps.append(m)
    res = run_bass_kernel_spmd(nc, in_maps, core_ids=list(range(8)))
    return np.stack([np.asarray(r["out"], dtype=np.float32) for r in res.results], axis=0)
```

```python
import numpy as np
from contextlib import ExitStack
import concourse.bass as bass
import concourse.mybir as mybir
from concourse.bass_utils import run_bass_kernel_spmd

F32, BF16, I32 = mybir.dt.float32, mybir.dt.bfloat16, mybir.dt.int32
AF = mybir.ActivationFunctionType
ALU = mybir.AluOpType

D = 1024
NT = 2304
LC = 256
LL = 2048
DFF = 2816
EPS = 1e-6
NJ = 288
TILES = [(0, 256)] + [(256 + 512 * i, 512) for i in range(4)]
TWO_PI = 6.283185307179586


class Sched:
    def __init__(self, nc):
        self.nc = nc
        self.stages = [[]]
        self.join_at = set()

    def op(self, eng, fn, dma=False):
        self.stages[-1].append((eng, dma, fn))

    def dma(self, eng, out, in_, slow=False):
        self.op(eng, lambda e, o=out, i=in_: e.dma_start(out=o, in_=i), dma=True)

    def dma_async(self, eng, out, in_):
        self.op(eng, lambda e, o=out, i=in_: e.dma_start(out=o, in_=i), dma="async")

    def bar(self):
        if self.stages[-1]:
            self.stages.append([])

    def join(self):
        self.bar()
        self.join_at.add(len(self.stages) - 1)

    def emit(self):
        nc = self.nc
        self.bar()
        names = ["c_scalar", "c_vector", "c_gpsimd", "c_tensor", "d_sync", "d_scalar", "d_gpsimd", "a_sync", "a_gpsimd"]

        def semname(eng, dma):
            if dma == "async":
                return "a_" + eng
            return ("d_" if dma else "c_") + eng

        cum = []
        cur = {n: 0 for n in names}
        for st in self.stages:
            cum.append(dict(cur))
            for (eng, dma, _) in st:
                cur[semname(eng, dma)] += 16 if dma else 1
        final = dict(cur)
        with ExitStack() as es:
            sems = {n: es.enter_context(nc.semaphore(n)) for n in names}
            block = es.enter_context(nc.Block())

            def make(engname):
                def body(eng):
                    waited = {n: 0 for n in names}
                    joined = {n: 0 for n in names}
                    for k, st in enumerate(self.stages):
                        if k in self.join_at:
                            for n in names:
                                if n.startswith("a_"):
                                    joined[n] = cum[k][n]
                        mine = [o for o in st if o[0] == engname]
                        if not mine:
                            continue
                        for n in names:
                            tgt = joined[n] if n.startswith("a_") else cum[k][n]
                            if tgt > waited[n]:
                                eng.wait_ge(sems[n], tgt)
                                waited[n] = tgt
                        for (_, dma, fn) in mine:
                            ins = fn(eng)
                            ins.then_inc(sems[semname(engname, dma)], 16 if dma else 1)
                    if engname == "sync":
                        for n in names:
                            if final[n] > waited[n]:
                                eng.wait_ge(sems[n], final[n])
                return body

            block.sync(make("sync"))
            block.scalar(make("scalar"))
            block.vector(make("vector"))
            block.gpsimd(make("gpsimd"))
            block.tensor(make("tensor"))


class _Stop(Exception):
    pass


def build_nc(n_layers, n_wl=4, stop=None, dbg=False):
    nc = bass.Bass("TRN2", target_bir_lowering=False)
    S = Sched(nc)
    W = n_wl

    def ckp(name):
        S.bar()
        if stop == name:
            raise _Stop()

    def din(name, shape):
        return nc.dram_tensor(name, list(shape), F32, kind="ExternalInput").ap()

    x_in = din("x", [LL, D])
    c_in = din("c", [D])
    ctx_in = din("ctx", [LC, D])
    cctx_in = din("c_ctx", [D])
    w_ada = din("w_ada", [W, D, 6 * D])
    b_ada = din("b_ada", [W, 6 * D])
    g_mix = din("g_mix", [W, D])
    w_in = din("w_in", [W, D, 1536])
    a_re = din("ssm_a_re", [W, 2, 32, 64])
    a_im = din("ssm_a_im", [W, 2, 32, 64])
    b_re = din("ssm_b_re", [W, 2, 32, 64, 16])
    b_im = din("ssm_b_im", [W, 2, 32, 64, 16])
    c_re = din("ssm_c_re", [W, 2, 32, 16, 64])
    c_im = din("ssm_c_im", [W, 2, 32, 16, 64])
    log_dt = din("ssm_log_dt", [W, 2, 32])
    ssm_d = din("ssm_d", [W, 512])
    w_glu = din("w_glu", [W, 512, 512])
    b_glu = din("b_glu", [W, 512])
    g_sgu = din("g_sgu", [W, 512])
    w_sp = din("w_spatial", [W, 4, 128, 128])
    b_sp = din("b_spatial", [W, 4, 128])
    w_out = din("w_out", [W, D, D])
    g_ffn = din("g_ffn", [W, D])
    w_up = din("w_up", [W, D, 2 * DFF])
    w_conv = din("w_conv", [W, 3, 3, 2 * DFF])
    w_down = din("w_down", [W, DFF, D])
    g_final = din("g_final", [D])
    ident_in = din("ident", [128, 128])
    mask_in = din("mask", [2, 128, 128])
    out = nc.dram_tensor("out", [LL, D], F32, kind="ExternalOutput").ap()

    SK = dict(kind="ExternalOutput") if dbg else {}
    XRES = nc.dram_tensor("XRES", [D, NT], F32, **SK).ap()
    USSMP = nc.dram_tensor("USSMP", [512, 8, NJ], BF16, **SK).ap()
    YSP = nc.dram_tensor("YSP", [128, 32, NJ], F32, **SK).ap()
    UG = nc.dram_tensor("UG", [512, NT], BF16, **SK).ap()
    VG = nc.dram_tensor("VG", [512, NT], F32, **SK).ap()
    MIX = nc.dram_tensor("MIX", [D, NT], BF16, **SK).ap()
    GD = nc.dram_tensor("GD", [DFF, NT], BF16, **SK).ap()

    es = ExitStack()

    def sb(name, shape, dt=F32):
        return es.enter_context(nc.sbuf_tensor(name, list(shape), dt))

    ps = [es.enter_context(nc.psum_tensor("ps%d" % i, [128, 512], F32)) for i in range(8)]

    IDENT = sb("IDENT", [128, 128])
    ONESB = sb("ONESB", [128, 128], BF16)
    MASKS = sb("MASKS", [128, 2, 128])
    SIGN = sb("SIGN", [128, 1])
    NSIGN = sb("NSIGN", [128, 1])
    SC = sb("SC", [128, 8, 2])
    MOD = sb("MOD", [128, 6, 8, 2])
    BADA = sb("BADA", [128, 6, 8])
    GM = sb("GM", [128, 8])
    GF = sb("GF", [128, 8])
    GFIN = sb("GFIN", [128, 8])
    A1 = sb("A1", [128, 8, 2])
    A2 = sb("A2", [128, 8, 2])
    HRAW = sb("HRAW", [128, 9216])
    H = HRAW[:].bitcast(BF16).rearrange("p (k t) -> p k t", t=NT)
    YS = HRAW[:].rearrange("p (g j) -> p g j", j=NJ)
    STG = sb("STG", [128, 4096])
    SQ = STG[:, 0:2048].bitcast(BF16).rearrange("p (k t) -> p k t", t=512)
    WBt = sb("WB", [128, 22528], BF16)
    WB = WBt[:]
    TOEP = WB[:, 0:4096].rearrange("p (g n) -> p g n", n=128)
    ET = WB[:, 4096:8192].rearrange("p (g n) -> p g n", n=128)
    ESW = WB[:, 8192:12288].rearrange("p (g n) -> p g n", n=128)
    IM = WB[:, 12288:21504].rearrange("p (g j) -> p g j", j=NJ)
    XTt = sb("XT", [128, 4096])
    XT = XTt[:].rearrange("p (k t) -> p k t", t=512)
    LT = XTt[:].rearrange("p (g s c) -> p g s c", s=8, c=16)
    SS = XTt[:].rearrange("p (j a g) -> p j a g", a=2, g=32)
    TMPt = sb("TMP", [128, 4096])
    TMP = TMPt[:].rearrange("p (k t) -> p k t", t=512)
    RT = TMPt[:].rearrange("p (g s c) -> p g s c", s=8, c=16)
    RS = sb("RS", [128, 512])
    OBt = sb("OB", [128, 4096], BF16)
    OB = OBt[:].rearrange("p (k t) -> p k t", t=512)
    RB = OBt[:].rearrange("p (g n) -> p g n", n=128)
    ACTBt = sb("ACTB", [128, 11264], BF16)
    ACTB = ACTBt[:].rearrange("p (k t) -> p k t", t=512)
    USP = ACTBt[:, 0:9216].rearrange("p (q s j) -> p q s j", s=8, j=NJ)
    HH = ACTBt[:, 0:9216].rearrange("p (j g) -> p j g", g=32)
    FB = sb("FB", [128, 9216])
    UPG = FB[:, 0:2304]; UPV = FB[:, 2304:4608]; CG = FB[:, 4608:6912]; CV = FB[:, 6912:9216]
    def ftab(i):
        return FB[:, i * 512:(i + 1) * 512].rearrange("p (g k) -> p g k", k=16)
    ARG, ARGC, EARG, NF, NFC, PRE, PIM, MAG, BX1, BX2, CX1, CX2 = [ftab(i) for i in range(12)]
    def qtab(i):
        return FB[:, 6144 + i * 256:6144 + (i + 1) * 256].rearrange("p (g k) -> p g k", k=8)
    QR, QI, QT, PA, PB = [qtab(i) for i in range(5)]
    NI = FB[:, 7424:7936].bitcast(I32).rearrange("p (g k) -> p g k", k=16)
    NIC = FB[:, 7936:8448].bitcast(I32).rearrange("p (g k) -> p g k", k=16)
    BS = FB[:, 0:512].rearrange("p (h q) -> p h q", q=128)
    VT = STG[:, 2048:4096].bitcast(BF16).rearrange("p (a n) -> p a n", n=128)
    W3 = sb("W3", [128, 2, 32])
    G3 = sb("G3", [128, 3, 32])
    T1 = sb("T1", [128, 2, 32])
    T2 = sb("T2", [128, 2, 32])
    ARp = sb("ARp", [128, 32]); AIp = sb("AIp", [128, 32]); LDT = sb("LDT", [128, 32])
    LR = sb("LR", [128, 32]); LI = sb("LI", [128, 32])
    CR = sb("CR", [128, 32]); CI = sb("CI", [128, 32]); NR = sb("NR", [128, 32]); DEN = sb("DEN", [128, 32])
    TA = sb("TA", [128, 32]); TB = sb("TB", [128, 32])
    ARC = sb("ARC", [128, 2, 32]); AIC = sb("AIC", [128, 2, 32])
    DV = sb("DV", [128, 32])
    BGLU = sb("BGLU", [128, 4]); GSGU = sb("GSGU", [128, 4])
    WST = sb("WST", [128, 4, 128], BF16)
    WC = sb("WC", [128, 9, 44])

    S.dma("sync", IDENT[:], ident_in[:, :])
    S.dma("sync", MASKS[:], mask_in.rearrange("m p q -> p m q"))
    S.op("vector", lambda e: e.memset(ONESB[:], 1.0))
    S.op("vector", lambda e: e.memset(SIGN[0:64, :], -1.0))
    S.op("vector", lambda e: e.memset(SIGN[64:128, :], 1.0))
    S.op("vector", lambda e: e.memset(NSIGN[0:64, :], 1.0))
    S.op("vector", lambda e: e.memset(NSIGN[64:128, :], -1.0))
    S.dma("sync", STG[0:8, 0:128], c_in.rearrange("(k p) -> k p", p=128))
    S.dma("sync", STG[8:16, 0:128], cctx_in.rearrange("(k p) -> k p", p=128))
    S.dma("sync", STG[16:24, 0:128], g_final.rearrange("(k p) -> k p", p=128))
    S.bar()
    S.op("tensor", lambda e: e.transpose(ps[7][:, 0:24], STG[0:24, 0:128], IDENT[0:24, 0:24]))
    S.bar()
    S.op("vector", lambda e: e.tensor_copy(out=SC[:, :, 0], in_=ps[7][:, 0:8]))
    S.op("vector", lambda e: e.tensor_copy(out=SC[:, :, 1], in_=ps[7][:, 8:16]))
    S.op("vector", lambda e: e.tensor_copy(out=GFIN[:], in_=ps[7][:, 16:24]))
    S.bar()
    S.op("scalar", lambda e: e.activation(out=SC[:], in_=SC[:], func=AF.Silu))
    S.bar()

    def in_transpose(src, nblk, tok0):
        for b in range(nblk):
            S.dma("sync", TMP[:, :, 0:128], src[b * 128:(b + 1) * 128, :].rearrange("t (k d) -> t k d", d=128))
            S.bar()
            for k in range(8):
                S.op("tensor", lambda e, k=k: e.transpose(ps[k // 4][:, (k % 4) * 128:(k % 4 + 1) * 128],
                                                           TMP[:, k, 0:128], IDENT[:]))
            S.bar()
            S.op("vector", lambda e: e.tensor_copy(out=XT[:, 0:4, 0:128], in_=ps[0][:].rearrange("p (k t) -> p k t", t=128)))
            S.op("scalar", lambda e: e.copy(out=XT[:, 4:8, 0:128], in_=ps[1][:].rearrange("p (k t) -> p k t", t=128)))
            S.bar()
            t0 = tok0 + b * 128
            S.dma("sync", XRES[:, t0:t0 + 128].rearrange("(k p) t -> p k t", p=128), XT[:, :, 0:128])
            S.bar()

    in_transpose(ctx_in, 2, 0)
    in_transpose(x_in, 16, 256)

    def load_weight(dst, wap, kch, ncols, cb):
        for c0 in range(0, ncols, cb):
            stg = STG[:, 0:kch * cb].rearrange("p (k n) -> p k n", n=cb)
            S.dma("sync", stg, wap[:, c0:c0 + cb].rearrange("(k p) n -> p k n", p=128))
            S.bar()
            S.op("vector", lambda e, stg=stg, c0=c0: e.tensor_copy(out=dst[:, :, c0:c0 + cb], in_=stg))
            S.bar()

    def rstd_from(src_sq, nk, w, inv_n):
        for k in range(nk):
            S.op("tensor", lambda e, k=k: e.matmul(ps[0][:, 0:w], lhsT=ONESB[:], rhs=src_sq[:, k, 0:w],
                                                    start=(k == 0), stop=(k == nk - 1)))
        S.bar()
        S.op("scalar", lambda e: e.activation(out=RS[:, 0:w], in_=ps[0][:, 0:w], func=AF.Sqrt, bias=EPS, scale=inv_n))
        S.bar()
        S.op("vector", lambda e: e.reciprocal(out=RS[:, 0:w], in_=RS[:, 0:w]))
        S.bar()

    def norm_mod(Acoef, which_shift, issue=None, convert=None):
        Xn = [XT, STG[:].rearrange("p (k t) -> p k t", t=512)]

        def xload(i):
            t0, w = TILES[i]
            S.dma_async("sync", Xn[i % 2][:, 0:4, 0:w], XRES[0:512, t0:t0 + w].rearrange("(k p) t -> p k t", p=128))
            S.dma_async("gpsimd", Xn[i % 2][:, 4:8, 0:w], XRES[512:1024, t0:t0 + w].rearrange("(k p) t -> p k t", p=128))

        xload(0)
        for i, (t0, w) in enumerate(TILES):
            sel = 1 if t0 == 0 else 0
            Xi = Xn[i % 2]
            S.join()
            if i + 1 < len(TILES):
                xload(i + 1)
            if issue and i in issue:
                issue[i]()
            if convert and i in convert:
                convert[i]()
            S.op("scalar", lambda e, w=w, Xi=Xi: e.activation(out=OB[:, :, 0:w], in_=Xi[:, :, 0:w], func=AF.Square))
            S.bar()
            rstd_from(OB, 8, w, 1.0 / D)
            for k in range(8):
                S.op("vector", lambda e, k=k, w=w, sel=sel, Xi=Xi: e.scalar_tensor_tensor(
                    out=TMP[:, k, 0:w], in0=Xi[:, k, 0:w], scalar=Acoef[:, k, sel:sel + 1], in1=RS[:, 0:w],
                    op0=ALU.mult, op1=ALU.mult))
            S.bar()
            for k in range(8):
                S.op("scalar", lambda e, k=k, w=w, sel=sel, t0=t0: e.activation(
                    out=H[:, k, t0:t0 + w], in_=TMP[:, k, 0:w], func=AF.Identity,
                    bias=MOD[:, which_shift, k, sel:sel + 1], scale=1.0))
            S.bar()

    def resid_linear(src_dram, kch, gate_idx, Abufs, Wv):
        Xb = [XT, TMP, STG[:].rearrange("p (k t) -> p k t", t=512)]

        def loads(i):
            t0, w = TILES[i]
            S.dma("sync", Abufs[i % 2][:, 0:kch, 0:w], src_dram[:, t0:t0 + w].rearrange("(k p) t -> p k t", p=128))
            S.dma("gpsimd", Xb[i % 3][:, :, 0:w], XRES[:, t0:t0 + w].rearrange("(k p) t -> p k t", p=128))

        def store(i):
            t0, w = TILES[i]
            S.dma("gpsimd", XRES[:, t0:t0 + w].rearrange("(k p) t -> p k t", p=128), Xb[i % 3][:, :, 0:w])

        loads(0)
        S.bar()
        nt = len(TILES)
        for i, (t0, w) in enumerate(TILES):
            sel = 1 if t0 == 0 else 0
            Ab = Abufs[i % 2]
            for m in range(8):
                for k in range(kch):
                    S.op("tensor", lambda e, m=m, k=k, w=w, Ab=Ab: e.matmul(
                        ps[m][:, 0:w], lhsT=Wv[:, k, m * 128:(m + 1) * 128], rhs=Ab[:, k, 0:w],
                        start=(k == 0), stop=(k == kch - 1)))
            if i + 1 < nt:
                loads(i + 1)
            if i >= 1:
                store(i - 1)
            S.bar()
            Xi = Xb[i % 3]
            for m in range(8):
                S.op("vector", lambda e, m=m, w=w, sel=sel, Xi=Xi: e.scalar_tensor_tensor(
                    out=Xi[:, m, 0:w], in0=ps[m][:, 0:w], scalar=MOD[:, gate_idx, m, sel:sel + 1], in1=Xi[:, m, 0:w],
                    op0=ALU.mult, op1=ALU.add))
            S.bar()
        store(nt - 1)
        S.bar()

    try:
      for li in range(n_layers):
        S.dma("sync", STG[0:48, 0:128], b_ada[li].rearrange("(k p) -> k p", p=128))
        S.dma("sync", STG[48:56, 0:128], g_mix[li].rearrange("(k p) -> k p", p=128))
        S.dma("sync", STG[56:64, 0:128], g_ffn[li].rearrange("(k p) -> k p", p=128))
        S.dma("sync", STG[64:68, 0:128], b_glu[li].rearrange("(k p) -> k p", p=128))
        S.dma("sync", STG[68:72, 0:128], g_sgu[li].rearrange("(k p) -> k p", p=128))
        S.bar()
        S.op("tensor", lambda e: e.transpose(ps[7][:, 0:72], STG[0:72, 0:128], IDENT[0:72, 0:72]))
        S.bar()
        S.op("vector", lambda e: e.tensor_copy(out=BADA[:].rearrange("p q k -> p (q k)"), in_=ps[7][:, 0:48]))
        S.op("vector", lambda e: e.tensor_copy(out=GM[:], in_=ps[7][:, 48:56]))
        S.op("vector", lambda e: e.tensor_copy(out=GF[:], in_=ps[7][:, 56:64]))
        S.op("vector", lambda e: e.tensor_copy(out=BGLU[:], in_=ps[7][:, 64:68]))
        S.op("vector", lambda e: e.tensor_copy(out=GSGU[:], in_=ps[7][:, 68:72]))
        S.bar()
        WAs = [STG[:, 0:2048].rearrange("p (k n) -> p k n", n=256), STG[:, 2048:4096].rearrange("p (k n) -> p k n", n=256)]

        def wada_dma(blk):
            WA = WAs[blk % 2]
            src = w_ada[li, :, blk * 256:(blk + 1) * 256].rearrange("(k p) n -> p k n", p=128)
            S.dma("sync", WA[:, 0:4, :], src[:, 0:4, :])
            S.dma("gpsimd", WA[:, 4:8, :], src[:, 4:8, :])

        wada_dma(0)
        S.bar()
        for blk in range(24):
            q, mq = blk // 4, blk % 4
            WA = WAs[blk % 2]
            if blk + 1 < 24:
                wada_dma(blk + 1)
            for mm in range(2):
                m = mq * 2 + mm
                for k in range(8):
                    S.op("tensor", lambda e, m=m, mm=mm, k=k, q=q, WA=WA: e.matmul(
                        ps[0][:, (q * 8 + m) * 2:(q * 8 + m) * 2 + 2], lhsT=WA[:, k, mm * 128:(mm + 1) * 128],
                        rhs=SC[:, k, :], start=(k == 0), stop=(k == 7)))
            S.bar()
        S.op("vector", lambda e: e.tensor_tensor(
            out=MOD[:].rearrange("p q k n -> p (q k) n"), in0=ps[0][:, 0:96].rearrange("p (a n) -> p a n", n=2),
            in1=BADA[:].rearrange("p q k -> p (q k)").unsqueeze(2).to_broadcast([128, 48, 2]), op=ALU.add))
        S.bar()
        S.op("vector", lambda e: e.scalar_tensor_tensor(
            out=A1[:], in0=MOD[:, 1, :, :], scalar=1.0, in1=GM[:].unsqueeze(2).to_broadcast([128, 8, 2]),
            op0=ALU.add, op1=ALU.mult))
        S.op("vector", lambda e: e.scalar_tensor_tensor(
            out=A2[:], in0=MOD[:, 4, :, :], scalar=1.0, in1=GF[:].unsqueeze(2).to_broadcast([128, 8, 2]),
            op0=ALU.add, op1=ALU.mult))
        S.bar()

        ckp("ada")
        WIN = WB[:, 0:8 * 1536].rearrange("p (k n) -> p k n", n=1536)
        FBs = [FB[:, 0:4096].rearrange("p (k n) -> p k n", n=512), FB[:, 4096:8192].rearrange("p (k n) -> p k n", n=512)]

        def win_issue(bk):
            def f():
                src = w_in[li][:, bk * 512:(bk + 1) * 512].rearrange("(k p) n -> p k n", p=128)
                S.dma_async("sync", FBs[bk % 2][:, 0:4, :], src[:, 0:4, :])
                S.dma_async("gpsimd", FBs[bk % 2][:, 4:8, :], src[:, 4:8, :])
            return f

        def win_conv(bk):
            def f():
                S.op("vector", lambda e: e.tensor_copy(out=WIN[:, :, bk * 512:(bk + 1) * 512], in_=FBs[bk % 2]))
            return f

        norm_mod(A1, 0, issue={0: win_issue(0), 1: win_issue(1), 2: win_issue(2)},
                 convert={1: win_conv(0), 2: win_conv(1), 3: win_conv(2)})

        ckp("norm1")
        for (t0, w) in TILES:
            j0, nj = t0 // 8, w // 8
            for grp in range(2):
                ms = list(range(8)) if grp == 0 else list(range(8, 12))
                for bi, m in enumerate(ms):
                    for k in range(8):
                        S.op("tensor", lambda e, bi=bi, m=m, k=k, w=w, t0=t0: e.matmul(
                            ps[bi][:, 0:w], lhsT=WIN[:, k, m * 128:(m + 1) * 128], rhs=H[:, k, t0:t0 + w],
                            start=(k == 0), stop=(k == 7)))
                if grp == 0:
                    S.join()
                else:
                    S.bar()
                for bi, m in enumerate(ms):
                    if m < 4:
                        S.op("vector", lambda e, bi=bi, m=m, w=w, j0=j0, nj=nj: e.tensor_copy(
                            out=USP[:, m, :, j0:j0 + nj].rearrange("p s j -> p j s"),
                            in_=ps[bi][:, 0:w].rearrange("p (j s) -> p j s", s=8)))
                    elif m < 8:
                        S.op("scalar", lambda e, bi=bi, m=m, w=w: e.activation(
                            out=OB[:, m - 4, 0:w], in_=ps[bi][:, 0:w], func=AF.Gelu_apprx_tanh))
                    else:
                        S.op("scalar", lambda e, bi=bi, m=m, w=w: e.activation(
                            out=TMP[:, m - 8, 0:w], in_=ps[bi][:, 0:w], func=AF.Gelu_apprx_tanh))
                S.bar()
            S.dma_async("sync", UG[:, t0:t0 + w].rearrange("(k p) t -> p k t", p=128), OB[:, 0:4, 0:w])
            S.dma_async("gpsimd", VG[:, t0:t0 + w].rearrange("(k p) t -> p k t", p=128), TMP[:, 0:4, 0:w])
            S.bar()

        S.join()
        ckp("win")
        S.dma("sync", USSMP.rearrange("(q p) s j -> p q s j", p=128), USP)
        S.dma("gpsimd", STG[0:32, 0:16], ssm_d[li].rearrange("(g c) -> g c", c=16))
        S.bar()
        for s_ in range(8):
            S.dma("sync" if s_ % 2 == 0 else "gpsimd", IM[s_ * 16:(s_ + 1) * 16, :, :],
                  USSMP[:, s_, :].rearrange("(g c) j -> c g j", c=16))
        S.bar()

        S.op("vector", lambda e: e.tensor_copy(out=STG[0:32, 128:256].rearrange("p (s c) -> p s c", c=16),
                                               in_=STG[0:32, 0:16].unsqueeze(1).to_broadcast([32, 8, 16])))
        S.bar()
        S.op("tensor", lambda e: e.transpose(ps[7][:, 0:32], STG[0:32, 128:256], IDENT[0:32, 0:32]))
        S.bar()
        S.op("vector", lambda e: e.tensor_copy(out=DV[:], in_=ps[7][:, 0:32]))
        ckp("im2col")
        for dr in range(2):
            CIN1 = STG[:, 0:512].rearrange("p (q n) -> p q n", n=128)
            CIN2 = STG[:, 512:1024].rearrange("p (q n) -> p q n", n=128)
            for hf in range(2):
                lo = slice(hf * 64, hf * 64 + 64)
                csrc = [c_re, c_im] if hf == 0 else [c_im, c_re]
                S.dma("sync", CIN1[:, :, lo], csrc[0][li, dr].rearrange("(q g) c p -> (g c) q p", q=4))
                S.dma("gpsimd", CIN2[:, :, lo], csrc[1][li, dr].rearrange("(q g) c p -> (g c) q p", q=4))
            S.bar()
            for q in range(4):
                S.op("tensor", lambda e, q=q: e.transpose(ps[0][:, q * 128:(q + 1) * 128], CIN1[:, q, :], IDENT[:]))
                S.op("tensor", lambda e, q=q: e.transpose(ps[1][:, q * 128:(q + 1) * 128], CIN2[:, q, :], IDENT[:]))
            S.bar()
            S.op("vector", lambda e: e.tensor_copy(out=CX1.rearrange("p g c -> p (g c)"), in_=ps[0][:]))
            S.op("scalar", lambda e: e.copy(out=CX2.rearrange("p g c -> p (g c)"), in_=ps[1][:]))
            S.bar()
            BIN1 = STG[0:32, 0:2048].rearrange("p (h q c) -> p h q c", h=2, c=16)
            BIN2 = STG[0:32, 2048:4096].rearrange("p (h q c) -> p h q c", h=2, c=16)
            S.dma("sync", BIN1[:, 0, :, :], b_re[li, dr])
            S.dma("gpsimd", BIN1[:, 1, :, :], b_im[li, dr])
            S.dma("sync", BIN2[:, 0, :, :], b_im[li, dr])
            S.dma("gpsimd", BIN2[:, 1, :, :], b_re[li, dr])
            AIN = RS[0:32, 0:256].rearrange("p (a q) -> p a q", q=64)
            S.dma("sync", AIN[:, 0, :], a_re[li, dr])
            S.dma("gpsimd", AIN[:, 1, :], a_re[li, dr])
            S.dma("sync", AIN[:, 2, :], a_im[li, dr])
            S.dma("gpsimd", AIN[:, 3, :], a_im[li, dr])
            S.bar()
            for c_ in range(16):
                S.op("tensor", lambda e, c_=c_: e.transpose(ps[2][:, c_ * 32:(c_ + 1) * 32], BIN1[:, :, :, c_], IDENT[0:32, 0:32]))
                S.op("tensor", lambda e, c_=c_: e.transpose(ps[3][:, c_ * 32:(c_ + 1) * 32], BIN2[:, :, :, c_], IDENT[0:32, 0:32]))
            S.op("tensor", lambda e: e.transpose(ps[4][:, 0:32], RS[0:32, 0:128], IDENT[0:32, 0:32]))
            S.op("tensor", lambda e: e.transpose(ps[4][:, 32:64], RS[0:32, 128:256], IDENT[0:32, 0:32]))
            S.bar()
            S.op("vector", lambda e: e.tensor_copy(out=BX1.rearrange("p g c -> p c g"), in_=ps[2][:].rearrange("p (c g) -> p c g", g=32)))
            S.op("scalar", lambda e: e.copy(out=BX2.rearrange("p g c -> p c g"), in_=ps[3][:].rearrange("p (c g) -> p c g", g=32)))
            S.op("vector", lambda e: e.tensor_copy(out=ARp[:], in_=ps[4][:, 0:32]))
            S.op("vector", lambda e: e.tensor_copy(out=AIp[:], in_=ps[4][:, 32:64]))
            S.dma("sync", LDT[:], log_dt[li, dr].partition_broadcast(128))
            S.bar()
            ckp("pl%d" % dr)
            S.op("scalar", lambda e: e.activation(out=LDT[:], in_=LDT[:], func=AF.Exp))
            S.bar()
            S.op("vector", lambda e: e.tensor_tensor(out=LR[:], in0=ARp[:], in1=LDT[:], op=ALU.mult))
            S.op("gpsimd", lambda e: e.tensor_tensor(out=LI[:], in0=AIp[:], in1=LDT[:], op=ALU.mult))
            S.bar()
            ckp("pb%d" % dr)
            ks = list(range(-8, 0)) + list(range(1, 9))
            for idx, kk in enumerate(ks):
                S.op("vector", lambda e, idx=idx, kk=kk: e.tensor_scalar(
                    out=ARG[:, :, idx], in0=LI[:], scalar1=float(kk), scalar2=None, op0=ALU.mult))
                S.op("gpsimd", lambda e, idx=idx, kk=kk: e.tensor_scalar(
                    out=EARG[:, :, idx], in0=LR[:], scalar1=float(kk), scalar2=None, op0=ALU.mult))
            S.bar()
            ckp("pc%d" % dr)
            S.op("vector", lambda e: e.tensor_scalar(out=ARGC, in0=ARG, scalar1=TWO_PI / 4, scalar2=None, op0=ALU.add))
            S.op("scalar", lambda e: e.activation(out=MAG, in_=EARG, func=AF.Exp))
            S.bar()
            ckp("pd%d" % dr)
            S.op("vector", lambda e: e.tensor_scalar(out=NI, in0=ARG, scalar1=1.0 / TWO_PI, scalar2=None, op0=ALU.mult))
            S.op("gpsimd", lambda e: e.tensor_scalar(out=NIC, in0=ARGC, scalar1=1.0 / TWO_PI, scalar2=None, op0=ALU.mult))
            S.bar()
            ckp("pe%d" % dr)
            S.op("vector", lambda e: e.tensor_copy(out=NF, in_=NI))
            S.op("gpsimd", lambda e: e.tensor_copy(out=NFC, in_=NIC))
            S.bar()
            ckp("pf%d" % dr)
            S.op("vector", lambda e: e.scalar_tensor_tensor(out=ARG, in0=NF, scalar=-TWO_PI, in1=ARG, op0=ALU.mult, op1=ALU.add))
            S.op("vector", lambda e: e.scalar_tensor_tensor(out=ARGC, in0=NFC, scalar=-TWO_PI, in1=ARGC, op0=ALU.mult, op1=ALU.add))
            S.bar()
            S.op("vector", lambda e: e.tensor_scalar(out=ARG, in0=ARG, scalar1=3.1415925, scalar2=-3.1415925, op0=ALU.min, op1=ALU.max))
            S.op("gpsimd", lambda e: e.tensor_scalar(out=ARGC, in0=ARGC, scalar1=3.1415925, scalar2=-3.1415925, op0=ALU.min, op1=ALU.max))
            S.bar()
            ckp("pg%d" % dr)
            S.op("scalar", lambda e: e.activation(out=PIM, in_=ARG, func=AF.Sin))
            S.op("scalar", lambda e: e.activation(out=PRE, in_=ARGC, func=AF.Sin))
            S.bar()
            S.op("vector", lambda e: e.tensor_tensor(out=PIM, in0=PIM, in1=MAG, op=ALU.mult))
            S.op("gpsimd", lambda e: e.tensor_tensor(out=PRE, in0=PRE, in1=MAG, op=ALU.mult))
            S.bar()
            ckp("ph%d" % dr)
            S.op("vector", lambda e: e.tensor_scalar(out=NR[:], in0=PRE[:, :, 8], scalar1=-1.0, scalar2=None, op0=ALU.add))
            S.op("gpsimd", lambda e: e.tensor_tensor(out=DEN[:], in0=ARp[:], in1=ARp[:], op=ALU.mult))
            ckp("c0")
            S.op("vector", lambda e: e.tensor_tensor(out=TA[:], in0=AIp[:], in1=AIp[:], op=ALU.mult))
            ckp("c1")
            S.op("vector", lambda e: e.tensor_tensor(out=DEN[:], in0=DEN[:], in1=TA[:], op=ALU.add))
            ckp("c2")
            S.op("vector", lambda e: e.reciprocal(out=DEN[:], in_=DEN[:]))
            ckp("c3")
            S.op("vector", lambda e: e.tensor_tensor(out=TA[:], in0=NR[:], in1=ARp[:], op=ALU.mult))
            S.op("gpsimd", lambda e: e.tensor_tensor(out=TB[:], in0=PIM[:, :, 8], in1=AIp[:], op=ALU.mult))
            ckp("c4")
            S.op("vector", lambda e: e.tensor_tensor(out=CR[:], in0=TA[:], in1=TB[:], op=ALU.add))
            ckp("c5")
            S.op("vector", lambda e: e.tensor_tensor(out=TA[:], in0=PIM[:, :, 8], in1=ARp[:], op=ALU.mult))
            S.op("gpsimd", lambda e: e.tensor_tensor(out=TB[:], in0=NR[:], in1=AIp[:], op=ALU.mult))
            ckp("c6")
            S.op("vector", lambda e: e.tensor_tensor(out=CI[:], in0=TA[:], in1=TB[:], op=ALU.subtract))
            ckp("c7")
            S.op("vector", lambda e: e.tensor_tensor(out=CR[:], in0=CR[:], in1=DEN[:], op=ALU.mult))
            S.op("gpsimd", lambda e: e.tensor_tensor(out=CI[:], in0=CI[:], in1=DEN[:], op=ALU.mult))
            ckp("c8")
            ckp("pi%d" % dr)
            CRb = CR[:].unsqueeze(2).to_broadcast([128, 32, 8])
            CIb = CI[:].unsqueeze(2).to_broadcast([128, 32, 8])
            S.op("vector", lambda e: e.tensor_tensor(out=QR, in0=PRE[:, :, 0:8], in1=CRb, op=ALU.mult))
            S.op("gpsimd", lambda e: e.tensor_tensor(out=QT, in0=PIM[:, :, 0:8], in1=CIb, op=ALU.mult))
            S.bar()
            S.op("vector", lambda e: e.tensor_tensor(out=QR, in0=QR, in1=QT, op=ALU.subtract))
            S.bar()
            S.op("vector", lambda e: e.tensor_tensor(out=QI, in0=PRE[:, :, 0:8], in1=CIb, op=ALU.mult))
            S.op("gpsimd", lambda e: e.tensor_tensor(out=QT, in0=PIM[:, :, 0:8], in1=CRb, op=ALU.mult))
            S.bar()
            S.op("vector", lambda e: e.tensor_tensor(out=QI, in0=QI, in1=QT, op=ALU.add))
            S.bar()
            S.op("vector", lambda e: e.tensor_scalar(out=QI, in0=QI, scalar1=SIGN[:, 0:1], scalar2=None, op0=ALU.mult))
            S.op("gpsimd", lambda e: e.tensor_scalar(out=PA, in0=PRE[:, :, 8:16], scalar1=NSIGN[:, 0:1], scalar2=None, op0=ALU.mult))
            S.op("scalar", lambda e: e.mul(out=PB, in_=PIM[:, :, 8:16], mul=-1.0))
            S.bar()
            S.op("vector", lambda e: e.tensor_copy(out=ARC[:, 0, :], in_=PRE[:, :, 15]))
            S.op("vector", lambda e: e.tensor_copy(out=ARC[:, 1, :], in_=PRE[:, :, 15]))
            S.op("gpsimd", lambda e: e.tensor_scalar(out=AIC[:, 0, :], in0=PIM[:, :, 15], scalar1=SIGN[:, 0:1], scalar2=None, op0=ALU.mult))
            S.op("gpsimd", lambda e: e.tensor_scalar(out=AIC[:, 1, :], in0=PIM[:, :, 15], scalar1=NSIGN[:, 0:1], scalar2=None, op0=ALU.mult))
            S.bar()
            ckp("prep%d" % dr + "")
            for s_ in range(8):
                qi = (7 - s_) if dr == 0 else s_
                ri = s_ if dr == 0 else (7 - s_)
                S.op("vector", lambda e, s_=s_, qi=qi: e.tensor_tensor(
                    out=LT[:, :, s_, :], in0=BX1, in1=QR[:, :, qi:qi + 1].to_broadcast([128, 32, 16]), op=ALU.mult))
                S.op("gpsimd", lambda e, s_=s_, ri=ri: e.tensor_tensor(
                    out=RT[:, :, s_, :], in0=CX1, in1=PA[:, :, ri:ri + 1].to_broadcast([128, 32, 16]), op=ALU.mult))
            S.bar()
            TL = STG[:].rearrange("p (g s c) -> p g s c", s=8, c=16)
            for s_ in range(8):
                qi = (7 - s_) if dr == 0 else s_
                S.op("vector" if s_ % 2 == 0 else "gpsimd", lambda e, s_=s_, qi=qi: e.tensor_tensor(
                    out=TL[:, :, s_, :], in0=BX2, in1=QI[:, :, qi:qi + 1].to_broadcast([128, 32, 16]), op=ALU.mult))
            S.bar()
            S.op("vector", lambda e: e.tensor_tensor(out=LT, in0=LT, in1=TL, op=ALU.add))
            S.bar()
            for s_ in range(8):
                ri = s_ if dr == 0 else (7 - s_)
                S.op("vector" if s_ % 2 == 0 else "gpsimd", lambda e, s_=s_, ri=ri: e.tensor_tensor(
                    out=TL[:, :, s_, :], in0=CX2, in1=PB[:, :, ri:ri + 1].to_broadcast([128, 32, 16]), op=ALU.mult))
            S.bar()
            S.op("vector", lambda e: e.tensor_tensor(out=RT, in0=RT, in1=TL, op=ALU.add))
            S.bar()
            ckp("lr%d" % dr + "")
            for rnd in range(4):
                for gi in range(8):
                    g = rnd * 8 + gi
                    Lg = LT[:, g, :, :].rearrange("p s c -> p (s c)")
                    Rg = RT[:, g, :, :].rearrange("p s c -> p (s c)")
                    S.op("tensor", lambda e, gi=gi, Lg=Lg, Rg=Rg: e.matmul(
                        ps[gi // 4][:, (gi % 4) * 128:(gi % 4 + 1) * 128], lhsT=Lg, rhs=Rg, start=True, stop=True))
                    S.op("tensor", lambda e, gi=gi, Lg=Lg: e.transpose(
                        ps[2 + gi // 4][:, (gi % 4) * 128:(gi % 4 + 1) * 128], Lg, IDENT[:]))
                S.bar()
                for bk in range(2):
                    g0 = rnd * 8 + bk * 4
                    S.op("vector", lambda e, bk=bk, g0=g0, dr=dr: e.tensor_tensor(
                        out=TOEP[:, g0:g0 + 4, :], in0=ps[bk][:].rearrange("p (g n) -> p g n", n=128),
                        in1=MASKS[:, dr:dr + 1, :].to_broadcast([128, 4, 128]), op=ALU.mult))
                    S.op("scalar", lambda e, bk=bk, g0=g0: e.copy(
                        out=ET[:, g0:g0 + 4, :], in_=ps[2 + bk][:].rearrange("p (g n) -> p g n", n=128)))
                S.bar()
                for bk in range(2):
                    g0 = rnd * 8 + bk * 4
                    S.op("vector", lambda e, bk=bk, g0=g0: e.tensor_copy(
                        out=ESW[:, g0:g0 + 4, 0:64], in_=ps[2 + bk][:].rearrange("p (g n) -> p g n", n=128)[:, :, 64:128]))
                    S.op("vector", lambda e, bk=bk, g0=g0: e.tensor_copy(
                        out=ESW[:, g0:g0 + 4, 64:128], in_=ps[2 + bk][:].rearrange("p (g n) -> p g n", n=128)[:, :, 0:64]))
                S.bar()
            S.op("scalar", lambda e: e.copy(out=RB.rearrange("p g n -> p (g n)"), in_=RT.rearrange("p g s c -> p (g s c)")))
            if dr == 0:
                for g in range(32):
                    S.op("vector", lambda e, g=g: e.scalar_tensor_tensor(
                        out=TOEP[:, g, :], in0=IDENT[:], scalar=DV[:, g:g + 1], in1=TOEP[:, g, :],
                        op0=ALU.mult, op1=ALU.add))
            S.op("gpsimd", lambda e: e.memset(W3[:], 0.0))
            S.bar()
            ckp("tiles%d" % dr + "")
            blocks = [(0, 32)] + [(32 + 64 * b, 64) for b in range(4)]
            order = blocks if dr == 0 else [blocks[0]] + blocks[:0:-1]
            for (jb, nb) in order:
                for half in range(2):
                    for gi in range(16):
                        g = half * 16 + gi
                        for arr in range(2):
                            ii = gi * 2 + arr
                            Em = ET if arr == 0 else ESW
                            S.op("tensor", lambda e, ii=ii, g=g, Em=Em, jb=jb, nb=nb: e.matmul(
                                ps[ii // 8][:, (ii % 8) * 64:(ii % 8) * 64 + nb], lhsT=Em[:, g, :],
                                rhs=IM[:, g, jb:jb + nb], start=True, stop=True))
                    S.bar()
                    for bk in range(4):
                        g0 = half * 16 + bk * 4
                        S.op("vector" if bk % 2 == 0 else "scalar", (lambda e, bk=bk, g0=g0, nb=nb: e.tensor_copy(
                            out=SS[:, 0:nb, :, g0:g0 + 4].rearrange("p j a g -> p g a j"),
                            in_=ps[bk][:].rearrange("p (g a j) -> p g a j", a=2, j=64)[:, :, :, 0:nb]))
                            if bk % 2 == 0 else (lambda e, bk=bk, g0=g0, nb=nb: e.copy(
                            out=SS[:, 0:nb, :, g0:g0 + 4].rearrange("p j a g -> p g a j"),
                            in_=ps[bk][:].rearrange("p (g a j) -> p g a j", a=2, j=64)[:, :, :, 0:nb])))
                    S.bar()
                js = list(range(jb, jb + nb)) if dr == 0 else list(range(jb + nb - 1, jb - 1, -1))
                pjl = None
                for j in js:
                    jl = j - jb
                    Wc = W3[:] if pjl is None else SS[:, pjl, :, :]
                    S.op("vector", lambda e, jl=jl, Wc=Wc: e.tensor_tensor(out=G3[:, 0:2, :], in0=Wc, in1=SS[:, jl, :, :], op=ALU.add))
                    S.op("vector", lambda e: e.tensor_tensor(out=T1[:], in0=G3[:, 0:2, :], in1=ARC[:], op=ALU.mult))
                    S.op("vector", lambda e: e.tensor_tensor(out=T2[:, 0, :], in0=G3[:, 1, :], in1=AIC[:, 0, :], op=ALU.mult))
                    S.op("vector", lambda e: e.tensor_tensor(out=T2[:, 1, :], in0=G3[:, 0, :], in1=AIC[:, 1, :], op=ALU.mult))
                    S.op("vector", lambda e, jl=jl: e.tensor_tensor(out=SS[:, jl, :, :], in0=T1[:], in1=T2[:], op=ALU.add))
                    pjl = jl
                S.bar()
                if dr == 0:
                    S.op("scalar", lambda e, jb=jb: e.copy(out=HH[:, jb, :], in_=W3[:, 0, :]))
                    S.op("scalar", lambda e, jb=jb, nb=nb: e.copy(out=HH[:, jb + 1:jb + nb, :], in_=SS[:, 0:nb - 1, 0, :]))
                else:
                    S.op("scalar", lambda e, jb=jb, nb=nb: e.copy(out=HH[:, jb + nb - 1, :], in_=W3[:, 0, :]))
                    S.op("scalar", lambda e, jb=jb, nb=nb: e.copy(out=HH[:, jb:jb + nb - 1, :], in_=SS[:, 1:nb, 0, :]))
                S.bar()
                S.op("vector", lambda e, pjl=pjl: e.tensor_copy(out=W3[:], in_=SS[:, pjl, :, :]))
                S.bar()
            ckp("rec%d" % dr + "")
            for rnd in range(4):
                for gi in range(8):
                    g = rnd * 8 + gi
                    S.op("tensor", lambda e, gi=gi, g=g: e.matmul(
                        ps[gi][:, 0:NJ], lhsT=TOEP[:, g, :], rhs=IM[:, g, :], start=True, stop=False))
                    S.op("tensor", lambda e, gi=gi, g=g: e.matmul(
                        ps[gi][:, 0:NJ], lhsT=RB[:, g, :], rhs=HH[:, :, g], start=False, stop=True))
                S.bar()
                for gi in range(8):
                    g = rnd * 8 + gi
                    if dr == 0:
                        S.op("vector" if gi % 2 == 0 else "scalar", (lambda e, gi=gi, g=g: e.tensor_copy(out=YS[:, g, :], in_=ps[gi][:, 0:NJ]))
                             if gi % 2 == 0 else (lambda e, gi=gi, g=g: e.copy(out=YS[:, g, :], in_=ps[gi][:, 0:NJ])))
                    else:
                        S.op("vector", lambda e, gi=gi, g=g: e.tensor_tensor(out=YS[:, g, :], in0=YS[:, g, :], in1=ps[gi][:, 0:NJ], op=ALU.add))
                S.bar()
        ckp("read")
        S.dma("sync", YSP[:, :, :], YS)
        S.bar()
        YV = YS.rearrange("p (q s) j -> p q s j", s=8)
        YSPv = YSP.rearrange("(s c) (q g) j -> g c q s j", c=16, g=8)
        for g8 in range(8):
            for q in range(4):
                S.dma(["sync", "gpsimd"][q % 2], YV[g8 * 16:(g8 + 1) * 16, q, :, :], YSPv[g8, :, q, :, :])
            S.bar()
        S.bar()

        ckp("unim")
        WG = WB[:, 0:4 * 512].rearrange("p (k n) -> p k n", n=512)
        load_weight(WG, w_glu[li], 4, 512, 512)
        for (t0, w) in TILES:
            j0, nj = t0 // 8, w // 8
            for q in range(4):
                S.op("scalar", lambda e, q=q, w=w, j0=j0, nj=nj: e.activation(
                    out=TMP[:, q, 0:w].rearrange("p (j s) -> p j s", s=8),
                    in_=YV[:, q, :, j0:j0 + nj].rearrange("p s j -> p j s"), func=AF.Gelu_apprx_tanh))
            S.bar()
            S.op("vector", lambda e, w=w: e.tensor_copy(out=OB[:, 0:4, 0:w], in_=TMP[:, 0:4, 0:w]))
            S.bar()
            for m in range(4):
                for k in range(4):
                    S.op("tensor", lambda e, m=m, k=k, w=w: e.matmul(
                        ps[m][:, 0:w], lhsT=WG[:, k, m * 128:(m + 1) * 128], rhs=OB[:, k, 0:w],
                        start=(k == 0), stop=(k == 3)))
            S.bar()
            for m in range(4):
                S.op("scalar", lambda e, m=m, w=w: e.activation(
                    out=XT[:, m, 0:w], in_=ps[m][:, 0:w], func=AF.Sigmoid, bias=BGLU[:, m:m + 1], scale=1.0))
            S.bar()
            S.join()
            S.op("vector", lambda e, w=w: e.tensor_tensor(out=OB[:, 4:8, 0:w], in0=TMP[:, 0:4, 0:w], in1=XT[:, 0:4, 0:w], op=ALU.mult))
            S.bar()
            S.dma_async("sync", MIX[0:512, t0:t0 + w].rearrange("(k p) t -> p k t", p=128), OB[:, 4:8, 0:w])

        S.join()
        ckp("glu")
        S.dma("sync", TMP[:, 0:4, 0:128], w_sp[li].rearrange("h p q -> p h q"))
        S.dma("gpsimd", BS.rearrange("p h q -> p (h q)"), b_sp[li].rearrange("h q -> (h q)").partition_broadcast(128))
        S.bar()
        for h in range(4):
            S.op("tensor", lambda e, h=h: e.transpose(ps[0][:, h * 128:(h + 1) * 128], TMP[:, h, 0:128], IDENT[:]))
        S.bar()
        S.op("vector", lambda e: e.tensor_copy(out=WST[:], in_=ps[0][:].rearrange("p (h n) -> p h n", n=128)))
        S.bar()
        VGb = [XT[:, 0:4, :], XT[:, 4:8, :]]
        UGb = [OB[:, 0:4, :], ACTB[:, 0:4, :]]

        def sgu_loads(i):
            t0, w = TILES[i]
            S.dma_async("sync", VGb[i % 2][:, :, 0:w], VG[:, t0:t0 + w].rearrange("(k p) t -> p k t", p=128))
            S.dma_async("gpsimd", UGb[i % 2][:, :, 0:w], UG[:, t0:t0 + w].rearrange("(k p) t -> p k t", p=128))

        WOv = WB[:, 4096:4096 + 8 * D].rearrange("p (k n) -> p k n", n=D)
        WOs = [FB[:, 1024:5120].rearrange("p (k n) -> p k n", n=512), FB[:, 5120:9216].rearrange("p (k n) -> p k n", n=512)]
        sgu_loads(0)
        for i, (t0, w) in enumerate(TILES):
            nchk = w // 128
            VGi, UGi = VGb[i % 2], UGb[i % 2]
            S.join()
            if i + 1 < len(TILES):
                sgu_loads(i + 1)
            if i < 2:
                wsrc = w_out[li][:, i * 512:(i + 1) * 512].rearrange("(k p) n -> p k n", p=128)
                S.dma_async("sync", WOs[i][:, 0:4, :], wsrc[:, 0:4, :])
                S.dma_async("gpsimd", WOs[i][:, 4:8, :], wsrc[:, 4:8, :])
            if i in (2, 3):
                S.op("vector", lambda e, i=i: e.tensor_copy(out=WOv[:, :, (i - 2) * 512:(i - 1) * 512], in_=WOs[i - 2]))
            S.op("scalar", lambda e, w=w, VGi=VGi: e.activation(out=SQ[:, 0:4, 0:w], in_=VGi[:, 0:4, 0:w], func=AF.Square))
            S.bar()
            rstd_from(SQ, 4, w, 1.0 / 512)
            for k in range(4):
                S.op("vector", lambda e, k=k, w=w, VGi=VGi: e.scalar_tensor_tensor(
                    out=TMP[:, k, 0:w], in0=VGi[:, k, 0:w], scalar=GSGU[:, k:k + 1], in1=RS[:, 0:w],
                    op0=ALU.mult, op1=ALU.mult))
            S.bar()
            for ck in range(nchk):
                for h in range(4):
                    ii = ck * 4 + h
                    S.op("tensor", lambda e, ii=ii, ck=ck, h=h: e.transpose(
                        ps[ii // 4][:, (ii % 4) * 128:(ii % 4 + 1) * 128], TMP[:, h, ck * 128:(ck + 1) * 128], IDENT[:]))
            S.bar()
            for ck in range(nchk):
                S.op("vector" if ck % 2 == 0 else "scalar", (lambda e, ck=ck: e.tensor_copy(
                    out=VT[:, ck * 4:(ck + 1) * 4, :], in_=ps[ck][:].rearrange("p (h n) -> p h n", n=128)))
                    if ck % 2 == 0 else (lambda e, ck=ck: e.copy(
                    out=VT[:, ck * 4:(ck + 1) * 4, :], in_=ps[ck][:].rearrange("p (h n) -> p h n", n=128))))
            S.bar()
            for ck in range(nchk):
                for h in range(4):
                    ii = ck * 4 + h
                    S.op("tensor", lambda e, ii=ii, ck=ck, h=h: e.matmul(
                        ps[4 + ck][:, h * 128:(h + 1) * 128], lhsT=VT[:, ii, :], rhs=WST[:, h, :], start=True, stop=True))
            S.bar()
            for ck in range(nchk):
                S.op("vector", lambda e, ck=ck: e.tensor_tensor(
                    out=TMP[:, 4:8, ck * 128:(ck + 1) * 128], in0=ps[4 + ck][:].rearrange("p (h n) -> p h n", n=128),
                    in1=BS, op=ALU.add))
            S.bar()
            S.op("vector", lambda e, w=w, UGi=UGi: e.tensor_tensor(out=OB[:, 4:8, 0:w], in0=TMP[:, 4:8, 0:w], in1=UGi[:, 0:4, 0:w], op=ALU.mult))
            S.bar()
            S.dma_async("sync", MIX[512:1024, t0:t0 + w].rearrange("(k p) t -> p k t", p=128), OB[:, 4:8, 0:w])

        S.join()
        ckp("sgu")
        resid_linear(MIX, 8, 2, [ACTB[:, 0:8, :], ACTB[:, 8:16, :]], WOv)

        ckp("wout")
        norm_mod(A2, 3)

        ckp("norm2")
        S.dma("sync", FB[0:9, 0:2 * DFF], w_conv[li].rearrange("a b n -> (a b) n"))
        S.bar()
        for ch in range(44):
            S.op("tensor", lambda e, ch=ch: e.transpose(ps[7][:, ch * 9:(ch + 1) * 9], FB[0:9, ch * 128:(ch + 1) * 128], IDENT[0:9, 0:9]))
        S.bar()
        S.op("vector", lambda e: e.tensor_copy(out=WC[:].rearrange("p t c -> p c t"), in_=ps[7][:, 0:396].rearrange("p (c t) -> p c t", t=9)))
        S.bar()
        WUb = [OBt[:, 0:2048].rearrange("p (k n) -> p k n", n=256), OBt[:, 2048:4096].rearrange("p (k n) -> p k n", n=256)]
        WDv = WB[:, 0:22 * D].rearrange("p (k n) -> p k n", n=D)
        wd_stage = [XTt[:, 0:2816].rearrange("p (k n) -> p k n", n=128), TMPt[:, 0:2816].rearrange("p (k n) -> p k n", n=128)]
        FBb = FB[:].bitcast(BF16)
        UPb = [FBb[:, 0:2304], FBb[:, 2304:4608]]
        DGb = [FBb[:, 4608:6912].rearrange("p (t n) -> p t n", n=128), FBb[:, 6912:9216].rearrange("p (t n) -> p t n", n=128)]
        SG = FB[:, 4608:6912]
        GB = ACTB[:, 0:5, :].rearrange("p a t -> p (a t)")[:, 0:NT]
        stgs = [STG[:, 0:2048].rearrange("p (k n) -> p k n", n=256), STG[:, 2048:4096].rearrange("p (k n) -> p k n", n=256)]

        def bank(m, sl):
            return ps[(sl + 4 * m) % 8]

        def wup_dma(m):
            st = stgs[m % 2]
            S.dma_async("sync", st[:, :, 0:128], w_up[li, :, m * 128:(m + 1) * 128].rearrange("(k p) n -> p k n", p=128))
            S.dma_async("gpsimd", st[:, :, 128:256], w_up[li, :, DFF + m * 128:DFF + (m + 1) * 128].rearrange("(k p) n -> p k n", p=128))

        def prep_w(m, which):
            if which == 0:
                S.op("vector", lambda e, m=m: e.tensor_copy(out=WUb[m % 2], in_=stgs[m % 2]))
            for idx in range(18):
                part, tap = divmod(idx, 9)
                ch = part * 22 + m
                if idx % 2 == 0 and which == 0:
                    S.op("vector", lambda e, idx=idx, tap=tap, ch=ch, m=m: e.tensor_scalar(
                        out=DGb[m % 2][:, idx, :], in0=IDENT[:], scalar1=WC[:, tap, ch:ch + 1], scalar2=None, op0=ALU.mult))
                if idx % 2 == 1 and which == 1:
                    S.op("scalar", lambda e, idx=idx, tap=tap, ch=ch, m=m: e.activation(
                        out=DGb[m % 2][:, idx, :], in_=IDENT[:], func=AF.Copy, scale=WC[:, tap, ch:ch + 1]))

        def up_mm(m, part, tiles, slots):
            for t, sl in zip(tiles, slots):
                t0, w = TILES[t]
                bk = bank(m, sl)
                for k in range(8):
                    S.op("tensor", lambda e, bk=bk, k=k, t0=t0, w=w, part=part, m=m: e.matmul(
                        bk[:, 0:w], lhsT=WUb[m % 2][:, k, part * 128:(part + 1) * 128], rhs=H[:, k, t0:t0 + w],
                        start=(k == 0), stop=(k == 7)))

        def evac_up(m, part, tiles, slots, engs):
            dst = UPb[part]
            for n_, (t, sl) in enumerate(zip(tiles, slots)):
                t0, w = TILES[t]
                bk = bank(m, sl)
                if engs[n_ % len(engs)] == "vector":
                    S.op("vector", lambda e, bk=bk, t0=t0, w=w, dst=dst: e.tensor_copy(out=dst[:, t0:t0 + w], in_=bk[:, 0:w]))
                else:
                    S.op("scalar", lambda e, bk=bk, t0=t0, w=w, dst=dst: e.copy(out=dst[:, t0:t0 + w], in_=bk[:, 0:w]))

        taps9 = [(1, 1)] + [(ky, kx) for ky in range(3) for kx in range(3) if not (ky == 1 and kx == 1)]

        def conv_mm(m, part, tiles, slots):
            src = UPb[part]
            d0 = part * 9
            DGm = DGb[m % 2]
            sv = src[:, LC:NT].rearrange("p (r c) -> p r c", c=64)
            for t, sl in zip(tiles, slots):
                bk = bank(m, sl)
                if t == 0:
                    S.op("tensor", lambda e, bk=bk, DGm=DGm: e.matmul(bk[:, 0:256], lhsT=DGm[:, d0 + 4, :], rhs=src[:, 0:256], start=True, stop=False))
                    S.op("tensor", lambda e, bk=bk, DGm=DGm: e.matmul(bk[:, 1:256], lhsT=DGm[:, d0 + 3, :], rhs=src[:, 0:255], start=False, stop=False))
                    S.op("tensor", lambda e, bk=bk, DGm=DGm: e.matmul(bk[:, 0:255], lhsT=DGm[:, d0 + 5, :], rhs=src[:, 1:256], start=False, stop=True))
                    continue
                R0 = 8 * (t - 1)
                pv = bk[:, 0:512].rearrange("p (r c) -> p r c", c=64)
                for n_, (ky, kx) in enumerate(taps9):
                    dy, dx = ky - 1, kx - 1
                    ra, rb = max(R0, -dy, 0), min(R0 + 8, 32 - max(0, dy))
                    c0, c1 = max(0, -dx), 64 - max(0, dx)
                    S.op("tensor", lambda e, pv=pv, ra=ra, rb=rb, c0=c0, c1=c1, dy=dy, dx=dx, R0=R0, ky=ky, kx=kx, n_=n_, DGm=DGm:
                         e.matmul(pv[:, ra - R0:rb - R0, c0:c1], lhsT=DGm[:, d0 + ky * 3 + kx, :],
                                  rhs=sv[:, ra + dy:rb + dy, c0 + dx:c1 + dx], start=(n_ == 0), stop=(n_ == 8)))

        def silu_ev(m, tiles, slots):
            for t, sl in zip(tiles, slots):
                t0, w = TILES[t]
                bk = bank(m, sl)
                S.op("scalar", lambda e, bk=bk, t0=t0, w=w: e.activation(out=SG[:, t0:t0 + w], in_=bk[:, 0:w], func=AF.Silu))

        def mult_ev(m, tiles, slots):
            for t, sl in zip(tiles, slots):
                t0, w = TILES[t]
                bk = bank(m, sl)
                S.op("vector", lambda e, bk=bk, t0=t0, w=w: e.tensor_tensor(out=GB[:, t0:t0 + w], in0=bk[:, 0:w], in1=SG[:, t0:t0 + w], op=ALU.mult))

        wup_dma(0)
        S.join()
        prep_w(0, 0)
        prep_w(0, 1)
        S.bar()
        wup_dma(1)
        for m in range(22):
            up_mm(m, 0, [0, 1, 2, 3, 4], [0, 1, 2, 3, 4])
            if m > 0:
                mult_ev(m - 1, [3, 4], [2, 3])
            if m % 2 == 0 and m // 2 < 8:
                S.dma_async("sync", wd_stage[(m // 2) % 2], w_down[li][:, (m // 2) * 128:(m // 2 + 1) * 128].rearrange("(k p) n -> p k n", p=128))
            S.bar()
            up_mm(m, 1, [0, 1, 2], [5, 6, 7])
            evac_up(m, 0, [0, 1, 2, 3, 4], [0, 1, 2, 3, 4], ["vector", "scalar"])
            if m > 0:
                S.dma_async("gpsimd", GD[(m - 1) * 128:m * 128, :], GB)
            S.bar()
            up_mm(m, 1, [3, 4], [0, 1])
            conv_mm(m, 0, [0, 1, 2], [2, 3, 4])
            evac_up(m, 1, [0, 1, 2], [5, 6, 7], ["vector", "scalar"])
            S.bar()
            conv_mm(m, 0, [3, 4], [5, 6])
            evac_up(m, 1, [3, 4], [0, 1], ["vector"])
            silu_ev(m, [0, 1, 2], [2, 3, 4])
            if m % 2 == 1 and m // 2 < 8:
                S.op("vector", lambda e, m=m: e.tensor_copy(out=WDv[:, :, (m // 2) * 128:(m // 2 + 1) * 128], in_=wd_stage[(m // 2) % 2]))
            S.join()
            conv_mm(m, 1, [0, 1, 2], [0, 1, 7])
            silu_ev(m, [3, 4], [5, 6])
            if m + 1 < 22:
                prep_w(m + 1, 0)
            S.bar()
            conv_mm(m, 1, [3, 4], [2, 3])
            mult_ev(m, [0, 1, 2], [0, 1, 7])
            if m + 1 < 22:
                prep_w(m + 1, 1)
            if m + 2 < 22:
                wup_dma(m + 2)
            S.bar()
        mult_ev(21, [3, 4], [2, 3])
        S.bar()
        S.dma_async("gpsimd", GD[21 * 128:22 * 128, :], GB)
        S.join()

        ckp("ffnup")
        resid_linear(GD, 22, 5, [ACTB, FB[:].bitcast(BF16)[:, 0:11264].rearrange("p (k t) -> p k t", t=512)], WB[:, 0:22 * D].rearrange("p (k n) -> p k n", n=D))

    except _Stop:
        pass
    S.bar()
    if dbg:
        DF = nc.dram_tensor("DBGF", [128, 32768], F32, kind="ExternalOutput").ap()
        DB = nc.dram_tensor("DBGB", [128, 40960], BF16, kind="ExternalOutput").ap()
        off = 0
        for t_, n_ in [(HRAW[:], 9216), (XTt[:], 4096), (TMPt[:], 4096), (FB[:], 9216), (STG[:], 4096), (RS[:], 512),
                       (MOD[:].rearrange("p q k n -> p (q k n)"), 96), (A1[:].rearrange("p k n -> p (k n)"), 16),
                       (A2[:].rearrange("p k n -> p (k n)"), 16), (W3[:].rearrange("p a g -> p (a g)"), 64),
                       (ARC[:].rearrange("p a g -> p (a g)"), 64), (AIC[:].rearrange("p a g -> p (a g)"), 64),
                       (CR[:], 32), (CI[:], 32), (DV[:], 32), (LR[:], 32), (LI[:], 32)]:
            S.dma("sync", DF[:, off:off + n_], t_)
            off += n_
        offb = 0
        for t_, n_ in [(WB, 22528), (OBt[:], 4096), (ACTBt[:], 11264)]:
            S.dma("gpsimd", DB[:, offb:offb + n_], t_)
            offb += n_
        S.bar()
    for b in range(16):
        t0 = LC + b * 128
        S.dma("sync", XT[:, :, 0:128], XRES[:, t0:t0 + 128].rearrange("(k p) t -> p k t", p=128))
        S.bar()
        S.op("scalar", lambda e: e.activation(out=SQ[:, :, 0:128], in_=XT[:, :, 0:128], func=AF.Square))
        S.bar()
        rstd_from(SQ, 8, 128, 1.0 / D)
        for k in range(8):
            S.op("vector", lambda e, k=k: e.scalar_tensor_tensor(
                out=TMP[:, k, 0:128], in0=XT[:, k, 0:128], scalar=GFIN[:, k:k + 1], in1=RS[:, 0:128],
                op0=ALU.mult, op1=ALU.mult))
        S.bar()
        for k in range(8):
            S.op("tensor", lambda e, k=k: e.transpose(ps[k // 4][:, (k % 4) * 128:(k % 4 + 1) * 128],
                                                       TMP[:, k, 0:128], IDENT[:]))
        S.bar()
        S.op("vector", lambda e: e.tensor_copy(out=XT[:, 0:4, 0:128], in_=ps[0][:].rearrange("p (k t) -> p k t", t=128)))
        S.op("scalar", lambda e: e.copy(out=XT[:, 4:8, 0:128], in_=ps[1][:].rearrange("p (k t) -> p k t", t=128)))
        S.bar()
        S.dma("sync", out[b * 128:(b + 1) * 128, :].rearrange("t (k d) -> t k d", d=128), XT[:, :, 0:128])
        S.bar()

    S.emit()
    es.close()
    return nc


_CONST = None


def _consts():
    ident = np.eye(128, dtype=np.float32)
    sp = np.arange(128) // 16
    m0 = (sp[None, :] >= sp[:, None]).astype(np.float32)
    m1 = (sp[None, :] <= sp[:, None]).astype(np.float32)
    return ident, np.stack([m0, m1])


def kernel(n_layers=4, **inputs):
    nc = build_nc(n_layers)
    ident, mask = _consts()
    in_maps = []
    for b in range(8):
        m = {}
        for k, v in inputs.items():
            v = np.asarray(v)
            if k in ("x", "c", "ctx"):
                m[k] = np.ascontiguousarray(v[b], dtype=np.float32)
            else:
                m[k] = np.ascontiguousarray(v, dtype=np.float32)
        m["ident"] = ident
        m["mask"] = mask
        in_maps.append(m)
    res = run_bass_kernel_spmd(nc, in_maps, core_ids=list(range(8)))
    return np.stack([np.asarray(r["out"], dtype=np.float32) for r in res.results], axis=0)
```

```python
import numpy as np
from contextlib import ExitStack
import concourse.bass as bass
import concourse.mybir as mybir
from concourse.bass_utils import run_bass_kernel_spmd

F32, BF16, I32 = mybir.dt.float32, mybir.dt.bfloat16, mybir.dt.int32
AF = mybir.ActivationFunctionType
ALU = mybir.AluOpType

D = 1024
NT = 2304
LC = 256
LL = 2048
DFF = 2816
EPS = 1e-6
NJ = 288
TILES = [(0, 256)] + [(256 + 512 * i, 512) for i in range(4)]
TWO_PI = 6.283185307179586


class Sched:
    def __init__(self, nc):
        self.nc = nc
        self.stages = [[]]
        self.join_at = set()

    def op(self, eng, fn, dma=False):
        self.stages[-1].append((eng, dma, fn))

    def dma(self, eng, out, in_, slow=False):
        self.op(eng, lambda e, o=out, i=in_: e.dma_start(out=o, in_=i), dma=True)

    def dma_async(self, eng, out, in_):
        self.op(eng, lambda e, o=out, i=in_: e.dma_start(out=o, in_=i), dma="async")

    def bar(self):
        if self.stages[-1]:
            self.stages.append([])

    def join(self):
        self.bar()
        self.join_at.add(len(self.stages) - 1)

    def emit(self):
        nc = self.nc
        self.bar()
        names = ["c_scalar", "c_vector", "c_gpsimd", "c_tensor", "d_sync", "d_scalar", "d_gpsimd", "a_sync", "a_gpsimd"]

        def semname(eng, dma):
            if dma == "async":
                return "a_" + eng
            return ("d_" if dma else "c_") + eng

        cum = []
        cur = {n: 0 for n in names}
        for st in self.stages:
            cum.append(dict(cur))
            for (eng, dma, _) in st:
                cur[semname(eng, dma)] += 16 if dma else 1
        final = dict(cur)
        with ExitStack() as es:
            sems = {n: es.enter_context(nc.semaphore(n)) for n in names}
            block = es.enter_context(nc.Block())

            def make(engname):
                def body(eng):
                    waited = {n: 0 for n in names}
                    joined = {n: 0 for n in names}
                    for k, st in enumerate(self.stages):
                        if k in self.join_at:
                            for n in names:
                                if n.startswith("a_"):
                                    joined[n] = cum[k][n]
                        mine = [o for o in st if o[0] == engname]
                        if not mine:
                            continue
                        for n in names:
                            tgt = joined[n] if n.startswith("a_") else cum[k][n]
                            if tgt > waited[n]:
                                eng.wait_ge(sems[n], tgt)
                                waited[n] = tgt
                        for (_, dma, fn) in mine:
                            ins = fn(eng)
                            ins.then_inc(sems[semname(engname, dma)], 16 if dma else 1)
                    if engname == "sync":
                        for n in names:
                            if final[n] > waited[n]:
                                eng.wait_ge(sems[n], final[n])
                return body

            block.sync(make("sync"))
            block.scalar(make("scalar"))
            block.vector(make("vector"))
            block.gpsimd(make("gpsimd"))
            block.tensor(make("tensor"))


class _Stop(Exception):
    pass


def build_nc(n_layers, n_wl=4, stop=None, dbg=False):
    nc = bass.Bass("TRN2", target_bir_lowering=False)
    S = Sched(nc)
    W = n_wl

    def ckp(name):
        S.bar()
        if stop == name:
            raise _Stop()

    def din(name, shape):
        return nc.dram_tensor(name, list(shape), F32, kind="ExternalInput").ap()

    x_in = din("x", [LL, D])
    c_in = din("c", [D])
    ctx_in = din("ctx", [LC, D])
    cctx_in = din("c_ctx", [D])
    w_ada = din("w_ada", [W, D, 6 * D])
    b_ada = din("b_ada", [W, 6 * D])
    g_mix = din("g_mix", [W, D])
    w_in = din("w_in", [W, D, 1536])
    a_re = din("ssm_a_re", [W, 2, 32, 64])
    a_im = din("ssm_a_im", [W, 2, 32, 64])
    b_re = din("ssm_b_re", [W, 2, 32, 64, 16])
    b_im = din("ssm_b_im", [W, 2, 32, 64, 16])
    c_re = din("ssm_c_re", [W, 2, 32, 16, 64])
    c_im = din("ssm_c_im", [W, 2, 32, 16, 64])
    log_dt = din("ssm_log_dt", [W, 2, 32])
    ssm_d = din("ssm_d", [W, 512])
    w_glu = din("w_glu", [W, 512, 512])
    b_glu = din("b_glu", [W, 512])
    g_sgu = din("g_sgu", [W, 512])
    w_sp = din("w_spatial", [W, 4, 128, 128])
    b_sp = din("b_spatial", [W, 4, 128])
    w_out = din("w_out", [W, D, D])
    g_ffn = din("g_ffn", [W, D])
    w_up = din("w_up", [W, D, 2 * DFF])
    w_conv = din("w_conv", [W, 3, 3, 2 * DFF])
    w_down = din("w_down", [W, DFF, D])
    g_final = din("g_final", [D])
    ident_in = din("ident", [128, 128])
    mask_in = din("mask", [2, 128, 128])
    out = nc.dram_tensor("out", [LL, D], F32, kind="ExternalOutput").ap()

    SK = dict(kind="ExternalOutput") if dbg else {}
    XRES = nc.dram_tensor("XRES", [D, NT], F32, **SK).ap()
    USSMP = nc.dram_tensor("USSMP", [512, 8, NJ], BF16, **SK).ap()
    YSP = nc.dram_tensor("YSP", [128, 32, NJ], F32, **SK).ap()
    UG = nc.dram_tensor("UG", [512, NT], BF16, **SK).ap()
    VG = nc.dram_tensor("VG", [512, NT], F32, **SK).ap()
    MIX = nc.dram_tensor("MIX", [D, NT], BF16, **SK).ap()
    GD = nc.dram_tensor("GD", [DFF, NT], BF16, **SK).ap()

    es = ExitStack()

    def sb(name, shape, dt=F32):
        return es.enter_context(nc.sbuf_tensor(name, list(shape), dt))

    ps = [es.enter_context(nc.psum_tensor("ps%d" % i, [128, 512], F32)) for i in range(8)]

    IDENT = sb("IDENT", [128, 128])
    ONESB = sb("ONESB", [128, 128], BF16)
    MASKS = sb("MASKS", [128, 2, 128])
    SIGN = sb("SIGN", [128, 1])
    NSIGN = sb("NSIGN", [128, 1])
    SC = sb("SC", [128, 8, 2])
    MOD = sb("MOD", [128, 6, 8, 2])
    BADA = sb("BADA", [128, 6, 8])
    MODN = sb("MODN", [128, 96])
    GM = sb("GM", [128, 8])
    GF = sb("GF", [128, 8])
    GFIN = sb("GFIN", [128, 8])
    A1 = sb("A1", [128, 8, 2])
    A2 = sb("A2", [128, 8, 2])
    HRAW = sb("HRAW", [128, 9216])
    H = HRAW[:].bitcast(BF16).rearrange("p (k t) -> p k t", t=NT)
    YS = HRAW[:].rearrange("p (g j) -> p g j", j=NJ)
    STG = sb("STG", [128, 4096])
    SQ = STG[:, 0:2048].bitcast(BF16).rearrange("p (k t) -> p k t", t=512)
    WBt = sb("WB", [128, 22528], BF16)
    WB = WBt[:]
    TOEP = WB[:, 0:4096].rearrange("p (g n) -> p g n", n=128)
    ET = WB[:, 4096:8192].rearrange("p (g n) -> p g n", n=128)
    ESW = WB[:, 8192:12288].rearrange("p (g n) -> p g n", n=128)
    IM = WB[:, 12288:21504].rearrange("p (g j) -> p g j", j=NJ)
    XTt = sb("XT", [128, 4096])
    XT = XTt[:].rearrange("p (k t) -> p k t", t=512)
    LT = XTt[:].rearrange("p (g s c) -> p g s c", s=8, c=16)
    SS = XTt[:].rearrange("p (j a g) -> p j a g", a=2, g=32)
    TMPt = sb("TMP", [128, 4096])
    TMP = TMPt[:].rearrange("p (k t) -> p k t", t=512)
    RT = TMPt[:].rearrange("p (g s c) -> p g s c", s=8, c=16)
    RS = sb("RS", [128, 512])
    OBt = sb("OB", [128, 4096], BF16)
    OB = OBt[:].rearrange("p (k t) -> p k t", t=512)
    RB = OBt[:].rearrange("p (g n) -> p g n", n=128)
    ACTBt = sb("ACTB", [128, 11264], BF16)
    ACTB = ACTBt[:].rearrange("p (k t) -> p k t", t=512)
    USP = ACTBt[:, 0:9216].rearrange("p (q s j) -> p q s j", s=8, j=NJ)
    HH = ACTBt[:, 0:9216].rearrange("p (j g) -> p j g", g=32)
    FB = sb("FB", [128, 9216])
    UPG = FB[:, 0:2304]; UPV = FB[:, 2304:4608]; CG = FB[:, 4608:6912]; CV = FB[:, 6912:9216]
    def ftab(i):
        return FB[:, i * 512:(i + 1) * 512].rearrange("p (g k) -> p g k", k=16)
    ARG, ARGC, EARG, NF, NFC, PRE, PIM, MAG, BX1, BX2, CX1, CX2 = [ftab(i) for i in range(12)]
    def qtab(i):
        return FB[:, 6144 + i * 256:6144 + (i + 1) * 256].rearrange("p (g k) -> p g k", k=8)
    QR, QI, QT, PA, PB = [qtab(i) for i in range(5)]
    NI = FB[:, 7424:7936].bitcast(I32).rearrange("p (g k) -> p g k", k=16)
    NIC = FB[:, 7936:8448].bitcast(I32).rearrange("p (g k) -> p g k", k=16)
    BS = FB[:, 0:512].rearrange("p (h q) -> p h q", q=128)
    VT = STG[:, 2048:4096].bitcast(BF16).rearrange("p (a n) -> p a n", n=128)
    W3 = sb("W3", [128, 2, 32])
    G3 = sb("G3", [128, 3, 32])
    T1 = sb("T1", [128, 2, 32])
    T2 = sb("T2", [128, 2, 32])
    ARp = sb("ARp", [128, 32]); AIp = sb("AIp", [128, 32]); LDT = sb("LDT", [128, 32])
    LR = sb("LR", [128, 32]); LI = sb("LI", [128, 32])
    CR = sb("CR", [128, 32]); CI = sb("CI", [128, 32]); NR = sb("NR", [128, 32]); DEN = sb("DEN", [128, 32])
    TA = sb("TA", [128, 32]); TB = sb("TB", [128, 32])
    ARC = sb("ARC", [128, 2, 32]); AIC = sb("AIC", [128, 2, 32])
    DV = sb("DV", [128, 32])
    BGLU = sb("BGLU", [128, 4]); GSGU = sb("GSGU", [128, 4])
    WST = sb("WST", [128, 4, 128], BF16)
    WC = sb("WC", [128, 9, 44])

    S.dma("sync", IDENT[:], ident_in[:, :])
    S.dma("sync", MASKS[:], mask_in.rearrange("m p q -> p m q"))
    S.op("vector", lambda e: e.memset(ONESB[:], 1.0))
    S.op("vector", lambda e: e.memset(SIGN[0:64, :], -1.0))
    S.op("vector", lambda e: e.memset(SIGN[64:128, :], 1.0))
    S.op("vector", lambda e: e.memset(NSIGN[0:64, :], 1.0))
    S.op("vector", lambda e: e.memset(NSIGN[64:128, :], -1.0))
    S.dma("sync", STG[0:8, 0:128], c_in.rearrange("(k p) -> k p", p=128))
    S.dma("sync", STG[8:16, 0:128], cctx_in.rearrange("(k p) -> k p", p=128))
    S.dma("sync", STG[16:24, 0:128], g_final.rearrange("(k p) -> k p", p=128))
    S.bar()
    S.op("tensor", lambda e: e.transpose(ps[7][:, 0:24], STG[0:24, 0:128], IDENT[0:24, 0:24]))
    S.bar()
    S.op("vector", lambda e: e.tensor_copy(out=SC[:, :, 0], in_=ps[7][:, 0:8]))
    S.op("vector", lambda e: e.tensor_copy(out=SC[:, :, 1], in_=ps[7][:, 8:16]))
    S.op("vector", lambda e: e.tensor_copy(out=GFIN[:], in_=ps[7][:, 16:24]))
    S.bar()
    S.op("scalar", lambda e: e.activation(out=SC[:], in_=SC[:], func=AF.Silu))
    S.bar()

    def in_transpose(src, nblk, tok0):
        for b in range(nblk):
            S.dma("sync", TMP[:, :, 0:128], src[b * 128:(b + 1) * 128, :].rearrange("t (k d) -> t k d", d=128))
            S.bar()
            for k in range(8):
                S.op("tensor", lambda e, k=k: e.transpose(ps[k // 4][:, (k % 4) * 128:(k % 4 + 1) * 128],
                                                           TMP[:, k, 0:128], IDENT[:]))
            S.bar()
            S.op("vector", lambda e: e.tensor_copy(out=XT[:, 0:4, 0:128], in_=ps[0][:].rearrange("p (k t) -> p k t", t=128)))
            S.op("scalar", lambda e: e.copy(out=XT[:, 4:8, 0:128], in_=ps[1][:].rearrange("p (k t) -> p k t", t=128)))
            S.bar()
            t0 = tok0 + b * 128
            S.dma("sync", XRES[:, t0:t0 + 128].rearrange("(k p) t -> p k t", p=128), XT[:, :, 0:128])
            S.bar()

    in_transpose(ctx_in, 2, 0)
    in_transpose(x_in, 16, 256)

    def load_weight(dst, wap, kch, ncols, cb):
        for c0 in range(0, ncols, cb):
            stg = STG[:, 0:kch * cb].rearrange("p (k n) -> p k n", n=cb)
            S.dma("sync", stg, wap[:, c0:c0 + cb].rearrange("(k p) n -> p k n", p=128))
            S.bar()
            S.op("vector", lambda e, stg=stg, c0=c0: e.tensor_copy(out=dst[:, :, c0:c0 + cb], in_=stg))
            S.bar()

    def rstd_from(src_sq, nk, w, inv_n):
        for k in range(nk):
            S.op("tensor", lambda e, k=k: e.matmul(ps[0][:, 0:w], lhsT=ONESB[:], rhs=src_sq[:, k, 0:w],
                                                    start=(k == 0), stop=(k == nk - 1)))
        S.bar()
        S.op("scalar", lambda e: e.activation(out=RS[:, 0:w], in_=ps[0][:, 0:w], func=AF.Sqrt, bias=EPS, scale=inv_n))
        S.bar()
        S.op("vector", lambda e: e.reciprocal(out=RS[:, 0:w], in_=RS[:, 0:w]))
        S.bar()

    def norm_mod(Acoef, which_shift, issue=None, convert=None):
        Xn = [XT, STG[:].rearrange("p (k t) -> p k t", t=512)]

        def xload(i):
            t0, w = TILES[i]
            S.dma_async("sync", Xn[i % 2][:, 0:4, 0:w], XRES[0:512, t0:t0 + w].rearrange("(k p) t -> p k t", p=128))
            S.dma_async("gpsimd", Xn[i % 2][:, 4:8, 0:w], XRES[512:1024, t0:t0 + w].rearrange("(k p) t -> p k t", p=128))

        xload(0)
        for i, (t0, w) in enumerate(TILES):
            sel = 1 if t0 == 0 else 0
            Xi = Xn[i % 2]
            S.join()
            if i + 1 < len(TILES):
                xload(i + 1)
            if issue and i in issue:
                issue[i]()
            if convert and i in convert:
                convert[i]()
            S.op("scalar", lambda e, w=w, Xi=Xi: e.activation(out=OB[:, :, 0:w], in_=Xi[:, :, 0:w], func=AF.Square))
            S.bar()
            rstd_from(OB, 8, w, 1.0 / D)
            for k in range(8):
                S.op("vector", lambda e, k=k, w=w, sel=sel, Xi=Xi: e.scalar_tensor_tensor(
                    out=TMP[:, k, 0:w], in0=Xi[:, k, 0:w], scalar=Acoef[:, k, sel:sel + 1], in1=RS[:, 0:w],
                    op0=ALU.mult, op1=ALU.mult))
            S.bar()
            for k in range(8):
                S.op("scalar", lambda e, k=k, w=w, sel=sel, t0=t0: e.activation(
                    out=H[:, k, t0:t0 + w], in_=TMP[:, k, 0:w], func=AF.Identity,
                    bias=MOD[:, which_shift, k, sel:sel + 1], scale=1.0))
            S.bar()

    def resid_linear(src_dram, kch, gate_idx, Abufs, Wv):
        Xb = [XT, TMP, STG[:].rearrange("p (k t) -> p k t", t=512)]

        def loads(i):
            t0, w = TILES[i]
            S.dma("sync", Abufs[i % 2][:, 0:kch, 0:w], src_dram[:, t0:t0 + w].rearrange("(k p) t -> p k t", p=128))
            S.dma("gpsimd", Xb[i % 3][:, :, 0:w], XRES[:, t0:t0 + w].rearrange("(k p) t -> p k t", p=128))

        def store(i):
            t0, w = TILES[i]
            S.dma("gpsimd", XRES[:, t0:t0 + w].rearrange("(k p) t -> p k t", p=128), Xb[i % 3][:, :, 0:w])

        loads(0)
        S.bar()
        nt = len(TILES)
        for i, (t0, w) in enumerate(TILES):
            sel = 1 if t0 == 0 else 0
            Ab = Abufs[i % 2]
            for m in range(8):
                for k in range(kch):
                    S.op("tensor", lambda e, m=m, k=k, w=w, Ab=Ab: e.matmul(
                        ps[m][:, 0:w], lhsT=Wv[:, k, m * 128:(m + 1) * 128], rhs=Ab[:, k, 0:w],
                        start=(k == 0), stop=(k == kch - 1)))
            if i + 1 < nt:
                loads(i + 1)
            if i >= 1:
                store(i - 1)
            S.bar()
            Xi = Xb[i % 3]
            for m in range(8):
                S.op("vector", lambda e, m=m, w=w, sel=sel, Xi=Xi: e.scalar_tensor_tensor(
                    out=Xi[:, m, 0:w], in0=ps[m][:, 0:w], scalar=MOD[:, gate_idx, m, sel:sel + 1], in1=Xi[:, m, 0:w],
                    op0=ALU.mult, op1=ALU.add))
            S.bar()
        store(nt - 1)
        S.bar()

    try:
      for li in range(n_layers):
        S.dma("sync", STG[0:48, 0:128], b_ada[li].rearrange("(k p) -> k p", p=128))
        S.dma("sync", STG[48:56, 0:128], g_mix[li].rearrange("(k p) -> k p", p=128))
        S.dma("sync", STG[56:64, 0:128], g_ffn[li].rearrange("(k p) -> k p", p=128))
        S.dma("sync", STG[64:68, 0:128], b_glu[li].rearrange("(k p) -> k p", p=128))
        S.dma("sync", STG[68:72, 0:128], g_sgu[li].rearrange("(k p) -> k p", p=128))
        S.bar()
        S.op("tensor", lambda e: e.transpose(ps[7][:, 0:72], STG[0:72, 0:128], IDENT[0:72, 0:72]))
        S.bar()
        S.op("vector", lambda e: e.tensor_copy(out=BADA[:].rearrange("p q k -> p (q k)"), in_=ps[7][:, 0:48]))
        S.op("vector", lambda e: e.tensor_copy(out=GM[:], in_=ps[7][:, 48:56]))
        S.op("vector", lambda e: e.tensor_copy(out=GF[:], in_=ps[7][:, 56:64]))
        S.op("vector", lambda e: e.tensor_copy(out=BGLU[:], in_=ps[7][:, 64:68]))
        S.op("vector", lambda e: e.tensor_copy(out=GSGU[:], in_=ps[7][:, 68:72]))
        S.bar()
        WAs = [STG[:, 0:2048].rearrange("p (k n) -> p k n", n=256), STG[:, 2048:4096].rearrange("p (k n) -> p k n", n=256)]

        def wada_dma(lyr, blk, asyn=False):
            WA = WAs[blk % 2]
            src = w_ada[lyr, :, blk * 256:(blk + 1) * 256].rearrange("(k p) n -> p k n", p=128)
            f = S.dma_async if asyn else S.dma
            f("sync", WA[:, 0:4, :], src[:, 0:4, :])
            f("gpsimd", WA[:, 4:8, :], src[:, 4:8, :])

        def wada_mm(blk, bank_ap, col0):
            q, mq = blk // 4, blk % 4
            WA = WAs[blk % 2]
            for mm in range(2):
                m = mq * 2 + mm
                for k in range(8):
                    S.op("tensor", lambda e, m=m, mm=mm, k=k, q=q, WA=WA: e.matmul(
                        bank_ap[:, col0 + (q * 8 + m) * 2:col0 + (q * 8 + m) * 2 + 2], lhsT=WA[:, k, mm * 128:(mm + 1) * 128],
                        rhs=SC[:, k, :], start=(k == 0), stop=(k == 7)))

        if li == 0:
            wada_dma(0, 0)
            S.bar()
            for blk in range(24):
                if blk + 1 < 24:
                    wada_dma(0, blk + 1)
                wada_mm(blk, ps[0], 0)
                S.bar()
            S.op("vector", lambda e: e.tensor_copy(out=MODN[:], in_=ps[0][:, 0:96]))
            S.bar()
        S.op("vector", lambda e: e.tensor_tensor(
            out=MOD[:].rearrange("p q k n -> p (q k) n"), in0=MODN[:].rearrange("p (a n) -> p a n", n=2),
            in1=BADA[:].rearrange("p q k -> p (q k)").unsqueeze(2).to_broadcast([128, 48, 2]), op=ALU.add))
        S.bar()
        S.op("vector", lambda e: e.scalar_tensor_tensor(
            out=A1[:], in0=MOD[:, 1, :, :], scalar=1.0, in1=GM[:].unsqueeze(2).to_broadcast([128, 8, 2]),
            op0=ALU.add, op1=ALU.mult))
        S.op("vector", lambda e: e.scalar_tensor_tensor(
            out=A2[:], in0=MOD[:, 4, :, :], scalar=1.0, in1=GF[:].unsqueeze(2).to_broadcast([128, 8, 2]),
            op0=ALU.add, op1=ALU.mult))
        S.bar()

        ckp("ada")
        WIN = WB[:, 0:8 * 1536].rearrange("p (k n) -> p k n", n=1536)
        FBs = [FB[:, 0:4096].rearrange("p (k n) -> p k n", n=512), FB[:, 4096:8192].rearrange("p (k n) -> p k n", n=512)]

        def win_issue(bk):
            def f():
                src = w_in[li][:, bk * 512:(bk + 1) * 512].rearrange("(k p) n -> p k n", p=128)
                S.dma_async("sync", FBs[bk % 2][:, 0:4, :], src[:, 0:4, :])
                S.dma_async("gpsimd", FBs[bk % 2][:, 4:8, :], src[:, 4:8, :])
            return f

        def win_conv(bk):
            def f():
                S.op("vector", lambda e: e.tensor_copy(out=WIN[:, :, bk * 512:(bk + 1) * 512], in_=FBs[bk % 2]))
            return f

        norm_mod(A1, 0, issue={0: win_issue(0), 1: win_issue(1), 2: win_issue(2)},
                 convert={1: win_conv(0), 2: win_conv(1), 3: win_conv(2)})

        ckp("norm1")
        for (t0, w) in TILES:
            j0, nj = t0 // 8, w // 8
            for grp in range(2):
                ms = list(range(8)) if grp == 0 else list(range(8, 12))
                for bi, m in enumerate(ms):
                    for k in range(8):
                        S.op("tensor", lambda e, bi=bi, m=m, k=k, w=w, t0=t0: e.matmul(
                            ps[bi][:, 0:w], lhsT=WIN[:, k, m * 128:(m + 1) * 128], rhs=H[:, k, t0:t0 + w],
                            start=(k == 0), stop=(k == 7)))
                if grp == 0:
                    S.join()
                else:
                    S.bar()
                for bi, m in enumerate(ms):
                    if m < 4:
                        S.op("vector", lambda e, bi=bi, m=m, w=w, j0=j0, nj=nj: e.tensor_copy(
                            out=USP[:, m, :, j0:j0 + nj].rearrange("p s j -> p j s"),
                            in_=ps[bi][:, 0:w].rearrange("p (j s) -> p j s", s=8)))
                    elif m < 8:
                        S.op("scalar", lambda e, bi=bi, m=m, w=w: e.activation(
                            out=OB[:, m - 4, 0:w], in_=ps[bi][:, 0:w], func=AF.Gelu_apprx_tanh))
                    else:
                        S.op("scalar", lambda e, bi=bi, m=m, w=w: e.activation(
                            out=TMP[:, m - 8, 0:w], in_=ps[bi][:, 0:w], func=AF.Gelu_apprx_tanh))
                S.bar()
            S.dma_async("sync", UG[:, t0:t0 + w].rearrange("(k p) t -> p k t", p=128), OB[:, 0:4, 0:w])
            S.dma_async("gpsimd", VG[:, t0:t0 + w].rearrange("(k p) t -> p k t", p=128), TMP[:, 0:4, 0:w])
            S.bar()

        S.join()
        ckp("win")
        S.dma("sync", USSMP.rearrange("(q p) s j -> p q s j", p=128), USP)
        S.dma("gpsimd", STG[0:32, 0:16], ssm_d[li].rearrange("(g c) -> g c", c=16))
        S.bar()
        for s_ in range(8):
            S.dma("sync" if s_ % 2 == 0 else "gpsimd", IM[s_ * 16:(s_ + 1) * 16, :, :],
                  USSMP[:, s_, :].rearrange("(g c) j -> c g j", c=16))
        S.bar()

        S.op("vector", lambda e: e.tensor_copy(out=STG[0:32, 128:256].rearrange("p (s c) -> p s c", c=16),
                                               in_=STG[0:32, 0:16].unsqueeze(1).to_broadcast([32, 8, 16])))
        S.bar()
        S.op("tensor", lambda e: e.transpose(ps[7][:, 0:32], STG[0:32, 128:256], IDENT[0:32, 0:32]))
        S.bar()
        S.op("vector", lambda e: e.tensor_copy(out=DV[:], in_=ps[7][:, 0:32]))
        ckp("im2col")
        ada_state = {"next": 0, "pending": None}
        for dr in range(2):
            CIN1 = STG[:, 0:512].rearrange("p (q n) -> p q n", n=128)
            CIN2 = STG[:, 512:1024].rearrange("p (q n) -> p q n", n=128)
            for hf in range(2):
                lo = slice(hf * 64, hf * 64 + 64)
                csrc = [c_re, c_im] if hf == 0 else [c_im, c_re]
                S.dma("sync", CIN1[:, :, lo], csrc[0][li, dr].rearrange("(q g) c p -> (g c) q p", q=4))
                S.dma("gpsimd", CIN2[:, :, lo], csrc[1][li, dr].rearrange("(q g) c p -> (g c) q p", q=4))
            S.bar()
            for q in range(4):
                S.op("tensor", lambda e, q=q: e.transpose(ps[0][:, q * 128:(q + 1) * 128], CIN1[:, q, :], IDENT[:]))
                S.op("tensor", lambda e, q=q: e.transpose(ps[1][:, q * 128:(q + 1) * 128], CIN2[:, q, :], IDENT[:]))
            S.bar()
            S.op("vector", lambda e: e.tensor_copy(out=CX1.rearrange("p g c -> p (g c)"), in_=ps[0][:]))
            S.op("scalar", lambda e: e.copy(out=CX2.rearrange("p g c -> p (g c)"), in_=ps[1][:]))
            S.bar()
            BIN1 = STG[0:32, 0:2048].rearrange("p (h q c) -> p h q c", h=2, c=16)
            BIN2 = STG[0:32, 2048:4096].rearrange("p (h q c) -> p h q c", h=2, c=16)
            S.dma("sync", BIN1[:, 0, :, :], b_re[li, dr])
            S.dma("gpsimd", BIN1[:, 1, :, :], b_im[li, dr])
            S.dma("sync", BIN2[:, 0, :, :], b_im[li, dr])
            S.dma("gpsimd", BIN2[:, 1, :, :], b_re[li, dr])
            AIN = RS[0:32, 0:256].rearrange("p (a q) -> p a q", q=64)
            S.dma("sync", AIN[:, 0, :], a_re[li, dr])
            S.dma("gpsimd", AIN[:, 1, :], a_re[li, dr])
            S.dma("sync", AIN[:, 2, :], a_im[li, dr])
            S.dma("gpsimd", AIN[:, 3, :], a_im[li, dr])
            S.bar()
            for c_ in range(16):
                S.op("tensor", lambda e, c_=c_: e.transpose(ps[2][:, c_ * 32:(c_ + 1) * 32], BIN1[:, :, :, c_], IDENT[0:32, 0:32]))
                S.op("tensor", lambda e, c_=c_: e.transpose(ps[3][:, c_ * 32:(c_ + 1) * 32], BIN2[:, :, :, c_], IDENT[0:32, 0:32]))
            S.op("tensor", lambda e: e.transpose(ps[4][:, 0:32], RS[0:32, 0:128], IDENT[0:32, 0:32]))
            S.op("tensor", lambda e: e.transpose(ps[4][:, 32:64], RS[0:32, 128:256], IDENT[0:32, 0:32]))
            S.bar()
            S.op("vector", lambda e: e.tensor_copy(out=BX1.rearrange("p g c -> p c g"), in_=ps[2][:].rearrange("p (c g) -> p c g", g=32)))
            S.op("scalar", lambda e: e.copy(out=BX2.rearrange("p g c -> p c g"), in_=ps[3][:].rearrange("p (c g) -> p c g", g=32)))
            S.op("vector", lambda e: e.tensor_copy(out=ARp[:], in_=ps[4][:, 0:32]))
            S.op("vector", lambda e: e.tensor_copy(out=AIp[:], in_=ps[4][:, 32:64]))
            S.dma("sync", LDT[:], log_dt[li, dr].partition_broadcast(128))
            S.bar()
            ckp("pl%d" % dr)
            S.op("scalar", lambda e: e.activation(out=LDT[:], in_=LDT[:], func=AF.Exp))
            S.bar()
            S.op("vector", lambda e: e.tensor_tensor(out=LR[:], in0=ARp[:], in1=LDT[:], op=ALU.mult))
            S.op("gpsimd", lambda e: e.tensor_tensor(out=LI[:], in0=AIp[:], in1=LDT[:], op=ALU.mult))
            S.bar()
            ckp("pb%d" % dr)
            ks = list(range(-8, 0)) + list(range(1, 9))
            for idx, kk in enumerate(ks):
                S.op("vector", lambda e, idx=idx, kk=kk: e.tensor_scalar(
                    out=ARG[:, :, idx], in0=LI[:], scalar1=float(kk), scalar2=None, op0=ALU.mult))
                S.op("gpsimd", lambda e, idx=idx, kk=kk: e.tensor_scalar(
                    out=EARG[:, :, idx], in0=LR[:], scalar1=float(kk), scalar2=None, op0=ALU.mult))
            S.bar()
            ckp("pc%d" % dr)
            S.op("vector", lambda e: e.tensor_scalar(out=ARGC, in0=ARG, scalar1=TWO_PI / 4, scalar2=None, op0=ALU.add))
            S.op("scalar", lambda e: e.activation(out=MAG, in_=EARG, func=AF.Exp))
            S.bar()
            ckp("pd%d" % dr)
            S.op("vector", lambda e: e.tensor_scalar(out=NI, in0=ARG, scalar1=1.0 / TWO_PI, scalar2=None, op0=ALU.mult))
            S.op("gpsimd", lambda e: e.tensor_scalar(out=NIC, in0=ARGC, scalar1=1.0 / TWO_PI, scalar2=None, op0=ALU.mult))
            S.bar()
            ckp("pe%d" % dr)
            S.op("vector", lambda e: e.tensor_copy(out=NF, in_=NI))
            S.op("gpsimd", lambda e: e.tensor_copy(out=NFC, in_=NIC))
            S.bar()
            ckp("pf%d" % dr)
            S.op("vector", lambda e: e.scalar_tensor_tensor(out=ARG, in0=NF, scalar=-TWO_PI, in1=ARG, op0=ALU.mult, op1=ALU.add))
            S.op("vector", lambda e: e.scalar_tensor_tensor(out=ARGC, in0=NFC, scalar=-TWO_PI, in1=ARGC, op0=ALU.mult, op1=ALU.add))
            S.bar()
            S.op("vector", lambda e: e.tensor_scalar(out=ARG, in0=ARG, scalar1=3.1415925, scalar2=-3.1415925, op0=ALU.min, op1=ALU.max))
            S.op("gpsimd", lambda e: e.tensor_scalar(out=ARGC, in0=ARGC, scalar1=3.1415925, scalar2=-3.1415925, op0=ALU.min, op1=ALU.max))
            S.bar()
            ckp("pg%d" % dr)
            S.op("scalar", lambda e: e.activation(out=PIM, in_=ARG, func=AF.Sin))
            S.op("scalar", lambda e: e.activation(out=PRE, in_=ARGC, func=AF.Sin))
            S.bar()
            S.op("vector", lambda e: e.tensor_tensor(out=PIM, in0=PIM, in1=MAG, op=ALU.mult))
            S.op("gpsimd", lambda e: e.tensor_tensor(out=PRE, in0=PRE, in1=MAG, op=ALU.mult))
            S.bar()
            ckp("ph%d" % dr)
            S.op("vector", lambda e: e.tensor_scalar(out=NR[:], in0=PRE[:, :, 8], scalar1=-1.0, scalar2=None, op0=ALU.add))
            S.op("gpsimd", lambda e: e.tensor_tensor(out=DEN[:], in0=ARp[:], in1=ARp[:], op=ALU.mult))
            ckp("c0")
            S.op("vector", lambda e: e.tensor_tensor(out=TA[:], in0=AIp[:], in1=AIp[:], op=ALU.mult))
            ckp("c1")
            S.op("vector", lambda e: e.tensor_tensor(out=DEN[:], in0=DEN[:], in1=TA[:], op=ALU.add))
            ckp("c2")
            S.op("vector", lambda e: e.reciprocal(out=DEN[:], in_=DEN[:]))
            ckp("c3")
            S.op("vector", lambda e: e.tensor_tensor(out=TA[:], in0=NR[:], in1=ARp[:], op=ALU.mult))
            S.op("gpsimd", lambda e: e.tensor_tensor(out=TB[:], in0=PIM[:, :, 8], in1=AIp[:], op=ALU.mult))
            ckp("c4")
            S.op("vector", lambda e: e.tensor_tensor(out=CR[:], in0=TA[:], in1=TB[:], op=ALU.add))
            ckp("c5")
            S.op("vector", lambda e: e.tensor_tensor(out=TA[:], in0=PIM[:, :, 8], in1=ARp[:], op=ALU.mult))
            S.op("gpsimd", lambda e: e.tensor_tensor(out=TB[:], in0=NR[:], in1=AIp[:], op=ALU.mult))
            ckp("c6")
            S.op("vector", lambda e: e.tensor_tensor(out=CI[:], in0=TA[:], in1=TB[:], op=ALU.subtract))
            ckp("c7")
            S.op("vector", lambda e: e.tensor_tensor(out=CR[:], in0=CR[:], in1=DEN[:], op=ALU.mult))
            S.op("gpsimd", lambda e: e.tensor_tensor(out=CI[:], in0=CI[:], in1=DEN[:], op=ALU.mult))
            ckp("c8")
            ckp("pi%d" % dr)
            CRb = CR[:].unsqueeze(2).to_broadcast([128, 32, 8])
            CIb = CI[:].unsqueeze(2).to_broadcast([128, 32, 8])
            S.op("vector", lambda e: e.tensor_tensor(out=QR, in0=PRE[:, :, 0:8], in1=CRb, op=ALU.mult))
            S.op("gpsimd", lambda e: e.tensor_tensor(out=QT, in0=PIM[:, :, 0:8], in1=CIb, op=ALU.mult))
            S.bar()
            S.op("vector", lambda e: e.tensor_tensor(out=QR, in0=QR, in1=QT, op=ALU.subtract))
            S.bar()
            S.op("vector", lambda e: e.tensor_tensor(out=QI, in0=PRE[:, :, 0:8], in1=CIb, op=ALU.mult))
            S.op("gpsimd", lambda e: e.tensor_tensor(out=QT, in0=PIM[:, :, 0:8], in1=CRb, op=ALU.mult))
            S.bar()
            S.op("vector", lambda e: e.tensor_tensor(out=QI, in0=QI, in1=QT, op=ALU.add))
            S.bar()
            S.op("vector", lambda e: e.tensor_scalar(out=QI, in0=QI, scalar1=SIGN[:, 0:1], scalar2=None, op0=ALU.mult))
            S.op("gpsimd", lambda e: e.tensor_scalar(out=PA, in0=PRE[:, :, 8:16], scalar1=NSIGN[:, 0:1], scalar2=None, op0=ALU.mult))
            S.op("scalar", lambda e: e.mul(out=PB, in_=PIM[:, :, 8:16], mul=-1.0))
            S.bar()
            S.op("vector", lambda e: e.tensor_copy(out=ARC[:, 0, :], in_=PRE[:, :, 15]))
            S.op("vector", lambda e: e.tensor_copy(out=ARC[:, 1, :], in_=PRE[:, :, 15]))
            S.op("gpsimd", lambda e: e.tensor_scalar(out=AIC[:, 0, :], in0=PIM[:, :, 15], scalar1=SIGN[:, 0:1], scalar2=None, op0=ALU.mult))
            S.op("gpsimd", lambda e: e.tensor_scalar(out=AIC[:, 1, :], in0=PIM[:, :, 15], scalar1=NSIGN[:, 0:1], scalar2=None, op0=ALU.mult))
            S.bar()
            ckp("prep%d" % dr + "")
            for s_ in range(8):
                qi = (7 - s_) if dr == 0 else s_
                ri = s_ if dr == 0 else (7 - s_)
                S.op("vector", lambda e, s_=s_, qi=qi: e.tensor_tensor(
                    out=LT[:, :, s_, :], in0=BX1, in1=QR[:, :, qi:qi + 1].to_broadcast([128, 32, 16]), op=ALU.mult))
                S.op("gpsimd", lambda e, s_=s_, ri=ri: e.tensor_tensor(
                    out=RT[:, :, s_, :], in0=CX1, in1=PA[:, :, ri:ri + 1].to_broadcast([128, 32, 16]), op=ALU.mult))
            S.bar()
            TL = STG[:].rearrange("p (g s c) -> p g s c", s=8, c=16)
            for s_ in range(8):
                qi = (7 - s_) if dr == 0 else s_
                S.op("vector" if s_ % 2 == 0 else "gpsimd", lambda e, s_=s_, qi=qi: e.tensor_tensor(
                    out=TL[:, :, s_, :], in0=BX2, in1=QI[:, :, qi:qi + 1].to_broadcast([128, 32, 16]), op=ALU.mult))
            S.bar()
            S.op("vector", lambda e: e.tensor_tensor(out=LT, in0=LT, in1=TL, op=ALU.add))
            S.bar()
            for s_ in range(8):
                ri = s_ if dr == 0 else (7 - s_)
                S.op("vector" if s_ % 2 == 0 else "gpsimd", lambda e, s_=s_, ri=ri: e.tensor_tensor(
                    out=TL[:, :, s_, :], in0=CX2, in1=PB[:, :, ri:ri + 1].to_broadcast([128, 32, 16]), op=ALU.mult))
            S.bar()
            S.op("vector", lambda e: e.tensor_tensor(out=RT, in0=RT, in1=TL, op=ALU.add))
            S.bar()
            ckp("lr%d" % dr + "")
            for rnd in range(4):
                for gi in range(8):
                    g = rnd * 8 + gi
                    Lg = LT[:, g, :, :].rearrange("p s c -> p (s c)")
                    Rg = RT[:, g, :, :].rearrange("p s c -> p (s c)")
                    S.op("tensor", lambda e, gi=gi, Lg=Lg, Rg=Rg: e.matmul(
                        ps[gi // 4][:, (gi % 4) * 128:(gi % 4 + 1) * 128], lhsT=Lg, rhs=Rg, start=True, stop=True))
                    S.op("tensor", lambda e, gi=gi, Lg=Lg: e.transpose(
                        ps[2 + gi // 4][:, (gi % 4) * 128:(gi % 4 + 1) * 128], Lg, IDENT[:]))
                S.bar()
                for bk in range(2):
                    g0 = rnd * 8 + bk * 4
                    S.op("vector", lambda e, bk=bk, g0=g0, dr=dr: e.tensor_tensor(
                        out=TOEP[:, g0:g0 + 4, :], in0=ps[bk][:].rearrange("p (g n) -> p g n", n=128),
                        in1=MASKS[:, dr:dr + 1, :].to_broadcast([128, 4, 128]), op=ALU.mult))
                    S.op("scalar", lambda e, bk=bk, g0=g0: e.copy(
                        out=ET[:, g0:g0 + 4, :], in_=ps[2 + bk][:].rearrange("p (g n) -> p g n", n=128)))
                S.bar()
                for bk in range(2):
                    g0 = rnd * 8 + bk * 4
                    S.op("vector", lambda e, bk=bk, g0=g0: e.tensor_copy(
                        out=ESW[:, g0:g0 + 4, 0:64], in_=ps[2 + bk][:].rearrange("p (g n) -> p g n", n=128)[:, :, 64:128]))
                    S.op("vector", lambda e, bk=bk, g0=g0: e.tensor_copy(
                        out=ESW[:, g0:g0 + 4, 64:128], in_=ps[2 + bk][:].rearrange("p (g n) -> p g n", n=128)[:, :, 0:64]))
                S.bar()
            S.op("scalar", lambda e: e.copy(out=RB.rearrange("p g n -> p (g n)"), in_=RT.rearrange("p g s c -> p (g s c)")))
            if dr == 0:
                for g in range(32):
                    S.op("vector", lambda e, g=g: e.scalar_tensor_tensor(
                        out=TOEP[:, g, :], in0=IDENT[:], scalar=DV[:, g:g + 1], in1=TOEP[:, g, :],
                        op0=ALU.mult, op1=ALU.add))
            S.op("gpsimd", lambda e: e.memset(W3[:], 0.0))
            S.bar()
            ckp("tiles%d" % dr + "")
            blocks = [(0, 32)] + [(32 + 64 * b, 64) for b in range(4)]
            order = blocks if dr == 0 else [blocks[0]] + blocks[:0:-1]
            for (jb, nb) in order:
                for half in range(2):
                    for gi in range(16):
                        g = half * 16 + gi
                        for arr in range(2):
                            ii = gi * 2 + arr
                            Em = ET if arr == 0 else ESW
                            S.op("tensor", lambda e, ii=ii, g=g, Em=Em, jb=jb, nb=nb: e.matmul(
                                ps[ii // 8][:, (ii % 8) * 64:(ii % 8) * 64 + nb], lhsT=Em[:, g, :],
                                rhs=IM[:, g, jb:jb + nb], start=True, stop=True))
                    S.bar()
                    for bk in range(4):
                        g0 = half * 16 + bk * 4
                        S.op("vector" if bk % 2 == 0 else "scalar", (lambda e, bk=bk, g0=g0, nb=nb: e.tensor_copy(
                            out=SS[:, 0:nb, :, g0:g0 + 4].rearrange("p j a g -> p g a j"),
                            in_=ps[bk][:].rearrange("p (g a j) -> p g a j", a=2, j=64)[:, :, :, 0:nb]))
                            if bk % 2 == 0 else (lambda e, bk=bk, g0=g0, nb=nb: e.copy(
                            out=SS[:, 0:nb, :, g0:g0 + 4].rearrange("p j a g -> p g a j"),
                            in_=ps[bk][:].rearrange("p (g a j) -> p g a j", a=2, j=64)[:, :, :, 0:nb])))
                    S.bar()
                js = list(range(jb, jb + nb)) if dr == 0 else list(range(jb + nb - 1, jb - 1, -1))
                pjl = None
                nsub = 3 if nb == 64 else 1
                cuts = [round(len(js) * x / nsub) for x in range(nsub + 1)]
                for j in js:
                    jl = j - jb
                    if (j - js[0]) * (1 if dr == 0 else -1) in cuts[:-1]:
                        last_sub = ((jb, nb) == order[-1]) and ((j - js[0]) * (1 if dr == 0 else -1) == cuts[-2])
                        S.join()
                        if li + 1 < n_layers:
                            if ada_state["pending"] is not None:
                                wada_mm(ada_state["pending"], ps[7], 400)
                                ada_state["pending"] = None
                            if (not last_sub) and ada_state["next"] < 24:
                                wada_dma(li + 1, ada_state["next"], asyn=True)
                                ada_state["pending"] = ada_state["next"]
                                ada_state["next"] += 1
                    Wc = W3[:] if pjl is None else SS[:, pjl, :, :]
                    S.op("vector", lambda e, jl=jl, Wc=Wc: e.tensor_tensor(out=G3[:, 0:2, :], in0=Wc, in1=SS[:, jl, :, :], op=ALU.add))
                    S.op("vector", lambda e: e.tensor_tensor(out=T1[:], in0=G3[:, 0:2, :], in1=ARC[:], op=ALU.mult))
                    S.op("vector", lambda e: e.tensor_tensor(out=T2[:, 0, :], in0=G3[:, 1, :], in1=AIC[:, 0, :], op=ALU.mult))
                    S.op("vector", lambda e: e.tensor_tensor(out=T2[:, 1, :], in0=G3[:, 0, :], in1=AIC[:, 1, :], op=ALU.mult))
                    S.op("vector", lambda e, jl=jl: e.tensor_tensor(out=SS[:, jl, :, :], in0=T1[:], in1=T2[:], op=ALU.add))
                    pjl = jl
                S.bar()
                if dr == 0:
                    S.op("scalar", lambda e, jb=jb: e.copy(out=HH[:, jb, :], in_=W3[:, 0, :]))
                    S.op("scalar", lambda e, jb=jb, nb=nb: e.copy(out=HH[:, jb + 1:jb + nb, :], in_=SS[:, 0:nb - 1, 0, :]))
                else:
                    S.op("scalar", lambda e, jb=jb, nb=nb: e.copy(out=HH[:, jb + nb - 1, :], in_=W3[:, 0, :]))
                    S.op("scalar", lambda e, jb=jb, nb=nb: e.copy(out=HH[:, jb:jb + nb - 1, :], in_=SS[:, 1:nb, 0, :]))
                S.bar()
                S.op("vector", lambda e, pjl=pjl: e.tensor_copy(out=W3[:], in_=SS[:, pjl, :, :]))
                S.bar()
            ckp("rec%d" % dr + "")
            for rnd in range(4):
                for gi in range(8):
                    g = rnd * 8 + gi
                    S.op("tensor", lambda e, gi=gi, g=g: e.matmul(
                        ps[gi][:, 0:NJ], lhsT=TOEP[:, g, :], rhs=IM[:, g, :], start=True, stop=False))
                    S.op("tensor", lambda e, gi=gi, g=g: e.matmul(
                        ps[gi][:, 0:NJ], lhsT=RB[:, g, :], rhs=HH[:, :, g], start=False, stop=True))
                S.bar()
                for gi in range(8):
                    g = rnd * 8 + gi
                    if dr == 0:
                        S.op("vector" if gi % 2 == 0 else "scalar", (lambda e, gi=gi, g=g: e.tensor_copy(out=YS[:, g, :], in_=ps[gi][:, 0:NJ]))
                             if gi % 2 == 0 else (lambda e, gi=gi, g=g: e.copy(out=YS[:, g, :], in_=ps[gi][:, 0:NJ])))
                    else:
                        S.op("vector", lambda e, gi=gi, g=g: e.tensor_tensor(out=YS[:, g, :], in0=YS[:, g, :], in1=ps[gi][:, 0:NJ], op=ALU.add))
                S.bar()
        if li + 1 < n_layers:
            assert ada_state["next"] == 24 and ada_state["pending"] is None, ada_state
            S.op("vector", lambda e: e.tensor_copy(out=MODN[:], in_=ps[7][:, 400:496]))
        ckp("read")
        S.dma("sync", YSP[:, :, :], YS)
        S.bar()
        YV = YS.rearrange("p (q s) j -> p q s j", s=8)
        YSPv = YSP.rearrange("(s c) (q g) j -> g c q s j", c=16, g=8)
        for g8 in range(8):
            for q in range(4):
                S.dma(["sync", "gpsimd"][q % 2], YV[g8 * 16:(g8 + 1) * 16, q, :, :], YSPv[g8, :, q, :, :])
            S.bar()
        S.bar()

        ckp("unim")
        WG = WB[:, 0:4 * 512].rearrange("p (k n) -> p k n", n=512)
        load_weight(WG, w_glu[li], 4, 512, 512)
        for (t0, w) in TILES:
            j0, nj = t0 // 8, w // 8
            for q in range(4):
                S.op("scalar", lambda e, q=q, w=w, j0=j0, nj=nj: e.activation(
                    out=TMP[:, q, 0:w].rearrange("p (j s) -> p j s", s=8),
                    in_=YV[:, q, :, j0:j0 + nj].rearrange("p s j -> p j s"), func=AF.Gelu_apprx_tanh))
            S.bar()
            S.op("vector", lambda e, w=w: e.tensor_copy(out=OB[:, 0:4, 0:w], in_=TMP[:, 0:4, 0:w]))
            S.bar()
            for m in range(4):
                for k in range(4):
                    S.op("tensor", lambda e, m=m, k=k, w=w: e.matmul(
                        ps[m][:, 0:w], lhsT=WG[:, k, m * 128:(m + 1) * 128], rhs=OB[:, k, 0:w],
                        start=(k == 0), stop=(k == 3)))
            S.bar()
            for m in range(4):
                S.op("scalar", lambda e, m=m, w=w: e.activation(
                    out=XT[:, m, 0:w], in_=ps[m][:, 0:w], func=AF.Sigmoid, bias=BGLU[:, m:m + 1], scale=1.0))
            S.bar()
            S.join()
            S.op("vector", lambda e, w=w: e.tensor_tensor(out=OB[:, 4:8, 0:w], in0=TMP[:, 0:4, 0:w], in1=XT[:, 0:4, 0:w], op=ALU.mult))
            S.bar()
            S.dma_async("sync", MIX[0:512, t0:t0 + w].rearrange("(k p) t -> p k t", p=128), OB[:, 4:8, 0:w])

        S.join()
        ckp("glu")
        S.dma("sync", TMP[:, 0:4, 0:128], w_sp[li].rearrange("h p q -> p h q"))
        S.dma("gpsimd", BS.rearrange("p h q -> p (h q)"), b_sp[li].rearrange("h q -> (h q)").partition_broadcast(128))
        S.bar()
        for h in range(4):
            S.op("tensor", lambda e, h=h: e.transpose(ps[0][:, h * 128:(h + 1) * 128], TMP[:, h, 0:128], IDENT[:]))
        S.bar()
        S.op("vector", lambda e: e.tensor_copy(out=WST[:], in_=ps[0][:].rearrange("p (h n) -> p h n", n=128)))
        S.bar()
        VGb = [XT[:, 0:4, :], XT[:, 4:8, :]]
        UGb = [OB[:, 0:4, :], ACTB[:, 0:4, :]]

        def sgu_loads(i):
            t0, w = TILES[i]
            S.dma_async("sync", VGb[i % 2][:, :, 0:w], VG[:, t0:t0 + w].rearrange("(k p) t -> p k t", p=128))
            S.dma_async("gpsimd", UGb[i % 2][:, :, 0:w], UG[:, t0:t0 + w].rearrange("(k p) t -> p k t", p=128))

        WOv = WB[:, 4096:4096 + 8 * D].rearrange("p (k n) -> p k n", n=D)
        WOs = [FB[:, 1024:5120].rearrange("p (k n) -> p k n", n=512), FB[:, 5120:9216].rearrange("p (k n) -> p k n", n=512)]
        sgu_loads(0)
        for i, (t0, w) in enumerate(TILES):
            nchk = w // 128
            VGi, UGi = VGb[i % 2], UGb[i % 2]
            S.join()
            if i + 1 < len(TILES):
                sgu_loads(i + 1)
            if i < 2:
                wsrc = w_out[li][:, i * 512:(i + 1) * 512].rearrange("(k p) n -> p k n", p=128)
                S.dma_async("sync", WOs[i][:, 0:4, :], wsrc[:, 0:4, :])
                S.dma_async("gpsimd", WOs[i][:, 4:8, :], wsrc[:, 4:8, :])
            if i in (2, 3):
                S.op("vector", lambda e, i=i: e.tensor_copy(out=WOv[:, :, (i - 2) * 512:(i - 1) * 512], in_=WOs[i - 2]))
            S.op("scalar", lambda e, w=w, VGi=VGi: e.activation(out=SQ[:, 0:4, 0:w], in_=VGi[:, 0:4, 0:w], func=AF.Square))
            S.bar()
            rstd_from(SQ, 4, w, 1.0 / 512)
            for k in range(4):
                S.op("vector", lambda e, k=k, w=w, VGi=VGi: e.scalar_tensor_tensor(
                    out=TMP[:, k, 0:w], in0=VGi[:, k, 0:w], scalar=GSGU[:, k:k + 1], in1=RS[:, 0:w],
                    op0=ALU.mult, op1=ALU.mult))
            S.bar()
            for ck in range(nchk):
                for h in range(4):
                    ii = ck * 4 + h
                    S.op("tensor", lambda e, ii=ii, ck=ck, h=h: e.transpose(
                        ps[ii // 4][:, (ii % 4) * 128:(ii % 4 + 1) * 128], TMP[:, h, ck * 128:(ck + 1) * 128], IDENT[:]))
            S.bar()
            for ck in range(nchk):
                S.op("vector" if ck % 2 == 0 else "scalar", (lambda e, ck=ck: e.tensor_copy(
                    out=VT[:, ck * 4:(ck + 1) * 4, :], in_=ps[ck][:].rearrange("p (h n) -> p h n", n=128)))
                    if ck % 2 == 0 else (lambda e, ck=ck: e.copy(
                    out=VT[:, ck * 4:(ck + 1) * 4, :], in_=ps[ck][:].rearrange("p (h n) -> p h n", n=128))))
            S.bar()
            for ck in range(nchk):
                for h in range(4):
                    ii = ck * 4 + h
                    S.op("tensor", lambda e, ii=ii, ck=ck, h=h: e.matmul(
                        ps[4 + ck][:, h * 128:(h + 1) * 128], lhsT=VT[:, ii, :], rhs=WST[:, h, :], start=True, stop=True))
            S.bar()
            for ck in range(nchk):
                S.op("vector", lambda e, ck=ck: e.tensor_tensor(
                    out=TMP[:, 4:8, ck * 128:(ck + 1) * 128], in0=ps[4 + ck][:].rearrange("p (h n) -> p h n", n=128),
                    in1=BS, op=ALU.add))
            S.bar()
            S.op("vector", lambda e, w=w, UGi=UGi: e.tensor_tensor(out=OB[:, 4:8, 0:w], in0=TMP[:, 4:8, 0:w], in1=UGi[:, 0:4, 0:w], op=ALU.mult))
            S.bar()
            S.dma_async("sync", MIX[512:1024, t0:t0 + w].rearrange("(k p) t -> p k t", p=128), OB[:, 4:8, 0:w])

        S.join()
        ckp("sgu")
        resid_linear(MIX, 8, 2, [ACTB[:, 0:8, :], ACTB[:, 8:16, :]], WOv)

        ckp("wout")
        norm_mod(A2, 3)

        ckp("norm2")
        S.dma("sync", FB[0:9, 0:2 * DFF], w_conv[li].rearrange("a b n -> (a b) n"))
        S.bar()
        for ch in range(44):
            S.op("tensor", lambda e, ch=ch: e.transpose(ps[7][:, ch * 9:(ch + 1) * 9], FB[0:9, ch * 128:(ch + 1) * 128], IDENT[0:9, 0:9]))
        S.bar()
        S.op("vector", lambda e: e.tensor_copy(out=WC[:].rearrange("p t c -> p c t"), in_=ps[7][:, 0:396].rearrange("p (c t) -> p c t", t=9)))
        S.bar()
        WUb = [OBt[:, 0:2048].rearrange("p (k n) -> p k n", n=256), OBt[:, 2048:4096].rearrange("p (k n) -> p k n", n=256)]
        WDv = WB[:, 0:22 * D].rearrange("p (k n) -> p k n", n=D)
        wd_stage = [XTt[:, 0:2816].rearrange("p (k n) -> p k n", n=128), TMPt[:, 0:2816].rearrange("p (k n) -> p k n", n=128)]
        FBb = FB[:].bitcast(BF16)
        UPb = [FBb[:, 0:2304], FBb[:, 2304:4608]]
        DGb = [FBb[:, 4608:6912].rearrange("p (t n) -> p t n", n=128), FBb[:, 6912:9216].rearrange("p (t n) -> p t n", n=128)]
        SG = FB[:, 4608:6912]
        GB = ACTB[:, 0:5, :].rearrange("p a t -> p (a t)")[:, 0:NT]
        stgs = [STG[:, 0:2048].rearrange("p (k n) -> p k n", n=256), STG[:, 2048:4096].rearrange("p (k n) -> p k n", n=256)]

        def bank(m, sl):
            return ps[(sl + 4 * m) % 8]

        def wup_dma(m):
            st = stgs[m % 2]
            S.dma_async("sync", st[:, :, 0:128], w_up[li, :, m * 128:(m + 1) * 128].rearrange("(k p) n -> p k n", p=128))
            S.dma_async("gpsimd", st[:, :, 128:256], w_up[li, :, DFF + m * 128:DFF + (m + 1) * 128].rearrange("(k p) n -> p k n", p=128))

        def prep_w(m, which):
            if which == 0:
                S.op("vector", lambda e, m=m: e.tensor_copy(out=WUb[m % 2], in_=stgs[m % 2]))
            for idx in range(18):
                part, tap = divmod(idx, 9)
                ch = part * 22 + m
                if idx % 2 == 0 and which == 0:
                    S.op("vector", lambda e, idx=idx, tap=tap, ch=ch, m=m: e.tensor_scalar(
                        out=DGb[m % 2][:, idx, :], in0=IDENT[:], scalar1=WC[:, tap, ch:ch + 1], scalar2=None, op0=ALU.mult))
                if idx % 2 == 1 and which == 1:
                    S.op("scalar", lambda e, idx=idx, tap=tap, ch=ch, m=m: e.activation(
                        out=DGb[m % 2][:, idx, :], in_=IDENT[:], func=AF.Copy, scale=WC[:, tap, ch:ch + 1]))

        def up_mm(m, part, tiles, slots):
            for t, sl in zip(tiles, slots):
                t0, w = TILES[t]
                bk = bank(m, sl)
                for k in range(8):
                    S.op("tensor", lambda e, bk=bk, k=k, t0=t0, w=w, part=part, m=m: e.matmul(
                        bk[:, 0:w], lhsT=WUb[m % 2][:, k, part * 128:(part + 1) * 128], rhs=H[:, k, t0:t0 + w],
                        start=(k == 0), stop=(k == 7)))

        def evac_up(m, part, tiles, slots, engs):
            dst = UPb[part]
            for n_, (t, sl) in enumerate(zip(tiles, slots)):
                t0, w = TILES[t]
                bk = bank(m, sl)
                if engs[n_ % len(engs)] == "vector":
                    S.op("vector", lambda e, bk=bk, t0=t0, w=w, dst=dst: e.tensor_copy(out=dst[:, t0:t0 + w], in_=bk[:, 0:w]))
                else:
                    S.op("scalar", lambda e, bk=bk, t0=t0, w=w, dst=dst: e.copy(out=dst[:, t0:t0 + w], in_=bk[:, 0:w]))

        taps9 = [(1, 1)] + [(ky, kx) for ky in range(3) for kx in range(3) if not (ky == 1 and kx == 1)]

        def conv_mm(m, part, tiles, slots):
            src = UPb[part]
            d0 = part * 9
            DGm = DGb[m % 2]
            sv = src[:, LC:NT].rearrange("p (r c) -> p r c", c=64)
            for t, sl in zip(tiles, slots):
                bk = bank(m, sl)
                if t == 0:
                    S.op("tensor", lambda e, bk=bk, DGm=DGm: e.matmul(bk[:, 0:256], lhsT=DGm[:, d0 + 4, :], rhs=src[:, 0:256], start=True, stop=False))
                    S.op("tensor", lambda e, bk=bk, DGm=DGm: e.matmul(bk[:, 1:256], lhsT=DGm[:, d0 + 3, :], rhs=src[:, 0:255], start=False, stop=False))
                    S.op("tensor", lambda e, bk=bk, DGm=DGm: e.matmul(bk[:, 0:255], lhsT=DGm[:, d0 + 5, :], rhs=src[:, 1:256], start=False, stop=True))
                    continue
                R0 = 8 * (t - 1)
                pv = bk[:, 0:512].rearrange("p (r c) -> p r c", c=64)
                for n_, (ky, kx) in enumerate(taps9):
                    dy, dx = ky - 1, kx - 1
                    ra, rb = max(R0, -dy, 0), min(R0 + 8, 32 - max(0, dy))
                    c0, c1 = max(0, -dx), 64 - max(0, dx)
                    S.op("tensor", lambda e, pv=pv, ra=ra, rb=rb, c0=c0, c1=c1, dy=dy, dx=dx, R0=R0, ky=ky, kx=kx, n_=n_, DGm=DGm:
                         e.matmul(pv[:, ra - R0:rb - R0, c0:c1], lhsT=DGm[:, d0 + ky * 3 + kx, :],
                                  rhs=sv[:, ra + dy:rb + dy, c0 + dx:c1 + dx], start=(n_ == 0), stop=(n_ == 8)))

        def silu_ev(m, tiles, slots):
            for t, sl in zip(tiles, slots):
                t0, w = TILES[t]
                bk = bank(m, sl)
                S.op("scalar", lambda e, bk=bk, t0=t0, w=w: e.activation(out=SG[:, t0:t0 + w], in_=bk[:, 0:w], func=AF.Silu))

        def mult_ev(m, tiles, slots):
            for t, sl in zip(tiles, slots):
                t0, w = TILES[t]
                bk = bank(m, sl)
                S.op("vector", lambda e, bk=bk, t0=t0, w=w: e.tensor_tensor(out=GB[:, t0:t0 + w], in0=bk[:, 0:w], in1=SG[:, t0:t0 + w], op=ALU.mult))

        wup_dma(0)
        S.join()
        prep_w(0, 0)
        prep_w(0, 1)
        S.bar()
        wup_dma(1)
        for m in range(22):
            up_mm(m, 0, [0, 1, 2, 3, 4], [0, 1, 2, 3, 4])
            if m > 0:
                mult_ev(m - 1, [3, 4], [2, 3])
            if m % 2 == 0 and m // 2 < 8:
                S.dma_async("sync", wd_stage[(m // 2) % 2], w_down[li][:, (m // 2) * 128:(m // 2 + 1) * 128].rearrange("(k p) n -> p k n", p=128))
            S.bar()
            up_mm(m, 1, [0, 1, 2], [5, 6, 7])
            evac_up(m, 0, [0, 1, 2, 3, 4], [0, 1, 2, 3, 4], ["vector", "scalar"])
            if m > 0:
                S.dma_async("gpsimd", GD[(m - 1) * 128:m * 128, :], GB)
            S.bar()
            up_mm(m, 1, [3, 4], [0, 1])
            conv_mm(m, 0, [0, 1, 2], [2, 3, 4])
            evac_up(m, 1, [0, 1, 2], [5, 6, 7], ["vector", "scalar"])
            S.bar()
            conv_mm(m, 0, [3, 4], [5, 6])
            evac_up(m, 1, [3, 4], [0, 1], ["vector"])
            silu_ev(m, [0, 1, 2], [2, 3, 4])
            if m % 2 == 1 and m // 2 < 8:
                S.op("vector", lambda e, m=m: e.tensor_copy(out=WDv[:, :, (m // 2) * 128:(m // 2 + 1) * 128], in_=wd_stage[(m // 2) % 2]))
            S.join()
            conv_mm(m, 1, [0, 1, 2], [0, 1, 7])
            silu_ev(m, [3, 4], [5, 6])
            if m + 1 < 22:
                prep_w(m + 1, 0)
            S.bar()
            conv_mm(m, 1, [3, 4], [2, 3])
            mult_ev(m, [0, 1, 2], [0, 1, 7])
            if m + 1 < 22:
                prep_w(m + 1, 1)
            if m + 2 < 22:
                wup_dma(m + 2)
            S.bar()
        mult_ev(21, [3, 4], [2, 3])
        S.bar()
        S.dma_async("gpsimd", GD[21 * 128:22 * 128, :], GB)
        S.join()

        ckp("ffnup")
        resid_linear(GD, 22, 5, [ACTB, FB[:].bitcast(BF16)[:, 0:11264].rearrange("p (k t) -> p k t", t=512)], WB[:, 0:22 * D].rearrange("p (k n) -> p k n", n=D))

    except _Stop:
        pass
    S.bar()
    if dbg:
        DF = nc.dram_tensor("DBGF", [128, 32768], F32, kind="ExternalOutput").ap()
        DB = nc.dram_tensor("DBGB", [128, 40960], BF16, kind="ExternalOutput").ap()
        off = 0
        for t_, n_ in [(HRAW[:], 9216), (XTt[:], 4096), (TMPt[:], 4096), (FB[:], 9216), (STG[:], 4096), (RS[:], 512),
                       (MOD[:].rearrange("p q k n -> p (q k n)"), 96), (A1[:].rearrange("p k n -> p (k n)"), 16),
                       (A2[:].rearrange("p k n -> p (k n)"), 16), (W3[:].rearrange("p a g -> p (a g)"), 64),
                       (ARC[:].rearrange("p a g -> p (a g)"), 64), (AIC[:].rearrange("p a g -> p (a g)"), 64),
                       (CR[:], 32), (CI[:], 32), (DV[:], 32), (LR[:], 32), (LI[:], 32)]:
            S.dma("sync", DF[:, off:off + n_], t_)
            off += n_
        offb = 0
        for t_, n_ in [(WB, 22528), (OBt[:], 4096), (ACTBt[:], 11264)]:
            S.dma("gpsimd", DB[:, offb:offb + n_], t_)
            offb += n_
        S.bar()
    for b in range(16):
        t0 = LC + b * 128
        S.dma("sync", XT[:, :, 0:128], XRES[:, t0:t0 + 128].rearrange("(k p) t -> p k t", p=128))
        S.bar()
        S.op("scalar", lambda e: e.activation(out=SQ[:, :, 0:128], in_=XT[:, :, 0:128], func=AF.Square))
        S.bar()
        rstd_from(SQ, 8, 128, 1.0 / D)
        for k in range(8):
            S.op("vector", lambda e, k=k: e.scalar_tensor_tensor(
                out=TMP[:, k, 0:128], in0=XT[:, k, 0:128], scalar=GFIN[:, k:k + 1], in1=RS[:, 0:128],
                op0=ALU.mult, op1=ALU.mult))
        S.bar()
        for k in range(8):
            S.op("tensor", lambda e, k=k: e.transpose(ps[k // 4][:, (k % 4) * 128:(k % 4 + 1) * 128],
                                                       TMP[:, k, 0:128], IDENT[:]))
        S.bar()
        S.op("vector", lambda e: e.tensor_copy(out=XT[:, 0:4, 0:128], in_=ps[0][:].rearrange("p (k t) -> p k t", t=128)))
        S.op("scalar", lambda e: e.copy(out=XT[:, 4:8, 0:128], in_=ps[1][:].rearrange("p (k t) -> p k t", t=128)))
        S.bar()
        S.dma("sync", out[b * 128:(b + 1) * 128, :].rearrange("t (k d) -> t k d", d=128), XT[:, :, 0:128])
        S.bar()

    S.emit()
    es.close()
    return nc


_CONST = None


def _consts():
    ident = np.eye(128, dtype=np.float32)
    sp = np.arange(128) // 16
    m0 = (sp[None, :] >= sp[:, None]).astype(np.float32)
    m1 = (sp[None, :] <= sp[:, None]).astype(np.float32)
    return ident, np.stack([m0, m1])


def kernel(n_layers=4, **inputs):
    nc = build_nc(n_layers)
    ident, mask = _consts()
    in_maps = []
    for b in range(8):
        m = {}
        for k, v in inputs.items():
            v = np.asarray(v)
            if k in ("x", "c", "ctx"):
                m[k] = np.ascontiguousarray(v[b], dtype=np.float32)
            else:
                m[k] = np.ascontiguousarray(v, dtype=np.float32)
        m["ident"] = ident
        m["mask"] = mask
        in_maps.append(m)
    res = run_bass_kernel_spmd(nc, in_maps, core_ids=list(range(8)))
    return np.stack([np.asarray(r["out"], dtype=np.float32) for r in res.results], axis=0)
```

```python
import numpy as np
from contextlib import ExitStack
import concourse.bass as bass
import concourse.mybir as mybir
from concourse.bass_utils import run_bass_kernel_spmd

F32, BF16, I32 = mybir.dt.float32, mybir.dt.bfloat16, mybir.dt.int32
AF = mybir.ActivationFunctionType
ALU = mybir.AluOpType

D = 1024
NT = 2304
LC = 256
LL = 2048
DFF = 2816
EPS = 1e-6
NJ = 288
TILES = [(0, 256)] + [(256 + 512 * i, 512) for i in range(4)]
TWO_PI = 6.283185307179586


class Sched:
    def __init__(self, nc):
        self.nc = nc
        self.stages = [[]]
        self.join_at = set()

    def op(self, eng, fn, dma=False):
        self.stages[-1].append((eng, dma, fn))

    def dma(self, eng, out, in_, slow=False):
        self.op(eng, lambda e, o=out, i=in_: e.dma_start(out=o, in_=i), dma=True)

    def dma_async(self, eng, out, in_):
        self.op(eng, lambda e, o=out, i=in_: e.dma_start(out=o, in_=i), dma="async")

    def bar(self):
        if self.stages[-1]:
            self.stages.append([])

    def join(self):
        self.bar()
        self.join_at.add(len(self.stages) - 1)

    def emit(self):
        nc = self.nc
        self.bar()
        merged, joins = [], set()
        for k, st in enumerate(self.stages):
            only_vec = len(st) > 0 and all((o[0] == "vector" and not o[1]) for o in st)
            if (merged and only_vec and k not in self.join_at and merged[-1][1]):
                merged[-1][0].extend(st)
            else:
                if k in self.join_at:
                    joins.add(len(merged))
                merged.append([list(st), only_vec])
        self.stages = [m[0] for m in merged]
        self.join_at = joins
        names = ["c_scalar", "c_vector", "c_gpsimd", "c_tensor", "d_sync", "d_scalar", "d_gpsimd", "a_sync", "a_gpsimd"]

        def semname(eng, dma):
            if dma == "async":
                return "a_" + eng
            return ("d_" if dma else "c_") + eng

        cum = []
        cur = {n: 0 for n in names}
        for st in self.stages:
            cum.append(dict(cur))
            for (eng, dma, _) in st:
                cur[semname(eng, dma)] += 16 if dma else 1
        final = dict(cur)
        with ExitStack() as es:
            sems = {n: es.enter_context(nc.semaphore(n)) for n in names}
            block = es.enter_context(nc.Block())

            def make(engname):
                def body(eng):
                    waited = {n: 0 for n in names}
                    joined = {n: 0 for n in names}
                    for k, st in enumerate(self.stages):
                        if k in self.join_at:
                            for n in names:
                                if n.startswith("a_"):
                                    joined[n] = cum[k][n]
                        mine = [o for o in st if o[0] == engname]
                        if not mine:
                            continue
                        for n in names:
                            tgt = joined[n] if n.startswith("a_") else cum[k][n]
                            if tgt > waited[n]:
                                eng.wait_ge(sems[n], tgt)
                                waited[n] = tgt
                        for (_, dma, fn) in mine:
                            ins = fn(eng)
                            ins.then_inc(sems[semname(engname, dma)], 16 if dma else 1)
                    if engname == "sync":
                        for n in names:
                            if final[n] > waited[n]:
                                eng.wait_ge(sems[n], final[n])
                return body

            block.sync(make("sync"))
            block.scalar(make("scalar"))
            block.vector(make("vector"))
            block.gpsimd(make("gpsimd"))
            block.tensor(make("tensor"))


class _Stop(Exception):
    pass


def build_nc(n_layers, n_wl=4, stop=None, dbg=False):
    nc = bass.Bass("TRN2", target_bir_lowering=False)
    S = Sched(nc)
    W = n_wl

    def ckp(name):
        S.bar()
        if stop == name:
            raise _Stop()

    def din(name, shape):
        return nc.dram_tensor(name, list(shape), F32, kind="ExternalInput").ap()

    x_in = din("x", [LL, D])
    c_in = din("c", [D])
    ctx_in = din("ctx", [LC, D])
    cctx_in = din("c_ctx", [D])
    w_ada = din("w_ada", [W, D, 6 * D])
    b_ada = din("b_ada", [W, 6 * D])
    g_mix = din("g_mix", [W, D])
    w_in = din("w_in", [W, D, 1536])
    a_re = din("ssm_a_re", [W, 2, 32, 64])
    a_im = din("ssm_a_im", [W, 2, 32, 64])
    b_re = din("ssm_b_re", [W, 2, 32, 64, 16])
    b_im = din("ssm_b_im", [W, 2, 32, 64, 16])
    c_re = din("ssm_c_re", [W, 2, 32, 16, 64])
    c_im = din("ssm_c_im", [W, 2, 32, 16, 64])
    log_dt = din("ssm_log_dt", [W, 2, 32])
    ssm_d = din("ssm_d", [W, 512])
    w_glu = din("w_glu", [W, 512, 512])
    b_glu = din("b_glu", [W, 512])
    g_sgu = din("g_sgu", [W, 512])
    w_sp = din("w_spatial", [W, 4, 128, 128])
    b_sp = din("b_spatial", [W, 4, 128])
    w_out = din("w_out", [W, D, D])
    g_ffn = din("g_ffn", [W, D])
    w_up = din("w_up", [W, D, 2 * DFF])
    w_conv = din("w_conv", [W, 3, 3, 2 * DFF])
    w_down = din("w_down", [W, DFF, D])
    g_final = din("g_final", [D])
    ident_in = din("ident", [128, 128])
    mask_in = din("mask", [2, 128, 128])
    out = nc.dram_tensor("out", [LL, D], F32, kind="ExternalOutput").ap()

    SK = dict(kind="ExternalOutput") if dbg else {}
    XRES = nc.dram_tensor("XRES", [D, NT], F32, **SK).ap()
    USSMP = nc.dram_tensor("USSMP", [512, 8, NJ], BF16, **SK).ap()
    YSP = nc.dram_tensor("YSP", [128, 32, NJ], F32, **SK).ap()
    UG = nc.dram_tensor("UG", [512, NT], BF16, **SK).ap()
    VG = nc.dram_tensor("VG", [512, NT], F32, **SK).ap()
    MIX = nc.dram_tensor("MIX", [D, NT], BF16, **SK).ap()
    GD = nc.dram_tensor("GD", [DFF, NT], BF16, **SK).ap()

    es = ExitStack()

    def sb(name, shape, dt=F32):
        return es.enter_context(nc.sbuf_tensor(name, list(shape), dt))

    ps = [es.enter_context(nc.psum_tensor("ps%d" % i, [128, 512], F32)) for i in range(8)]

    IDENT = sb("IDENT", [128, 128])
    ONESB = sb("ONESB", [128, 128], BF16)
    MASKS = sb("MASKS", [128, 2, 128])
    SIGN = sb("SIGN", [128, 1])
    NSIGN = sb("NSIGN", [128, 1])
    SC = sb("SC", [128, 8, 2])
    MOD = sb("MOD", [128, 6, 8, 2])
    BADA = sb("BADA", [128, 6, 8])
    MODN = sb("MODN", [128, 96])
    GM = sb("GM", [128, 8])
    GF = sb("GF", [128, 8])
    GFIN = sb("GFIN", [128, 8])
    A1 = sb("A1", [128, 8, 2])
    A2 = sb("A2", [128, 8, 2])
    HRAW = sb("HRAW", [128, 9216])
    H = HRAW[:].bitcast(BF16).rearrange("p (k t) -> p k t", t=NT)
    YS = HRAW[:].rearrange("p (g j) -> p g j", j=NJ)
    STG = sb("STG", [128, 4096])
    SQ = STG[:, 0:2048].bitcast(BF16).rearrange("p (k t) -> p k t", t=512)
    WBt = sb("WB", [128, 22528], BF16)
    WB = WBt[:]
    TOEP = WB[:, 0:4096].rearrange("p (g n) -> p g n", n=128)
    ET = WB[:, 4096:8192].rearrange("p (g n) -> p g n", n=128)
    ESW = WB[:, 8192:12288].rearrange("p (g n) -> p g n", n=128)
    IM = WB[:, 12288:21504].rearrange("p (g j) -> p g j", j=NJ)
    XTt = sb("XT", [128, 4096])
    XT = XTt[:].rearrange("p (k t) -> p k t", t=512)
    LT = XTt[:].rearrange("p (g s c) -> p g s c", s=8, c=16)
    SS = XTt[:].rearrange("p (j a g) -> p j a g", a=2, g=32)
    TMPt = sb("TMP", [128, 4096])
    TMP = TMPt[:].rearrange("p (k t) -> p k t", t=512)
    RT = TMPt[:].rearrange("p (g s c) -> p g s c", s=8, c=16)
    RS = sb("RS", [128, 512])
    OBt = sb("OB", [128, 4096], BF16)
    OB = OBt[:].rearrange("p (k t) -> p k t", t=512)
    RB = OBt[:].rearrange("p (g n) -> p g n", n=128)
    ACTBt = sb("ACTB", [128, 11264], BF16)
    ACTB = ACTBt[:].rearrange("p (k t) -> p k t", t=512)
    USP = ACTBt[:, 0:9216].rearrange("p (q s j) -> p q s j", s=8, j=NJ)
    HH = ACTBt[:, 0:9216].rearrange("p (j g) -> p j g", g=32)
    FB = sb("FB", [128, 9216])
    UPG = FB[:, 0:2304]; UPV = FB[:, 2304:4608]; CG = FB[:, 4608:6912]; CV = FB[:, 6912:9216]
    def ftab(i):
        return FB[:, i * 512:(i + 1) * 512].rearrange("p (g k) -> p g k", k=16)
    ARG, ARGC, EARG, NF, NFC, PRE, PIM, MAG, BX1, BX2, CX1, CX2 = [ftab(i) for i in range(12)]
    def qtab(i):
        return FB[:, 6144 + i * 256:6144 + (i + 1) * 256].rearrange("p (g k) -> p g k", k=8)
    QR, QI, QT, PA, PB = [qtab(i) for i in range(5)]
    NI = FB[:, 7424:7936].bitcast(I32).rearrange("p (g k) -> p g k", k=16)
    NIC = FB[:, 7936:8448].bitcast(I32).rearrange("p (g k) -> p g k", k=16)
    BS = FB[:, 0:512].rearrange("p (h q) -> p h q", q=128)
    VT = STG[:, 2048:4096].bitcast(BF16).rearrange("p (a n) -> p a n", n=128)
    W3 = sb("W3", [128, 2, 32])
    G3 = sb("G3", [128, 3, 32])
    T1 = sb("T1", [128, 2, 32])
    T2 = sb("T2", [128, 2, 32])
    ARp = sb("ARp", [128, 32]); AIp = sb("AIp", [128, 32]); LDT = sb("LDT", [128, 32])
    LR = sb("LR", [128, 32]); LI = sb("LI", [128, 32])
    CR = sb("CR", [128, 32]); CI = sb("CI", [128, 32]); NR = sb("NR", [128, 32]); DEN = sb("DEN", [128, 32])
    TA = sb("TA", [128, 32]); TB = sb("TB", [128, 32])
    ARC = sb("ARC", [128, 2, 32]); AIC = sb("AIC", [128, 2, 32])
    DV = sb("DV", [128, 32])
    BGLU = sb("BGLU", [128, 4]); GSGU = sb("GSGU", [128, 4])
    WST = sb("WST", [128, 4, 128], BF16)
    WC = sb("WC", [128, 9, 44])

    S.dma("sync", IDENT[:], ident_in[:, :])
    S.dma("sync", MASKS[:], mask_in.rearrange("m p q -> p m q"))
    S.op("vector", lambda e: e.memset(ONESB[:], 1.0))
    S.op("vector", lambda e: e.memset(SIGN[0:64, :], -1.0))
    S.op("vector", lambda e: e.memset(SIGN[64:128, :], 1.0))
    S.op("vector", lambda e: e.memset(NSIGN[0:64, :], 1.0))
    S.op("vector", lambda e: e.memset(NSIGN[64:128, :], -1.0))
    S.dma("sync", STG[0:8, 0:128], c_in.rearrange("(k p) -> k p", p=128))
    S.dma("sync", STG[8:16, 0:128], cctx_in.rearrange("(k p) -> k p", p=128))
    S.dma("sync", STG[16:24, 0:128], g_final.rearrange("(k p) -> k p", p=128))
    S.bar()
    S.op("tensor", lambda e: e.transpose(ps[7][:, 0:24], STG[0:24, 0:128], IDENT[0:24, 0:24]))
    S.bar()
    S.op("vector", lambda e: e.tensor_copy(out=SC[:, :, 0], in_=ps[7][:, 0:8]))
    S.op("vector", lambda e: e.tensor_copy(out=SC[:, :, 1], in_=ps[7][:, 8:16]))
    S.op("vector", lambda e: e.tensor_copy(out=GFIN[:], in_=ps[7][:, 16:24]))
    S.bar()
    S.op("scalar", lambda e: e.activation(out=SC[:], in_=SC[:], func=AF.Silu))
    S.bar()

    def in_transpose(src, nblk, tok0):
        for b in range(nblk):
            S.dma("sync", TMP[:, :, 0:128], src[b * 128:(b + 1) * 128, :].rearrange("t (k d) -> t k d", d=128))
            S.bar()
            for k in range(8):
                S.op("tensor", lambda e, k=k: e.transpose(ps[k // 4][:, (k % 4) * 128:(k % 4 + 1) * 128],
                                                           TMP[:, k, 0:128], IDENT[:]))
            S.bar()
            S.op("vector", lambda e: e.tensor_copy(out=XT[:, 0:4, 0:128], in_=ps[0][:].rearrange("p (k t) -> p k t", t=128)))
            S.op("scalar", lambda e: e.copy(out=XT[:, 4:8, 0:128], in_=ps[1][:].rearrange("p (k t) -> p k t", t=128)))
            S.bar()
            t0 = tok0 + b * 128
            S.dma("sync", XRES[:, t0:t0 + 128].rearrange("(k p) t -> p k t", p=128), XT[:, :, 0:128])
            S.bar()

    in_transpose(ctx_in, 2, 0)
    in_transpose(x_in, 16, 256)

    def load_weight(dst, wap, kch, ncols, cb):
        for c0 in range(0, ncols, cb):
            stg = STG[:, 0:kch * cb].rearrange("p (k n) -> p k n", n=cb)
            S.dma("sync", stg, wap[:, c0:c0 + cb].rearrange("(k p) n -> p k n", p=128))
            S.bar()
            S.op("vector", lambda e, stg=stg, c0=c0: e.tensor_copy(out=dst[:, :, c0:c0 + cb], in_=stg))
            S.bar()

    def rstd_from(src_sq, nk, w, inv_n):
        for k in range(nk):
            S.op("tensor", lambda e, k=k: e.matmul(ps[0][:, 0:w], lhsT=ONESB[:], rhs=src_sq[:, k, 0:w],
                                                    start=(k == 0), stop=(k == nk - 1)))
        S.bar()
        S.op("scalar", lambda e: e.activation(out=RS[:, 0:w], in_=ps[0][:, 0:w], func=AF.Sqrt, bias=EPS, scale=inv_n))
        S.bar()
        S.op("vector", lambda e: e.reciprocal(out=RS[:, 0:w], in_=RS[:, 0:w]))
        S.bar()

    def norm_mod(Acoef, which_shift, issue=None, convert=None):
        Xn = [XT, STG[:].rearrange("p (k t) -> p k t", t=512)]

        def xload(i):
            t0, w = TILES[i]
            S.dma_async("sync", Xn[i % 2][:, 0:4, 0:w], XRES[0:512, t0:t0 + w].rearrange("(k p) t -> p k t", p=128))
            S.dma_async("gpsimd", Xn[i % 2][:, 4:8, 0:w], XRES[512:1024, t0:t0 + w].rearrange("(k p) t -> p k t", p=128))

        xload(0)
        for i, (t0, w) in enumerate(TILES):
            sel = 1 if t0 == 0 else 0
            Xi = Xn[i % 2]
            S.join()
            if i + 1 < len(TILES):
                xload(i + 1)
            if issue and i in issue:
                issue[i]()
            if convert and i in convert:
                convert[i]()
            S.op("scalar", lambda e, w=w, Xi=Xi: e.activation(out=OB[:, :, 0:w], in_=Xi[:, :, 0:w], func=AF.Square))
            S.bar()
            rstd_from(OB, 8, w, 1.0 / D)
            for k in range(8):
                S.op("vector", lambda e, k=k, w=w, sel=sel, Xi=Xi: e.scalar_tensor_tensor(
                    out=TMP[:, k, 0:w], in0=Xi[:, k, 0:w], scalar=Acoef[:, k, sel:sel + 1], in1=RS[:, 0:w],
                    op0=ALU.mult, op1=ALU.mult))
            S.bar()
            for k in range(8):
                S.op("scalar", lambda e, k=k, w=w, sel=sel, t0=t0: e.activation(
                    out=H[:, k, t0:t0 + w], in_=TMP[:, k, 0:w], func=AF.Identity,
                    bias=MOD[:, which_shift, k, sel:sel + 1], scale=1.0))
            S.bar()

    def resid_linear(src_dram, kch, gate_idx, Abufs, Wv):
        Xb = [XT, TMP, STG[:].rearrange("p (k t) -> p k t", t=512)]

        def loads(i):
            t0, w = TILES[i]
            S.dma("sync", Abufs[i % 2][:, 0:kch, 0:w], src_dram[:, t0:t0 + w].rearrange("(k p) t -> p k t", p=128))
            S.dma("gpsimd", Xb[i % 3][:, :, 0:w], XRES[:, t0:t0 + w].rearrange("(k p) t -> p k t", p=128))

        def store(i):
            t0, w = TILES[i]
            S.dma("gpsimd", XRES[:, t0:t0 + w].rearrange("(k p) t -> p k t", p=128), Xb[i % 3][:, :, 0:w])

        loads(0)
        S.bar()
        nt = len(TILES)
        for i, (t0, w) in enumerate(TILES):
            sel = 1 if t0 == 0 else 0
            Ab = Abufs[i % 2]
            for m in range(8):
                for k in range(kch):
                    S.op("tensor", lambda e, m=m, k=k, w=w, Ab=Ab: e.matmul(
                        ps[m][:, 0:w], lhsT=Wv[:, k, m * 128:(m + 1) * 128], rhs=Ab[:, k, 0:w],
                        start=(k == 0), stop=(k == kch - 1)))
            if i + 1 < nt:
                loads(i + 1)
            if i >= 1:
                store(i - 1)
            S.bar()
            Xi = Xb[i % 3]
            for m in range(8):
                S.op("vector", lambda e, m=m, w=w, sel=sel, Xi=Xi: e.scalar_tensor_tensor(
                    out=Xi[:, m, 0:w], in0=ps[m][:, 0:w], scalar=MOD[:, gate_idx, m, sel:sel + 1], in1=Xi[:, m, 0:w],
                    op0=ALU.mult, op1=ALU.add))
            S.bar()
        store(nt - 1)
        S.bar()

    try:
      for li in range(n_layers):
        S.dma("sync", STG[0:48, 0:128], b_ada[li].rearrange("(k p) -> k p", p=128))
        S.dma("sync", STG[48:56, 0:128], g_mix[li].rearrange("(k p) -> k p", p=128))
        S.dma("sync", STG[56:64, 0:128], g_ffn[li].rearrange("(k p) -> k p", p=128))
        S.dma("sync", STG[64:68, 0:128], b_glu[li].rearrange("(k p) -> k p", p=128))
        S.dma("sync", STG[68:72, 0:128], g_sgu[li].rearrange("(k p) -> k p", p=128))
        S.bar()
        S.op("tensor", lambda e: e.transpose(ps[7][:, 0:72], STG[0:72, 0:128], IDENT[0:72, 0:72]))
        S.bar()
        S.op("vector", lambda e: e.tensor_copy(out=BADA[:].rearrange("p q k -> p (q k)"), in_=ps[7][:, 0:48]))
        S.op("vector", lambda e: e.tensor_copy(out=GM[:], in_=ps[7][:, 48:56]))
        S.op("vector", lambda e: e.tensor_copy(out=GF[:], in_=ps[7][:, 56:64]))
        S.op("vector", lambda e: e.tensor_copy(out=BGLU[:], in_=ps[7][:, 64:68]))
        S.op("vector", lambda e: e.tensor_copy(out=GSGU[:], in_=ps[7][:, 68:72]))
        S.bar()
        WAs = [STG[:, 0:2048].rearrange("p (k n) -> p k n", n=256), STG[:, 2048:4096].rearrange("p (k n) -> p k n", n=256)]

        def wada_dma(lyr, blk, asyn=False):
            WA = WAs[blk % 2]
            src = w_ada[lyr, :, blk * 256:(blk + 1) * 256].rearrange("(k p) n -> p k n", p=128)
            f = S.dma_async if asyn else S.dma
            f("sync", WA[:, 0:4, :], src[:, 0:4, :])
            f("gpsimd", WA[:, 4:8, :], src[:, 4:8, :])

        def wada_mm(blk, bank_ap, col0):
            q, mq = blk // 4, blk % 4
            WA = WAs[blk % 2]
            for mm in range(2):
                m = mq * 2 + mm
                for k in range(8):
                    S.op("tensor", lambda e, m=m, mm=mm, k=k, q=q, WA=WA: e.matmul(
                        bank_ap[:, col0 + (q * 8 + m) * 2:col0 + (q * 8 + m) * 2 + 2], lhsT=WA[:, k, mm * 128:(mm + 1) * 128],
                        rhs=SC[:, k, :], start=(k == 0), stop=(k == 7)))

        if li == 0:
            wada_dma(0, 0)
            S.bar()
            for blk in range(24):
                if blk + 1 < 24:
                    wada_dma(0, blk + 1)
                wada_mm(blk, ps[0], 0)
                S.bar()
            S.op("vector", lambda e: e.tensor_copy(out=MODN[:], in_=ps[0][:, 0:96]))
            S.bar()
        S.op("vector", lambda e: e.tensor_tensor(
            out=MOD[:].rearrange("p q k n -> p (q k) n"), in0=MODN[:].rearrange("p (a n) -> p a n", n=2),
            in1=BADA[:].rearrange("p q k -> p (q k)").unsqueeze(2).to_broadcast([128, 48, 2]), op=ALU.add))
        S.bar()
        S.op("vector", lambda e: e.scalar_tensor_tensor(
            out=A1[:], in0=MOD[:, 1, :, :], scalar=1.0, in1=GM[:].unsqueeze(2).to_broadcast([128, 8, 2]),
            op0=ALU.add, op1=ALU.mult))
        S.op("vector", lambda e: e.scalar_tensor_tensor(
            out=A2[:], in0=MOD[:, 4, :, :], scalar=1.0, in1=GF[:].unsqueeze(2).to_broadcast([128, 8, 2]),
            op0=ALU.add, op1=ALU.mult))
        S.bar()

        ckp("ada")
        WIN = WB[:, 0:8 * 1536].rearrange("p (k n) -> p k n", n=1536)
        FBs = [FB[:, 0:4096].rearrange("p (k n) -> p k n", n=512), FB[:, 4096:8192].rearrange("p (k n) -> p k n", n=512)]

        def win_issue(bk):
            def f():
                src = w_in[li][:, bk * 512:(bk + 1) * 512].rearrange("(k p) n -> p k n", p=128)
                S.dma_async("sync", FBs[bk % 2][:, 0:4, :], src[:, 0:4, :])
                S.dma_async("gpsimd", FBs[bk % 2][:, 4:8, :], src[:, 4:8, :])
            return f

        def win_conv(bk):
            def f():
                S.op("vector", lambda e: e.tensor_copy(out=WIN[:, :, bk * 512:(bk + 1) * 512], in_=FBs[bk % 2]))
            return f

        norm_mod(A1, 0, issue={0: win_issue(0), 1: win_issue(1), 2: win_issue(2)},
                 convert={1: win_conv(0), 2: win_conv(1), 3: win_conv(2)})

        ckp("norm1")
        groups = [(ti, g) for ti in range(len(TILES)) for g in range(3)]

        def win_mm(n):
            ti, g = groups[n]
            t0, w = TILES[ti]
            for bi in range(4):
                m = g * 4 + bi
                bk = ps[(n % 2) * 4 + bi]
                for k in range(8):
                    S.op("tensor", lambda e, bk=bk, m=m, k=k, w=w, t0=t0: e.matmul(
                        bk[:, 0:w], lhsT=WIN[:, k, m * 128:(m + 1) * 128], rhs=H[:, k, t0:t0 + w],
                        start=(k == 0), stop=(k == 7)))

        def win_ev(n):
            ti, g = groups[n]
            t0, w = TILES[ti]
            j0, nj = t0 // 8, w // 8
            for bi in range(4):
                m = g * 4 + bi
                bk = ps[(n % 2) * 4 + bi]
                if g == 0:
                    S.op("vector", lambda e, bk=bk, m=m, w=w, j0=j0, nj=nj: e.tensor_copy(
                        out=USP[:, m, :, j0:j0 + nj].rearrange("p s j -> p j s"),
                        in_=bk[:, 0:w].rearrange("p (j s) -> p j s", s=8)))
                elif g == 1:
                    S.op("scalar", lambda e, bk=bk, m=m, w=w: e.activation(
                        out=OB[:, m - 4, 0:w], in_=bk[:, 0:w], func=AF.Gelu_apprx_tanh))
                else:
                    S.op("scalar", lambda e, bk=bk, m=m, w=w: e.activation(
                        out=TMP[:, m - 8, 0:w], in_=bk[:, 0:w], func=AF.Gelu_apprx_tanh))

        for n in range(len(groups) + 1):
            if n >= 1 and groups[n - 1][1] == 1:
                S.join()
            if n < len(groups):
                win_mm(n)
            if n >= 1:
                win_ev(n - 1)
                ti, g = groups[n - 1]
                t0, w = TILES[ti]
            S.bar()
            if n >= 1 and groups[n - 1][1] == 1:
                S.dma_async("sync", UG[:, t0:t0 + w].rearrange("(k p) t -> p k t", p=128), OB[:, 0:4, 0:w])
            if n >= 1 and groups[n - 1][1] == 2:
                S.dma_async("gpsimd", VG[:, t0:t0 + w].rearrange("(k p) t -> p k t", p=128), TMP[:, 0:4, 0:w])

        S.join()
        ckp("win")
        S.dma("sync", USSMP.rearrange("(q p) s j -> p q s j", p=128), USP)
        S.dma("gpsimd", STG[0:32, 0:16], ssm_d[li].rearrange("(g c) -> g c", c=16))
        S.bar()
        for s_ in range(8):
            S.dma("sync" if s_ % 2 == 0 else "gpsimd", IM[s_ * 16:(s_ + 1) * 16, :, :],
                  USSMP[:, s_, :].rearrange("(g c) j -> c g j", c=16))
        S.bar()

        S.op("vector", lambda e: e.tensor_copy(out=STG[0:32, 128:256].rearrange("p (s c) -> p s c", c=16),
                                               in_=STG[0:32, 0:16].unsqueeze(1).to_broadcast([32, 8, 16])))
        S.bar()
        S.op("tensor", lambda e: e.transpose(ps[7][:, 0:32], STG[0:32, 128:256], IDENT[0:32, 0:32]))
        S.bar()
        S.op("vector", lambda e: e.tensor_copy(out=DV[:], in_=ps[7][:, 0:32]))
        ckp("im2col")
        ada_state = {"next": 0, "pending": None}
        for dr in range(2):
            CIN1 = STG[:, 0:512].rearrange("p (q n) -> p q n", n=128)
            CIN2 = STG[:, 512:1024].rearrange("p (q n) -> p q n", n=128)
            for hf in range(2):
                lo = slice(hf * 64, hf * 64 + 64)
                csrc = [c_re, c_im] if hf == 0 else [c_im, c_re]
                S.dma("sync", CIN1[:, :, lo], csrc[0][li, dr].rearrange("(q g) c p -> (g c) q p", q=4))
                S.dma("gpsimd", CIN2[:, :, lo], csrc[1][li, dr].rearrange("(q g) c p -> (g c) q p", q=4))
            S.bar()
            for q in range(4):
                S.op("tensor", lambda e, q=q: e.transpose(ps[0][:, q * 128:(q + 1) * 128], CIN1[:, q, :], IDENT[:]))
                S.op("tensor", lambda e, q=q: e.transpose(ps[1][:, q * 128:(q + 1) * 128], CIN2[:, q, :], IDENT[:]))
            S.bar()
            S.op("vector", lambda e: e.tensor_copy(out=CX1.rearrange("p g c -> p (g c)"), in_=ps[0][:]))
            S.op("scalar", lambda e: e.copy(out=CX2.rearrange("p g c -> p (g c)"), in_=ps[1][:]))
            S.bar()
            BIN1 = STG[0:32, 0:2048].rearrange("p (h q c) -> p h q c", h=2, c=16)
            BIN2 = STG[0:32, 2048:4096].rearrange("p (h q c) -> p h q c", h=2, c=16)
            S.dma("sync", BIN1[:, 0, :, :], b_re[li, dr])
            S.dma("gpsimd", BIN1[:, 1, :, :], b_im[li, dr])
            S.dma("sync", BIN2[:, 0, :, :], b_im[li, dr])
            S.dma("gpsimd", BIN2[:, 1, :, :], b_re[li, dr])
            AIN = RS[0:32, 0:256].rearrange("p (a q) -> p a q", q=64)
            S.dma("sync", AIN[:, 0, :], a_re[li, dr])
            S.dma("gpsimd", AIN[:, 1, :], a_re[li, dr])
            S.dma("sync", AIN[:, 2, :], a_im[li, dr])
            S.dma("gpsimd", AIN[:, 3, :], a_im[li, dr])
            S.bar()
            for c_ in range(16):
                S.op("tensor", lambda e, c_=c_: e.transpose(ps[2][:, c_ * 32:(c_ + 1) * 32], BIN1[:, :, :, c_], IDENT[0:32, 0:32]))
                S.op("tensor", lambda e, c_=c_: e.transpose(ps[3][:, c_ * 32:(c_ + 1) * 32], BIN2[:, :, :, c_], IDENT[0:32, 0:32]))
            S.op("tensor", lambda e: e.transpose(ps[4][:, 0:32], RS[0:32, 0:128], IDENT[0:32, 0:32]))
            S.op("tensor", lambda e: e.transpose(ps[4][:, 32:64], RS[0:32, 128:256], IDENT[0:32, 0:32]))
            S.bar()
            S.op("vector", lambda e: e.tensor_copy(out=BX1.rearrange("p g c -> p c g"), in_=ps[2][:].rearrange("p (c g) -> p c g", g=32)))
            S.op("scalar", lambda e: e.copy(out=BX2.rearrange("p g c -> p c g"), in_=ps[3][:].rearrange("p (c g) -> p c g", g=32)))
            S.op("vector", lambda e: e.tensor_copy(out=ARp[:], in_=ps[4][:, 0:32]))
            S.op("vector", lambda e: e.tensor_copy(out=AIp[:], in_=ps[4][:, 32:64]))
            S.dma("sync", LDT[:], log_dt[li, dr].partition_broadcast(128))
            S.bar()
            ckp("pl%d" % dr)
            S.op("scalar", lambda e: e.activation(out=LDT[:], in_=LDT[:], func=AF.Exp))
            S.bar()
            S.op("vector", lambda e: e.tensor_tensor(out=LR[:], in0=ARp[:], in1=LDT[:], op=ALU.mult))
            S.op("gpsimd", lambda e: e.tensor_tensor(out=LI[:], in0=AIp[:], in1=LDT[:], op=ALU.mult))
            S.bar()
            ckp("pb%d" % dr)
            ks = list(range(-8, 0)) + list(range(1, 9))
            for idx, kk in enumerate(ks):
                S.op("vector", lambda e, idx=idx, kk=kk: e.tensor_scalar(
                    out=ARG[:, :, idx], in0=LI[:], scalar1=float(kk), scalar2=None, op0=ALU.mult))
                S.op("vector", lambda e, idx=idx, kk=kk: e.tensor_scalar(
                    out=EARG[:, :, idx], in0=LR[:], scalar1=float(kk), scalar2=None, op0=ALU.mult))
            S.bar()
            ckp("pc%d" % dr)
            S.op("vector", lambda e: e.tensor_scalar(out=ARGC, in0=ARG, scalar1=TWO_PI / 4, scalar2=None, op0=ALU.add))
            S.op("scalar", lambda e: e.activation(out=MAG, in_=EARG, func=AF.Exp))
            S.bar()
            ckp("pd%d" % dr)
            S.op("vector", lambda e: e.tensor_scalar(out=NI, in0=ARG, scalar1=1.0 / TWO_PI, scalar2=None, op0=ALU.mult))
            S.op("vector", lambda e: e.tensor_scalar(out=NIC, in0=ARGC, scalar1=1.0 / TWO_PI, scalar2=None, op0=ALU.mult))
            S.bar()
            ckp("pe%d" % dr)
            S.op("vector", lambda e: e.tensor_copy(out=NF, in_=NI))
            S.op("vector", lambda e: e.tensor_copy(out=NFC, in_=NIC))
            S.bar()
            ckp("pf%d" % dr)
            S.op("vector", lambda e: e.scalar_tensor_tensor(out=ARG, in0=NF, scalar=-TWO_PI, in1=ARG, op0=ALU.mult, op1=ALU.add))
            S.op("vector", lambda e: e.scalar_tensor_tensor(out=ARGC, in0=NFC, scalar=-TWO_PI, in1=ARGC, op0=ALU.mult, op1=ALU.add))
            S.bar()
            S.op("vector", lambda e: e.tensor_scalar(out=ARG, in0=ARG, scalar1=3.1415925, scalar2=-3.1415925, op0=ALU.min, op1=ALU.max))
            S.op("vector", lambda e: e.tensor_scalar(out=ARGC, in0=ARGC, scalar1=3.1415925, scalar2=-3.1415925, op0=ALU.min, op1=ALU.max))
            S.bar()
            ckp("pg%d" % dr)
            S.op("scalar", lambda e: e.activation(out=PIM, in_=ARG, func=AF.Sin))
            S.op("scalar", lambda e: e.activation(out=PRE, in_=ARGC, func=AF.Sin))
            S.bar()
            S.op("vector", lambda e: e.tensor_tensor(out=PIM, in0=PIM, in1=MAG, op=ALU.mult))
            S.op("vector", lambda e: e.tensor_tensor(out=PRE, in0=PRE, in1=MAG, op=ALU.mult))
            S.bar()
            ckp("ph%d" % dr)
            S.op("vector", lambda e: e.tensor_scalar(out=NR[:], in0=PRE[:, :, 8], scalar1=-1.0, scalar2=None, op0=ALU.add))
            S.op("vector", lambda e: e.tensor_tensor(out=DEN[:], in0=ARp[:], in1=ARp[:], op=ALU.mult))
            ckp("c0")
            S.op("vector", lambda e: e.tensor_tensor(out=TA[:], in0=AIp[:], in1=AIp[:], op=ALU.mult))
            ckp("c1")
            S.op("vector", lambda e: e.tensor_tensor(out=DEN[:], in0=DEN[:], in1=TA[:], op=ALU.add))
            ckp("c2")
            S.op("vector", lambda e: e.reciprocal(out=DEN[:], in_=DEN[:]))
            ckp("c3")
            S.op("vector", lambda e: e.tensor_tensor(out=TA[:], in0=NR[:], in1=ARp[:], op=ALU.mult))
            S.op("vector", lambda e: e.tensor_tensor(out=TB[:], in0=PIM[:, :, 8], in1=AIp[:], op=ALU.mult))
            ckp("c4")
            S.op("vector", lambda e: e.tensor_tensor(out=CR[:], in0=TA[:], in1=TB[:], op=ALU.add))
            ckp("c5")
            S.op("vector", lambda e: e.tensor_tensor(out=TA[:], in0=PIM[:, :, 8], in1=ARp[:], op=ALU.mult))
            S.op("vector", lambda e: e.tensor_tensor(out=TB[:], in0=NR[:], in1=AIp[:], op=ALU.mult))
            ckp("c6")
            S.op("vector", lambda e: e.tensor_tensor(out=CI[:], in0=TA[:], in1=TB[:], op=ALU.subtract))
            ckp("c7")
            S.op("vector", lambda e: e.tensor_tensor(out=CR[:], in0=CR[:], in1=DEN[:], op=ALU.mult))
            S.op("vector", lambda e: e.tensor_tensor(out=CI[:], in0=CI[:], in1=DEN[:], op=ALU.mult))
            ckp("c8")
            ckp("pi%d" % dr)
            CRb = CR[:].unsqueeze(2).to_broadcast([128, 32, 8])
            CIb = CI[:].unsqueeze(2).to_broadcast([128, 32, 8])
            S.op("vector", lambda e: e.tensor_tensor(out=QR, in0=PRE[:, :, 0:8], in1=CRb, op=ALU.mult))
            S.op("vector", lambda e: e.tensor_tensor(out=QT, in0=PIM[:, :, 0:8], in1=CIb, op=ALU.mult))
            S.bar()
            S.op("vector", lambda e: e.tensor_tensor(out=QR, in0=QR, in1=QT, op=ALU.subtract))
            S.bar()
            S.op("vector", lambda e: e.tensor_tensor(out=QI, in0=PRE[:, :, 0:8], in1=CIb, op=ALU.mult))
            S.op("vector", lambda e: e.tensor_tensor(out=QT, in0=PIM[:, :, 0:8], in1=CRb, op=ALU.mult))
            S.bar()
            S.op("vector", lambda e: e.tensor_tensor(out=QI, in0=QI, in1=QT, op=ALU.add))
            S.bar()
            S.op("vector", lambda e: e.tensor_scalar(out=QI, in0=QI, scalar1=SIGN[:, 0:1], scalar2=None, op0=ALU.mult))
            S.op("vector", lambda e: e.tensor_scalar(out=PA, in0=PRE[:, :, 8:16], scalar1=NSIGN[:, 0:1], scalar2=None, op0=ALU.mult))
            S.op("vector", lambda e: e.tensor_scalar(out=PB, in0=PIM[:, :, 8:16], scalar1=-1.0, scalar2=None, op0=ALU.mult))
            S.bar()
            S.op("vector", lambda e: e.tensor_copy(out=ARC[:, 0, :], in_=PRE[:, :, 15]))
            S.op("vector", lambda e: e.tensor_copy(out=ARC[:, 1, :], in_=PRE[:, :, 15]))
            S.op("vector", lambda e: e.tensor_scalar(out=AIC[:, 0, :], in0=PIM[:, :, 15], scalar1=SIGN[:, 0:1], scalar2=None, op0=ALU.mult))
            S.op("vector", lambda e: e.tensor_scalar(out=AIC[:, 1, :], in0=PIM[:, :, 15], scalar1=NSIGN[:, 0:1], scalar2=None, op0=ALU.mult))
            S.bar()
            ckp("prep%d" % dr + "")
            for s_ in range(8):
                qi = (7 - s_) if dr == 0 else s_
                ri = s_ if dr == 0 else (7 - s_)
                S.op("vector", lambda e, s_=s_, qi=qi: e.tensor_tensor(
                    out=LT[:, :, s_, :], in0=BX1, in1=QR[:, :, qi:qi + 1].to_broadcast([128, 32, 16]), op=ALU.mult))
                S.op("vector", lambda e, s_=s_, ri=ri: e.tensor_tensor(
                    out=RT[:, :, s_, :], in0=CX1, in1=PA[:, :, ri:ri + 1].to_broadcast([128, 32, 16]), op=ALU.mult))
            S.bar()
            TL = STG[:].rearrange("p (g s c) -> p g s c", s=8, c=16)
            for s_ in range(8):
                qi = (7 - s_) if dr == 0 else s_
                S.op("vector", lambda e, s_=s_, qi=qi: e.tensor_tensor(
                    out=TL[:, :, s_, :], in0=BX2, in1=QI[:, :, qi:qi + 1].to_broadcast([128, 32, 16]), op=ALU.mult))
            S.bar()
            S.op("vector", lambda e: e.tensor_tensor(out=LT, in0=LT, in1=TL, op=ALU.add))
            S.bar()
            for s_ in range(8):
                ri = s_ if dr == 0 else (7 - s_)
                S.op("vector", lambda e, s_=s_, ri=ri: e.tensor_tensor(
                    out=TL[:, :, s_, :], in0=CX2, in1=PB[:, :, ri:ri + 1].to_broadcast([128, 32, 16]), op=ALU.mult))
            S.bar()
            S.op("vector", lambda e: e.tensor_tensor(out=RT, in0=RT, in1=TL, op=ALU.add))
            S.bar()
            ckp("lr%d" % dr + "")
            for rnd in range(4):
                for gi in range(8):
                    g = rnd * 8 + gi
                    Lg = LT[:, g, :, :].rearrange("p s c -> p (s c)")
                    Rg = RT[:, g, :, :].rearrange("p s c -> p (s c)")
                    S.op("tensor", lambda e, gi=gi, Lg=Lg, Rg=Rg: e.matmul(
                        ps[gi // 4][:, (gi % 4) * 128:(gi % 4 + 1) * 128], lhsT=Lg, rhs=Rg, start=True, stop=True))
                    S.op("tensor", lambda e, gi=gi, Lg=Lg: e.transpose(
                        ps[2 + gi // 4][:, (gi % 4) * 128:(gi % 4 + 1) * 128], Lg, IDENT[:]))
                S.bar()
                for bk in range(2):
                    g0 = rnd * 8 + bk * 4
                    S.op("vector", lambda e, bk=bk, g0=g0, dr=dr: e.tensor_tensor(
                        out=TOEP[:, g0:g0 + 4, :], in0=ps[bk][:].rearrange("p (g n) -> p g n", n=128),
                        in1=MASKS[:, dr:dr + 1, :].to_broadcast([128, 4, 128]), op=ALU.mult))
                    S.op("scalar", lambda e, bk=bk, g0=g0: e.copy(
                        out=ET[:, g0:g0 + 4, :], in_=ps[2 + bk][:].rearrange("p (g n) -> p g n", n=128)))
                S.bar()
                for bk in range(2):
                    g0 = rnd * 8 + bk * 4
                    S.op("vector", lambda e, bk=bk, g0=g0: e.tensor_copy(
                        out=ESW[:, g0:g0 + 4, 0:64], in_=ps[2 + bk][:].rearrange("p (g n) -> p g n", n=128)[:, :, 64:128]))
                    S.op("vector", lambda e, bk=bk, g0=g0: e.tensor_copy(
                        out=ESW[:, g0:g0 + 4, 64:128], in_=ps[2 + bk][:].rearrange("p (g n) -> p g n", n=128)[:, :, 0:64]))
                S.bar()
            S.op("scalar", lambda e: e.copy(out=RB.rearrange("p g n -> p (g n)"), in_=RT.rearrange("p g s c -> p (g s c)")))
            if dr == 0:
                for g in range(32):
                    S.op("vector", lambda e, g=g: e.scalar_tensor_tensor(
                        out=TOEP[:, g, :], in0=IDENT[:], scalar=DV[:, g:g + 1], in1=TOEP[:, g, :],
                        op0=ALU.mult, op1=ALU.add))
            S.op("gpsimd", lambda e: e.memset(W3[:], 0.0))
            S.bar()
            ckp("tiles%d" % dr + "")
            blocks = [(0, 32)] + [(32 + 64 * b, 64) for b in range(4)]
            order = blocks if dr == 0 else [blocks[0]] + blocks[:0:-1]
            for (jb, nb) in order:
                for half in range(2):
                    for gi in range(16):
                        g = half * 16 + gi
                        for arr in range(2):
                            ii = gi * 2 + arr
                            Em = ET if arr == 0 else ESW
                            S.op("tensor", lambda e, ii=ii, g=g, Em=Em, jb=jb, nb=nb: e.matmul(
                                ps[ii // 8][:, (ii % 8) * 64:(ii % 8) * 64 + nb], lhsT=Em[:, g, :],
                                rhs=IM[:, g, jb:jb + nb], start=True, stop=True))
                    S.bar()
                    for bk in range(4):
                        g0 = half * 16 + bk * 4
                        S.op("vector" if bk % 2 == 0 else "scalar", (lambda e, bk=bk, g0=g0, nb=nb: e.tensor_copy(
                            out=SS[:, 0:nb, :, g0:g0 + 4].rearrange("p j a g -> p g a j"),
                            in_=ps[bk][:].rearrange("p (g a j) -> p g a j", a=2, j=64)[:, :, :, 0:nb]))
                            if bk % 2 == 0 else (lambda e, bk=bk, g0=g0, nb=nb: e.copy(
                            out=SS[:, 0:nb, :, g0:g0 + 4].rearrange("p j a g -> p g a j"),
                            in_=ps[bk][:].rearrange("p (g a j) -> p g a j", a=2, j=64)[:, :, :, 0:nb])))
                    S.bar()
                js = list(range(jb, jb + nb)) if dr == 0 else list(range(jb + nb - 1, jb - 1, -1))
                pjl = None
                nsub = 3 if nb == 64 else 1
                cuts = [round(len(js) * x / nsub) for x in range(nsub + 1)]
                for j in js:
                    jl = j - jb
                    if (j - js[0]) * (1 if dr == 0 else -1) in cuts[:-1]:
                        last_sub = ((jb, nb) == order[-1]) and ((j - js[0]) * (1 if dr == 0 else -1) == cuts[-2])
                        S.join()
                        if li + 1 < n_layers:
                            if ada_state["pending"] is not None:
                                wada_mm(ada_state["pending"], ps[7], 400)
                                ada_state["pending"] = None
                            if (not last_sub) and ada_state["next"] < 24:
                                wada_dma(li + 1, ada_state["next"], asyn=True)
                                ada_state["pending"] = ada_state["next"]
                                ada_state["next"] += 1
                    Wc = W3[:] if pjl is None else SS[:, pjl, :, :]
                    S.op("vector", lambda e, jl=jl, Wc=Wc: e.tensor_tensor(out=G3[:, 0:2, :], in0=Wc, in1=SS[:, jl, :, :], op=ALU.add))
                    S.op("vector", lambda e: e.tensor_tensor(out=T1[:], in0=G3[:, 0:2, :], in1=ARC[:], op=ALU.mult))
                    S.op("vector", lambda e: e.tensor_tensor(out=T2[:, 0, :], in0=G3[:, 1, :], in1=AIC[:, 0, :], op=ALU.mult))
                    S.op("vector", lambda e: e.tensor_tensor(out=T2[:, 1, :], in0=G3[:, 0, :], in1=AIC[:, 1, :], op=ALU.mult))
                    S.op("vector", lambda e, jl=jl: e.tensor_tensor(out=SS[:, jl, :, :], in0=T1[:], in1=T2[:], op=ALU.add))
                    pjl = jl
                S.bar()
                if dr == 0:
                    S.op("scalar", lambda e, jb=jb: e.copy(out=HH[:, jb, :], in_=W3[:, 0, :]))
                    S.op("scalar", lambda e, jb=jb, nb=nb: e.copy(out=HH[:, jb + 1:jb + nb, :], in_=SS[:, 0:nb - 1, 0, :]))
                else:
                    S.op("scalar", lambda e, jb=jb, nb=nb: e.copy(out=HH[:, jb + nb - 1, :], in_=W3[:, 0, :]))
                    S.op("scalar", lambda e, jb=jb, nb=nb: e.copy(out=HH[:, jb:jb + nb - 1, :], in_=SS[:, 1:nb, 0, :]))
                S.bar()
                S.op("vector", lambda e, pjl=pjl: e.tensor_copy(out=W3[:], in_=SS[:, pjl, :, :]))
                S.bar()
            ckp("rec%d" % dr + "")
            for rnd in range(4):
                for gi in range(8):
                    g = rnd * 8 + gi
                    S.op("tensor", lambda e, gi=gi, g=g: e.matmul(
                        ps[gi][:, 0:NJ], lhsT=TOEP[:, g, :], rhs=IM[:, g, :], start=True, stop=False))
                    S.op("tensor", lambda e, gi=gi, g=g: e.matmul(
                        ps[gi][:, 0:NJ], lhsT=RB[:, g, :], rhs=HH[:, :, g], start=False, stop=True))
                S.bar()
                for gi in range(8):
                    g = rnd * 8 + gi
                    if dr == 0:
                        S.op("vector" if gi % 2 == 0 else "scalar", (lambda e, gi=gi, g=g: e.tensor_copy(out=YS[:, g, :], in_=ps[gi][:, 0:NJ]))
                             if gi % 2 == 0 else (lambda e, gi=gi, g=g: e.copy(out=YS[:, g, :], in_=ps[gi][:, 0:NJ])))
                    else:
                        S.op("vector", lambda e, gi=gi, g=g: e.tensor_tensor(out=YS[:, g, :], in0=YS[:, g, :], in1=ps[gi][:, 0:NJ], op=ALU.add))
                S.bar()
        if li + 1 < n_layers:
            assert ada_state["next"] == 24 and ada_state["pending"] is None, ada_state
            S.op("vector", lambda e: e.tensor_copy(out=MODN[:], in_=ps[7][:, 400:496]))
        ckp("read")
        S.dma("sync", YSP[:, :, :], YS)
        S.bar()
        YV = YS.rearrange("p (q s) j -> p q s j", s=8)
        YSPv = YSP.rearrange("(s c) (q g) j -> g c q s j", c=16, g=8)
        for g8 in range(8):
            for q in range(4):
                S.dma(["sync", "gpsimd"][q % 2], YV[g8 * 16:(g8 + 1) * 16, q, :, :], YSPv[g8, :, q, :, :])
            S.bar()
        S.bar()

        ckp("unim")
        WG = WB[:, 0:4 * 512].rearrange("p (k n) -> p k n", n=512)
        load_weight(WG, w_glu[li], 4, 512, 512)
        for (t0, w) in TILES:
            j0, nj = t0 // 8, w // 8
            for q in range(4):
                S.op("scalar", lambda e, q=q, w=w, j0=j0, nj=nj: e.activation(
                    out=TMP[:, q, 0:w].rearrange("p (j s) -> p j s", s=8),
                    in_=YV[:, q, :, j0:j0 + nj].rearrange("p s j -> p j s"), func=AF.Gelu_apprx_tanh))
            S.bar()
            S.op("vector", lambda e, w=w: e.tensor_copy(out=OB[:, 0:4, 0:w], in_=TMP[:, 0:4, 0:w]))
            S.bar()
            for m in range(4):
                for k in range(4):
                    S.op("tensor", lambda e, m=m, k=k, w=w: e.matmul(
                        ps[m][:, 0:w], lhsT=WG[:, k, m * 128:(m + 1) * 128], rhs=OB[:, k, 0:w],
                        start=(k == 0), stop=(k == 3)))
            S.bar()
            for m in range(4):
                S.op("scalar", lambda e, m=m, w=w: e.activation(
                    out=XT[:, m, 0:w], in_=ps[m][:, 0:w], func=AF.Sigmoid, bias=BGLU[:, m:m + 1], scale=1.0))
            S.bar()
            S.join()
            S.op("vector", lambda e, w=w: e.tensor_tensor(out=OB[:, 4:8, 0:w], in0=TMP[:, 0:4, 0:w], in1=XT[:, 0:4, 0:w], op=ALU.mult))
            S.bar()
            S.dma_async("sync", MIX[0:512, t0:t0 + w].rearrange("(k p) t -> p k t", p=128), OB[:, 4:8, 0:w])

        S.join()
        ckp("glu")
        S.dma("sync", TMP[:, 0:4, 0:128], w_sp[li].rearrange("h p q -> p h q"))
        S.dma("gpsimd", BS.rearrange("p h q -> p (h q)"), b_sp[li].rearrange("h q -> (h q)").partition_broadcast(128))
        S.bar()
        for h in range(4):
            S.op("tensor", lambda e, h=h: e.transpose(ps[0][:, h * 128:(h + 1) * 128], TMP[:, h, 0:128], IDENT[:]))
        S.bar()
        S.op("vector", lambda e: e.tensor_copy(out=WST[:], in_=ps[0][:].rearrange("p (h n) -> p h n", n=128)))
        S.bar()
        VGb = [XT[:, 0:4, :], XT[:, 4:8, :]]
        UGb = [OB[:, 0:4, :], ACTB[:, 0:4, :]]

        def sgu_loads(i):
            t0, w = TILES[i]
            S.dma_async("sync", VGb[i % 2][:, :, 0:w], VG[:, t0:t0 + w].rearrange("(k p) t -> p k t", p=128))
            S.dma_async("gpsimd", UGb[i % 2][:, :, 0:w], UG[:, t0:t0 + w].rearrange("(k p) t -> p k t", p=128))

        WOv = WB[:, 4096:4096 + 8 * D].rearrange("p (k n) -> p k n", n=D)
        WOs = [FB[:, 1024:5120].rearrange("p (k n) -> p k n", n=512), FB[:, 5120:9216].rearrange("p (k n) -> p k n", n=512)]
        sgu_loads(0)
        for i, (t0, w) in enumerate(TILES):
            nchk = w // 128
            VGi, UGi = VGb[i % 2], UGb[i % 2]
            S.join()
            if i + 1 < len(TILES):
                sgu_loads(i + 1)
            if i < 2:
                wsrc = w_out[li][:, i * 512:(i + 1) * 512].rearrange("(k p) n -> p k n", p=128)
                S.dma_async("sync", WOs[i][:, 0:4, :], wsrc[:, 0:4, :])
                S.dma_async("gpsimd", WOs[i][:, 4:8, :], wsrc[:, 4:8, :])
            if i in (2, 3):
                S.op("vector", lambda e, i=i: e.tensor_copy(out=WOv[:, :, (i - 2) * 512:(i - 1) * 512], in_=WOs[i - 2]))
            S.op("scalar", lambda e, w=w, VGi=VGi: e.activation(out=SQ[:, 0:4, 0:w], in_=VGi[:, 0:4, 0:w], func=AF.Square))
            S.bar()
            rstd_from(SQ, 4, w, 1.0 / 512)
            for k in range(4):
                S.op("vector", lambda e, k=k, w=w, VGi=VGi: e.scalar_tensor_tensor(
                    out=TMP[:, k, 0:w], in0=VGi[:, k, 0:w], scalar=GSGU[:, k:k + 1], in1=RS[:, 0:w],
                    op0=ALU.mult, op1=ALU.mult))
            S.bar()
            for ck in range(nchk):
                for h in range(4):
                    ii = ck * 4 + h
                    S.op("tensor", lambda e, ii=ii, ck=ck, h=h: e.transpose(
                        ps[ii // 4][:, (ii % 4) * 128:(ii % 4 + 1) * 128], TMP[:, h, ck * 128:(ck + 1) * 128], IDENT[:]))
            S.bar()
            for ck in range(nchk):
                S.op("vector" if ck % 2 == 0 else "scalar", (lambda e, ck=ck: e.tensor_copy(
                    out=VT[:, ck * 4:(ck + 1) * 4, :], in_=ps[ck][:].rearrange("p (h n) -> p h n", n=128)))
                    if ck % 2 == 0 else (lambda e, ck=ck: e.copy(
                    out=VT[:, ck * 4:(ck + 1) * 4, :], in_=ps[ck][:].rearrange("p (h n) -> p h n", n=128))))
            S.bar()
            for ck in range(nchk):
                for h in range(4):
                    ii = ck * 4 + h
                    S.op("tensor", lambda e, ii=ii, ck=ck, h=h: e.matmul(
                        ps[4 + ck][:, h * 128:(h + 1) * 128], lhsT=VT[:, ii, :], rhs=WST[:, h, :], start=True, stop=True))
            S.bar()
            for ck in range(nchk):
                S.op("vector", lambda e, ck=ck: e.tensor_tensor(
                    out=TMP[:, 4:8, ck * 128:(ck + 1) * 128], in0=ps[4 + ck][:].rearrange("p (h n) -> p h n", n=128),
                    in1=BS, op=ALU.add))
            S.bar()
            S.op("vector", lambda e, w=w, UGi=UGi: e.tensor_tensor(out=OB[:, 4:8, 0:w], in0=TMP[:, 4:8, 0:w], in1=UGi[:, 0:4, 0:w], op=ALU.mult))
            S.bar()
            S.dma_async("sync", MIX[512:1024, t0:t0 + w].rearrange("(k p) t -> p k t", p=128), OB[:, 4:8, 0:w])

        S.join()
        ckp("sgu")
        resid_linear(MIX, 8, 2, [ACTB[:, 0:8, :], ACTB[:, 8:16, :]], WOv)

        ckp("wout")
        norm_mod(A2, 3)

        ckp("norm2")
        S.dma("sync", FB[0:9, 0:2 * DFF], w_conv[li].rearrange("a b n -> (a b) n"))
        S.bar()
        for ch in range(44):
            S.op("tensor", lambda e, ch=ch: e.transpose(ps[7][:, ch * 9:(ch + 1) * 9], FB[0:9, ch * 128:(ch + 1) * 128], IDENT[0:9, 0:9]))
        S.bar()
        S.op("vector", lambda e: e.tensor_copy(out=WC[:].rearrange("p t c -> p c t"), in_=ps[7][:, 0:396].rearrange("p (c t) -> p c t", t=9)))
        S.bar()
        WUb = [OBt[:, 0:2048].rearrange("p (k n) -> p k n", n=256), OBt[:, 2048:4096].rearrange("p (k n) -> p k n", n=256)]
        WDv = WB[:, 0:22 * D].rearrange("p (k n) -> p k n", n=D)
        wd_stage = [XTt[:, 0:2816].rearrange("p (k n) -> p k n", n=128), TMPt[:, 0:2816].rearrange("p (k n) -> p k n", n=128)]
        FBb = FB[:].bitcast(BF16)
        UPb = [FBb[:, 0:2304], FBb[:, 2304:4608]]
        DGb = [FBb[:, 4608:6912].rearrange("p (t n) -> p t n", n=128), FBb[:, 6912:9216].rearrange("p (t n) -> p t n", n=128)]
        SG = FB[:, 4608:6912]
        GB = ACTB[:, 0:5, :].rearrange("p a t -> p (a t)")[:, 0:NT]
        stgs = [STG[:, 0:2048].rearrange("p (k n) -> p k n", n=256), STG[:, 2048:4096].rearrange("p (k n) -> p k n", n=256)]

        def bank(m, sl):
            return ps[(sl + 4 * m) % 8]

        def wup_dma(m):
            st = stgs[m % 2]
            S.dma_async("sync", st[:, :, 0:128], w_up[li, :, m * 128:(m + 1) * 128].rearrange("(k p) n -> p k n", p=128))
            S.dma_async("gpsimd", st[:, :, 128:256], w_up[li, :, DFF + m * 128:DFF + (m + 1) * 128].rearrange("(k p) n -> p k n", p=128))

        def prep_w(m, which):
            if which == 0:
                S.op("vector", lambda e, m=m: e.tensor_copy(out=WUb[m % 2], in_=stgs[m % 2]))
            for idx in range(18):
                part, tap = divmod(idx, 9)
                ch = part * 22 + m
                if idx % 2 == 0 and which == 0:
                    S.op("vector", lambda e, idx=idx, tap=tap, ch=ch, m=m: e.tensor_scalar(
                        out=DGb[m % 2][:, idx, :], in0=IDENT[:], scalar1=WC[:, tap, ch:ch + 1], scalar2=None, op0=ALU.mult))
                if idx % 2 == 1 and which == 1:
                    S.op("scalar", lambda e, idx=idx, tap=tap, ch=ch, m=m: e.activation(
                        out=DGb[m % 2][:, idx, :], in_=IDENT[:], func=AF.Copy, scale=WC[:, tap, ch:ch + 1]))

        def up_mm(m, part, tiles, slots):
            for t, sl in zip(tiles, slots):
                t0, w = TILES[t]
                bk = bank(m, sl)
                for k in range(8):
                    S.op("tensor", lambda e, bk=bk, k=k, t0=t0, w=w, part=part, m=m: e.matmul(
                        bk[:, 0:w], lhsT=WUb[m % 2][:, k, part * 128:(part + 1) * 128], rhs=H[:, k, t0:t0 + w],
                        start=(k == 0), stop=(k == 7)))

        def evac_up(m, part, tiles, slots, engs):
            dst = UPb[part]
            for n_, (t, sl) in enumerate(zip(tiles, slots)):
                t0, w = TILES[t]
                bk = bank(m, sl)
                if engs[n_ % len(engs)] == "vector":
                    S.op("vector", lambda e, bk=bk, t0=t0, w=w, dst=dst: e.tensor_copy(out=dst[:, t0:t0 + w], in_=bk[:, 0:w]))
                else:
                    S.op("scalar", lambda e, bk=bk, t0=t0, w=w, dst=dst: e.copy(out=dst[:, t0:t0 + w], in_=bk[:, 0:w]))

        taps9 = [(1, 1)] + [(ky, kx) for ky in range(3) for kx in range(3) if not (ky == 1 and kx == 1)]

        def conv_mm(m, part, tiles, slots):
            src = UPb[part]
            d0 = part * 9
            DGm = DGb[m % 2]
            sv = src[:, LC:NT].rearrange("p (r c) -> p r c", c=64)
            for t, sl in zip(tiles, slots):
                bk = bank(m, sl)
                if t == 0:
                    S.op("tensor", lambda e, bk=bk, DGm=DGm: e.matmul(bk[:, 0:256], lhsT=DGm[:, d0 + 4, :], rhs=src[:, 0:256], start=True, stop=False))
                    S.op("tensor", lambda e, bk=bk, DGm=DGm: e.matmul(bk[:, 1:256], lhsT=DGm[:, d0 + 3, :], rhs=src[:, 0:255], start=False, stop=False))
                    S.op("tensor", lambda e, bk=bk, DGm=DGm: e.matmul(bk[:, 0:255], lhsT=DGm[:, d0 + 5, :], rhs=src[:, 1:256], start=False, stop=True))
                    continue
                R0 = 8 * (t - 1)
                pv = bk[:, 0:512].rearrange("p (r c) -> p r c", c=64)
                for n_, (ky, kx) in enumerate(taps9):
                    dy, dx = ky - 1, kx - 1
                    ra, rb = max(R0, -dy, 0), min(R0 + 8, 32 - max(0, dy))
                    c0, c1 = max(0, -dx), 64 - max(0, dx)
                    S.op("tensor", lambda e, pv=pv, ra=ra, rb=rb, c0=c0, c1=c1, dy=dy, dx=dx, R0=R0, ky=ky, kx=kx, n_=n_, DGm=DGm:
                         e.matmul(pv[:, ra - R0:rb - R0, c0:c1], lhsT=DGm[:, d0 + ky * 3 + kx, :],
                                  rhs=sv[:, ra + dy:rb + dy, c0 + dx:c1 + dx], start=(n_ == 0), stop=(n_ == 8)))

        def silu_ev(m, tiles, slots):
            for t, sl in zip(tiles, slots):
                t0, w = TILES[t]
                bk = bank(m, sl)
                S.op("scalar", lambda e, bk=bk, t0=t0, w=w: e.activation(out=SG[:, t0:t0 + w], in_=bk[:, 0:w], func=AF.Silu))

        def mult_ev(m, tiles, slots):
            for t, sl in zip(tiles, slots):
                t0, w = TILES[t]
                bk = bank(m, sl)
                S.op("vector", lambda e, bk=bk, t0=t0, w=w: e.tensor_tensor(out=GB[:, t0:t0 + w], in0=bk[:, 0:w], in1=SG[:, t0:t0 + w], op=ALU.mult))

        wup_dma(0)
        S.join()
        prep_w(0, 0)
        prep_w(0, 1)
        S.bar()
        wup_dma(1)
        for m in range(22):
            up_mm(m, 0, [0, 1, 2, 3, 4], [0, 1, 2, 3, 4])
            if m > 0:
                mult_ev(m - 1, [3, 4], [2, 3])
            if m % 2 == 0 and m // 2 < 8:
                S.dma_async("sync", wd_stage[(m // 2) % 2], w_down[li][:, (m // 2) * 128:(m // 2 + 1) * 128].rearrange("(k p) n -> p k n", p=128))
            S.bar()
            up_mm(m, 1, [0, 1, 2], [5, 6, 7])
            evac_up(m, 0, [0, 1, 2, 3, 4], [0, 1, 2, 3, 4], ["vector", "scalar"])
            if m > 0:
                S.dma_async("gpsimd", GD[(m - 1) * 128:m * 128, :], GB)
            S.bar()
            up_mm(m, 1, [3, 4], [0, 1])
            conv_mm(m, 0, [0, 1, 2], [2, 3, 4])
            evac_up(m, 1, [0, 1, 2], [5, 6, 7], ["vector", "scalar"])
            S.bar()
            conv_mm(m, 0, [3, 4], [5, 6])
            evac_up(m, 1, [3, 4], [0, 1], ["vector"])
            silu_ev(m, [0, 1, 2], [2, 3, 4])
            if m % 2 == 1 and m // 2 < 8:
                S.op("vector", lambda e, m=m: e.tensor_copy(out=WDv[:, :, (m // 2) * 128:(m // 2 + 1) * 128], in_=wd_stage[(m // 2) % 2]))
            S.join()
            conv_mm(m, 1, [0, 1, 2], [0, 1, 7])
            silu_ev(m, [3, 4], [5, 6])
            if m + 1 < 22:
                prep_w(m + 1, 0)
            S.bar()
            conv_mm(m, 1, [3, 4], [2, 3])
            mult_ev(m, [0, 1, 2], [0, 1, 7])
            if m + 1 < 22:
                prep_w(m + 1, 1)
            if m + 2 < 22:
                wup_dma(m + 2)
            S.bar()
        mult_ev(21, [3, 4], [2, 3])
        S.bar()
        S.dma_async("gpsimd", GD[21 * 128:22 * 128, :], GB)
        S.join()

        ckp("ffnup")
        resid_linear(GD, 22, 5, [ACTB, FB[:].bitcast(BF16)[:, 0:11264].rearrange("p (k t) -> p k t", t=512)], WB[:, 0:22 * D].rearrange("p (k n) -> p k n", n=D))

    except _Stop:
        pass
    S.bar()
    if dbg:
        DF = nc.dram_tensor("DBGF", [128, 32768], F32, kind="ExternalOutput").ap()
        DB = nc.dram_tensor("DBGB", [128, 40960], BF16, kind="ExternalOutput").ap()
        off = 0
        for t_, n_ in [(HRAW[:], 9216), (XTt[:], 4096), (TMPt[:], 4096), (FB[:], 9216), (STG[:], 4096), (RS[:], 512),
                       (MOD[:].rearrange("p q k n -> p (q k n)"), 96), (A1[:].rearrange("p k n -> p (k n)"), 16),
                       (A2[:].rearrange("p k n -> p (k n)"), 16), (W3[:].rearrange("p a g -> p (a g)"), 64),
                       (ARC[:].rearrange("p a g -> p (a g)"), 64), (AIC[:].rearrange("p a g -> p (a g)"), 64),
                       (CR[:], 32), (CI[:], 32), (DV[:], 32), (LR[:], 32), (LI[:], 32)]:
            S.dma("sync", DF[:, off:off + n_], t_)
            off += n_
        offb = 0
        for t_, n_ in [(WB, 22528), (OBt[:], 4096), (ACTBt[:], 11264)]:
            S.dma("gpsimd", DB[:, offb:offb + n_], t_)
            offb += n_
        S.bar()
    for b in range(16):
        t0 = LC + b * 128
        S.dma("sync", XT[:, :, 0:128], XRES[:, t0:t0 + 128].rearrange("(k p) t -> p k t", p=128))
        S.bar()
        S.op("scalar", lambda e: e.activation(out=SQ[:, :, 0:128], in_=XT[:, :, 0:128], func=AF.Square))
        S.bar()
        rstd_from(SQ, 8, 128, 1.0 / D)
        for k in range(8):
            S.op("vector", lambda e, k=k: e.scalar_tensor_tensor(
                out=TMP[:, k, 0:128], in0=XT[:, k, 0:128], scalar=GFIN[:, k:k + 1], in1=RS[:, 0:128],
                op0=ALU.mult, op1=ALU.mult))
        S.bar()
        for k in range(8):
            S.op("tensor", lambda e, k=k: e.transpose(ps[k // 4][:, (k % 4) * 128:(k % 4 + 1) * 128],
                                                       TMP[:, k, 0:128], IDENT[:]))
        S.bar()
        S.op("vector", lambda e: e.tensor_copy(out=XT[:, 0:4, 0:128], in_=ps[0][:].rearrange("p (k t) -> p k t", t=128)))
        S.op("scalar", lambda e: e.copy(out=XT[:, 4:8, 0:128], in_=ps[1][:].rearrange("p (k t) -> p k t", t=128)))
        S.bar()
        S.dma("sync", out[b * 128:(b + 1) * 128, :].rearrange("t (k d) -> t k d", d=128), XT[:, :, 0:128])
        S.bar()

    S.emit()
    es.close()
    return nc


_CONST = None


def _consts():
    ident = np.eye(128, dtype=np.float32)
    sp = np.arange(128) // 16
    m0 = (sp[None, :] >= sp[:, None]).astype(np.float32)
    m1 = (sp[None, :] <= sp[:, None]).astype(np.float32)
    return ident, np.stack([m0, m1])


def kernel(n_layers=4, **inputs):
    nc = build_nc(n_layers)
    ident, mask = _consts()
    in_maps = []
    for b in range(8):
        m = {}
        for k, v in inputs.items():
            v = np.asarray(v)
            if k in ("x", "c", "ctx"):
                m[k] = np.ascontiguousarray(v[b], dtype=np.float32)
            else:
                m[k] = np.ascontiguousarray(v, dtype=np.float32)
        m["ident"] = ident
        m["mask"] = mask
        in_maps.append(m)
    res = run_bass_kernel_spmd(nc, in_maps, core_ids=list(range(8)))
    return np.stack([np.asarray(r["out"], dtype=np.float32) for r in res.results], axis=0)
```

```python
import numpy as np
from contextlib import ExitStack
import concourse.bass as bass
import concourse.mybir as mybir
from concourse.bass_utils import run_bass_kernel_spmd

F32, BF16, I32 = mybir.dt.float32, mybir.dt.bfloat16, mybir.dt.int32
AF = mybir.ActivationFunctionType
ALU = mybir.AluOpType

D = 1024
NT = 2304
LC = 256
LL = 2048
DFF = 2816
EPS = 1e-6
NJ = 288
TILES = [(0, 256)] + [(256 + 512 * i, 512) for i in range(4)]
TWO_PI = 6.283185307179586


class Sched:
    def __init__(self, nc):
        self.nc = nc
        self.stages = [[]]
        self.join_at = set()

    def op(self, eng, fn, dma=False):
        self.stages[-1].append((eng, dma, fn))

    def dma(self, eng, out, in_, slow=False):
        self.op(eng, lambda e, o=out, i=in_: e.dma_start(out=o, in_=i), dma=True)

    def dma_async(self, eng, out, in_):
        self.op(eng, lambda e, o=out, i=in_: e.dma_start(out=o, in_=i), dma="async")

    def bar(self):
        if self.stages[-1]:
            self.stages.append([])

    def join(self):
        self.bar()
        self.join_at.add(len(self.stages) - 1)

    def emit(self):
        nc = self.nc
        self.bar()
        merged, joins = [], set()
        for k, st in enumerate(self.stages):
            only_vec = len(st) > 0 and all((o[0] == "vector" and not o[1]) for o in st)
            if (merged and only_vec and k not in self.join_at and merged[-1][1]):
                merged[-1][0].extend(st)
            else:
                if k in self.join_at:
                    joins.add(len(merged))
                merged.append([list(st), only_vec])
        self.stages = [m[0] for m in merged]
        self.join_at = joins
        names = ["c_scalar", "c_vector", "c_gpsimd", "c_tensor", "d_sync", "d_scalar", "d_gpsimd", "a_sync", "a_gpsimd"]

        def semname(eng, dma):
            if dma == "async":
                return "a_" + eng
            return ("d_" if dma else "c_") + eng

        cum = []
        cur = {n: 0 for n in names}
        for st in self.stages:
            cum.append(dict(cur))
            for (eng, dma, _) in st:
                cur[semname(eng, dma)] += 16 if dma else 1
        final = dict(cur)
        with ExitStack() as es:
            sems = {n: es.enter_context(nc.semaphore(n)) for n in names}
            block = es.enter_context(nc.Block())

            def make(engname):
                def body(eng):
                    waited = {n: 0 for n in names}
                    joined = {n: 0 for n in names}
                    for k, st in enumerate(self.stages):
                        if k in self.join_at:
                            for n in names:
                                if n.startswith("a_"):
                                    joined[n] = cum[k][n]
                        mine = [o for o in st if o[0] == engname]
                        if not mine:
                            continue
                        for n in names:
                            tgt = joined[n] if n.startswith("a_") else cum[k][n]
                            if tgt > waited[n]:
                                eng.wait_ge(sems[n], tgt)
                                waited[n] = tgt
                        for (_, dma, fn) in mine:
                            ins = fn(eng)
                            ins.then_inc(sems[semname(engname, dma)], 16 if dma else 1)
                    if engname == "sync":
                        for n in names:
                            if final[n] > waited[n]:
                                eng.wait_ge(sems[n], final[n])
                return body

            block.sync(make("sync"))
            block.scalar(make("scalar"))
            block.vector(make("vector"))
            block.gpsimd(make("gpsimd"))
            block.tensor(make("tensor"))


class _Stop(Exception):
    pass


def build_nc(n_layers, n_wl=4, stop=None, dbg=False):
    nc = bass.Bass("TRN2", target_bir_lowering=False)
    S = Sched(nc)
    W = n_wl

    def ckp(name):
        S.bar()
        if stop == name:
            raise _Stop()

    def din(name, shape):
        return nc.dram_tensor(name, list(shape), F32, kind="ExternalInput").ap()

    x_in = din("x", [LL, D])
    c_in = din("c", [D])
    ctx_in = din("ctx", [LC, D])
    cctx_in = din("c_ctx", [D])
    w_ada = din("w_ada", [W, D, 6 * D])
    b_ada = din("b_ada", [W, 6 * D])
    g_mix = din("g_mix", [W, D])
    w_in = din("w_in", [W, D, 1536])
    a_re = din("ssm_a_re", [W, 2, 32, 64])
    a_im = din("ssm_a_im", [W, 2, 32, 64])
    b_re = din("ssm_b_re", [W, 2, 32, 64, 16])
    b_im = din("ssm_b_im", [W, 2, 32, 64, 16])
    c_re = din("ssm_c_re", [W, 2, 32, 16, 64])
    c_im = din("ssm_c_im", [W, 2, 32, 16, 64])
    log_dt = din("ssm_log_dt", [W, 2, 32])
    ssm_d = din("ssm_d", [W, 512])
    w_glu = din("w_glu", [W, 512, 512])
    b_glu = din("b_glu", [W, 512])
    g_sgu = din("g_sgu", [W, 512])
    w_sp = din("w_spatial", [W, 4, 128, 128])
    b_sp = din("b_spatial", [W, 4, 128])
    w_out = din("w_out", [W, D, D])
    g_ffn = din("g_ffn", [W, D])
    w_up = din("w_up", [W, D, 2 * DFF])
    w_conv = din("w_conv", [W, 3, 3, 2 * DFF])
    w_down = din("w_down", [W, DFF, D])
    g_final = din("g_final", [D])
    ident_in = din("ident", [128, 128])
    mask_in = din("mask", [2, 128, 128])
    out = nc.dram_tensor("out", [LL, D], F32, kind="ExternalOutput").ap()

    SK = dict(kind="ExternalOutput") if dbg else {}
    XRES = nc.dram_tensor("XRES", [D, NT], F32, **SK).ap()
    USSMP = nc.dram_tensor("USSMP", [512, 8, NJ], BF16, **SK).ap()
    YSP = nc.dram_tensor("YSP", [128, 32, NJ], F32, **SK).ap()
    UG = nc.dram_tensor("UG", [512, NT], BF16, **SK).ap()
    VG = nc.dram_tensor("VG", [512, NT], F32, **SK).ap()
    MIX = nc.dram_tensor("MIX", [D, NT], BF16, **SK).ap()
    GD = nc.dram_tensor("GD", [DFF, NT], BF16, **SK).ap()

    es = ExitStack()

    def sb(name, shape, dt=F32):
        return es.enter_context(nc.sbuf_tensor(name, list(shape), dt))

    ps = [es.enter_context(nc.psum_tensor("ps%d" % i, [128, 512], F32)) for i in range(8)]

    IDENT = sb("IDENT", [128, 128])
    ONESB = sb("ONESB", [128, 128], BF16)
    MASKS = sb("MASKS", [128, 2, 128])
    SIGN = sb("SIGN", [128, 1])
    NSIGN = sb("NSIGN", [128, 1])
    SC = sb("SC", [128, 8, 2])
    MOD = sb("MOD", [128, 6, 8, 2])
    BADA = sb("BADA", [128, 6, 8])
    MODN = sb("MODN", [128, 96])
    GM = sb("GM", [128, 8])
    GF = sb("GF", [128, 8])
    GFIN = sb("GFIN", [128, 8])
    A1 = sb("A1", [128, 8, 2])
    A2 = sb("A2", [128, 8, 2])
    HRAW = sb("HRAW", [128, 9216])
    H = HRAW[:].bitcast(BF16).rearrange("p (k t) -> p k t", t=NT)
    YS = HRAW[:].rearrange("p (g j) -> p g j", j=NJ)
    STG = sb("STG", [128, 4096])
    SQ = STG[:, 0:2048].bitcast(BF16).rearrange("p (k t) -> p k t", t=512)
    WBt = sb("WB", [128, 22528], BF16)
    WB = WBt[:]
    TOEP = WB[:, 0:4096].rearrange("p (g n) -> p g n", n=128)
    ET = WB[:, 4096:8192].rearrange("p (g n) -> p g n", n=128)
    ESW = WB[:, 8192:12288].rearrange("p (g n) -> p g n", n=128)
    IM = WB[:, 12288:21504].rearrange("p (g j) -> p g j", j=NJ)
    XTt = sb("XT", [128, 4096])
    XT = XTt[:].rearrange("p (k t) -> p k t", t=512)
    LT = XTt[:].rearrange("p (g s c) -> p g s c", s=8, c=16)
    SS = XTt[:].rearrange("p (j a g) -> p j a g", a=2, g=32)
    TMPt = sb("TMP", [128, 4096])
    TMP = TMPt[:].rearrange("p (k t) -> p k t", t=512)
    RT = TMPt[:].rearrange("p (g s c) -> p g s c", s=8, c=16)
    RS = sb("RS", [128, 512])
    OBt = sb("OB", [128, 4096], BF16)
    OB = OBt[:].rearrange("p (k t) -> p k t", t=512)
    RB = OBt[:].rearrange("p (g n) -> p g n", n=128)
    ACTBt = sb("ACTB", [128, 11264], BF16)
    ACTB = ACTBt[:].rearrange("p (k t) -> p k t", t=512)
    USP = ACTBt[:, 0:9216].rearrange("p (q s j) -> p q s j", s=8, j=NJ)
    HH = ACTBt[:, 0:9216].rearrange("p (j g) -> p j g", g=32)
    FB = sb("FB", [128, 9216])
    UPG = FB[:, 0:2304]; UPV = FB[:, 2304:4608]; CG = FB[:, 4608:6912]; CV = FB[:, 6912:9216]
    def ftab(i):
        return FB[:, i * 512:(i + 1) * 512].rearrange("p (g k) -> p g k", k=16)
    ARG, ARGC, EARG, NF, NFC, PRE, PIM, MAG, BX1, BX2, CX1, CX2 = [ftab(i) for i in range(12)]
    def qtab(i):
        return FB[:, 6144 + i * 256:6144 + (i + 1) * 256].rearrange("p (g k) -> p g k", k=8)
    QR, QI, QT, PA, PB = [qtab(i) for i in range(5)]
    NI = FB[:, 7424:7936].bitcast(I32).rearrange("p (g k) -> p g k", k=16)
    NIC = FB[:, 7936:8448].bitcast(I32).rearrange("p (g k) -> p g k", k=16)
    BS = FB[:, 0:512].rearrange("p (h q) -> p h q", q=128)
    VT = STG[:, 2048:4096].bitcast(BF16).rearrange("p (a n) -> p a n", n=128)
    W3 = sb("W3", [128, 2, 32])
    G3 = sb("G3", [128, 3, 32])
    T1 = sb("T1", [128, 2, 32])
    T2 = sb("T2", [128, 2, 32])
    ARp = sb("ARp", [128, 32]); AIp = sb("AIp", [128, 32]); LDT = sb("LDT", [128, 32])
    LR = sb("LR", [128, 32]); LI = sb("LI", [128, 32])
    CR = sb("CR", [128, 32]); CI = sb("CI", [128, 32]); NR = sb("NR", [128, 32]); DEN = sb("DEN", [128, 32])
    TA = sb("TA", [128, 32]); TB = sb("TB", [128, 32])
    ARC = sb("ARC", [128, 2, 32]); AIC = sb("AIC", [128, 2, 32])
    DV = sb("DV", [128, 32])
    BGLU = sb("BGLU", [128, 4]); GSGU = sb("GSGU", [128, 4])
    WST = sb("WST", [128, 4, 128], BF16)
    WC = sb("WC", [128, 9, 44])

    S.dma("sync", IDENT[:], ident_in[:, :])
    S.dma("sync", MASKS[:], mask_in.rearrange("m p q -> p m q"))
    S.op("vector", lambda e: e.memset(ONESB[:], 1.0))
    S.op("vector", lambda e: e.memset(SIGN[0:64, :], -1.0))
    S.op("vector", lambda e: e.memset(SIGN[64:128, :], 1.0))
    S.op("vector", lambda e: e.memset(NSIGN[0:64, :], 1.0))
    S.op("vector", lambda e: e.memset(NSIGN[64:128, :], -1.0))
    S.dma("sync", STG[0:8, 0:128], c_in.rearrange("(k p) -> k p", p=128))
    S.dma("sync", STG[8:16, 0:128], cctx_in.rearrange("(k p) -> k p", p=128))
    S.dma("sync", STG[16:24, 0:128], g_final.rearrange("(k p) -> k p", p=128))
    S.bar()
    S.op("tensor", lambda e: e.transpose(ps[7][:, 0:24], STG[0:24, 0:128], IDENT[0:24, 0:24]))
    S.bar()
    S.op("vector", lambda e: e.tensor_copy(out=SC[:, :, 0], in_=ps[7][:, 0:8]))
    S.op("vector", lambda e: e.tensor_copy(out=SC[:, :, 1], in_=ps[7][:, 8:16]))
    S.op("vector", lambda e: e.tensor_copy(out=GFIN[:], in_=ps[7][:, 16:24]))
    S.bar()
    S.op("scalar", lambda e: e.activation(out=SC[:], in_=SC[:], func=AF.Silu))
    S.bar()

    def in_transpose_all():
        blocks = [(ctx_in, i, i * 128) for i in range(2)] + [(x_in, i, 256 + i * 128) for i in range(16)]
        Lb = [TMP[:, :, 0:128], TMP[:, :, 128:256]]
        Ob = [XT[:, :, 0:128], XT[:, :, 128:256]]

        def load(b):
            src, i, _ = blocks[b]
            S.dma_async("sync", Lb[b % 2], src[i * 128:(i + 1) * 128, :].rearrange("t (k d) -> t k d", d=128))

        def store(b):
            t0 = blocks[b][2]
            S.dma_async("gpsimd", XRES[:, t0:t0 + 128].rearrange("(k p) t -> p k t", p=128), Ob[b % 2])

        nb_ = len(blocks)
        load(0)
        for n in range(nb_ + 2):
            S.join()
            if n < nb_:
                for k in range(8):
                    S.op("tensor", lambda e, k=k, n=n: e.transpose(ps[(n % 2) * 2 + k // 4][:, (k % 4) * 128:(k % 4 + 1) * 128],
                                                                   Lb[n % 2][:, k, :], IDENT[:]))
            if 1 <= n <= nb_:
                m_ = n - 1
                S.op("vector", lambda e, m_=m_: e.tensor_copy(out=Ob[m_ % 2][:, 0:4, :], in_=ps[(m_ % 2) * 2][:].rearrange("p (k t) -> p k t", t=128)))
                S.op("scalar", lambda e, m_=m_: e.copy(out=Ob[m_ % 2][:, 4:8, :], in_=ps[(m_ % 2) * 2 + 1][:].rearrange("p (k t) -> p k t", t=128)))
            if n + 1 < nb_:
                load(n + 1)
            if n >= 2:
                store(n - 2)
            S.bar()
        S.join()

    in_transpose_all()

    def load_weight(dst, wap, kch, ncols, cb):
        for c0 in range(0, ncols, cb):
            stg = STG[:, 0:kch * cb].rearrange("p (k n) -> p k n", n=cb)
            S.dma("sync", stg, wap[:, c0:c0 + cb].rearrange("(k p) n -> p k n", p=128))
            S.bar()
            S.op("vector", lambda e, stg=stg, c0=c0: e.tensor_copy(out=dst[:, :, c0:c0 + cb], in_=stg))
            S.bar()

    def rstd_from(src_sq, nk, w, inv_n):
        for k in range(nk):
            S.op("tensor", lambda e, k=k: e.matmul(ps[0][:, 0:w], lhsT=ONESB[:], rhs=src_sq[:, k, 0:w],
                                                    start=(k == 0), stop=(k == nk - 1)))
        S.bar()
        S.op("scalar", lambda e: e.activation(out=RS[:, 0:w], in_=ps[0][:, 0:w], func=AF.Sqrt, bias=EPS, scale=inv_n))
        S.bar()
        S.op("vector", lambda e: e.reciprocal(out=RS[:, 0:w], in_=RS[:, 0:w]))
        S.bar()

    def norm_mod(Acoef, which_shift, issue=None, convert=None):
        Xn = [XT, STG[:].rearrange("p (k t) -> p k t", t=512)]

        def xload(i):
            t0, w = TILES[i]
            S.dma_async("sync", Xn[i % 2][:, 0:4, 0:w], XRES[0:512, t0:t0 + w].rearrange("(k p) t -> p k t", p=128))
            S.dma_async("gpsimd", Xn[i % 2][:, 4:8, 0:w], XRES[512:1024, t0:t0 + w].rearrange("(k p) t -> p k t", p=128))

        xload(0)
        for i, (t0, w) in enumerate(TILES):
            sel = 1 if t0 == 0 else 0
            Xi = Xn[i % 2]
            S.join()
            if i + 1 < len(TILES):
                xload(i + 1)
            if issue and i in issue:
                issue[i]()
            if convert and i in convert:
                convert[i]()
            S.op("scalar", lambda e, w=w, Xi=Xi: e.activation(out=OB[:, :, 0:w], in_=Xi[:, :, 0:w], func=AF.Square))
            S.bar()
            rstd_from(OB, 8, w, 1.0 / D)
            for k in range(8):
                S.op("vector", lambda e, k=k, w=w, sel=sel, Xi=Xi: e.scalar_tensor_tensor(
                    out=TMP[:, k, 0:w], in0=Xi[:, k, 0:w], scalar=Acoef[:, k, sel:sel + 1], in1=RS[:, 0:w],
                    op0=ALU.mult, op1=ALU.mult))
            S.bar()
            for k in range(8):
                S.op("scalar", lambda e, k=k, w=w, sel=sel, t0=t0: e.activation(
                    out=H[:, k, t0:t0 + w], in_=TMP[:, k, 0:w], func=AF.Identity,
                    bias=MOD[:, which_shift, k, sel:sel + 1], scale=1.0))
            S.bar()

    def resid_linear(src_dram, kch, gate_idx, Abufs, Wv):
        Xb = [XT, TMP, STG[:].rearrange("p (k t) -> p k t", t=512)]

        def loads(i):
            t0, w = TILES[i]
            S.dma("sync", Abufs[i % 2][:, 0:kch, 0:w], src_dram[:, t0:t0 + w].rearrange("(k p) t -> p k t", p=128))
            S.dma("gpsimd", Xb[i % 3][:, :, 0:w], XRES[:, t0:t0 + w].rearrange("(k p) t -> p k t", p=128))

        def store(i):
            t0, w = TILES[i]
            S.dma("gpsimd", XRES[:, t0:t0 + w].rearrange("(k p) t -> p k t", p=128), Xb[i % 3][:, :, 0:w])

        loads(0)
        S.bar()
        nt = len(TILES)
        for i, (t0, w) in enumerate(TILES):
            sel = 1 if t0 == 0 else 0
            Ab = Abufs[i % 2]
            for m in range(8):
                for k in range(kch):
                    S.op("tensor", lambda e, m=m, k=k, w=w, Ab=Ab: e.matmul(
                        ps[m][:, 0:w], lhsT=Wv[:, k, m * 128:(m + 1) * 128], rhs=Ab[:, k, 0:w],
                        start=(k == 0), stop=(k == kch - 1)))
            if i + 1 < nt:
                loads(i + 1)
            if i >= 1:
                store(i - 1)
            S.bar()
            Xi = Xb[i % 3]
            for m in range(8):
                S.op("vector", lambda e, m=m, w=w, sel=sel, Xi=Xi: e.scalar_tensor_tensor(
                    out=Xi[:, m, 0:w], in0=ps[m][:, 0:w], scalar=MOD[:, gate_idx, m, sel:sel + 1], in1=Xi[:, m, 0:w],
                    op0=ALU.mult, op1=ALU.add))
            S.bar()
        store(nt - 1)
        S.bar()

    try:
      for li in range(n_layers):
        S.dma("sync", STG[0:48, 0:128], b_ada[li].rearrange("(k p) -> k p", p=128))
        S.dma("sync", STG[48:56, 0:128], g_mix[li].rearrange("(k p) -> k p", p=128))
        S.dma("sync", STG[56:64, 0:128], g_ffn[li].rearrange("(k p) -> k p", p=128))
        S.dma("sync", STG[64:68, 0:128], b_glu[li].rearrange("(k p) -> k p", p=128))
        S.dma("sync", STG[68:72, 0:128], g_sgu[li].rearrange("(k p) -> k p", p=128))
        S.bar()
        S.op("tensor", lambda e: e.transpose(ps[7][:, 0:72], STG[0:72, 0:128], IDENT[0:72, 0:72]))
        S.bar()
        S.op("vector", lambda e: e.tensor_copy(out=BADA[:].rearrange("p q k -> p (q k)"), in_=ps[7][:, 0:48]))
        S.op("vector", lambda e: e.tensor_copy(out=GM[:], in_=ps[7][:, 48:56]))
        S.op("vector", lambda e: e.tensor_copy(out=GF[:], in_=ps[7][:, 56:64]))
        S.op("vector", lambda e: e.tensor_copy(out=BGLU[:], in_=ps[7][:, 64:68]))
        S.op("vector", lambda e: e.tensor_copy(out=GSGU[:], in_=ps[7][:, 68:72]))
        S.bar()
        WAs = [STG[:, 0:2048].rearrange("p (k n) -> p k n", n=256), STG[:, 2048:4096].rearrange("p (k n) -> p k n", n=256)]

        def wada_dma(lyr, blk, asyn=False):
            WA = WAs[blk % 2]
            src = w_ada[lyr, :, blk * 256:(blk + 1) * 256].rearrange("(k p) n -> p k n", p=128)
            f = S.dma_async if asyn else S.dma
            f("sync", WA[:, 0:4, :], src[:, 0:4, :])
            f("gpsimd", WA[:, 4:8, :], src[:, 4:8, :])

        def wada_mm(blk, bank_ap, col0):
            q, mq = blk // 4, blk % 4
            WA = WAs[blk % 2]
            for mm in range(2):
                m = mq * 2 + mm
                for k in range(8):
                    S.op("tensor", lambda e, m=m, mm=mm, k=k, q=q, WA=WA: e.matmul(
                        bank_ap[:, col0 + (q * 8 + m) * 2:col0 + (q * 8 + m) * 2 + 2], lhsT=WA[:, k, mm * 128:(mm + 1) * 128],
                        rhs=SC[:, k, :], start=(k == 0), stop=(k == 7)))

        if li == 0:
            wada_dma(0, 0)
            S.bar()
            for blk in range(24):
                if blk + 1 < 24:
                    wada_dma(0, blk + 1)
                wada_mm(blk, ps[0], 0)
                S.bar()
            S.op("vector", lambda e: e.tensor_copy(out=MODN[:], in_=ps[0][:, 0:96]))
            S.bar()
        S.op("vector", lambda e: e.tensor_tensor(
            out=MOD[:].rearrange("p q k n -> p (q k) n"), in0=MODN[:].rearrange("p (a n) -> p a n", n=2),
            in1=BADA[:].rearrange("p q k -> p (q k)").unsqueeze(2).to_broadcast([128, 48, 2]), op=ALU.add))
        S.bar()
        S.op("vector", lambda e: e.scalar_tensor_tensor(
            out=A1[:], in0=MOD[:, 1, :, :], scalar=1.0, in1=GM[:].unsqueeze(2).to_broadcast([128, 8, 2]),
            op0=ALU.add, op1=ALU.mult))
        S.op("vector", lambda e: e.scalar_tensor_tensor(
            out=A2[:], in0=MOD[:, 4, :, :], scalar=1.0, in1=GF[:].unsqueeze(2).to_broadcast([128, 8, 2]),
            op0=ALU.add, op1=ALU.mult))
        S.bar()

        ckp("ada")
        WIN = WB[:, 0:8 * 1536].rearrange("p (k n) -> p k n", n=1536)
        FBs = [FB[:, 0:4096].rearrange("p (k n) -> p k n", n=512), FB[:, 4096:8192].rearrange("p (k n) -> p k n", n=512)]

        def win_issue(bk):
            def f():
                src = w_in[li][:, bk * 512:(bk + 1) * 512].rearrange("(k p) n -> p k n", p=128)
                S.dma_async("sync", FBs[bk % 2][:, 0:4, :], src[:, 0:4, :])
                S.dma_async("gpsimd", FBs[bk % 2][:, 4:8, :], src[:, 4:8, :])
            return f

        def win_conv(bk):
            def f():
                S.op("vector", lambda e: e.tensor_copy(out=WIN[:, :, bk * 512:(bk + 1) * 512], in_=FBs[bk % 2]))
            return f

        norm_mod(A1, 0, issue={0: win_issue(0), 1: win_issue(1), 2: win_issue(2)},
                 convert={1: win_conv(0), 2: win_conv(1), 3: win_conv(2)})

        ckp("norm1")
        groups = [(ti, g) for ti in range(len(TILES)) for g in range(3)]

        def win_mm(n):
            ti, g = groups[n]
            t0, w = TILES[ti]
            for bi in range(4):
                m = g * 4 + bi
                bk = ps[(n % 2) * 4 + bi]
                for k in range(8):
                    S.op("tensor", lambda e, bk=bk, m=m, k=k, w=w, t0=t0: e.matmul(
                        bk[:, 0:w], lhsT=WIN[:, k, m * 128:(m + 1) * 128], rhs=H[:, k, t0:t0 + w],
                        start=(k == 0), stop=(k == 7)))

        def win_ev(n):
            ti, g = groups[n]
            t0, w = TILES[ti]
            j0, nj = t0 // 8, w // 8
            for bi in range(4):
                m = g * 4 + bi
                bk = ps[(n % 2) * 4 + bi]
                if g == 0:
                    S.op("vector", lambda e, bk=bk, m=m, w=w, j0=j0, nj=nj: e.tensor_copy(
                        out=USP[:, m, :, j0:j0 + nj].rearrange("p s j -> p j s"),
                        in_=bk[:, 0:w].rearrange("p (j s) -> p j s", s=8)))
                elif g == 1:
                    S.op("scalar", lambda e, bk=bk, m=m, w=w: e.activation(
                        out=OB[:, m - 4, 0:w], in_=bk[:, 0:w], func=AF.Gelu_apprx_tanh))
                else:
                    S.op("scalar", lambda e, bk=bk, m=m, w=w: e.activation(
                        out=TMP[:, m - 8, 0:w], in_=bk[:, 0:w], func=AF.Gelu_apprx_tanh))

        for n in range(len(groups) + 1):
            if n >= 1 and groups[n - 1][1] == 1:
                S.join()
            if n < len(groups):
                win_mm(n)
            if n >= 1:
                win_ev(n - 1)
                ti, g = groups[n - 1]
                t0, w = TILES[ti]
            S.bar()
            if n >= 1 and groups[n - 1][1] == 1:
                S.dma_async("sync", UG[:, t0:t0 + w].rearrange("(k p) t -> p k t", p=128), OB[:, 0:4, 0:w])
            if n >= 1 and groups[n - 1][1] == 2:
                S.dma_async("gpsimd", VG[:, t0:t0 + w].rearrange("(k p) t -> p k t", p=128), TMP[:, 0:4, 0:w])

        S.join()
        ckp("win")
        S.dma("sync", USSMP.rearrange("(q p) s j -> p q s j", p=128), USP)
        S.dma("gpsimd", STG[0:32, 0:16], ssm_d[li].rearrange("(g c) -> g c", c=16))
        S.bar()
        for s_ in range(8):
            S.dma("sync" if s_ % 2 == 0 else "gpsimd", IM[s_ * 16:(s_ + 1) * 16, :, :],
                  USSMP[:, s_, :].rearrange("(g c) j -> c g j", c=16))
        S.bar()

        S.op("vector", lambda e: e.tensor_copy(out=STG[0:32, 128:256].rearrange("p (s c) -> p s c", c=16),
                                               in_=STG[0:32, 0:16].unsqueeze(1).to_broadcast([32, 8, 16])))
        S.bar()
        S.op("tensor", lambda e: e.transpose(ps[7][:, 0:32], STG[0:32, 128:256], IDENT[0:32, 0:32]))
        S.bar()
        S.op("vector", lambda e: e.tensor_copy(out=DV[:], in_=ps[7][:, 0:32]))
        ckp("im2col")
        ada_state = {"next": 0, "pending": None}
        for dr in range(2):
            CIN1 = STG[:, 0:512].rearrange("p (q n) -> p q n", n=128)
            CIN2 = STG[:, 512:1024].rearrange("p (q n) -> p q n", n=128)
            for hf in range(2):
                lo = slice(hf * 64, hf * 64 + 64)
                csrc = [c_re, c_im] if hf == 0 else [c_im, c_re]
                S.dma("sync", CIN1[:, :, lo], csrc[0][li, dr].rearrange("(q g) c p -> (g c) q p", q=4))
                S.dma("gpsimd", CIN2[:, :, lo], csrc[1][li, dr].rearrange("(q g) c p -> (g c) q p", q=4))
            S.bar()
            for q in range(4):
                S.op("tensor", lambda e, q=q: e.transpose(ps[0][:, q * 128:(q + 1) * 128], CIN1[:, q, :], IDENT[:]))
                S.op("tensor", lambda e, q=q: e.transpose(ps[1][:, q * 128:(q + 1) * 128], CIN2[:, q, :], IDENT[:]))
            S.bar()
            S.op("vector", lambda e: e.tensor_copy(out=CX1.rearrange("p g c -> p (g c)"), in_=ps[0][:]))
            S.op("scalar", lambda e: e.copy(out=CX2.rearrange("p g c -> p (g c)"), in_=ps[1][:]))
            S.bar()
            BIN1 = STG[0:32, 0:2048].rearrange("p (h q c) -> p h q c", h=2, c=16)
            BIN2 = STG[0:32, 2048:4096].rearrange("p (h q c) -> p h q c", h=2, c=16)
            S.dma("sync", BIN1[:, 0, :, :], b_re[li, dr])
            S.dma("gpsimd", BIN1[:, 1, :, :], b_im[li, dr])
            S.dma("sync", BIN2[:, 0, :, :], b_im[li, dr])
            S.dma("gpsimd", BIN2[:, 1, :, :], b_re[li, dr])
            AIN = RS[0:32, 0:256].rearrange("p (a q) -> p a q", q=64)
            S.dma("sync", AIN[:, 0, :], a_re[li, dr])
            S.dma("gpsimd", AIN[:, 1, :], a_re[li, dr])
            S.dma("sync", AIN[:, 2, :], a_im[li, dr])
            S.dma("gpsimd", AIN[:, 3, :], a_im[li, dr])
            S.bar()
            for c_ in range(16):
                S.op("tensor", lambda e, c_=c_: e.transpose(ps[2][:, c_ * 32:(c_ + 1) * 32], BIN1[:, :, :, c_], IDENT[0:32, 0:32]))
                S.op("tensor", lambda e, c_=c_: e.transpose(ps[3][:, c_ * 32:(c_ + 1) * 32], BIN2[:, :, :, c_], IDENT[0:32, 0:32]))
            S.op("tensor", lambda e: e.transpose(ps[4][:, 0:32], RS[0:32, 0:128], IDENT[0:32, 0:32]))
            S.op("tensor", lambda e: e.transpose(ps[4][:, 32:64], RS[0:32, 128:256], IDENT[0:32, 0:32]))
            S.bar()
            S.op("vector", lambda e: e.tensor_copy(out=BX1.rearrange("p g c -> p c g"), in_=ps[2][:].rearrange("p (c g) -> p c g", g=32)))
            S.op("scalar", lambda e: e.copy(out=BX2.rearrange("p g c -> p c g"), in_=ps[3][:].rearrange("p (c g) -> p c g", g=32)))
            S.op("vector", lambda e: e.tensor_copy(out=ARp[:], in_=ps[4][:, 0:32]))
            S.op("vector", lambda e: e.tensor_copy(out=AIp[:], in_=ps[4][:, 32:64]))
            S.dma("sync", LDT[:], log_dt[li, dr].partition_broadcast(128))
            S.bar()
            ckp("pl%d" % dr)
            S.op("scalar", lambda e: e.activation(out=LDT[:], in_=LDT[:], func=AF.Exp))
            S.bar()
            S.op("vector", lambda e: e.tensor_tensor(out=LR[:], in0=ARp[:], in1=LDT[:], op=ALU.mult))
            S.op("gpsimd", lambda e: e.tensor_tensor(out=LI[:], in0=AIp[:], in1=LDT[:], op=ALU.mult))
            S.bar()
            ckp("pb%d" % dr)
            ks = list(range(-8, 0)) + list(range(1, 9))
            for idx, kk in enumerate(ks):
                S.op("vector", lambda e, idx=idx, kk=kk: e.tensor_scalar(
                    out=ARG[:, :, idx], in0=LI[:], scalar1=float(kk), scalar2=None, op0=ALU.mult))
                S.op("vector", lambda e, idx=idx, kk=kk: e.tensor_scalar(
                    out=EARG[:, :, idx], in0=LR[:], scalar1=float(kk), scalar2=None, op0=ALU.mult))
            S.bar()
            ckp("pc%d" % dr)
            S.op("vector", lambda e: e.tensor_scalar(out=ARGC, in0=ARG, scalar1=TWO_PI / 4, scalar2=None, op0=ALU.add))
            S.op("scalar", lambda e: e.activation(out=MAG, in_=EARG, func=AF.Exp))
            S.bar()
            ckp("pd%d" % dr)
            S.op("vector", lambda e: e.tensor_scalar(out=NI, in0=ARG, scalar1=1.0 / TWO_PI, scalar2=None, op0=ALU.mult))
            S.op("vector", lambda e: e.tensor_scalar(out=NIC, in0=ARGC, scalar1=1.0 / TWO_PI, scalar2=None, op0=ALU.mult))
            S.bar()
            ckp("pe%d" % dr)
            S.op("vector", lambda e: e.tensor_copy(out=NF, in_=NI))
            S.op("vector", lambda e: e.tensor_copy(out=NFC, in_=NIC))
            S.bar()
            ckp("pf%d" % dr)
            S.op("vector", lambda e: e.scalar_tensor_tensor(out=ARG, in0=NF, scalar=-TWO_PI, in1=ARG, op0=ALU.mult, op1=ALU.add))
            S.op("vector", lambda e: e.scalar_tensor_tensor(out=ARGC, in0=NFC, scalar=-TWO_PI, in1=ARGC, op0=ALU.mult, op1=ALU.add))
            S.bar()
            S.op("vector", lambda e: e.tensor_scalar(out=ARG, in0=ARG, scalar1=3.1415925, scalar2=-3.1415925, op0=ALU.min, op1=ALU.max))
            S.op("vector", lambda e: e.tensor_scalar(out=ARGC, in0=ARGC, scalar1=3.1415925, scalar2=-3.1415925, op0=ALU.min, op1=ALU.max))
            S.bar()
            ckp("pg%d" % dr)
            S.op("scalar", lambda e: e.activation(out=PIM, in_=ARG, func=AF.Sin))
            S.op("scalar", lambda e: e.activation(out=PRE, in_=ARGC, func=AF.Sin))
            S.bar()
            S.op("vector", lambda e: e.tensor_tensor(out=PIM, in0=PIM, in1=MAG, op=ALU.mult))
            S.op("vector", lambda e: e.tensor_tensor(out=PRE, in0=PRE, in1=MAG, op=ALU.mult))
            S.bar()
            ckp("ph%d" % dr)
            S.op("vector", lambda e: e.tensor_scalar(out=NR[:], in0=PRE[:, :, 8], scalar1=-1.0, scalar2=None, op0=ALU.add))
            S.op("vector", lambda e: e.tensor_tensor(out=DEN[:], in0=ARp[:], in1=ARp[:], op=ALU.mult))
            ckp("c0")
            S.op("vector", lambda e: e.tensor_tensor(out=TA[:], in0=AIp[:], in1=AIp[:], op=ALU.mult))
            ckp("c1")
            S.op("vector", lambda e: e.tensor_tensor(out=DEN[:], in0=DEN[:], in1=TA[:], op=ALU.add))
            ckp("c2")
            S.op("vector", lambda e: e.reciprocal(out=DEN[:], in_=DEN[:]))
            ckp("c3")
            S.op("vector", lambda e: e.tensor_tensor(out=TA[:], in0=NR[:], in1=ARp[:], op=ALU.mult))
            S.op("vector", lambda e: e.tensor_tensor(out=TB[:], in0=PIM[:, :, 8], in1=AIp[:], op=ALU.mult))
            ckp("c4")
            S.op("vector", lambda e: e.tensor_tensor(out=CR[:], in0=TA[:], in1=TB[:], op=ALU.add))
            ckp("c5")
            S.op("vector", lambda e: e.tensor_tensor(out=TA[:], in0=PIM[:, :, 8], in1=ARp[:], op=ALU.mult))
            S.op("vector", lambda e: e.tensor_tensor(out=TB[:], in0=NR[:], in1=AIp[:], op=ALU.mult))
            ckp("c6")
            S.op("vector", lambda e: e.tensor_tensor(out=CI[:], in0=TA[:], in1=TB[:], op=ALU.subtract))
            ckp("c7")
            S.op("vector", lambda e: e.tensor_tensor(out=CR[:], in0=CR[:], in1=DEN[:], op=ALU.mult))
            S.op("vector", lambda e: e.tensor_tensor(out=CI[:], in0=CI[:], in1=DEN[:], op=ALU.mult))
            ckp("c8")
            ckp("pi%d" % dr)
            CRb = CR[:].unsqueeze(2).to_broadcast([128, 32, 8])
            CIb = CI[:].unsqueeze(2).to_broadcast([128, 32, 8])
            S.op("vector", lambda e: e.tensor_tensor(out=QR, in0=PRE[:, :, 0:8], in1=CRb, op=ALU.mult))
            S.op("vector", lambda e: e.tensor_tensor(out=QT, in0=PIM[:, :, 0:8], in1=CIb, op=ALU.mult))
            S.bar()
            S.op("vector", lambda e: e.tensor_tensor(out=QR, in0=QR, in1=QT, op=ALU.subtract))
            S.bar()
            S.op("vector", lambda e: e.tensor_tensor(out=QI, in0=PRE[:, :, 0:8], in1=CIb, op=ALU.mult))
            S.op("vector", lambda e: e.tensor_tensor(out=QT, in0=PIM[:, :, 0:8], in1=CRb, op=ALU.mult))
            S.bar()
            S.op("vector", lambda e: e.tensor_tensor(out=QI, in0=QI, in1=QT, op=ALU.add))
            S.bar()
            S.op("vector", lambda e: e.tensor_scalar(out=QI, in0=QI, scalar1=SIGN[:, 0:1], scalar2=None, op0=ALU.mult))
            S.op("vector", lambda e: e.tensor_scalar(out=PA, in0=PRE[:, :, 8:16], scalar1=NSIGN[:, 0:1], scalar2=None, op0=ALU.mult))
            S.op("vector", lambda e: e.tensor_scalar(out=PB, in0=PIM[:, :, 8:16], scalar1=-1.0, scalar2=None, op0=ALU.mult))
            S.bar()
            S.op("vector", lambda e: e.tensor_copy(out=ARC[:, 0, :], in_=PRE[:, :, 15]))
            S.op("vector", lambda e: e.tensor_copy(out=ARC[:, 1, :], in_=PRE[:, :, 15]))
            S.op("vector", lambda e: e.tensor_scalar(out=AIC[:, 0, :], in0=PIM[:, :, 15], scalar1=SIGN[:, 0:1], scalar2=None, op0=ALU.mult))
            S.op("vector", lambda e: e.tensor_scalar(out=AIC[:, 1, :], in0=PIM[:, :, 15], scalar1=NSIGN[:, 0:1], scalar2=None, op0=ALU.mult))
            S.bar()
            ckp("prep%d" % dr + "")
            for s_ in range(8):
                qi = (7 - s_) if dr == 0 else s_
                ri = s_ if dr == 0 else (7 - s_)
                S.op("vector", lambda e, s_=s_, qi=qi: e.tensor_tensor(
                    out=LT[:, :, s_, :], in0=BX1, in1=QR[:, :, qi:qi + 1].to_broadcast([128, 32, 16]), op=ALU.mult))
                S.op("vector", lambda e, s_=s_, ri=ri: e.tensor_tensor(
                    out=RT[:, :, s_, :], in0=CX1, in1=PA[:, :, ri:ri + 1].to_broadcast([128, 32, 16]), op=ALU.mult))
            S.bar()
            TL = STG[:].rearrange("p (g s c) -> p g s c", s=8, c=16)
            for s_ in range(8):
                qi = (7 - s_) if dr == 0 else s_
                S.op("vector", lambda e, s_=s_, qi=qi: e.tensor_tensor(
                    out=TL[:, :, s_, :], in0=BX2, in1=QI[:, :, qi:qi + 1].to_broadcast([128, 32, 16]), op=ALU.mult))
            S.bar()
            S.op("vector", lambda e: e.tensor_tensor(out=LT, in0=LT, in1=TL, op=ALU.add))
            S.bar()
            for s_ in range(8):
                ri = s_ if dr == 0 else (7 - s_)
                S.op("vector", lambda e, s_=s_, ri=ri: e.tensor_tensor(
                    out=TL[:, :, s_, :], in0=CX2, in1=PB[:, :, ri:ri + 1].to_broadcast([128, 32, 16]), op=ALU.mult))
            S.bar()
            S.op("vector", lambda e: e.tensor_tensor(out=RT, in0=RT, in1=TL, op=ALU.add))
            S.bar()
            ckp("lr%d" % dr + "")
            for rnd in range(4):
                for gi in range(8):
                    g = rnd * 8 + gi
                    Lg = LT[:, g, :, :].rearrange("p s c -> p (s c)")
                    Rg = RT[:, g, :, :].rearrange("p s c -> p (s c)")
                    S.op("tensor", lambda e, gi=gi, Lg=Lg, Rg=Rg: e.matmul(
                        ps[gi // 4][:, (gi % 4) * 128:(gi % 4 + 1) * 128], lhsT=Lg, rhs=Rg, start=True, stop=True))
                    S.op("tensor", lambda e, gi=gi, Lg=Lg: e.transpose(
                        ps[2 + gi // 4][:, (gi % 4) * 128:(gi % 4 + 1) * 128], Lg, IDENT[:]))
                S.bar()
                for bk in range(2):
                    g0 = rnd * 8 + bk * 4
                    S.op("vector", lambda e, bk=bk, g0=g0, dr=dr: e.tensor_tensor(
                        out=TOEP[:, g0:g0 + 4, :], in0=ps[bk][:].rearrange("p (g n) -> p g n", n=128),
                        in1=MASKS[:, dr:dr + 1, :].to_broadcast([128, 4, 128]), op=ALU.mult))
                    S.op("scalar", lambda e, bk=bk, g0=g0: e.copy(
                        out=ET[:, g0:g0 + 4, :], in_=ps[2 + bk][:].rearrange("p (g n) -> p g n", n=128)))
                S.bar()
                for bk in range(2):
                    g0 = rnd * 8 + bk * 4
                    S.op("vector", lambda e, bk=bk, g0=g0: e.tensor_copy(
                        out=ESW[:, g0:g0 + 4, 0:64], in_=ps[2 + bk][:].rearrange("p (g n) -> p g n", n=128)[:, :, 64:128]))
                    S.op("vector", lambda e, bk=bk, g0=g0: e.tensor_copy(
                        out=ESW[:, g0:g0 + 4, 64:128], in_=ps[2 + bk][:].rearrange("p (g n) -> p g n", n=128)[:, :, 0:64]))
                S.bar()
            S.op("scalar", lambda e: e.copy(out=RB.rearrange("p g n -> p (g n)"), in_=RT.rearrange("p g s c -> p (g s c)")))
            if dr == 0:
                for g in range(32):
                    S.op("vector", lambda e, g=g: e.scalar_tensor_tensor(
                        out=TOEP[:, g, :], in0=IDENT[:], scalar=DV[:, g:g + 1], in1=TOEP[:, g, :],
                        op0=ALU.mult, op1=ALU.add))
            S.op("gpsimd", lambda e: e.memset(W3[:], 0.0))
            S.bar()
            ckp("tiles%d" % dr + "")
            blocks = [(0, 32)] + [(32 + 64 * b, 64) for b in range(4)]
            order = blocks if dr == 0 else [blocks[0]] + blocks[:0:-1]
            for (jb, nb) in order:
                for half in range(2):
                    for gi in range(16):
                        g = half * 16 + gi
                        for arr in range(2):
                            ii = gi * 2 + arr
                            Em = ET if arr == 0 else ESW
                            S.op("tensor", lambda e, ii=ii, g=g, Em=Em, jb=jb, nb=nb: e.matmul(
                                ps[ii // 8][:, (ii % 8) * 64:(ii % 8) * 64 + nb], lhsT=Em[:, g, :],
                                rhs=IM[:, g, jb:jb + nb], start=True, stop=True))
                    S.bar()
                    for bk in range(4):
                        g0 = half * 16 + bk * 4
                        S.op("vector" if bk % 2 == 0 else "scalar", (lambda e, bk=bk, g0=g0, nb=nb: e.tensor_copy(
                            out=SS[:, 0:nb, :, g0:g0 + 4].rearrange("p j a g -> p g a j"),
                            in_=ps[bk][:].rearrange("p (g a j) -> p g a j", a=2, j=64)[:, :, :, 0:nb]))
                            if bk % 2 == 0 else (lambda e, bk=bk, g0=g0, nb=nb: e.copy(
                            out=SS[:, 0:nb, :, g0:g0 + 4].rearrange("p j a g -> p g a j"),
                            in_=ps[bk][:].rearrange("p (g a j) -> p g a j", a=2, j=64)[:, :, :, 0:nb])))
                    S.bar()
                js = list(range(jb, jb + nb)) if dr == 0 else list(range(jb + nb - 1, jb - 1, -1))
                pjl = None
                nsub = 3 if nb == 64 else 1
                cuts = [round(len(js) * x / nsub) for x in range(nsub + 1)]
                for j in js:
                    jl = j - jb
                    if (j - js[0]) * (1 if dr == 0 else -1) in cuts[:-1]:
                        last_sub = ((jb, nb) == order[-1]) and ((j - js[0]) * (1 if dr == 0 else -1) == cuts[-2])
                        S.join()
                        if li + 1 < n_layers:
                            if ada_state["pending"] is not None:
                                wada_mm(ada_state["pending"], ps[7], 400)
                                ada_state["pending"] = None
                            if (not last_sub) and ada_state["next"] < 24:
                                wada_dma(li + 1, ada_state["next"], asyn=True)
                                ada_state["pending"] = ada_state["next"]
                                ada_state["next"] += 1
                    Wc = W3[:] if pjl is None else SS[:, pjl, :, :]
                    S.op("vector", lambda e, jl=jl, Wc=Wc: e.tensor_tensor(out=G3[:, 0:2, :], in0=Wc, in1=SS[:, jl, :, :], op=ALU.add))
                    S.op("vector", lambda e: e.tensor_tensor(out=T1[:], in0=G3[:, 0:2, :], in1=ARC[:], op=ALU.mult))
                    S.op("vector", lambda e: e.tensor_tensor(out=T2[:, 0, :], in0=G3[:, 1, :], in1=AIC[:, 0, :], op=ALU.mult))
                    S.op("vector", lambda e: e.tensor_tensor(out=T2[:, 1, :], in0=G3[:, 0, :], in1=AIC[:, 1, :], op=ALU.mult))
                    S.op("vector", lambda e, jl=jl: e.tensor_tensor(out=SS[:, jl, :, :], in0=T1[:], in1=T2[:], op=ALU.add))
                    pjl = jl
                S.bar()
                if dr == 0:
                    S.op("scalar", lambda e, jb=jb: e.copy(out=HH[:, jb, :], in_=W3[:, 0, :]))
                    S.op("scalar", lambda e, jb=jb, nb=nb: e.copy(out=HH[:, jb + 1:jb + nb, :], in_=SS[:, 0:nb - 1, 0, :]))
                else:
                    S.op("scalar", lambda e, jb=jb, nb=nb: e.copy(out=HH[:, jb + nb - 1, :], in_=W3[:, 0, :]))
                    S.op("scalar", lambda e, jb=jb, nb=nb: e.copy(out=HH[:, jb:jb + nb - 1, :], in_=SS[:, 1:nb, 0, :]))
                S.bar()
                S.op("vector", lambda e, pjl=pjl: e.tensor_copy(out=W3[:], in_=SS[:, pjl, :, :]))
                S.bar()
            ckp("rec%d" % dr + "")
            for rnd in range(4):
                for gi in range(8):
                    g = rnd * 8 + gi
                    S.op("tensor", lambda e, gi=gi, g=g: e.matmul(
                        ps[gi][:, 0:NJ], lhsT=TOEP[:, g, :], rhs=IM[:, g, :], start=True, stop=False))
                    S.op("tensor", lambda e, gi=gi, g=g: e.matmul(
                        ps[gi][:, 0:NJ], lhsT=RB[:, g, :], rhs=HH[:, :, g], start=False, stop=True))
                S.bar()
                for gi in range(8):
                    g = rnd * 8 + gi
                    if dr == 0:
                        S.op("vector" if gi % 2 == 0 else "scalar", (lambda e, gi=gi, g=g: e.tensor_copy(out=YS[:, g, :], in_=ps[gi][:, 0:NJ]))
                             if gi % 2 == 0 else (lambda e, gi=gi, g=g: e.copy(out=YS[:, g, :], in_=ps[gi][:, 0:NJ])))
                    else:
                        S.op("vector", lambda e, gi=gi, g=g: e.tensor_tensor(out=YS[:, g, :], in0=YS[:, g, :], in1=ps[gi][:, 0:NJ], op=ALU.add))
                S.bar()
        if li + 1 < n_layers:
            assert ada_state["next"] == 24 and ada_state["pending"] is None, ada_state
            S.op("vector", lambda e: e.tensor_copy(out=MODN[:], in_=ps[7][:, 400:496]))
        ckp("read")
        S.dma("sync", YSP[:, :, :], YS)
        S.bar()
        YV = YS.rearrange("p (q s) j -> p q s j", s=8)
        YSPv = YSP.rearrange("(s c) (q g) j -> g c q s j", c=16, g=8)
        for g8 in range(8):
            for q in range(4):
                S.dma(["sync", "gpsimd"][q % 2], YV[g8 * 16:(g8 + 1) * 16, q, :, :], YSPv[g8, :, q, :, :])
            S.bar()
        S.bar()

        ckp("unim")
        WG = WB[:, 0:4 * 512].rearrange("p (k n) -> p k n", n=512)
        load_weight(WG, w_glu[li], 4, 512, 512)
        for (t0, w) in TILES:
            j0, nj = t0 // 8, w // 8
            for q in range(4):
                S.op("scalar", lambda e, q=q, w=w, j0=j0, nj=nj: e.activation(
                    out=TMP[:, q, 0:w].rearrange("p (j s) -> p j s", s=8),
                    in_=YV[:, q, :, j0:j0 + nj].rearrange("p s j -> p j s"), func=AF.Gelu_apprx_tanh))
            S.bar()
            S.op("vector", lambda e, w=w: e.tensor_copy(out=OB[:, 0:4, 0:w], in_=TMP[:, 0:4, 0:w]))
            S.bar()
            for m in range(4):
                for k in range(4):
                    S.op("tensor", lambda e, m=m, k=k, w=w: e.matmul(
                        ps[m][:, 0:w], lhsT=WG[:, k, m * 128:(m + 1) * 128], rhs=OB[:, k, 0:w],
                        start=(k == 0), stop=(k == 3)))
            S.bar()
            for m in range(4):
                S.op("scalar", lambda e, m=m, w=w: e.activation(
                    out=XT[:, m, 0:w], in_=ps[m][:, 0:w], func=AF.Sigmoid, bias=BGLU[:, m:m + 1], scale=1.0))
            S.bar()
            S.join()
            S.op("vector", lambda e, w=w: e.tensor_tensor(out=OB[:, 4:8, 0:w], in0=TMP[:, 0:4, 0:w], in1=XT[:, 0:4, 0:w], op=ALU.mult))
            S.bar()
            S.dma_async("sync", MIX[0:512, t0:t0 + w].rearrange("(k p) t -> p k t", p=128), OB[:, 4:8, 0:w])

        S.join()
        ckp("glu")
        S.dma("sync", TMP[:, 0:4, 0:128], w_sp[li].rearrange("h p q -> p h q"))
        S.dma("gpsimd", BS.rearrange("p h q -> p (h q)"), b_sp[li].rearrange("h q -> (h q)").partition_broadcast(128))
        S.bar()
        for h in range(4):
            S.op("tensor", lambda e, h=h: e.transpose(ps[0][:, h * 128:(h + 1) * 128], TMP[:, h, 0:128], IDENT[:]))
        S.bar()
        S.op("vector", lambda e: e.tensor_copy(out=WST[:], in_=ps[0][:].rearrange("p (h n) -> p h n", n=128)))
        S.bar()
        VGb = [XT[:, 0:4, :], XT[:, 4:8, :]]
        UGb = [OB[:, 0:4, :], ACTB[:, 0:4, :]]

        def sgu_loads(i):
            t0, w = TILES[i]
            S.dma_async("sync", VGb[i % 2][:, :, 0:w], VG[:, t0:t0 + w].rearrange("(k p) t -> p k t", p=128))
            S.dma_async("gpsimd", UGb[i % 2][:, :, 0:w], UG[:, t0:t0 + w].rearrange("(k p) t -> p k t", p=128))

        WOv = WB[:, 4096:4096 + 8 * D].rearrange("p (k n) -> p k n", n=D)
        WOs = [FB[:, 1024:5120].rearrange("p (k n) -> p k n", n=512), FB[:, 5120:9216].rearrange("p (k n) -> p k n", n=512)]
        sgu_loads(0)
        for i, (t0, w) in enumerate(TILES):
            nchk = w // 128
            VGi, UGi = VGb[i % 2], UGb[i % 2]
            S.join()
            if i + 1 < len(TILES):
                sgu_loads(i + 1)
            if i < 2:
                wsrc = w_out[li][:, i * 512:(i + 1) * 512].rearrange("(k p) n -> p k n", p=128)
                S.dma_async("sync", WOs[i][:, 0:4, :], wsrc[:, 0:4, :])
                S.dma_async("gpsimd", WOs[i][:, 4:8, :], wsrc[:, 4:8, :])
            if i in (2, 3):
                S.op("vector", lambda e, i=i: e.tensor_copy(out=WOv[:, :, (i - 2) * 512:(i - 1) * 512], in_=WOs[i - 2]))
            S.op("scalar", lambda e, w=w, VGi=VGi: e.activation(out=SQ[:, 0:4, 0:w], in_=VGi[:, 0:4, 0:w], func=AF.Square))
            S.bar()
            rstd_from(SQ, 4, w, 1.0 / 512)
            for k in range(4):
                S.op("vector", lambda e, k=k, w=w, VGi=VGi: e.scalar_tensor_tensor(
                    out=TMP[:, k, 0:w], in0=VGi[:, k, 0:w], scalar=GSGU[:, k:k + 1], in1=RS[:, 0:w],
                    op0=ALU.mult, op1=ALU.mult))
            S.bar()
            for ck in range(nchk):
                for h in range(4):
                    ii = ck * 4 + h
                    S.op("tensor", lambda e, ii=ii, ck=ck, h=h: e.transpose(
                        ps[ii // 4][:, (ii % 4) * 128:(ii % 4 + 1) * 128], TMP[:, h, ck * 128:(ck + 1) * 128], IDENT[:]))
            S.bar()
            for ck in range(nchk):
                S.op("vector" if ck % 2 == 0 else "scalar", (lambda e, ck=ck: e.tensor_copy(
                    out=VT[:, ck * 4:(ck + 1) * 4, :], in_=ps[ck][:].rearrange("p (h n) -> p h n", n=128)))
                    if ck % 2 == 0 else (lambda e, ck=ck: e.copy(
                    out=VT[:, ck * 4:(ck + 1) * 4, :], in_=ps[ck][:].rearrange("p (h n) -> p h n", n=128))))
            S.bar()
            for ck in range(nchk):
                for h in range(4):
                    ii = ck * 4 + h
                    S.op("tensor", lambda e, ii=ii, ck=ck, h=h: e.matmul(
                        ps[4 + ck][:, h * 128:(h + 1) * 128], lhsT=VT[:, ii, :], rhs=WST[:, h, :], start=True, stop=True))
            S.bar()
            for ck in range(nchk):
                S.op("vector", lambda e, ck=ck: e.tensor_tensor(
                    out=TMP[:, 4:8, ck * 128:(ck + 1) * 128], in0=ps[4 + ck][:].rearrange("p (h n) -> p h n", n=128),
                    in1=BS, op=ALU.add))
            S.bar()
            S.op("vector", lambda e, w=w, UGi=UGi: e.tensor_tensor(out=OB[:, 4:8, 0:w], in0=TMP[:, 4:8, 0:w], in1=UGi[:, 0:4, 0:w], op=ALU.mult))
            S.bar()
            S.dma_async("sync", MIX[512:1024, t0:t0 + w].rearrange("(k p) t -> p k t", p=128), OB[:, 4:8, 0:w])

        S.join()
        ckp("sgu")
        resid_linear(MIX, 8, 2, [ACTB[:, 0:8, :], ACTB[:, 8:16, :]], WOv)

        ckp("wout")
        norm_mod(A2, 3)

        ckp("norm2")
        S.dma("sync", FB[0:9, 0:2 * DFF], w_conv[li].rearrange("a b n -> (a b) n"))
        S.bar()
        for ch in range(44):
            S.op("tensor", lambda e, ch=ch: e.transpose(ps[7][:, ch * 9:(ch + 1) * 9], FB[0:9, ch * 128:(ch + 1) * 128], IDENT[0:9, 0:9]))
        S.bar()
        S.op("vector", lambda e: e.tensor_copy(out=WC[:].rearrange("p t c -> p c t"), in_=ps[7][:, 0:396].rearrange("p (c t) -> p c t", t=9)))
        S.bar()
        WUb = [OBt[:, 0:2048].rearrange("p (k n) -> p k n", n=256), OBt[:, 2048:4096].rearrange("p (k n) -> p k n", n=256)]
        WDv = WB[:, 0:22 * D].rearrange("p (k n) -> p k n", n=D)
        wd_stage = [XTt[:, 0:2816].rearrange("p (k n) -> p k n", n=128), TMPt[:, 0:2816].rearrange("p (k n) -> p k n", n=128)]
        FBb = FB[:].bitcast(BF16)
        UPb = [FBb[:, 0:2304], FBb[:, 2304:4608]]
        DGb = [FBb[:, 4608:6912].rearrange("p (t n) -> p t n", n=128), FBb[:, 6912:9216].rearrange("p (t n) -> p t n", n=128)]
        SG = FB[:, 4608:6912]
        GB = ACTB[:, 0:5, :].rearrange("p a t -> p (a t)")[:, 0:NT]
        stgs = [STG[:, 0:2048].rearrange("p (k n) -> p k n", n=256), STG[:, 2048:4096].rearrange("p (k n) -> p k n", n=256)]

        def bank(m, sl):
            return ps[(sl + 4 * m) % 8]

        def wup_dma(m):
            st = stgs[m % 2]
            S.dma_async("sync", st[:, :, 0:128], w_up[li, :, m * 128:(m + 1) * 128].rearrange("(k p) n -> p k n", p=128))
            S.dma_async("gpsimd", st[:, :, 128:256], w_up[li, :, DFF + m * 128:DFF + (m + 1) * 128].rearrange("(k p) n -> p k n", p=128))

        def prep_w(m, which):
            if which == 0:
                S.op("vector", lambda e, m=m: e.tensor_copy(out=WUb[m % 2], in_=stgs[m % 2]))
            for idx in range(18):
                part, tap = divmod(idx, 9)
                ch = part * 22 + m
                if idx % 2 == 0 and which == 0:
                    S.op("vector", lambda e, idx=idx, tap=tap, ch=ch, m=m: e.tensor_scalar(
                        out=DGb[m % 2][:, idx, :], in0=IDENT[:], scalar1=WC[:, tap, ch:ch + 1], scalar2=None, op0=ALU.mult))
                if idx % 2 == 1 and which == 1:
                    S.op("scalar", lambda e, idx=idx, tap=tap, ch=ch, m=m: e.activation(
                        out=DGb[m % 2][:, idx, :], in_=IDENT[:], func=AF.Copy, scale=WC[:, tap, ch:ch + 1]))

        def up_mm(m, part, tiles, slots):
            for t, sl in zip(tiles, slots):
                t0, w = TILES[t]
                bk = bank(m, sl)
                for k in range(8):
                    S.op("tensor", lambda e, bk=bk, k=k, t0=t0, w=w, part=part, m=m: e.matmul(
                        bk[:, 0:w], lhsT=WUb[m % 2][:, k, part * 128:(part + 1) * 128], rhs=H[:, k, t0:t0 + w],
                        start=(k == 0), stop=(k == 7)))

        def evac_up(m, part, tiles, slots, engs):
            dst = UPb[part]
            for n_, (t, sl) in enumerate(zip(tiles, slots)):
                t0, w = TILES[t]
                bk = bank(m, sl)
                if engs[n_ % len(engs)] == "vector":
                    S.op("vector", lambda e, bk=bk, t0=t0, w=w, dst=dst: e.tensor_copy(out=dst[:, t0:t0 + w], in_=bk[:, 0:w]))
                else:
                    S.op("scalar", lambda e, bk=bk, t0=t0, w=w, dst=dst: e.copy(out=dst[:, t0:t0 + w], in_=bk[:, 0:w]))

        taps9 = [(1, 1)] + [(ky, kx) for ky in range(3) for kx in range(3) if not (ky == 1 and kx == 1)]

        def conv_mm(m, part, tiles, slots):
            src = UPb[part]
            d0 = part * 9
            DGm = DGb[m % 2]
            sv = src[:, LC:NT].rearrange("p (r c) -> p r c", c=64)
            for t, sl in zip(tiles, slots):
                bk = bank(m, sl)
                if t == 0:
                    S.op("tensor", lambda e, bk=bk, DGm=DGm: e.matmul(bk[:, 0:256], lhsT=DGm[:, d0 + 4, :], rhs=src[:, 0:256], start=True, stop=False))
                    S.op("tensor", lambda e, bk=bk, DGm=DGm: e.matmul(bk[:, 1:256], lhsT=DGm[:, d0 + 3, :], rhs=src[:, 0:255], start=False, stop=False))
                    S.op("tensor", lambda e, bk=bk, DGm=DGm: e.matmul(bk[:, 0:255], lhsT=DGm[:, d0 + 5, :], rhs=src[:, 1:256], start=False, stop=True))
                    continue
                R0 = 8 * (t - 1)
                pv = bk[:, 0:512].rearrange("p (r c) -> p r c", c=64)
                for n_, (ky, kx) in enumerate(taps9):
                    dy, dx = ky - 1, kx - 1
                    ra, rb = max(R0, -dy, 0), min(R0 + 8, 32 - max(0, dy))
                    c0, c1 = max(0, -dx), 64 - max(0, dx)
                    S.op("tensor", lambda e, pv=pv, ra=ra, rb=rb, c0=c0, c1=c1, dy=dy, dx=dx, R0=R0, ky=ky, kx=kx, n_=n_, DGm=DGm:
                         e.matmul(pv[:, ra - R0:rb - R0, c0:c1], lhsT=DGm[:, d0 + ky * 3 + kx, :],
                                  rhs=sv[:, ra + dy:rb + dy, c0 + dx:c1 + dx], start=(n_ == 0), stop=(n_ == 8)))

        def silu_ev(m, tiles, slots):
            for t, sl in zip(tiles, slots):
                t0, w = TILES[t]
                bk = bank(m, sl)
                S.op("scalar", lambda e, bk=bk, t0=t0, w=w: e.activation(out=SG[:, t0:t0 + w], in_=bk[:, 0:w], func=AF.Silu))

        def mult_ev(m, tiles, slots):
            for t, sl in zip(tiles, slots):
                t0, w = TILES[t]
                bk = bank(m, sl)
                S.op("vector", lambda e, bk=bk, t0=t0, w=w: e.tensor_tensor(out=GB[:, t0:t0 + w], in0=bk[:, 0:w], in1=SG[:, t0:t0 + w], op=ALU.mult))

        wup_dma(0)
        S.join()
        prep_w(0, 0)
        prep_w(0, 1)
        S.bar()
        wup_dma(1)
        for m in range(22):
            up_mm(m, 0, [0, 1, 2, 3, 4], [0, 1, 2, 3, 4])
            if m > 0:
                mult_ev(m - 1, [3, 4], [2, 3])
            if m % 2 == 0 and m // 2 < 8:
                S.dma_async("sync", wd_stage[(m // 2) % 2], w_down[li][:, (m // 2) * 128:(m // 2 + 1) * 128].rearrange("(k p) n -> p k n", p=128))
            S.bar()
            up_mm(m, 1, [0, 1, 2], [5, 6, 7])
            evac_up(m, 0, [0, 1, 2, 3, 4], [0, 1, 2, 3, 4], ["vector", "scalar"])
            if m > 0:
                S.dma_async("gpsimd", GD[(m - 1) * 128:m * 128, :], GB)
            S.bar()
            up_mm(m, 1, [3, 4], [0, 1])
            conv_mm(m, 0, [0, 1, 2], [2, 3, 4])
            evac_up(m, 1, [0, 1, 2], [5, 6, 7], ["vector", "scalar"])
            S.bar()
            conv_mm(m, 0, [3, 4], [5, 6])
            evac_up(m, 1, [3, 4], [0, 1], ["vector"])
            silu_ev(m, [0, 1, 2], [2, 3, 4])
            if m % 2 == 1 and m // 2 < 8:
                S.op("vector", lambda e, m=m: e.tensor_copy(out=WDv[:, :, (m // 2) * 128:(m // 2 + 1) * 128], in_=wd_stage[(m // 2) % 2]))
            S.join()
            conv_mm(m, 1, [0, 1, 2], [0, 1, 7])
            silu_ev(m, [3, 4], [5, 6])
            if m + 1 < 22:
                prep_w(m + 1, 0)
            S.bar()
            conv_mm(m, 1, [3, 4], [2, 3])
            mult_ev(m, [0, 1, 2], [0, 1, 7])
            if m + 1 < 22:
                prep_w(m + 1, 1)
            if m + 2 < 22:
                wup_dma(m + 2)
            S.bar()
        mult_ev(21, [3, 4], [2, 3])
        S.bar()
        S.dma_async("gpsimd", GD[21 * 128:22 * 128, :], GB)
        S.join()

        ckp("ffnup")
        resid_linear(GD, 22, 5, [ACTB, FB[:].bitcast(BF16)[:, 0:11264].rearrange("p (k t) -> p k t", t=512)], WB[:, 0:22 * D].rearrange("p (k n) -> p k n", n=D))

    except _Stop:
        pass
    S.bar()
    if dbg:
        DF = nc.dram_tensor("DBGF", [128, 32768], F32, kind="ExternalOutput").ap()
        DB = nc.dram_tensor("DBGB", [128, 40960], BF16, kind="ExternalOutput").ap()
        off = 0
        for t_, n_ in [(HRAW[:], 9216), (XTt[:], 4096), (TMPt[:], 4096), (FB[:], 9216), (STG[:], 4096), (RS[:], 512),
                       (MOD[:].rearrange("p q k n -> p (q k n)"), 96), (A1[:].rearrange("p k n -> p (k n)"), 16),
                       (A2[:].rearrange("p k n -> p (k n)"), 16), (W3[:].rearrange("p a g -> p (a g)"), 64),
                       (ARC[:].rearrange("p a g -> p (a g)"), 64), (AIC[:].rearrange("p a g -> p (a g)"), 64),
                       (CR[:], 32), (CI[:], 32), (DV[:], 32), (LR[:], 32), (LI[:], 32)]:
            S.dma("sync", DF[:, off:off + n_], t_)
            off += n_
        offb = 0
        for t_, n_ in [(WB, 22528), (OBt[:], 4096), (ACTBt[:], 11264)]:
            S.dma("gpsimd", DB[:, offb:offb + n_], t_)
            offb += n_
        S.bar()
    Xl = [XT[:, :, 0:128], XT[:, :, 128:256]]
    Ofin = [XT[:, :, 256:384], XT[:, :, 384:512]]

    def fin_load(b):
        t0 = LC + b * 128
        S.dma_async("sync", Xl[b % 2], XRES[:, t0:t0 + 128].rearrange("(k p) t -> p k t", p=128))

    fin_load(0)
    for b in range(16):
        Xi = Xl[b % 2]
        S.join()
        if b + 1 < 16:
            fin_load(b + 1)
        S.op("scalar", lambda e, Xi=Xi: e.activation(out=SQ[:, :, 0:128], in_=Xi, func=AF.Square))
        S.bar()
        rstd_from(SQ, 8, 128, 1.0 / D)
        for k in range(8):
            S.op("vector", lambda e, k=k, Xi=Xi: e.scalar_tensor_tensor(
                out=TMP[:, k, 0:128], in0=Xi[:, k, :], scalar=GFIN[:, k:k + 1], in1=RS[:, 0:128],
                op0=ALU.mult, op1=ALU.mult))
        S.bar()
        for k in range(8):
            S.op("tensor", lambda e, k=k: e.transpose(ps[2 + k // 4][:, (k % 4) * 128:(k % 4 + 1) * 128],
                                                       TMP[:, k, 0:128], IDENT[:]))
        S.bar()
        Oi = Ofin[b % 2]
        S.op("vector", lambda e, Oi=Oi: e.tensor_copy(out=Oi[:, 0:4, :], in_=ps[2][:].rearrange("p (k t) -> p k t", t=128)))
        S.op("scalar", lambda e, Oi=Oi: e.copy(out=Oi[:, 4:8, :], in_=ps[3][:].rearrange("p (k t) -> p k t", t=128)))
        S.bar()
        S.dma_async("gpsimd", out[b * 128:(b + 1) * 128, :].rearrange("t (k d) -> t k d", d=128), Oi)
    S.join()

    S.emit()
    es.close()
    return nc


_CONST = None


def _consts():
    ident = np.eye(128, dtype=np.float32)
    sp = np.arange(128) // 16
    m0 = (sp[None, :] >= sp[:, None]).astype(np.float32)
    m1 = (sp[None, :] <= sp[:, None]).astype(np.float32)
    return ident, np.stack([m0, m1])


def kernel(n_layers=4, **inputs):
    nc = build_nc(n_layers)
    ident, mask = _consts()
    in_maps = []
    for b in range(8):
        m = {}
        for k, v in inputs.items():
            v = np.asarray(v)
            if k in ("x", "c", "ctx"):
                m[k] = np.ascontiguousarray(v[b], dtype=np.float32)
            else:
                m[k] = np.ascontiguousarray(v, dtype=np.float32)
        m["ident"] = ident
        m["mask"] = mask
        in_maps.append(m)
    res = run_bass_kernel_spmd(nc, in_maps, core_ids=list(range(8)))
    return np.stack([np.asarray(r["out"], dtype=np.float32) for r in res.results], axis=0)
```

```python
import numpy as np
from contextlib import ExitStack
import concourse.bass as bass
import concourse.mybir as mybir
from concourse.bass_utils import run_bass_kernel_spmd

F32, BF16, I32 = mybir.dt.float32, mybir.dt.bfloat16, mybir.dt.int32
AF = mybir.ActivationFunctionType
ALU = mybir.AluOpType

D = 1024
NT = 2304
LC = 256
LL = 2048
DFF = 2816
EPS = 1e-6
NJ = 288
TILES = [(0, 256)] + [(256 + 512 * i, 512) for i in range(4)]
TWO_PI = 6.283185307179586


class Sched:
    def __init__(self, nc):
        self.nc = nc
        self.stages = [[]]
        self.join_at = set()

    def op(self, eng, fn, dma=False):
        self.stages[-1].append((eng, dma, fn))

    def dma(self, eng, out, in_, slow=False):
        self.op(eng, lambda e, o=out, i=in_: e.dma_start(out=o, in_=i), dma=True)

    def dma_async(self, eng, out, in_):
        self.op(eng, lambda e, o=out, i=in_: e.dma_start(out=o, in_=i), dma="async")

    def bar(self):
        if self.stages[-1]:
            self.stages.append([])

    def join(self):
        self.bar()
        self.join_at.add(len(self.stages) - 1)

    def emit(self):
        nc = self.nc
        self.bar()
        merged, joins = [], set()
        for k, st in enumerate(self.stages):
            only_vec = len(st) > 0 and all((o[0] == "vector" and not o[1]) for o in st)
            if (merged and only_vec and k not in self.join_at and merged[-1][1]):
                merged[-1][0].extend(st)
            else:
                if k in self.join_at:
                    joins.add(len(merged))
                merged.append([list(st), only_vec])
        self.stages = [m[0] for m in merged]
        self.join_at = joins
        names = ["c_scalar", "c_vector", "c_gpsimd", "c_tensor", "d_sync", "d_scalar", "d_gpsimd", "a_sync", "a_gpsimd"]

        def semname(eng, dma):
            if dma == "async":
                return "a_" + eng
            return ("d_" if dma else "c_") + eng

        cum = []
        cur = {n: 0 for n in names}
        for st in self.stages:
            cum.append(dict(cur))
            for (eng, dma, _) in st:
                cur[semname(eng, dma)] += 16 if dma else 1
        final = dict(cur)
        with ExitStack() as es:
            sems = {n: es.enter_context(nc.semaphore(n)) for n in names}
            block = es.enter_context(nc.Block())

            def make(engname):
                def body(eng):
                    waited = {n: 0 for n in names}
                    joined = {n: 0 for n in names}
                    for k, st in enumerate(self.stages):
                        if k in self.join_at:
                            for n in names:
                                if n.startswith("a_"):
                                    joined[n] = cum[k][n]
                        mine = [o for o in st if o[0] == engname]
                        if not mine:
                            continue
                        for n in names:
                            tgt = joined[n] if n.startswith("a_") else cum[k][n]
                            if tgt > waited[n]:
                                eng.wait_ge(sems[n], tgt)
                                waited[n] = tgt
                        for (_, dma, fn) in mine:
                            ins = fn(eng)
                            ins.then_inc(sems[semname(engname, dma)], 16 if dma else 1)
                    if engname == "sync":
                        for n in names:
                            if final[n] > waited[n]:
                                eng.wait_ge(sems[n], final[n])
                return body

            block.sync(make("sync"))
            block.scalar(make("scalar"))
            block.vector(make("vector"))
            block.gpsimd(make("gpsimd"))
            block.tensor(make("tensor"))


class _Stop(Exception):
    pass


def build_nc(n_layers, n_wl=4, stop=None, dbg=False):
    nc = bass.Bass("TRN2", target_bir_lowering=False)
    S = Sched(nc)
    W = n_wl

    def ckp(name):
        S.bar()
        if stop == name:
            raise _Stop()

    def din(name, shape):
        return nc.dram_tensor(name, list(shape), F32, kind="ExternalInput").ap()

    x_in = din("x", [LL, D])
    c_in = din("c", [D])
    ctx_in = din("ctx", [LC, D])
    cctx_in = din("c_ctx", [D])
    w_ada = din("w_ada", [W, D, 6 * D])
    b_ada = din("b_ada", [W, 6 * D])
    g_mix = din("g_mix", [W, D])
    w_in = din("w_in", [W, D, 1536])
    a_re = din("ssm_a_re", [W, 2, 32, 64])
    a_im = din("ssm_a_im", [W, 2, 32, 64])
    b_re = din("ssm_b_re", [W, 2, 32, 64, 16])
    b_im = din("ssm_b_im", [W, 2, 32, 64, 16])
    c_re = din("ssm_c_re", [W, 2, 32, 16, 64])
    c_im = din("ssm_c_im", [W, 2, 32, 16, 64])
    log_dt = din("ssm_log_dt", [W, 2, 32])
    ssm_d = din("ssm_d", [W, 512])
    w_glu = din("w_glu", [W, 512, 512])
    b_glu = din("b_glu", [W, 512])
    g_sgu = din("g_sgu", [W, 512])
    w_sp = din("w_spatial", [W, 4, 128, 128])
    b_sp = din("b_spatial", [W, 4, 128])
    w_out = din("w_out", [W, D, D])
    g_ffn = din("g_ffn", [W, D])
    w_up = din("w_up", [W, D, 2 * DFF])
    w_conv = din("w_conv", [W, 3, 3, 2 * DFF])
    w_down = din("w_down", [W, DFF, D])
    g_final = din("g_final", [D])
    ident_in = din("ident", [128, 128])
    mask_in = din("mask", [2, 128, 128])
    out = nc.dram_tensor("out", [LL, D], F32, kind="ExternalOutput").ap()

    SK = dict(kind="ExternalOutput") if dbg else {}
    XRES = nc.dram_tensor("XRES", [D, NT], F32, **SK).ap()
    USSMP = nc.dram_tensor("USSMP", [512, 8, NJ], BF16, **SK).ap()
    YSP = nc.dram_tensor("YSP", [128, 32, NJ], F32, **SK).ap()
    UG = nc.dram_tensor("UG", [512, NT], BF16, **SK).ap()
    VG = nc.dram_tensor("VG", [512, NT], F32, **SK).ap()
    MIX = nc.dram_tensor("MIX", [D, NT], BF16, **SK).ap()
    GD = nc.dram_tensor("GD", [DFF, NT], BF16, **SK).ap()

    es = ExitStack()

    def sb(name, shape, dt=F32):
        return es.enter_context(nc.sbuf_tensor(name, list(shape), dt))

    ps = [es.enter_context(nc.psum_tensor("ps%d" % i, [128, 512], F32)) for i in range(8)]

    IDENT = sb("IDENT", [128, 128])
    ONESB = sb("ONESB", [128, 128], BF16)
    MASKS = sb("MASKS", [128, 2, 128])
    SIGN = sb("SIGN", [128, 1])
    NSIGN = sb("NSIGN", [128, 1])
    SC = sb("SC", [128, 8, 2])
    MOD = sb("MOD", [128, 6, 8, 2])
    BADA = sb("BADA", [128, 6, 8])
    MODN = sb("MODN", [128, 96])
    GM = sb("GM", [128, 8])
    GF = sb("GF", [128, 8])
    GFIN = sb("GFIN", [128, 8])
    A1 = sb("A1", [128, 8, 2])
    A2 = sb("A2", [128, 8, 2])
    HRAW = sb("HRAW", [128, 9216])
    H = HRAW[:].bitcast(BF16).rearrange("p (k t) -> p k t", t=NT)
    YS = HRAW[:].rearrange("p (g j) -> p g j", j=NJ)
    STG = sb("STG", [128, 4096])
    SQ = STG[:, 0:2048].bitcast(BF16).rearrange("p (k t) -> p k t", t=512)
    WBt = sb("WB", [128, 22528], BF16)
    WB = WBt[:]
    TOEP = WB[:, 0:4096].rearrange("p (g n) -> p g n", n=128)
    ET = WB[:, 4096:8192].rearrange("p (g n) -> p g n", n=128)
    ESW = WB[:, 8192:12288].rearrange("p (g n) -> p g n", n=128)
    IM = WB[:, 12288:21504].rearrange("p (g j) -> p g j", j=NJ)
    XTt = sb("XT", [128, 4096])
    XT = XTt[:].rearrange("p (k t) -> p k t", t=512)
    LT = XTt[:].rearrange("p (g s c) -> p g s c", s=8, c=16)
    SS = XTt[:].rearrange("p (j a g) -> p j a g", a=2, g=32)
    TMPt = sb("TMP", [128, 4096])
    TMP = TMPt[:].rearrange("p (k t) -> p k t", t=512)
    RT = TMPt[:].rearrange("p (g s c) -> p g s c", s=8, c=16)
    RS = sb("RS", [128, 512])
    OBt = sb("OB", [128, 4096], BF16)
    OB = OBt[:].rearrange("p (k t) -> p k t", t=512)
    RB = OBt[:].rearrange("p (g n) -> p g n", n=128)
    ACTBt = sb("ACTB", [128, 11264], BF16)
    ACTB = ACTBt[:].rearrange("p (k t) -> p k t", t=512)
    USP = ACTBt[:, 0:9216].rearrange("p (q s j) -> p q s j", s=8, j=NJ)
    HH = ACTBt[:, 0:9216].rearrange("p (j g) -> p j g", g=32)
    FB = sb("FB", [128, 9216])
    UPG = FB[:, 0:2304]; UPV = FB[:, 2304:4608]; CG = FB[:, 4608:6912]; CV = FB[:, 6912:9216]
    def ftab(i):
        return FB[:, i * 512:(i + 1) * 512].rearrange("p (g k) -> p g k", k=16)
    ARG, ARGC, EARG, NF, NFC, PRE, PIM, MAG, BX1, BX2, CX1, CX2 = [ftab(i) for i in range(12)]
    def qtab(i):
        return FB[:, 6144 + i * 256:6144 + (i + 1) * 256].rearrange("p (g k) -> p g k", k=8)
    QR, QI, QT, PA, PB = [qtab(i) for i in range(5)]
    NI = FB[:, 7424:7936].bitcast(I32).rearrange("p (g k) -> p g k", k=16)
    NIC = FB[:, 7936:8448].bitcast(I32).rearrange("p (g k) -> p g k", k=16)
    BS = FB[:, 0:512].rearrange("p (h q) -> p h q", q=128)
    VT = STG[:, 2048:4096].bitcast(BF16).rearrange("p (a n) -> p a n", n=128)
    W3 = sb("W3", [128, 2, 32])
    G3 = sb("G3", [128, 3, 32])
    T1 = sb("T1", [128, 2, 32])
    T2 = sb("T2", [128, 2, 32])
    ARp = sb("ARp", [128, 32]); AIp = sb("AIp", [128, 32]); LDT = sb("LDT", [128, 32])
    LR = sb("LR", [128, 32]); LI = sb("LI", [128, 32])
    CR = sb("CR", [128, 32]); CI = sb("CI", [128, 32]); NR = sb("NR", [128, 32]); DEN = sb("DEN", [128, 32])
    TA = sb("TA", [128, 32]); TB = sb("TB", [128, 32])
    ARC = sb("ARC", [128, 2, 32]); AIC = sb("AIC", [128, 2, 32])
    DV = sb("DV", [128, 32])
    BGLU = sb("BGLU", [128, 4]); GSGU = sb("GSGU", [128, 4])
    WST = sb("WST", [128, 4, 128], BF16)
    WC = sb("WC", [128, 9, 44])

    S.dma("sync", IDENT[:], ident_in[:, :])
    S.dma("sync", MASKS[:], mask_in.rearrange("m p q -> p m q"))
    S.op("vector", lambda e: e.memset(ONESB[:], 1.0))
    S.op("vector", lambda e: e.memset(SIGN[0:64, :], -1.0))
    S.op("vector", lambda e: e.memset(SIGN[64:128, :], 1.0))
    S.op("vector", lambda e: e.memset(NSIGN[0:64, :], 1.0))
    S.op("vector", lambda e: e.memset(NSIGN[64:128, :], -1.0))
    S.dma("sync", STG[0:8, 0:128], c_in.rearrange("(k p) -> k p", p=128))
    S.dma("sync", STG[8:16, 0:128], cctx_in.rearrange("(k p) -> k p", p=128))
    S.dma("sync", STG[16:24, 0:128], g_final.rearrange("(k p) -> k p", p=128))
    S.bar()
    S.op("tensor", lambda e: e.transpose(ps[7][:, 0:24], STG[0:24, 0:128], IDENT[0:24, 0:24]))
    S.bar()
    S.op("vector", lambda e: e.tensor_copy(out=SC[:, :, 0], in_=ps[7][:, 0:8]))
    S.op("vector", lambda e: e.tensor_copy(out=SC[:, :, 1], in_=ps[7][:, 8:16]))
    S.op("vector", lambda e: e.tensor_copy(out=GFIN[:], in_=ps[7][:, 16:24]))
    S.bar()
    S.op("scalar", lambda e: e.activation(out=SC[:], in_=SC[:], func=AF.Silu))
    S.bar()

    def in_transpose_all(hook=None):
        blocks = [(ctx_in, i, i * 128) for i in range(2)] + [(x_in, i, 256 + i * 128) for i in range(16)]
        Lb = [TMP[:, :, 0:128], TMP[:, :, 128:256]]
        Ob = [XT[:, :, 0:128], XT[:, :, 128:256]]

        def load(b):
            src, i, _ = blocks[b]
            S.dma_async("sync", Lb[b % 2], src[i * 128:(i + 1) * 128, :].rearrange("t (k d) -> t k d", d=128))

        def store(b):
            t0 = blocks[b][2]
            S.dma_async("gpsimd", XRES[:, t0:t0 + 128].rearrange("(k p) t -> p k t", p=128), Ob[b % 2])

        nb_ = len(blocks)
        load(0)
        for n in range(nb_ + 2):
            S.join()
            if hook is not None:
                hook(n, nb_ + 2)
            if n < nb_:
                for k in range(8):
                    S.op("tensor", lambda e, k=k, n=n: e.transpose(ps[(n % 2) * 2 + k // 4][:, (k % 4) * 128:(k % 4 + 1) * 128],
                                                                   Lb[n % 2][:, k, :], IDENT[:]))
            if 1 <= n <= nb_:
                m_ = n - 1
                S.op("vector", lambda e, m_=m_: e.tensor_copy(out=Ob[m_ % 2][:, 0:4, :], in_=ps[(m_ % 2) * 2][:].rearrange("p (k t) -> p k t", t=128)))
                S.op("scalar", lambda e, m_=m_: e.copy(out=Ob[m_ % 2][:, 4:8, :], in_=ps[(m_ % 2) * 2 + 1][:].rearrange("p (k t) -> p k t", t=128)))
            if n + 1 < nb_:
                load(n + 1)
            if n >= 2:
                store(n - 2)
            S.bar()
        S.join()

    WAs = [STG[:, 0:2048].rearrange("p (k n) -> p k n", n=256), STG[:, 2048:4096].rearrange("p (k n) -> p k n", n=256)]

    def wada_dma(lyr, blk, asyn=False):
        WA = WAs[blk % 2]
        src = w_ada[lyr, :, blk * 256:(blk + 1) * 256].rearrange("(k p) n -> p k n", p=128)
        f = S.dma_async if asyn else S.dma
        f("sync", WA[:, 0:4, :], src[:, 0:4, :])
        f("gpsimd", WA[:, 4:8, :], src[:, 4:8, :])

    def wada_mm(blk, bank_ap, col0):
        q, mq = blk // 4, blk % 4
        WA = WAs[blk % 2]
        for mm in range(2):
            m = mq * 2 + mm
            for k in range(8):
                S.op("tensor", lambda e, m=m, mm=mm, k=k, q=q, WA=WA: e.matmul(
                    bank_ap[:, col0 + (q * 8 + m) * 2:col0 + (q * 8 + m) * 2 + 2], lhsT=WA[:, k, mm * 128:(mm + 1) * 128],
                    rhs=SC[:, k, :], start=(k == 0), stop=(k == 7)))

    ada0 = {"next": 0, "pending": None}

    def ada0_hook(n, nstages):
        if ada0["pending"] is not None:
            wada_mm(ada0["pending"], ps[7], 400)
            ada0["pending"] = None
        if ada0["next"] < 24 and n < nstages - 1:
            wada_dma(0, ada0["next"], asyn=True)
            ada0["pending"] = ada0["next"]
            ada0["next"] += 1

    in_transpose_all(hook=ada0_hook)

    def load_weight(dst, wap, kch, ncols, cb):
        for c0 in range(0, ncols, cb):
            stg = STG[:, 0:kch * cb].rearrange("p (k n) -> p k n", n=cb)
            S.dma("sync", stg, wap[:, c0:c0 + cb].rearrange("(k p) n -> p k n", p=128))
            S.bar()
            S.op("vector", lambda e, stg=stg, c0=c0: e.tensor_copy(out=dst[:, :, c0:c0 + cb], in_=stg))
            S.bar()

    def rstd_from(src_sq, nk, w, inv_n):
        for k in range(nk):
            S.op("tensor", lambda e, k=k: e.matmul(ps[0][:, 0:w], lhsT=ONESB[:], rhs=src_sq[:, k, 0:w],
                                                    start=(k == 0), stop=(k == nk - 1)))
        S.bar()
        S.op("scalar", lambda e: e.activation(out=RS[:, 0:w], in_=ps[0][:, 0:w], func=AF.Sqrt, bias=EPS, scale=inv_n))
        S.bar()
        S.op("vector", lambda e: e.reciprocal(out=RS[:, 0:w], in_=RS[:, 0:w]))
        S.bar()

    def norm_mod(Acoef, which_shift, issue=None, convert=None):
        Xn = [XT, STG[:].rearrange("p (k t) -> p k t", t=512)]

        def xload(i):
            t0, w = TILES[i]
            S.dma_async("sync", Xn[i % 2][:, 0:4, 0:w], XRES[0:512, t0:t0 + w].rearrange("(k p) t -> p k t", p=128))
            S.dma_async("gpsimd", Xn[i % 2][:, 4:8, 0:w], XRES[512:1024, t0:t0 + w].rearrange("(k p) t -> p k t", p=128))

        xload(0)
        for i, (t0, w) in enumerate(TILES):
            sel = 1 if t0 == 0 else 0
            Xi = Xn[i % 2]
            S.join()
            if i + 1 < len(TILES):
                xload(i + 1)
            if issue and i in issue:
                issue[i]()
            if convert and i in convert:
                convert[i]()
            S.op("scalar", lambda e, w=w, Xi=Xi: e.activation(out=OB[:, 0:4, 0:w], in_=Xi[:, 0:4, 0:w], func=AF.Square))
            S.op("vector", lambda e, w=w, Xi=Xi: e.tensor_tensor(out=OB[:, 4:8, 0:w], in0=Xi[:, 4:8, 0:w], in1=Xi[:, 4:8, 0:w], op=ALU.mult))
            S.bar()
            rstd_from(OB, 8, w, 1.0 / D)
            for k in range(8):
                S.op("vector", lambda e, k=k, w=w, sel=sel, Xi=Xi: e.scalar_tensor_tensor(
                    out=TMP[:, k, 0:w], in0=Xi[:, k, 0:w], scalar=Acoef[:, k, sel:sel + 1], in1=RS[:, 0:w],
                    op0=ALU.mult, op1=ALU.mult))
            S.bar()
            for k in range(8):
                if k % 2 == 0:
                    S.op("scalar", lambda e, k=k, w=w, sel=sel, t0=t0: e.activation(
                        out=H[:, k, t0:t0 + w], in_=TMP[:, k, 0:w], func=AF.Identity,
                        bias=MOD[:, which_shift, k, sel:sel + 1], scale=1.0))
                else:
                    S.op("vector", lambda e, k=k, w=w, sel=sel, t0=t0: e.tensor_scalar(
                        out=H[:, k, t0:t0 + w], in0=TMP[:, k, 0:w], scalar1=MOD[:, which_shift, k, sel:sel + 1],
                        scalar2=None, op0=ALU.add))
            S.bar()

    def resid_linear(src_dram, kch, gate_idx, Abufs, Wv):
        Xb = [XT, TMP, STG[:].rearrange("p (k t) -> p k t", t=512)]

        def loads(i):
            t0, w = TILES[i]
            S.dma("sync", Abufs[i % 2][:, 0:kch, 0:w], src_dram[:, t0:t0 + w].rearrange("(k p) t -> p k t", p=128))
            S.dma("gpsimd", Xb[i % 3][:, :, 0:w], XRES[:, t0:t0 + w].rearrange("(k p) t -> p k t", p=128))

        def store(i):
            t0, w = TILES[i]
            S.dma("gpsimd", XRES[:, t0:t0 + w].rearrange("(k p) t -> p k t", p=128), Xb[i % 3][:, :, 0:w])

        loads(0)
        S.bar()
        nt = len(TILES)
        for i, (t0, w) in enumerate(TILES):
            sel = 1 if t0 == 0 else 0
            Ab = Abufs[i % 2]
            for m in range(8):
                for k in range(kch):
                    S.op("tensor", lambda e, m=m, k=k, w=w, Ab=Ab: e.matmul(
                        ps[m][:, 0:w], lhsT=Wv[:, k, m * 128:(m + 1) * 128], rhs=Ab[:, k, 0:w],
                        start=(k == 0), stop=(k == kch - 1)))
            if i + 1 < nt:
                loads(i + 1)
            if i >= 1:
                store(i - 1)
            S.bar()
            Xi = Xb[i % 3]
            for m in range(8):
                S.op("vector", lambda e, m=m, w=w, sel=sel, Xi=Xi: e.scalar_tensor_tensor(
                    out=Xi[:, m, 0:w], in0=ps[m][:, 0:w], scalar=MOD[:, gate_idx, m, sel:sel + 1], in1=Xi[:, m, 0:w],
                    op0=ALU.mult, op1=ALU.add))
            S.bar()
        store(nt - 1)
        S.bar()

    try:
      for li in range(n_layers):
        S.dma("sync", STG[0:48, 0:128], b_ada[li].rearrange("(k p) -> k p", p=128))
        S.dma("sync", STG[48:56, 0:128], g_mix[li].rearrange("(k p) -> k p", p=128))
        S.dma("sync", STG[56:64, 0:128], g_ffn[li].rearrange("(k p) -> k p", p=128))
        S.dma("sync", STG[64:68, 0:128], b_glu[li].rearrange("(k p) -> k p", p=128))
        S.dma("sync", STG[68:72, 0:128], g_sgu[li].rearrange("(k p) -> k p", p=128))
        S.bar()
        S.op("tensor", lambda e: e.transpose(ps[7][:, 0:72], STG[0:72, 0:128], IDENT[0:72, 0:72]))
        S.bar()
        S.op("vector", lambda e: e.tensor_copy(out=BADA[:].rearrange("p q k -> p (q k)"), in_=ps[7][:, 0:48]))
        S.op("vector", lambda e: e.tensor_copy(out=GM[:], in_=ps[7][:, 48:56]))
        S.op("vector", lambda e: e.tensor_copy(out=GF[:], in_=ps[7][:, 56:64]))
        S.op("vector", lambda e: e.tensor_copy(out=BGLU[:], in_=ps[7][:, 64:68]))
        S.op("vector", lambda e: e.tensor_copy(out=GSGU[:], in_=ps[7][:, 68:72]))
        S.bar()
        if li == 0:
            first = ada0["next"]
            if first < 24:
                wada_dma(0, first)
                S.bar()
                for blk in range(first, 24):
                    if blk + 1 < 24:
                        wada_dma(0, blk + 1)
                    wada_mm(blk, ps[7], 400)
                    S.bar()
            S.op("vector", lambda e: e.tensor_copy(out=MODN[:], in_=ps[7][:, 400:496]))
            S.bar()
        S.op("vector", lambda e: e.tensor_tensor(
            out=MOD[:].rearrange("p q k n -> p (q k) n"), in0=MODN[:].rearrange("p (a n) -> p a n", n=2),
            in1=BADA[:].rearrange("p q k -> p (q k)").unsqueeze(2).to_broadcast([128, 48, 2]), op=ALU.add))
        S.bar()
        S.op("vector", lambda e: e.scalar_tensor_tensor(
            out=A1[:], in0=MOD[:, 1, :, :], scalar=1.0, in1=GM[:].unsqueeze(2).to_broadcast([128, 8, 2]),
            op0=ALU.add, op1=ALU.mult))
        S.op("vector", lambda e: e.scalar_tensor_tensor(
            out=A2[:], in0=MOD[:, 4, :, :], scalar=1.0, in1=GF[:].unsqueeze(2).to_broadcast([128, 8, 2]),
            op0=ALU.add, op1=ALU.mult))
        S.bar()

        ckp("ada")
        WIN = WB[:, 0:8 * 1536].rearrange("p (k n) -> p k n", n=1536)
        FBs = [FB[:, 0:4096].rearrange("p (k n) -> p k n", n=512), FB[:, 4096:8192].rearrange("p (k n) -> p k n", n=512)]

        def win_issue(bk):
            def f():
                src = w_in[li][:, bk * 512:(bk + 1) * 512].rearrange("(k p) n -> p k n", p=128)
                S.dma_async("sync", FBs[bk % 2][:, 0:4, :], src[:, 0:4, :])
                S.dma_async("gpsimd", FBs[bk % 2][:, 4:8, :], src[:, 4:8, :])
            return f

        def win_conv(bk):
            def f():
                S.op("vector", lambda e: e.tensor_copy(out=WIN[:, :, bk * 512:(bk + 1) * 512], in_=FBs[bk % 2]))
            return f

        norm_mod(A1, 0, issue={0: win_issue(0), 1: win_issue(1), 2: win_issue(2)},
                 convert={1: win_conv(0), 2: win_conv(1), 3: win_conv(2)})

        ckp("norm1")
        groups = [(ti, g) for ti in range(len(TILES)) for g in range(3)]

        def win_mm(n):
            ti, g = groups[n]
            t0, w = TILES[ti]
            for bi in range(4):
                m = g * 4 + bi
                bk = ps[(n % 2) * 4 + bi]
                for k in range(8):
                    S.op("tensor", lambda e, bk=bk, m=m, k=k, w=w, t0=t0: e.matmul(
                        bk[:, 0:w], lhsT=WIN[:, k, m * 128:(m + 1) * 128], rhs=H[:, k, t0:t0 + w],
                        start=(k == 0), stop=(k == 7)))

        def win_ev(n):
            ti, g = groups[n]
            t0, w = TILES[ti]
            j0, nj = t0 // 8, w // 8
            for bi in range(4):
                m = g * 4 + bi
                bk = ps[(n % 2) * 4 + bi]
                if g == 0:
                    S.op("vector", lambda e, bk=bk, m=m, w=w, j0=j0, nj=nj: e.tensor_copy(
                        out=USP[:, m, :, j0:j0 + nj].rearrange("p s j -> p j s"),
                        in_=bk[:, 0:w].rearrange("p (j s) -> p j s", s=8)))
                elif g == 1:
                    S.op("scalar", lambda e, bk=bk, m=m, w=w: e.activation(
                        out=OB[:, m - 4, 0:w], in_=bk[:, 0:w], func=AF.Gelu_apprx_tanh))
                else:
                    S.op("scalar", lambda e, bk=bk, m=m, w=w: e.activation(
                        out=TMP[:, m - 8, 0:w], in_=bk[:, 0:w], func=AF.Gelu_apprx_tanh))

        for n in range(len(groups) + 1):
            if n >= 1 and groups[n - 1][1] == 1:
                S.join()
            if n < len(groups):
                win_mm(n)
            if n >= 1:
                win_ev(n - 1)
                ti, g = groups[n - 1]
                t0, w = TILES[ti]
            S.bar()
            if n >= 1 and groups[n - 1][1] == 1:
                S.dma_async("sync", UG[:, t0:t0 + w].rearrange("(k p) t -> p k t", p=128), OB[:, 0:4, 0:w])
            if n >= 1 and groups[n - 1][1] == 2:
                S.dma_async("gpsimd", VG[:, t0:t0 + w].rearrange("(k p) t -> p k t", p=128), TMP[:, 0:4, 0:w])

        S.join()
        ckp("win")
        S.dma("sync", USSMP.rearrange("(q p) s j -> p q s j", p=128), USP)
        S.dma("gpsimd", STG[0:32, 0:16], ssm_d[li].rearrange("(g c) -> g c", c=16))
        S.bar()
        for s_ in range(8):
            S.dma("sync" if s_ % 2 == 0 else "gpsimd", IM[s_ * 16:(s_ + 1) * 16, :, :],
                  USSMP[:, s_, :].rearrange("(g c) j -> c g j", c=16))
        S.bar()

        S.op("vector", lambda e: e.tensor_copy(out=STG[0:32, 128:256].rearrange("p (s c) -> p s c", c=16),
                                               in_=STG[0:32, 0:16].unsqueeze(1).to_broadcast([32, 8, 16])))
        S.bar()
        S.op("tensor", lambda e: e.transpose(ps[7][:, 0:32], STG[0:32, 128:256], IDENT[0:32, 0:32]))
        S.bar()
        S.op("vector", lambda e: e.tensor_copy(out=DV[:], in_=ps[7][:, 0:32]))
        ckp("im2col")
        ada_state = {"next": 0, "pending": None}
        for dr in range(2):
            CIN1 = STG[:, 0:512].rearrange("p (q n) -> p q n", n=128)
            CIN2 = STG[:, 512:1024].rearrange("p (q n) -> p q n", n=128)
            for hf in range(2):
                lo = slice(hf * 64, hf * 64 + 64)
                csrc = [c_re, c_im] if hf == 0 else [c_im, c_re]
                S.dma("sync", CIN1[:, :, lo], csrc[0][li, dr].rearrange("(q g) c p -> (g c) q p", q=4))
                S.dma("gpsimd", CIN2[:, :, lo], csrc[1][li, dr].rearrange("(q g) c p -> (g c) q p", q=4))
            S.bar()
            for q in range(4):
                S.op("tensor", lambda e, q=q: e.transpose(ps[0][:, q * 128:(q + 1) * 128], CIN1[:, q, :], IDENT[:]))
                S.op("tensor", lambda e, q=q: e.transpose(ps[1][:, q * 128:(q + 1) * 128], CIN2[:, q, :], IDENT[:]))
            S.bar()
            S.op("vector", lambda e: e.tensor_copy(out=CX1.rearrange("p g c -> p (g c)"), in_=ps[0][:]))
            S.op("scalar", lambda e: e.copy(out=CX2.rearrange("p g c -> p (g c)"), in_=ps[1][:]))
            S.bar()
            BIN1 = STG[0:32, 0:2048].rearrange("p (h q c) -> p h q c", h=2, c=16)
            BIN2 = STG[0:32, 2048:4096].rearrange("p (h q c) -> p h q c", h=2, c=16)
            S.dma("sync", BIN1[:, 0, :, :], b_re[li, dr])
            S.dma("gpsimd", BIN1[:, 1, :, :], b_im[li, dr])
            S.dma("sync", BIN2[:, 0, :, :], b_im[li, dr])
            S.dma("gpsimd", BIN2[:, 1, :, :], b_re[li, dr])
            AIN = RS[0:32, 0:256].rearrange("p (a q) -> p a q", q=64)
            S.dma("sync", AIN[:, 0, :], a_re[li, dr])
            S.dma("gpsimd", AIN[:, 1, :], a_re[li, dr])
            S.dma("sync", AIN[:, 2, :], a_im[li, dr])
            S.dma("gpsimd", AIN[:, 3, :], a_im[li, dr])
            S.bar()
            for c_ in range(16):
                S.op("tensor", lambda e, c_=c_: e.transpose(ps[2][:, c_ * 32:(c_ + 1) * 32], BIN1[:, :, :, c_], IDENT[0:32, 0:32]))
                S.op("tensor", lambda e, c_=c_: e.transpose(ps[3][:, c_ * 32:(c_ + 1) * 32], BIN2[:, :, :, c_], IDENT[0:32, 0:32]))
            S.op("tensor", lambda e: e.transpose(ps[4][:, 0:32], RS[0:32, 0:128], IDENT[0:32, 0:32]))
            S.op("tensor", lambda e: e.transpose(ps[4][:, 32:64], RS[0:32, 128:256], IDENT[0:32, 0:32]))
            S.bar()
            S.op("vector", lambda e: e.tensor_copy(out=BX1.rearrange("p g c -> p c g"), in_=ps[2][:].rearrange("p (c g) -> p c g", g=32)))
            S.op("scalar", lambda e: e.copy(out=BX2.rearrange("p g c -> p c g"), in_=ps[3][:].rearrange("p (c g) -> p c g", g=32)))
            S.op("vector", lambda e: e.tensor_copy(out=ARp[:], in_=ps[4][:, 0:32]))
            S.op("vector", lambda e: e.tensor_copy(out=AIp[:], in_=ps[4][:, 32:64]))
            S.dma("sync", LDT[:], log_dt[li, dr].partition_broadcast(128))
            S.bar()
            ckp("pl%d" % dr)
            S.op("scalar", lambda e: e.activation(out=LDT[:], in_=LDT[:], func=AF.Exp))
            S.bar()
            S.op("vector", lambda e: e.tensor_tensor(out=LR[:], in0=ARp[:], in1=LDT[:], op=ALU.mult))
            S.op("gpsimd", lambda e: e.tensor_tensor(out=LI[:], in0=AIp[:], in1=LDT[:], op=ALU.mult))
            S.bar()
            ckp("pb%d" % dr)
            ks = list(range(-8, 0)) + list(range(1, 9))
            for idx, kk in enumerate(ks):
                S.op("vector", lambda e, idx=idx, kk=kk: e.tensor_scalar(
                    out=ARG[:, :, idx], in0=LI[:], scalar1=float(kk), scalar2=None, op0=ALU.mult))
                S.op("vector", lambda e, idx=idx, kk=kk: e.tensor_scalar(
                    out=EARG[:, :, idx], in0=LR[:], scalar1=float(kk), scalar2=None, op0=ALU.mult))
            S.bar()
            ckp("pc%d" % dr)
            S.op("vector", lambda e: e.tensor_scalar(out=ARGC, in0=ARG, scalar1=TWO_PI / 4, scalar2=None, op0=ALU.add))
            S.op("scalar", lambda e: e.activation(out=MAG, in_=EARG, func=AF.Exp))
            S.bar()
            ckp("pd%d" % dr)
            S.op("vector", lambda e: e.tensor_scalar(out=NI, in0=ARG, scalar1=1.0 / TWO_PI, scalar2=None, op0=ALU.mult))
            S.op("vector", lambda e: e.tensor_scalar(out=NIC, in0=ARGC, scalar1=1.0 / TWO_PI, scalar2=None, op0=ALU.mult))
            S.bar()
            ckp("pe%d" % dr)
            S.op("vector", lambda e: e.tensor_copy(out=NF, in_=NI))
            S.op("vector", lambda e: e.tensor_copy(out=NFC, in_=NIC))
            S.bar()
            ckp("pf%d" % dr)
            S.op("vector", lambda e: e.scalar_tensor_tensor(out=ARG, in0=NF, scalar=-TWO_PI, in1=ARG, op0=ALU.mult, op1=ALU.add))
            S.op("vector", lambda e: e.scalar_tensor_tensor(out=ARGC, in0=NFC, scalar=-TWO_PI, in1=ARGC, op0=ALU.mult, op1=ALU.add))
            S.bar()
            S.op("vector", lambda e: e.tensor_scalar(out=ARG, in0=ARG, scalar1=3.1415925, scalar2=-3.1415925, op0=ALU.min, op1=ALU.max))
            S.op("vector", lambda e: e.tensor_scalar(out=ARGC, in0=ARGC, scalar1=3.1415925, scalar2=-3.1415925, op0=ALU.min, op1=ALU.max))
            S.bar()
            ckp("pg%d" % dr)
            S.op("scalar", lambda e: e.activation(out=PIM, in_=ARG, func=AF.Sin))
            S.op("scalar", lambda e: e.activation(out=PRE, in_=ARGC, func=AF.Sin))
            S.bar()
            S.op("vector", lambda e: e.tensor_tensor(out=PIM, in0=PIM, in1=MAG, op=ALU.mult))
            S.op("vector", lambda e: e.tensor_tensor(out=PRE, in0=PRE, in1=MAG, op=ALU.mult))
            S.bar()
            ckp("ph%d" % dr)
            S.op("vector", lambda e: e.tensor_scalar(out=NR[:], in0=PRE[:, :, 8], scalar1=-1.0, scalar2=None, op0=ALU.add))
            S.op("vector", lambda e: e.tensor_tensor(out=DEN[:], in0=ARp[:], in1=ARp[:], op=ALU.mult))
            ckp("c0")
            S.op("vector", lambda e: e.tensor_tensor(out=TA[:], in0=AIp[:], in1=AIp[:], op=ALU.mult))
            ckp("c1")
            S.op("vector", lambda e: e.tensor_tensor(out=DEN[:], in0=DEN[:], in1=TA[:], op=ALU.add))
            ckp("c2")
            S.op("vector", lambda e: e.reciprocal(out=DEN[:], in_=DEN[:]))
            ckp("c3")
            S.op("vector", lambda e: e.tensor_tensor(out=TA[:], in0=NR[:], in1=ARp[:], op=ALU.mult))
            S.op("vector", lambda e: e.tensor_tensor(out=TB[:], in0=PIM[:, :, 8], in1=AIp[:], op=ALU.mult))
            ckp("c4")
            S.op("vector", lambda e: e.tensor_tensor(out=CR[:], in0=TA[:], in1=TB[:], op=ALU.add))
            ckp("c5")
            S.op("vector", lambda e: e.tensor_tensor(out=TA[:], in0=PIM[:, :, 8], in1=ARp[:], op=ALU.mult))
            S.op("vector", lambda e: e.tensor_tensor(out=TB[:], in0=NR[:], in1=AIp[:], op=ALU.mult))
            ckp("c6")
            S.op("vector", lambda e: e.tensor_tensor(out=CI[:], in0=TA[:], in1=TB[:], op=ALU.subtract))
            ckp("c7")
            S.op("vector", lambda e: e.tensor_tensor(out=CR[:], in0=CR[:], in1=DEN[:], op=ALU.mult))
            S.op("vector", lambda e: e.tensor_tensor(out=CI[:], in0=CI[:], in1=DEN[:], op=ALU.mult))
            ckp("c8")
            ckp("pi%d" % dr)
            CRb = CR[:].unsqueeze(2).to_broadcast([128, 32, 8])
            CIb = CI[:].unsqueeze(2).to_broadcast([128, 32, 8])
            S.op("vector", lambda e: e.tensor_tensor(out=QR, in0=PRE[:, :, 0:8], in1=CRb, op=ALU.mult))
            S.op("vector", lambda e: e.tensor_tensor(out=QT, in0=PIM[:, :, 0:8], in1=CIb, op=ALU.mult))
            S.bar()
            S.op("vector", lambda e: e.tensor_tensor(out=QR, in0=QR, in1=QT, op=ALU.subtract))
            S.bar()
            S.op("vector", lambda e: e.tensor_tensor(out=QI, in0=PRE[:, :, 0:8], in1=CIb, op=ALU.mult))
            S.op("vector", lambda e: e.tensor_tensor(out=QT, in0=PIM[:, :, 0:8], in1=CRb, op=ALU.mult))
            S.bar()
            S.op("vector", lambda e: e.tensor_tensor(out=QI, in0=QI, in1=QT, op=ALU.add))
            S.bar()
            S.op("vector", lambda e: e.tensor_scalar(out=QI, in0=QI, scalar1=SIGN[:, 0:1], scalar2=None, op0=ALU.mult))
            S.op("vector", lambda e: e.tensor_scalar(out=PA, in0=PRE[:, :, 8:16], scalar1=NSIGN[:, 0:1], scalar2=None, op0=ALU.mult))
            S.op("vector", lambda e: e.tensor_scalar(out=PB, in0=PIM[:, :, 8:16], scalar1=-1.0, scalar2=None, op0=ALU.mult))
            S.bar()
            S.op("vector", lambda e: e.tensor_copy(out=ARC[:, 0, :], in_=PRE[:, :, 15]))
            S.op("vector", lambda e: e.tensor_copy(out=ARC[:, 1, :], in_=PRE[:, :, 15]))
            S.op("vector", lambda e: e.tensor_scalar(out=AIC[:, 0, :], in0=PIM[:, :, 15], scalar1=SIGN[:, 0:1], scalar2=None, op0=ALU.mult))
            S.op("vector", lambda e: e.tensor_scalar(out=AIC[:, 1, :], in0=PIM[:, :, 15], scalar1=NSIGN[:, 0:1], scalar2=None, op0=ALU.mult))
            S.bar()
            ckp("prep%d" % dr + "")
            for s_ in range(8):
                qi = (7 - s_) if dr == 0 else s_
                ri = s_ if dr == 0 else (7 - s_)
                S.op("vector", lambda e, s_=s_, qi=qi: e.tensor_tensor(
                    out=LT[:, :, s_, :], in0=BX1, in1=QR[:, :, qi:qi + 1].to_broadcast([128, 32, 16]), op=ALU.mult))
                S.op("vector", lambda e, s_=s_, ri=ri: e.tensor_tensor(
                    out=RT[:, :, s_, :], in0=CX1, in1=PA[:, :, ri:ri + 1].to_broadcast([128, 32, 16]), op=ALU.mult))
            S.bar()
            TL = STG[:].rearrange("p (g s c) -> p g s c", s=8, c=16)
            for s_ in range(8):
                qi = (7 - s_) if dr == 0 else s_
                S.op("vector", lambda e, s_=s_, qi=qi: e.tensor_tensor(
                    out=TL[:, :, s_, :], in0=BX2, in1=QI[:, :, qi:qi + 1].to_broadcast([128, 32, 16]), op=ALU.mult))
            S.bar()
            S.op("vector", lambda e: e.tensor_tensor(out=LT, in0=LT, in1=TL, op=ALU.add))
            S.bar()
            for s_ in range(8):
                ri = s_ if dr == 0 else (7 - s_)
                S.op("vector", lambda e, s_=s_, ri=ri: e.tensor_tensor(
                    out=TL[:, :, s_, :], in0=CX2, in1=PB[:, :, ri:ri + 1].to_broadcast([128, 32, 16]), op=ALU.mult))
            S.bar()
            S.op("vector", lambda e: e.tensor_tensor(out=RT, in0=RT, in1=TL, op=ALU.add))
            S.bar()
            ckp("lr%d" % dr + "")
            for rnd in range(4):
                for gi in range(8):
                    g = rnd * 8 + gi
                    Lg = LT[:, g, :, :].rearrange("p s c -> p (s c)")
                    Rg = RT[:, g, :, :].rearrange("p s c -> p (s c)")
                    S.op("tensor", lambda e, gi=gi, Lg=Lg, Rg=Rg: e.matmul(
                        ps[gi // 4][:, (gi % 4) * 128:(gi % 4 + 1) * 128], lhsT=Lg, rhs=Rg, start=True, stop=True))
                    S.op("tensor", lambda e, gi=gi, Lg=Lg: e.transpose(
                        ps[2 + gi // 4][:, (gi % 4) * 128:(gi % 4 + 1) * 128], Lg, IDENT[:]))
                S.bar()
                for bk in range(2):
                    g0 = rnd * 8 + bk * 4
                    S.op("vector", lambda e, bk=bk, g0=g0, dr=dr: e.tensor_tensor(
                        out=TOEP[:, g0:g0 + 4, :], in0=ps[bk][:].rearrange("p (g n) -> p g n", n=128),
                        in1=MASKS[:, dr:dr + 1, :].to_broadcast([128, 4, 128]), op=ALU.mult))
                    S.op("scalar", lambda e, bk=bk, g0=g0: e.copy(
                        out=ET[:, g0:g0 + 4, :], in_=ps[2 + bk][:].rearrange("p (g n) -> p g n", n=128)))
                S.bar()
                for bk in range(2):
                    g0 = rnd * 8 + bk * 4
                    S.op("vector", lambda e, bk=bk, g0=g0: e.tensor_copy(
                        out=ESW[:, g0:g0 + 4, 0:64], in_=ps[2 + bk][:].rearrange("p (g n) -> p g n", n=128)[:, :, 64:128]))
                    S.op("vector", lambda e, bk=bk, g0=g0: e.tensor_copy(
                        out=ESW[:, g0:g0 + 4, 64:128], in_=ps[2 + bk][:].rearrange("p (g n) -> p g n", n=128)[:, :, 0:64]))
                S.bar()
            S.op("scalar", lambda e: e.copy(out=RB.rearrange("p g n -> p (g n)"), in_=RT.rearrange("p g s c -> p (g s c)")))
            if dr == 0:
                for g in range(32):
                    S.op("vector", lambda e, g=g: e.scalar_tensor_tensor(
                        out=TOEP[:, g, :], in0=IDENT[:], scalar=DV[:, g:g + 1], in1=TOEP[:, g, :],
                        op0=ALU.mult, op1=ALU.add))
            S.op("gpsimd", lambda e: e.memset(W3[:], 0.0))
            S.bar()
            ckp("tiles%d" % dr + "")
            blocks = [(0, 32)] + [(32 + 64 * b, 64) for b in range(4)]
            order = blocks if dr == 0 else [blocks[0]] + blocks[:0:-1]
            for (jb, nb) in order:
                for half in range(2):
                    for gi in range(16):
                        g = half * 16 + gi
                        for arr in range(2):
                            ii = gi * 2 + arr
                            Em = ET if arr == 0 else ESW
                            S.op("tensor", lambda e, ii=ii, g=g, Em=Em, jb=jb, nb=nb: e.matmul(
                                ps[ii // 8][:, (ii % 8) * 64:(ii % 8) * 64 + nb], lhsT=Em[:, g, :],
                                rhs=IM[:, g, jb:jb + nb], start=True, stop=True))
                    S.bar()
                    for bk in range(4):
                        g0 = half * 16 + bk * 4
                        S.op("vector" if bk % 2 == 0 else "scalar", (lambda e, bk=bk, g0=g0, nb=nb: e.tensor_copy(
                            out=SS[:, 0:nb, :, g0:g0 + 4].rearrange("p j a g -> p g a j"),
                            in_=ps[bk][:].rearrange("p (g a j) -> p g a j", a=2, j=64)[:, :, :, 0:nb]))
                            if bk % 2 == 0 else (lambda e, bk=bk, g0=g0, nb=nb: e.copy(
                            out=SS[:, 0:nb, :, g0:g0 + 4].rearrange("p j a g -> p g a j"),
                            in_=ps[bk][:].rearrange("p (g a j) -> p g a j", a=2, j=64)[:, :, :, 0:nb])))
                    S.bar()
                js = list(range(jb, jb + nb)) if dr == 0 else list(range(jb + nb - 1, jb - 1, -1))
                pjl = None
                nsub = 3 if nb == 64 else 1
                cuts = [round(len(js) * x / nsub) for x in range(nsub + 1)]
                for j in js:
                    jl = j - jb
                    if (j - js[0]) * (1 if dr == 0 else -1) in cuts[:-1]:
                        last_sub = ((jb, nb) == order[-1]) and ((j - js[0]) * (1 if dr == 0 else -1) == cuts[-2])
                        S.join()
                        if li + 1 < n_layers:
                            if ada_state["pending"] is not None:
                                wada_mm(ada_state["pending"], ps[7], 400)
                                ada_state["pending"] = None
                            if (not last_sub) and ada_state["next"] < 24:
                                wada_dma(li + 1, ada_state["next"], asyn=True)
                                ada_state["pending"] = ada_state["next"]
                                ada_state["next"] += 1
                    Wc = W3[:] if pjl is None else SS[:, pjl, :, :]
                    S.op("vector", lambda e, jl=jl, Wc=Wc: e.tensor_tensor(out=G3[:, 0:2, :], in0=Wc, in1=SS[:, jl, :, :], op=ALU.add))
                    S.op("vector", lambda e: e.tensor_tensor(out=T1[:], in0=G3[:, 0:2, :], in1=ARC[:], op=ALU.mult))
                    S.op("vector", lambda e: e.tensor_tensor(out=T2[:, 0, :], in0=G3[:, 1, :], in1=AIC[:, 0, :], op=ALU.mult))
                    S.op("vector", lambda e: e.tensor_tensor(out=T2[:, 1, :], in0=G3[:, 0, :], in1=AIC[:, 1, :], op=ALU.mult))
                    S.op("vector", lambda e, jl=jl: e.tensor_tensor(out=SS[:, jl, :, :], in0=T1[:], in1=T2[:], op=ALU.add))
                    pjl = jl
                S.bar()
                if dr == 0:
                    S.op("scalar", lambda e, jb=jb: e.copy(out=HH[:, jb, :], in_=W3[:, 0, :]))
                    S.op("scalar", lambda e, jb=jb, nb=nb: e.copy(out=HH[:, jb + 1:jb + nb, :], in_=SS[:, 0:nb - 1, 0, :]))
                else:
                    S.op("scalar", lambda e, jb=jb, nb=nb: e.copy(out=HH[:, jb + nb - 1, :], in_=W3[:, 0, :]))
                    S.op("scalar", lambda e, jb=jb, nb=nb: e.copy(out=HH[:, jb:jb + nb - 1, :], in_=SS[:, 1:nb, 0, :]))
                S.bar()
                S.op("vector", lambda e, pjl=pjl: e.tensor_copy(out=W3[:], in_=SS[:, pjl, :, :]))
                S.bar()
            ckp("rec%d" % dr + "")
            for rnd in range(4):
                for gi in range(8):
                    g = rnd * 8 + gi
                    S.op("tensor", lambda e, gi=gi, g=g: e.matmul(
                        ps[gi][:, 0:NJ], lhsT=TOEP[:, g, :], rhs=IM[:, g, :], start=True, stop=False))
                    S.op("tensor", lambda e, gi=gi, g=g: e.matmul(
                        ps[gi][:, 0:NJ], lhsT=RB[:, g, :], rhs=HH[:, :, g], start=False, stop=True))
                S.bar()
                for gi in range(8):
                    g = rnd * 8 + gi
                    if dr == 0:
                        S.op("vector" if gi % 2 == 0 else "scalar", (lambda e, gi=gi, g=g: e.tensor_copy(out=YS[:, g, :], in_=ps[gi][:, 0:NJ]))
                             if gi % 2 == 0 else (lambda e, gi=gi, g=g: e.copy(out=YS[:, g, :], in_=ps[gi][:, 0:NJ])))
                    else:
                        S.op("vector", lambda e, gi=gi, g=g: e.tensor_tensor(out=YS[:, g, :], in0=YS[:, g, :], in1=ps[gi][:, 0:NJ], op=ALU.add))
                S.bar()
        if li + 1 < n_layers:
            assert ada_state["next"] == 24 and ada_state["pending"] is None, ada_state
            S.op("vector", lambda e: e.tensor_copy(out=MODN[:], in_=ps[7][:, 400:496]))
        ckp("read")
        S.dma("sync", YSP[:, :, :], YS)
        S.bar()
        YV = YS.rearrange("p (q s) j -> p q s j", s=8)
        YSPv = YSP.rearrange("(s c) (q g) j -> g c q s j", c=16, g=8)
        for g8 in range(8):
            for q in range(4):
                S.dma(["sync", "gpsimd"][q % 2], YV[g8 * 16:(g8 + 1) * 16, q, :, :], YSPv[g8, :, q, :, :])
            S.bar()
        S.bar()

        ckp("unim")
        WG = WB[:, 0:4 * 512].rearrange("p (k n) -> p k n", n=512)
        load_weight(WG, w_glu[li], 4, 512, 512)
        for (t0, w) in TILES:
            j0, nj = t0 // 8, w // 8
            for q in range(4):
                S.op("scalar", lambda e, q=q, w=w, j0=j0, nj=nj: e.activation(
                    out=TMP[:, q, 0:w].rearrange("p (j s) -> p j s", s=8),
                    in_=YV[:, q, :, j0:j0 + nj].rearrange("p s j -> p j s"), func=AF.Gelu_apprx_tanh))
            S.bar()
            S.op("vector", lambda e, w=w: e.tensor_copy(out=OB[:, 0:4, 0:w], in_=TMP[:, 0:4, 0:w]))
            S.bar()
            for m in range(4):
                for k in range(4):
                    S.op("tensor", lambda e, m=m, k=k, w=w: e.matmul(
                        ps[m][:, 0:w], lhsT=WG[:, k, m * 128:(m + 1) * 128], rhs=OB[:, k, 0:w],
                        start=(k == 0), stop=(k == 3)))
            S.bar()
            for m in range(4):
                S.op("scalar", lambda e, m=m, w=w: e.activation(
                    out=XT[:, m, 0:w], in_=ps[m][:, 0:w], func=AF.Sigmoid, bias=BGLU[:, m:m + 1], scale=1.0))
            S.bar()
            S.join()
            S.op("vector", lambda e, w=w: e.tensor_tensor(out=OB[:, 4:8, 0:w], in0=TMP[:, 0:4, 0:w], in1=XT[:, 0:4, 0:w], op=ALU.mult))
            S.bar()
            S.dma_async("sync", MIX[0:512, t0:t0 + w].rearrange("(k p) t -> p k t", p=128), OB[:, 4:8, 0:w])

        S.join()
        ckp("glu")
        S.dma("sync", TMP[:, 0:4, 0:128], w_sp[li].rearrange("h p q -> p h q"))
        S.dma("gpsimd", BS.rearrange("p h q -> p (h q)"), b_sp[li].rearrange("h q -> (h q)").partition_broadcast(128))
        S.bar()
        for h in range(4):
            S.op("tensor", lambda e, h=h: e.transpose(ps[0][:, h * 128:(h + 1) * 128], TMP[:, h, 0:128], IDENT[:]))
        S.bar()
        S.op("vector", lambda e: e.tensor_copy(out=WST[:], in_=ps[0][:].rearrange("p (h n) -> p h n", n=128)))
        S.bar()
        VGb = [XT[:, 0:4, :], XT[:, 4:8, :]]
        UGb = [OB[:, 0:4, :], ACTB[:, 0:4, :]]

        def sgu_loads(i):
            t0, w = TILES[i]
            S.dma_async("sync", VGb[i % 2][:, :, 0:w], VG[:, t0:t0 + w].rearrange("(k p) t -> p k t", p=128))
            S.dma_async("gpsimd", UGb[i % 2][:, :, 0:w], UG[:, t0:t0 + w].rearrange("(k p) t -> p k t", p=128))

        WOv = WB[:, 4096:4096 + 8 * D].rearrange("p (k n) -> p k n", n=D)
        WOs = [FB[:, 1024:5120].rearrange("p (k n) -> p k n", n=512), FB[:, 5120:9216].rearrange("p (k n) -> p k n", n=512)]
        sgu_loads(0)
        for i, (t0, w) in enumerate(TILES):
            nchk = w // 128
            VGi, UGi = VGb[i % 2], UGb[i % 2]
            S.join()
            if i + 1 < len(TILES):
                sgu_loads(i + 1)
            if i < 2:
                wsrc = w_out[li][:, i * 512:(i + 1) * 512].rearrange("(k p) n -> p k n", p=128)
                S.dma_async("sync", WOs[i][:, 0:4, :], wsrc[:, 0:4, :])
                S.dma_async("gpsimd", WOs[i][:, 4:8, :], wsrc[:, 4:8, :])
            if i in (2, 3):
                S.op("vector", lambda e, i=i: e.tensor_copy(out=WOv[:, :, (i - 2) * 512:(i - 1) * 512], in_=WOs[i - 2]))
            S.op("scalar", lambda e, w=w, VGi=VGi: e.activation(out=SQ[:, 0:4, 0:w], in_=VGi[:, 0:4, 0:w], func=AF.Square))
            S.bar()
            rstd_from(SQ, 4, w, 1.0 / 512)
            for k in range(4):
                S.op("vector", lambda e, k=k, w=w, VGi=VGi: e.scalar_tensor_tensor(
                    out=TMP[:, k, 0:w], in0=VGi[:, k, 0:w], scalar=GSGU[:, k:k + 1], in1=RS[:, 0:w],
                    op0=ALU.mult, op1=ALU.mult))
            S.bar()
            for ck in range(nchk):
                for h in range(4):
                    ii = ck * 4 + h
                    S.op("tensor", lambda e, ii=ii, ck=ck, h=h: e.transpose(
                        ps[ii // 4][:, (ii % 4) * 128:(ii % 4 + 1) * 128], TMP[:, h, ck * 128:(ck + 1) * 128], IDENT[:]))
            S.bar()
            for ck in range(nchk):
                S.op("vector" if ck % 2 == 0 else "scalar", (lambda e, ck=ck: e.tensor_copy(
                    out=VT[:, ck * 4:(ck + 1) * 4, :], in_=ps[ck][:].rearrange("p (h n) -> p h n", n=128)))
                    if ck % 2 == 0 else (lambda e, ck=ck: e.copy(
                    out=VT[:, ck * 4:(ck + 1) * 4, :], in_=ps[ck][:].rearrange("p (h n) -> p h n", n=128))))
            S.bar()
            for ck in range(nchk):
                for h in range(4):
                    ii = ck * 4 + h
                    S.op("tensor", lambda e, ii=ii, ck=ck, h=h: e.matmul(
                        ps[4 + ck][:, h * 128:(h + 1) * 128], lhsT=VT[:, ii, :], rhs=WST[:, h, :], start=True, stop=True))
            S.bar()
            for ck in range(nchk):
                S.op("vector", lambda e, ck=ck: e.tensor_tensor(
                    out=TMP[:, 4:8, ck * 128:(ck + 1) * 128], in0=ps[4 + ck][:].rearrange("p (h n) -> p h n", n=128),
                    in1=BS, op=ALU.add))
            S.bar()
            S.op("vector", lambda e, w=w, UGi=UGi: e.tensor_tensor(out=OB[:, 4:8, 0:w], in0=TMP[:, 4:8, 0:w], in1=UGi[:, 0:4, 0:w], op=ALU.mult))
            S.bar()
            S.dma_async("sync", MIX[512:1024, t0:t0 + w].rearrange("(k p) t -> p k t", p=128), OB[:, 4:8, 0:w])

        S.join()
        ckp("sgu")
        resid_linear(MIX, 8, 2, [ACTB[:, 0:8, :], ACTB[:, 8:16, :]], WOv)

        ckp("wout")
        norm_mod(A2, 3)

        ckp("norm2")
        S.dma("sync", FB[0:9, 0:2 * DFF], w_conv[li].rearrange("a b n -> (a b) n"))
        S.bar()
        for ch in range(44):
            S.op("tensor", lambda e, ch=ch: e.transpose(ps[7][:, ch * 9:(ch + 1) * 9], FB[0:9, ch * 128:(ch + 1) * 128], IDENT[0:9, 0:9]))
        S.bar()
        S.op("vector", lambda e: e.tensor_copy(out=WC[:].rearrange("p t c -> p c t"), in_=ps[7][:, 0:396].rearrange("p (c t) -> p c t", t=9)))
        S.bar()
        WUb = [OBt[:, 0:2048].rearrange("p (k n) -> p k n", n=256), OBt[:, 2048:4096].rearrange("p (k n) -> p k n", n=256)]
        WDv = WB[:, 0:22 * D].rearrange("p (k n) -> p k n", n=D)
        wd_stage = [XTt[:, 0:2816].rearrange("p (k n) -> p k n", n=128), TMPt[:, 0:2816].rearrange("p (k n) -> p k n", n=128)]
        FBb = FB[:].bitcast(BF16)
        UPb = [FBb[:, 0:2304], FBb[:, 2304:4608]]
        DGb = [FBb[:, 4608:6912].rearrange("p (t n) -> p t n", n=128), FBb[:, 6912:9216].rearrange("p (t n) -> p t n", n=128)]
        SG = FB[:, 4608:6912]
        GB = ACTB[:, 0:5, :].rearrange("p a t -> p (a t)")[:, 0:NT]
        stgs = [STG[:, 0:2048].rearrange("p (k n) -> p k n", n=256), STG[:, 2048:4096].rearrange("p (k n) -> p k n", n=256)]

        def bank(m, sl):
            return ps[(sl + 4 * m) % 8]

        def wup_dma(m):
            st = stgs[m % 2]
            S.dma_async("sync", st[:, :, 0:128], w_up[li, :, m * 128:(m + 1) * 128].rearrange("(k p) n -> p k n", p=128))
            S.dma_async("gpsimd", st[:, :, 128:256], w_up[li, :, DFF + m * 128:DFF + (m + 1) * 128].rearrange("(k p) n -> p k n", p=128))

        def prep_w(m, which):
            if which == 0:
                S.op("vector", lambda e, m=m: e.tensor_copy(out=WUb[m % 2], in_=stgs[m % 2]))
            for idx in range(18):
                part, tap = divmod(idx, 9)
                ch = part * 22 + m
                if idx % 2 == 0 and which == 0:
                    S.op("vector", lambda e, idx=idx, tap=tap, ch=ch, m=m: e.tensor_scalar(
                        out=DGb[m % 2][:, idx, :], in0=IDENT[:], scalar1=WC[:, tap, ch:ch + 1], scalar2=None, op0=ALU.mult))
                if idx % 2 == 1 and which == 1:
                    S.op("scalar", lambda e, idx=idx, tap=tap, ch=ch, m=m: e.activation(
                        out=DGb[m % 2][:, idx, :], in_=IDENT[:], func=AF.Copy, scale=WC[:, tap, ch:ch + 1]))

        def up_mm(m, part, tiles, slots):
            for t, sl in zip(tiles, slots):
                t0, w = TILES[t]
                bk = bank(m, sl)
                for k in range(8):
                    S.op("tensor", lambda e, bk=bk, k=k, t0=t0, w=w, part=part, m=m: e.matmul(
                        bk[:, 0:w], lhsT=WUb[m % 2][:, k, part * 128:(part + 1) * 128], rhs=H[:, k, t0:t0 + w],
                        start=(k == 0), stop=(k == 7)))

        def evac_up(m, part, tiles, slots, engs):
            dst = UPb[part]
            for n_, (t, sl) in enumerate(zip(tiles, slots)):
                t0, w = TILES[t]
                bk = bank(m, sl)
                if engs[n_ % len(engs)] == "vector":
                    S.op("vector", lambda e, bk=bk, t0=t0, w=w, dst=dst: e.tensor_copy(out=dst[:, t0:t0 + w], in_=bk[:, 0:w]))
                else:
                    S.op("scalar", lambda e, bk=bk, t0=t0, w=w, dst=dst: e.copy(out=dst[:, t0:t0 + w], in_=bk[:, 0:w]))

        taps9 = [(1, 1)] + [(ky, kx) for ky in range(3) for kx in range(3) if not (ky == 1 and kx == 1)]

        def conv_mm(m, part, tiles, slots):
            src = UPb[part]
            d0 = part * 9
            DGm = DGb[m % 2]
            sv = src[:, LC:NT].rearrange("p (r c) -> p r c", c=64)
            for t, sl in zip(tiles, slots):
                bk = bank(m, sl)
                if t == 0:
                    S.op("tensor", lambda e, bk=bk, DGm=DGm: e.matmul(bk[:, 0:256], lhsT=DGm[:, d0 + 4, :], rhs=src[:, 0:256], start=True, stop=False))
                    S.op("tensor", lambda e, bk=bk, DGm=DGm: e.matmul(bk[:, 1:256], lhsT=DGm[:, d0 + 3, :], rhs=src[:, 0:255], start=False, stop=False))
                    S.op("tensor", lambda e, bk=bk, DGm=DGm: e.matmul(bk[:, 0:255], lhsT=DGm[:, d0 + 5, :], rhs=src[:, 1:256], start=False, stop=True))
                    continue
                R0 = 8 * (t - 1)
                pv = bk[:, 0:512].rearrange("p (r c) -> p r c", c=64)
                for n_, (ky, kx) in enumerate(taps9):
                    dy, dx = ky - 1, kx - 1
                    ra, rb = max(R0, -dy, 0), min(R0 + 8, 32 - max(0, dy))
                    c0, c1 = max(0, -dx), 64 - max(0, dx)
                    S.op("tensor", lambda e, pv=pv, ra=ra, rb=rb, c0=c0, c1=c1, dy=dy, dx=dx, R0=R0, ky=ky, kx=kx, n_=n_, DGm=DGm:
                         e.matmul(pv[:, ra - R0:rb - R0, c0:c1], lhsT=DGm[:, d0 + ky * 3 + kx, :],
                                  rhs=sv[:, ra + dy:rb + dy, c0 + dx:c1 + dx], start=(n_ == 0), stop=(n_ == 8)))

        def silu_ev(m, tiles, slots):
            for t, sl in zip(tiles, slots):
                t0, w = TILES[t]
                bk = bank(m, sl)
                S.op("scalar", lambda e, bk=bk, t0=t0, w=w: e.activation(out=SG[:, t0:t0 + w], in_=bk[:, 0:w], func=AF.Silu))

        def mult_ev(m, tiles, slots):
            for t, sl in zip(tiles, slots):
                t0, w = TILES[t]
                bk = bank(m, sl)
                S.op("vector", lambda e, bk=bk, t0=t0, w=w: e.tensor_tensor(out=GB[:, t0:t0 + w], in0=bk[:, 0:w], in1=SG[:, t0:t0 + w], op=ALU.mult))

        wup_dma(0)
        S.join()
        prep_w(0, 0)
        prep_w(0, 1)
        S.bar()
        wup_dma(1)
        for m in range(22):
            up_mm(m, 0, [0, 1, 2, 3, 4], [0, 1, 2, 3, 4])
            if m > 0:
                mult_ev(m - 1, [3, 4], [2, 3])
            if m % 2 == 0 and m // 2 < 8:
                S.dma_async("sync", wd_stage[(m // 2) % 2], w_down[li][:, (m // 2) * 128:(m // 2 + 1) * 128].rearrange("(k p) n -> p k n", p=128))
            S.bar()
            up_mm(m, 1, [0, 1, 2], [5, 6, 7])
            evac_up(m, 0, [0, 1, 2, 3, 4], [0, 1, 2, 3, 4], ["vector", "scalar"])
            if m > 0:
                S.dma_async("gpsimd", GD[(m - 1) * 128:m * 128, :], GB)
            S.bar()
            up_mm(m, 1, [3, 4], [0, 1])
            conv_mm(m, 0, [0, 1, 2], [2, 3, 4])
            evac_up(m, 1, [0, 1, 2], [5, 6, 7], ["vector", "scalar"])
            S.bar()
            conv_mm(m, 0, [3, 4], [5, 6])
            evac_up(m, 1, [3, 4], [0, 1], ["vector"])
            silu_ev(m, [0, 1, 2], [2, 3, 4])
            if m % 2 == 1 and m // 2 < 8:
                S.op("vector", lambda e, m=m: e.tensor_copy(out=WDv[:, :, (m // 2) * 128:(m // 2 + 1) * 128], in_=wd_stage[(m // 2) % 2]))
            S.join()
            conv_mm(m, 1, [0, 1, 2], [0, 1, 7])
            silu_ev(m, [3, 4], [5, 6])
            if m + 1 < 22:
                prep_w(m + 1, 0)
            S.bar()
            conv_mm(m, 1, [3, 4], [2, 3])
            mult_ev(m, [0, 1, 2], [0, 1, 7])
            if m + 1 < 22:
                prep_w(m + 1, 1)
            if m + 2 < 22:
                wup_dma(m + 2)
            S.bar()
        mult_ev(21, [3, 4], [2, 3])
        S.bar()
        S.dma_async("gpsimd", GD[21 * 128:22 * 128, :], GB)
        S.join()

        ckp("ffnup")
        resid_linear(GD, 22, 5, [ACTB, FB[:].bitcast(BF16)[:, 0:11264].rearrange("p (k t) -> p k t", t=512)], WB[:, 0:22 * D].rearrange("p (k n) -> p k n", n=D))

    except _Stop:
        pass
    S.bar()
    if dbg:
        DF = nc.dram_tensor("DBGF", [128, 32768], F32, kind="ExternalOutput").ap()
        DB = nc.dram_tensor("DBGB", [128, 40960], BF16, kind="ExternalOutput").ap()
        off = 0
        for t_, n_ in [(HRAW[:], 9216), (XTt[:], 4096), (TMPt[:], 4096), (FB[:], 9216), (STG[:], 4096), (RS[:], 512),
                       (MOD[:].rearrange("p q k n -> p (q k n)"), 96), (A1[:].rearrange("p k n -> p (k n)"), 16),
                       (A2[:].rearrange("p k n -> p (k n)"), 16), (W3[:].rearrange("p a g -> p (a g)"), 64),
                       (ARC[:].rearrange("p a g -> p (a g)"), 64), (AIC[:].rearrange("p a g -> p (a g)"), 64),
                       (CR[:], 32), (CI[:], 32), (DV[:], 32), (LR[:], 32), (LI[:], 32)]:
            S.dma("sync", DF[:, off:off + n_], t_)
            off += n_
        offb = 0
        for t_, n_ in [(WB, 22528), (OBt[:], 4096), (ACTBt[:], 11264)]:
            S.dma("gpsimd", DB[:, offb:offb + n_], t_)
            offb += n_
        S.bar()
    Xl = [XT[:, :, 0:128], XT[:, :, 128:256]]
    Ofin = [XT[:, :, 256:384], XT[:, :, 384:512]]

    def fin_load(b):
        t0 = LC + b * 128
        S.dma_async("sync", Xl[b % 2], XRES[:, t0:t0 + 128].rearrange("(k p) t -> p k t", p=128))

    fin_load(0)
    for b in range(16):
        Xi = Xl[b % 2]
        S.join()
        if b + 1 < 16:
            fin_load(b + 1)
        S.op("scalar", lambda e, Xi=Xi: e.activation(out=SQ[:, :, 0:128], in_=Xi, func=AF.Square))
        S.bar()
        rstd_from(SQ, 8, 128, 1.0 / D)
        for k in range(8):
            S.op("vector", lambda e, k=k, Xi=Xi: e.scalar_tensor_tensor(
                out=TMP[:, k, 0:128], in0=Xi[:, k, :], scalar=GFIN[:, k:k + 1], in1=RS[:, 0:128],
                op0=ALU.mult, op1=ALU.mult))
        S.bar()
        for k in range(8):
            S.op("tensor", lambda e, k=k: e.transpose(ps[2 + k // 4][:, (k % 4) * 128:(k % 4 + 1) * 128],
                                                       TMP[:, k, 0:128], IDENT[:]))
        S.bar()
        Oi = Ofin[b % 2]
        S.op("vector", lambda e, Oi=Oi: e.tensor_copy(out=Oi[:, 0:4, :], in_=ps[2][:].rearrange("p (k t) -> p k t", t=128)))
        S.op("scalar", lambda e, Oi=Oi: e.copy(out=Oi[:, 4:8, :], in_=ps[3][:].rearrange("p (k t) -> p k t", t=128)))
        S.bar()
        S.dma_async("gpsimd", out[b * 128:(b + 1) * 128, :].rearrange("t (k d) -> t k d", d=128), Oi)
    S.join()

    S.emit()
    es.close()
    return nc


_CONST = None


def _consts():
    ident = np.eye(128, dtype=np.float32)
    sp = np.arange(128) // 16
    m0 = (sp[None, :] >= sp[:, None]).astype(np.float32)
    m1 = (sp[None, :] <= sp[:, None]).astype(np.float32)
    return ident, np.stack([m0, m1])


def kernel(n_layers=4, **inputs):
    nc = build_nc(n_layers)
    ident, mask = _consts()
    in_maps = []
    for b in range(8):
        m = {}
        for k, v in inputs.items():
            v = np.asarray(v)
            if k in ("x", "c", "ctx"):
                m[k] = np.ascontiguousarray(v[b], dtype=np.float32)
            else:
                m[k] = np.ascontiguousarray(v, dtype=np.float32)
        m["ident"] = ident
        m["mask"] = mask
        in_maps.append(m)
    res = run_bass_kernel_spmd(nc, in_maps, core_ids=list(range(8)))
    return np.stack([np.asarray(r["out"], dtype=np.float32) for r in res.results], axis=0)
```
